# Optimizing a Trainium2 kernel written in Bass

```python
import jax, jax.numpy as jnp
from jax import lax
import numpy as np

D_MODEL = 2048
BATCH = 2
SEQ = 4096
DEPTH = 2
DEC_BATCH = 8
DEC_SEQ = 16
PAST_LEN = 1024

CHUNK = 64
N_HEADS = 16
HEAD_DIM = 128
D_INNER = N_HEADS * HEAD_DIM
Q_BLOCK = 128
EPS = 1e-6
N_MIXERS = 2
SCALE = HEAD_DIM ** -0.5

kernel_name = "fox_stickbreak_interleaved_streaming_step"


def rmsnorm(x, g):
    xf = x.astype(jnp.float32)
    y = xf * lax.rsqrt(jnp.mean(xf * xf, axis=-1, keepdims=True) + EPS)
    return (y * g.astype(jnp.float32)).astype(x.dtype)


def fox_block(q, cq, qpos, k, v, ck, kpos):
    s = jnp.einsum('bqhd,bkhd->bhqk', q, k, preferred_element_type=jnp.float32) * SCALE
    decay = jnp.transpose(cq, (0, 2, 1))[..., :, None] - jnp.transpose(ck, (0, 2, 1))[..., None, :]
    mask = kpos[None, :] <= qpos[:, None]
    p = jax.nn.softmax(jnp.where(mask, s + decay, -jnp.inf), axis=-1)
    return jnp.einsum('bhqk,bkhd->bqhd', p.astype(v.dtype), v)


def sb_block(q, qpos, k, v, kpos):
    z = jnp.einsum('bqhd,bkhd->bhqk', q, k, preferred_element_type=jnp.float32) * SCALE
    mask = kpos[None, :] < qpos[:, None]
    u = jnp.where(mask, jax.nn.log_sigmoid(-z), 0.0)
    after = lax.cumsum(u, axis=3, reverse=True) - u
    a = jnp.where(mask, jnp.exp(jax.nn.log_sigmoid(z) + after), 0.0)
    return jnp.einsum('bhqk,bkhd->bqhd', a.astype(v.dtype), v)


def sweep_query_blocks(block_fn, q_arrays):
    b, t = q_arrays[0].shape[:2]
    nb = t // Q_BLOCK
    blocks = tuple(jnp.moveaxis(a.reshape(b, nb, Q_BLOCK, *a.shape[2:]), 1, 0) for a in q_arrays)
    qpos = jnp.arange(t, dtype=jnp.int32).reshape(nb, Q_BLOCK)
    out = lax.map(lambda args: block_fn(args[0], args[1]), (blocks, qpos))
    return jnp.moveaxis(out, 0, 1).reshape(b, t, N_HEADS, HEAD_DIM)


def split_heads(a):
    return a.reshape(a.shape[0], a.shape[1], N_HEADS, HEAD_DIM)


def fox_layer(x, norm, w_in, b_f, w_out, past_k=None, past_v=None, past_logf=None):
    b, t, _ = x.shape
    proj = rmsnorm(x, norm) @ w_in
    q, k, v, gate = [proj[..., i * D_INNER:(i + 1) * D_INNER] for i in range(4)]
    q, k, v = split_heads(q), split_heads(k), split_heads(v)
    logf = jax.nn.log_sigmoid((proj[..., 4 * D_INNER:] + b_f).astype(jnp.float32))
    if past_k is None:
        c = jnp.cumsum(logf, axis=1)
        kpos = jnp.arange(t, dtype=jnp.int32)
        o = sweep_query_blocks(lambda blk, qp: fox_block(blk[0], blk[1], qp, k, v, c, kpos), (q, c))
    else:
        p_len = past_k.shape[1]
        k_all = jnp.concatenate([past_k.astype(k.dtype), k], axis=1)
        v_all = jnp.concatenate([past_v.astype(v.dtype), v], axis=1)
        c_all = jnp.cumsum(jnp.concatenate([past_logf.astype(jnp.float32), logf], axis=1), axis=1)
        qpos = p_len + jnp.arange(t, dtype=jnp.int32)
        kpos = jnp.arange(p_len + t, dtype=jnp.int32)
        o = fox_block(q, c_all[:, p_len:], qpos, k_all, v_all, c_all, kpos)
    y = o.reshape(b, t, D_INNER) * jax.nn.silu(gate)
    return x + y @ w_out, (k, v, logf)


def sb_layer(x, norm, w_in, w_out, past_k=None, past_v=None):
    b, t, _ = x.shape
    proj = rmsnorm(x, norm) @ w_in
    q, k, v, gate = [proj[..., i * D_INNER:(i + 1) * D_INNER] for i in range(4)]
    q, k, v = split_heads(q), split_heads(k), split_heads(v)
    if past_k is None:
        kpos = jnp.arange(t, dtype=jnp.int32)
        o = sweep_query_blocks(lambda blk, qp: sb_block(blk[0], qp, k, v, kpos), (q,))
    else:
        p_len = past_k.shape[1]
        k_all = jnp.concatenate([past_k.astype(k.dtype), k], axis=1)
        v_all = jnp.concatenate([past_v.astype(v.dtype), v], axis=1)
        qpos = p_len + jnp.arange(t, dtype=jnp.int32)
        kpos = jnp.arange(p_len + t, dtype=jnp.int32)
        o = sb_block(q, qpos, k_all, v_all, kpos)
    y = o.reshape(b, t, D_INNER) * jax.nn.silu(gate)
    return x + y @ w_out, (k, v)


def setup_inputs(seed: int = 0) -> dict:
    key = jax.random.key(seed)
    ks = jax.random.split(key, 16)
    f32 = jnp.float32
    n = lambda k, s: jax.random.normal(k, s, dtype=f32)
    cshape = (DEC_BATCH, PAST_LEN, N_HEADS, HEAD_DIM)
    return {
        "x_prompt": n(ks[0], (BATCH, SEQ, D_MODEL)),
        "x_sample": n(ks[1], (DEC_BATCH, DEC_SEQ, D_MODEL)),
        "cache_fox_k": n(ks[2], cshape),
        "cache_fox_v": n(ks[3], cshape),
        "cache_fox_logf": jax.nn.log_sigmoid(1.0 + n(ks[4], (DEC_BATCH, PAST_LEN, N_HEADS))),
        "cache_sb_k": n(ks[5], cshape),
        "cache_sb_v": n(ks[6], cshape),
        "norm_0": 1.0 + 0.02 * n(ks[7], (D_MODEL,)),
        "w_in_0": n(ks[8], (D_MODEL, 4 * D_INNER + N_HEADS)) * D_MODEL ** -0.5,
        "b_f_0": 1.0 + 0.1 * n(ks[9], (N_HEADS,)),
        "w_out_0": n(ks[10], (D_INNER, D_MODEL)) * D_INNER ** -0.5,
        "norm_1": 1.0 + 0.02 * n(ks[11], (D_MODEL,)),
        "w_in_1": n(ks[12], (D_MODEL, 4 * D_INNER)) * D_MODEL ** -0.5,
        "w_out_1": n(ks[13], (D_INNER, D_MODEL)) * D_INNER ** -0.5,
        "norm_f": 1.0 + 0.02 * n(ks[14], (D_MODEL,)),
    }


def reference(x_prompt, x_sample, cache_fox_k, cache_fox_v, cache_fox_logf, cache_sb_k, cache_sb_v,
              norm_0, w_in_0, b_f_0, w_out_0, norm_1, w_in_1, w_out_1, norm_f):
    xp, xs = x_prompt, x_sample
    for i in range(DEPTH):
        if i % N_MIXERS == 0:
            xp, (pk_f, pv_f, pl_f) = fox_layer(xp, norm_0, w_in_0, b_f_0, w_out_0)
            xs, (sk_f, sv_f, sl_f) = fox_layer(xs, norm_0, w_in_0, b_f_0, w_out_0,
                                               cache_fox_k, cache_fox_v, cache_fox_logf)
        else:
            xp, (pk_s, pv_s) = sb_layer(xp, norm_1, w_in_1, w_out_1)
            xs, (sk_s, sv_s) = sb_layer(xs, norm_1, w_in_1, w_out_1, cache_sb_k, cache_sb_v)
    y_prompt = rmsnorm(xp, norm_f)
    y_sample = rmsnorm(xs, norm_f)
    return (y_prompt, y_sample, pk_f, pv_f, pl_f, pk_s, pv_s, sk_f, sv_f, sl_f, sk_s, sv_s)
```

```python
import numpy as np
import concourse.bass as bass
import concourse.mybir as mybir
from concourse.bass_utils import run_bass_kernel_spmd

F32 = mybir.dt.float32
BF16 = mybir.dt.bfloat16
AF = mybir.ActivationFunctionType
ALU = mybir.AluOpType

D = 2048
T = 4096
TS = 64
TT = T + TS
NB = 33
PAST = 1024
SCALE = 128 ** -0.5
EPS = 1e-6
GROUPS = [[0, 1, 2, 3], [4, 5, 6, 7]]
ENG = ("sp", "act", "dve", "pool", "pe")
FOXW = 1
SBW = 4

C_U, C_E127, C_UB16, C_LS, C_ONES, C_MLT, C_MLTC, C_MS, C_MSC, C_ID, C_MLE, C_M64 = [128 * i for i in range(12)]
NCST = 128 * 12


class Sched:
    def __init__(self):
        self.prog = {e: [] for e in ENG}
        self.cnt = {}
        self.waited = {}
        self.lastw = {}
        self.readers = {}

    def _need(self, eng, events):
        for sn, val in events:
            if sn.startswith("d_"):
                val = self.cnt[sn]
            if sn == "pe" and eng == "pe":
                continue
            key = (eng, sn)
            if self.waited.get(key, 0) >= val:
                continue
            self.waited[key] = val
            self.prog[eng].append(("wait", sn, val))

    def op(self, eng, fn, reads=(), writes=(), sig=True, chan=None, cc=None):
        ps_reads = [r for r in reads if r.startswith("ps") and r not in writes]
        if ps_reads:
            writes = list(writes) + ps_reads
        ev = []
        for r in reads:
            if r in self.lastw:
                ev.append(self.lastw[r])
        for w in writes:
            if w in self.lastw:
                ev.append(self.lastw[w])
            ev.extend(self.readers.get(w, {}).items())
        self._need(eng, ev)
        if cc is not None:
            sn = "c_" + cc
            self.cnt[sn] = 1
            myev = (sn, 1)
            self.prog[eng].append(("op", fn, sn, None))
        elif chan is not None:
            sn = "d_" + chan
            self.cnt[sn] = self.cnt.get(sn, 0) + 16
            myev = (sn, self.cnt[sn])
            self.prog[eng].append(("op", fn, sn, 16))
        elif sig:
            self.cnt[eng] = self.cnt.get(eng, 0) + 1
            myev = (eng, self.cnt[eng])
            self.prog[eng].append(("op", fn, eng, 1))
        else:
            myev = (eng, self.cnt.get(eng, 0) + 1)
            self.prog[eng].append(("op", fn, None, 0))
        for r in reads:
            d = self.readers.setdefault(r, {})
            d[myev[0]] = max(d.get(myev[0], 0), myev[1])
        for w in writes:
            self.lastw[w] = myev
            self.readers[w] = {}


def build_program():
    nc = bass.Bass("TRN2", target_bir_lowering=False)
    S = Sched()

    def din(name, shape, dt=F32):
        return nc.dram_tensor(name, shape, dt, kind="ExternalInput").ap()

    def dout(name, shape, dt=F32):
        return nc.dram_tensor(name, shape, dt, kind="ExternalOutput").ap()

    xc0 = din("xc0", [TT, 512])
    nwd = [din("nw0", [128, 512]), din("nw1", [128, 512]), din("nwf", [128, 512])]
    wind = [din("win0", [D, 2052]), din("win1", [D, 2048])]
    woutd = [din("wout0", [D, 512]), din("wout1", [D, 512])]
    bfd = din("bfb", [128, 4])
    cstd = din("cst", [128, NCST])
    cfk = din("cfk", [4, PAST, 512])
    cfv = din("cfv", [4, PAST, 512])
    csk = din("csk", [4, PAST, 512])
    csv = din("csv", [4, PAST, 512])
    cfl = din("cfl", [4, PAST, 4])

    yout = dout("yout", [TT, 512])
    kvout = [[dout("kf", [TT, 512]), dout("vf", [TT, 512])], [dout("ks", [TT, 512]), dout("vs", [TT, 512])]]
    lfout = dout("lf", [TT, 4])

    PW = [1024, 1024, 1024, 1024, TS]
    xT_loc = [nc.dram_tensor("xT_loc%d" % p, [512, PW[p]], BF16) for p in range(5)]
    xT_all = [nc.dram_tensor("xT_all%d" % p, [D, PW[p]], BF16) for p in range(5)]
    yT_loc = [nc.dram_tensor("yT_loc%d" % p, [512, PW[p]], BF16) for p in range(5)]
    yT_all = [nc.dram_tensor("yT_all%d" % p, [D, PW[p]], BF16) for p in range(5)]

    def piece_of(tb):
        return (tb // 8, (tb % 8) * 128) if tb < 32 else (4, 0)
    ssq_loc = nc.dram_tensor("ssq_loc", [128, NB], F32)
    ssq_all = nc.dram_tensor("ssq_all", [512, NB], F32)
    xres = [None, nc.dram_tensor("x1s", [TT, 512], F32), nc.dram_tensor("x2s", [TT, 512], F32)]

    from contextlib import ExitStack
    es = ExitStack()

    def sb(name, shape, dt):
        return es.enter_context(nc.sbuf_tensor(name, shape, dt))

    def ps(name, shape, dt):
        return es.enter_context(nc.psum_tensor(name, shape, dt))

    with es:
        cstf = sb("cstf", [128, C_ID + 128], F32)
        cstb = sb("cstb", [128, 3 * 128], BF16)
        zeros = sb("zeros", [128, 512], BF16)
        nwt = sb("nwt", [128, 512], F32)
        nw = [nwt, nwt, nwt]
        bfb = sb("bfbs", [128, 4], F32)
        w_in = sb("w_in", [128, 16, 2052], BF16)
        regB = sb("regB", [128, 4 * T + 32 * 4 * 130], BF16)
        actT = [sb("actT0", [128, 16, 256], BF16), sb("actT1", [128, 16, 256], BF16)]
        ssq_part = sb("ssq_part", [128, NB], F32)
        ssq4 = sb("ssq4", [128, 4, NB], F32)
        ssum = sb("ssum", [128, NB], F32)
        rstd = sb("rstd", [128, NB], F32)
        nrstd = sb("nrstd", [128, NB], F32)
        hrstd = sb("hrstd", [128, NB], F32)
        cpall = sb("cpall", [128, NB, 8], F32)
        lfo = sb("lfo", [128, NB, 4], F32)
        lps = sb("lps", [128, 4], F32)
        xl = sb("xl", [128, 4], F32)
        el = sb("el", [128, 4], F32)
        q_tok = sb("q_tok", [128, 512], BF16)
        k_tok = sb("k_tok", [128, 512], BF16)
        kf32b_big = sb("kf32b", [128, 516], F32)
        kf32 = [sb("kf32a", [128, 512], F32)[:, :], kf32b_big[:, 0:512]]
        vf32 = [sb("vf32a", [128, 512], F32), sb("vf32b", [128, 512], F32)]
        gsil = [sb("gsil0", [128, 512], F32), sb("gsil1", [128, 512], F32)]
        gtmp = sb("gtmp", [128, 512], F32)
        qT = sb("qT", [128, 4, 256], BF16)
        ybuf = [sb("y0", [128, 512], BF16), sb("y1", [128, 512], BF16)]
        yTs = [sb("yTs0", [128, 4, 128], BF16), sb("yTs1", [128, 4, 128], BF16)]
        Pball = sb("Pball", [128, 6, 256], BF16)
        Pb = [Pball[:, i, :] for i in range(6)]
        biasT = sb("biasT", [128, 32, 4], F32)
        rden = sb("rden", [128, 4], F32)
        thb = [sb("th%d" % i, [128, 512], F32) for i in range(3)] + [gtmp]
        Rext = [sb("Rx%d" % i, [128, 516], F32) for i in range(3)] + [kf32b_big]
        ab = [sb("a%d" % i, [128, 512], BF16) for i in range(3)] + [k_tok]
        aTb = [sb("aT0", [128, 4, 128], BF16), sb("aT1", [128, 4, 128], BF16),
               Pball[:, 0:2, :].rearrange("p a (b t) -> p (a b) t", t=128),
               q_tok[:, :].rearrange("p (j t) -> p j t", t=128)]
        xcb = kf32
        xnb = vf32
        xtb = q_tok
        sqj = gtmp
        xTs = yTs
        Pw = [sb("Pw%d" % i, [128, 64], BF16) for i in range(4)]
        Pn = sb("Pn", [64, 64], BF16)
        clf = sb("clf", [128, 8, 16], F32)
        sufb = sb("sufb", [128, 8, 16], F32)
        cpn = sb("cpn", [64, 4], F32)
        aTw = [sb("aTw%d" % i, [128, 8, 64], BF16) for i in range(4)]
        aTn = sb("aTn", [64, 64], BF16)
        Vn_t = sb("Vn_t", [128, 520], BF16)
        carry0 = sb("carry0", [64, 4], F32)
        cbias = sb("cbias", [128, 2], F32)

        psA = ps("psA", [128, 512], F32)
        psB = ps("psB", [128, 512], F32)
        psC = ps("psC", [128, 512], F32)
        psD = ps("psD", [128, 512], F32)
        psE = ps("psE", [128, 512], F32)
        psF = ps("psF", [128, 512], F32)
        psG = ps("psG", [128, 512], F32)
        pst = ps("pst", [128, 1024], BF16)

        KT_OFF = 0
        V_OFF = 4 * T

        def kT(h, lo, hi):
            return regB[:, KT_OFF + h * T + lo: KT_OFF + h * T + hi]

        def Vaug(blk, h, n):
            o = V_OFF + (blk * 4 + h) * 130
            return regB[:, o:o + n]

        def Vaug_blk(blk):
            o = V_OFF + blk * 4 * 130
            return regB[:, o:o + 520].rearrange("p (h e) -> p h e", e=130)

        def w_out():
            o = 4096 + 8 * 520 + 4096
            return regB[:, o:o + 16 * 512].rearrange("p (c n) -> p c n", n=512)

        PST_ALL = ["pstbank"]
        pst_full = pst
        psE_bf = psE[:, :].bitcast(BF16)
        pst_f32 = pst[:, :].bitcast(F32)
        psG_bf = psG[:, :].bitcast(BF16)
        REGB_ALL = ["kT%d_%d" % (h, b) for h in range(4) for b in range(32)] + ["V%d" % b for b in range(32)]

        ident = cstb[:, 0:128]
        maskLE = cstb[:, 128:256]
        mask64 = cstb[0:64, 256:320]

        def cf(c0, n=128, rows=128):
            return cstf[0:rows, c0:c0 + n]

        S.op("sp", lambda e: e.dma_start(out=cstf[:], in_=cstd[:, 0:C_ID + 128]), writes=["cstf"], chan="cst")
        S.op("pool", lambda e: e.dma_start(out=cstb[:], in_=cstd[:, C_ID:C_ID + 384]), writes=["cstb"], chan="cstb")
        def load_nw(i):
            S.op("sp", lambda e: e.dma_start(out=nwt[:], in_=nwd[i][:, :]), writes=["nw"], chan="cst")
        load_nw(0)
        S.op("sp", lambda e: e.dma_start(out=bfb[:], in_=bfd[:, :]), writes=["bfb"], chan="cst")
        S.op("dve", lambda e: e.memset(zeros[:], 0.0), writes=["zeros"])
        S.op("dve", lambda e: e.memset(cbias[:, 0:1], EPS), writes=["cbias"])
        S.op("dve", lambda e: e.memset(cbias[:, 1:2], 1.0), writes=["cbias"])
        S.op("dve", lambda e: e.memset(ssq_part[:], 1.0), writes=["ssq_part"])
        S.op("dve", lambda e: e.memset(cpall[:], 0.0), writes=["cpall"])
        S.op("dve", lambda e: e.memset(lfo[:], 0.0), writes=["lfo"])
        for i in range(4):
            S.op("pool", lambda e, i=i: e.memset(Pw[i][:], 0.0), writes=["Pw%d" % i])
            S.op("pool", lambda e, i=i: e.memset(aTw[i][:], 0.0), writes=["aTw%d" % i])

        if False:
            for nm_, ap_ in [("win0", wind[0][0:128, 0:512]), ("win1", wind[1][0:128, 0:512]), ("wout0", woutd[0][0:128, :]),
                             ("wout1", woutd[1][0:128, :]), ("cfk", cfk[0][0:128, :]), ("cfv", cfv[0][0:128, :]),
                             ("csk", csk[0][0:128, :]), ("csv", csv[0][0:128, :])]:
                S.op("sp", lambda e, ap_=ap_: e.dma_start(out=gtmp[:], in_=ap_), writes=["gtmp"], chan="dbg")
            S.op("sp", lambda e: e.dma_start(out=gtmp[:, 0:4], in_=cfl[0][0:128, :]), writes=["gtmp"], chan="dbg")

        def load_w_in(l):
            ncol = 2052 if l == 0 else 2048
            src = wind[l].rearrange("(c p) n -> p c n", p=128)
            for c in range(16):
                yield
                si = c % 3
                o = 20544 + si * 4104
                stg = regB[:, o:o + 4104].bitcast(F32)
                S.op("sp", lambda e, c=c, stg=stg: e.dma_start(out=stg[:, 0:ncol], in_=src[:, c, :]),
                     writes=["wst%d" % si] + (REGB_ALL + ["sVc%d" % q for q in range(4)] + ["skT%d" % q for q in range(4)]
                                              if c < 3 else []), chan="wst%d" % si)
                if c % 2 == 0:
                    S.op("act", lambda e, c=c, stg=stg: e.activation(out=w_in[:, c, 0:ncol], in_=stg[:, 0:ncol], func=AF.Identity),
                         reads=["wst%d" % si] + (REGB_ALL if c >= 13 else []), writes=["w_in"])
                else:
                    S.op("dve", lambda e, c=c, stg=stg: e.tensor_copy(out=w_in[:, c, 0:ncol], in_=stg[:, 0:ncol]),
                         reads=["wst%d" % si] + (REGB_ALL if c >= 13 else []), writes=["w_in"])

        def load_w_out(l):
            src = woutd[l].rearrange("(c p) n -> p c n", p=128)
            wv = w_out()
            for c in range(0, 16, 4):
                S.op("pool", lambda e, c=c: e.dma_start(out=wv[:, c:c + 4, :], in_=src[:, c:c + 4, :]),
                     writes=["w_out"] + REGB_ALL + ["sVc%d" % q for q in range(4)] + ["skT%d" % q for q in range(4)], chan="wout")

        def blk_rows(tb):
            return 128 if tb < 32 else TS

        cnt = {"xts": 0, "tr": 0}

        def norm_prep(xblk, xres_name, tb, rows, nwi, want_xT, want_gather=True):
            S.op("act", lambda e: e.activation(out=sqj[0:rows, :], in_=xblk[0:rows, :], func=AF.Square,
                                               accum_out=ssq_part[0:rows, tb:tb + 1]),
                 reads=[xres_name], writes=["gtmp", "ssq_part"])
            if not want_xT:
                return
            S.op("dve", lambda e: e.tensor_tensor(out=xtb[0:rows, :], in0=xblk[0:rows, :], in1=nw[nwi][0:rows, :],
                                                  op=ALU.mult),
                 reads=[xres_name, "nw"], writes=["q_tok"])
            for c in range(4):
                S.op("pe", lambda e, c=c: e.transpose(out=pst[:, c * 128:c * 128 + rows],
                                                      in_=xtb[0:rows, c * 128:(c + 1) * 128],
                                                      identity=ident[0:rows, 0:rows]),
                     reads=["q_tok", "cstb"], writes=PST_ALL, sig=(c == 3))
            i = cnt["tr"] % 2
            cnt["tr"] += 1
            pv = pst[:, 0:512].rearrange("p (c t) -> p c t", t=128)
            S.op("act", lambda e: e.activation(out=xTs[i][:, :, 0:rows], in_=pv[:, :, 0:rows], func=AF.Identity),
                 reads=PST_ALL, writes=["yTs%d" % i])
            pc, c0 = piece_of(tb)
            dst = xT_loc[pc].ap().rearrange("(c p) t -> p c t", p=128)
            S.op("sp", lambda e: e.dma_start(out=dst[:, :, c0:c0 + rows], in_=xTs[i][:, :, 0:rows]),
                 reads=["yTs%d" % i], writes=["xT_loc%d" % pc], chan="yTs%d" % i)
            if want_gather and (tb % 8 == 7 or tb == 32):
                gather(xT_loc[pc], xT_all[pc], "xT", pc)

        def rstd_from_ssq(tag):
            S.op("sp", lambda e: e.dma_start(out=ssq_loc.ap(), in_=ssq_part[:]), reads=["ssq_part"],
                 writes=["ssq_loc"], chan="ssq")
            S.op("pool", lambda e: e.collective_compute("AllGather", ALU.bypass, replica_groups=GROUPS,
                                                        ins=[ssq_loc.ap().opt()], outs=[ssq_all.ap().opt()]),
                 reads=["ssq_loc"], writes=["ssq_all"], cc="ssq" + tag)
            S.op("sp", lambda e: e.dma_start(out=ssq4[:], in_=ssq_all.ap().rearrange("(r p) b -> p r b", p=128)),
                 reads=["ssq_all"], writes=["ssq4"], chan="ssq")
            S.op("dve", lambda e: e.tensor_tensor(out=ssum[:], in0=ssq4[:, 0, :], in1=ssq4[:, 1, :], op=ALU.add),
                 reads=["ssq4"], writes=["ssum"])
            S.op("dve", lambda e: e.tensor_tensor(out=ssum[:], in0=ssum[:], in1=ssq4[:, 2, :], op=ALU.add),
                 reads=["ssq4", "ssum"], writes=["ssum"])
            S.op("dve", lambda e: e.tensor_tensor(out=ssum[:], in0=ssum[:], in1=ssq4[:, 3, :], op=ALU.add),
                 reads=["ssq4", "ssum"], writes=["ssum"])
            S.op("act", lambda e: e.activation(out=ssum[:], in_=ssum[:], func=AF.Ln, scale=1.0 / D, bias=cbias[:, 0:1]),
                 reads=["ssum", "cbias"], writes=["ssum"])
            S.op("act", lambda e: e.activation(out=rstd[:], in_=ssum[:], func=AF.Exp, scale=-0.5),
                 reads=["ssum"], writes=["rstd"])
            S.op("dve", lambda e: e.tensor_scalar(out=nrstd[:], in0=rstd[:], scalar1=-1.0, scalar2=None, op0=ALU.mult),
                 reads=["rstd"], writes=["nrstd"])
            S.op("dve", lambda e: e.tensor_scalar(out=hrstd[:], in0=rstd[:], scalar1=0.5, scalar2=None, op0=ALU.mult),
                 reads=["rstd"], writes=["hrstd"])
            S.op("dve", lambda e: e.memset(ssq_part[:], 1.0), reads=[], writes=["ssq_part"])

        gcount = {"n": 0}

        def gather(loc, allt, rname, pc):
            gcount["n"] += 1
            S.op("pool", lambda e: e.collective_compute("AllGather", ALU.bypass, replica_groups=GROUPS,
                                                        ins=[loc.ap().opt()], outs=[allt.ap().opt()]),
                 reads=["%s_loc%d" % (rname, pc)], writes=["%s_all%d" % (rname, pc)], cc="g%d" % gcount["n"])

        wq = load_w_in(0)

        def n0_load(tb):
            rows = blk_rows(tb)
            i = tb % 2
            S.op("sp", lambda e: e.dma_start(out=xcb[i][0:rows, :], in_=xc0[tb * 128:tb * 128 + rows, :]),
                 writes=["kf32%d" % i], chan="kf32%d" % i)

        for tb in range(NB):
            rows = blk_rows(tb)
            i = tb % 2
            next(wq, None)
            if tb == 0:
                n0_load(0)
            if tb + 1 < NB:
                n0_load(tb + 1)
            norm_prep(xcb[i], "kf32%d" % i, tb, rows, 0, True)
        for _ in wq:
            pass
        rstd_from_ssq("0")

        def load_act(slot, src_all, rname, c0, ncols):
            pc, lc = (c0 // 1024, c0 % 1024) if c0 < T else (4, 0)
            src = src_all[pc].ap().rearrange("(c p) t -> p c t", p=128)
            S.op("sp", lambda e: e.dma_start(out=actT[slot][:, :, 0:ncols], in_=src[:, :, lc:lc + ncols]),
                 reads=["%s_all%d" % (rname, pc)], writes=["actT%d" % slot], chan="actT%d" % slot)

        def transposes_to(src_tok, src_name, rows, dst_fn, dst_names, evac_eng):
            for h in range(4):
                S.op("pe", lambda e, h=h: e.transpose(out=pst[:, h * 128:h * 128 + rows],
                                                      in_=src_tok[0:rows, h * 128:(h + 1) * 128],
                                                      identity=ident[0:rows, 0:rows]),
                     reads=[src_name, "cstb"], writes=PST_ALL, sig=(h == 3))
            pv4 = pst[:, 0:512].rearrange("p (h t) -> p h t", t=128)
            if evac_eng == "act":
                S.op("act", lambda e: e.activation(out=dst_fn(None), in_=pv4[:, :, 0:rows], func=AF.Identity),
                     reads=PST_ALL, writes=dst_names)
            else:
                S.op("dve", lambda e: e.tensor_copy(out=dst_fn(None), in_=pv4[:, :, 0:rows]), reads=PST_ALL, writes=dst_names)

        def project_block(l, slot, sub, tb, rows, kT_dst, kT_names, v_dst_fn, v_names, qcol0):
            banks = [psA, psB, psC, psD]
            bn = ["psA", "psB", "psC", "psD"]
            for c in range(16):
                lhsT = actT[slot][:, c, sub * 128: sub * 128 + rows]
                for j in range(4):
                    S.op("pe", lambda e, c=c, j=j, lhsT=lhsT: e.matmul(banks[j][0:rows, :], lhsT=lhsT,
                                                                        rhs=w_in[:, c, j * 512:(j + 1) * 512],
                                                                        start=(c == 0), stop=(c == 15)),
                         reads=["actT%d" % slot, "w_in"], writes=[bn[j]], sig=(c == 15))
                if l == 0:
                    S.op("pe", lambda e, c=c, lhsT=lhsT: e.matmul(psE[0:rows, 0:4], lhsT=lhsT,
                                                                   rhs=w_in[:, c, 2048:2052],
                                                                   start=(c == 0), stop=(c == 15)),
                         reads=["actT%d" % slot, "w_in"], writes=["psE"], sig=(c == 15))
            rs = rstd[0:rows, tb:tb + 1]
            nrs = nrstd[0:rows, tb:tb + 1]
            i = tb % 2
            S.op("dve", lambda e: e.tensor_scalar(out=q_tok[0:rows, :], in0=psA[0:rows, :], scalar1=rs, scalar2=None,
                                                  op0=ALU.mult),
                 reads=["psA", "rstd"], writes=["q_tok"])
            S.op("act", lambda e: e.activation(out=kf32[i][0:rows, :], in_=psB[0:rows, :], func=AF.Identity, scale=rs),
                 reads=["psB", "rstd"], writes=["kf32%d" % i])
            S.op("dve", lambda e: e.tensor_scalar(vf32[i][0:rows, :], psC[0:rows, :], rs, None, ALU.mult),
                 reads=["psC", "rstd"], writes=["vf32%d" % i])
            S.op("sp", lambda e: e.dma_start(out=kvout[l][0][tb * 128:tb * 128 + rows, :], in_=kf32[i][0:rows, :]),
                 reads=["kf32%d" % i], writes=[], chan="kf32%d" % i)
            S.op("sp", lambda e: e.dma_start(out=kvout[l][1][tb * 128:tb * 128 + rows, :], in_=vf32[i][0:rows, :]),
                 reads=["vf32%d" % i], writes=[], chan="vf32%d" % i)
            S.op("act", lambda e: e.activation(out=k_tok[0:rows, :], in_=psB[0:rows, :], func=AF.Identity, scale=rs),
                 reads=["psB", "rstd"], writes=["k_tok"])
            vd = v_dst_fn()
            S.op("dve", lambda e: e.tensor_copy(out=vd[0:rows, :, 0:128],
                                                in_=vf32[i][0:rows, :].rearrange("p (h d) -> p h d", d=128)),
                 reads=["vf32%d" % i], writes=v_names)
            S.op("pool", lambda e: e.memset(vd[0:rows, :, 128:129], 1.0), reads=[], writes=v_names)
            if l == 0:
                S.op("act", lambda e: e.activation(out=gtmp[0:rows, :], in_=psD[0:rows, :], func=AF.Exp, scale=nrs),
                     reads=["psD", "nrstd"], writes=["gtmp"])
                S.op("act", lambda e: e.activation(out=gtmp[0:rows, :], in_=gtmp[0:rows, :], func=AF.Ln, bias=cbias[0:rows, 1:2]),
                     reads=["gtmp", "cbias"], writes=["gtmp"])
                S.op("act", lambda e: e.activation(out=gtmp[0:rows, :], in_=gtmp[0:rows, :], func=AF.Exp, scale=-1.0),
                     reads=["gtmp"], writes=["gtmp"])
            else:
                S.op("act", lambda e: e.activation(out=gtmp[0:rows, :], in_=psD[0:rows, :], func=AF.Tanh,
                                                   scale=hrstd[0:rows, tb:tb + 1]),
                     reads=["psD", "hrstd"], writes=["gtmp"])
                S.op("dve", lambda e: e.tensor_scalar(gtmp[0:rows, :], gtmp[0:rows, :], 0.5, 0.5, ALU.mult, ALU.add),
                     reads=["gtmp"], writes=["gtmp"])
            S.op("dve", lambda e: e.scalar_tensor_tensor(out=gsil[sub][0:rows, :], in0=psD[0:rows, :], scalar=rs,
                                                         in1=gtmp[0:rows, :], op0=ALU.mult, op1=ALU.mult),
                 reads=["psD", "rstd", "gtmp"], writes=["gsil%d" % sub])
            if l == 0:
                S.op("dve", lambda e: e.scalar_tensor_tensor(out=xl[0:rows, :], in0=psE[0:rows, 0:4], scalar=rs,
                                                             in1=bfb[0:rows, :], op0=ALU.mult, op1=ALU.add),
                     reads=["psE", "rstd", "bfb"], writes=["xl"])
                S.op("act", lambda e: e.activation(out=el[0:rows, :], in_=xl[0:rows, :], func=AF.Exp, scale=-1.0),
                     reads=["xl"], writes=["el"])
                S.op("act", lambda e: e.activation(out=lps[0:rows, :], in_=el[0:rows, :], func=AF.Ln, bias=cbias[0:rows, 1:2]),
                     reads=["el", "cbias"], writes=["lps"])
                S.op("pool", lambda e: e.tensor_scalar(out=lfo[0:rows, tb, :], in0=lps[0:rows, :], scalar1=-1.0,
                                                       scalar2=0.0, op0=ALU.mult, op1=ALU.add),
                     reads=["lps"], writes=["lfo"])
            transposes_to(q_tok, "q_tok", rows, lambda h: qT[:, :, qcol0:qcol0 + rows], ["qT"], "dve")
            transposes_to(k_tok, "k_tok", rows, kT_dst, kT_names, "act")

        def cumsum_block(tb):
            first = (tb == 0)
            S.op("pe", lambda e: e.matmul(psE[:, 8:12], lhsT=cf(C_U), rhs=lps[:, 0:4], start=True, stop=first),
                 reads=["cstf", "lps"], writes=["psE"], sig=first)
            if not first:
                S.op("pe", lambda e: e.matmul(psE[:, 8:12], lhsT=cf(C_E127), rhs=cpall[:, tb - 1, 0:4], start=False,
                                              stop=True),
                     reads=["cstf", "cpall"], writes=["psE"], sig=False)
                S.op("pe", lambda e: e.matmul(psE[:, 12:16], lhsT=cf(C_E127), rhs=cpall[:, tb - 1, 0:4], start=True,
                                              stop=True),
                     reads=["cstf", "cpall"], writes=["psE"])
                S.op("dve", lambda e: e.tensor_copy(out=cpall[:, tb, 0:8], in_=psE[:, 8:16]), reads=["psE"],
                     writes=["cpall"])
            else:
                S.op("dve", lambda e: e.tensor_copy(out=cpall[:, tb, 0:4], in_=psE[:, 8:12]), reads=["psE"],
                     writes=["cpall"])

        sc_banks = [(psA, "psA"), (psB, "psB")]
        o_banks = [((psC, "psC"), (psD, "psD")), ((psF, "psF"), (psG, "psG"))]

        def y_out(tbs, rows_l):
            for sub, tb in enumerate(tbs):
                rows = rows_l[sub]
                for c in range(4):
                    S.op("pe", lambda e, c=c, sub=sub, rows=rows: e.transpose(out=pst[:, c * 128:c * 128 + rows],
                                                                               in_=ybuf[sub][0:rows, c * 128:(c + 1) * 128],
                                                                               identity=ident[0:rows, 0:rows]),
                         reads=["y%d_%d" % (sub, hh) for hh in range(4)] + ["cstb"], writes=PST_ALL, sig=(c == 3))
                i = cnt["tr"] % 2
                cnt["tr"] += 1
                pv = pst[:, 0:512].rearrange("p (c t) -> p c t", t=128)
                S.op("act", lambda e, i=i, rows=rows, pv=pv: e.activation(out=yTs[i][:, :, 0:rows], in_=pv[:, :, 0:rows],
                                                                           func=AF.Identity),
                     reads=PST_ALL, writes=["yTs%d" % i])
                pc, c0 = piece_of(tb)
                dst = yT_loc[pc].ap().rearrange("(c p) t -> p c t", p=128)
                S.op("sp", lambda e, i=i, rows=rows, c0=c0, dst=dst: e.dma_start(out=dst[:, :, c0:c0 + rows],
                                                                                  in_=yTs[i][:, :, 0:rows]),
                     reads=["yTs%d" % i], writes=["yT_loc%d" % pc], chan="yTs%d" % i)
                if tb % 8 == 7 or tb == 32:
                    gather(yT_loc[pc], yT_all[pc], "yT", pc)

        pcount = {"p": 0, "sc": 0, "ch": 0}

        def run_streams(makers, width):
            pending = list(makers)
            active = []
            free = list(range(width))
            eng_free = {}
            now = 0.0
            while pending or active:
                while pending and free:
                    sl = free.pop(0)
                    g = pending.pop(0)(sl)
                    try:
                        nxt = next(g)
                    except StopIteration:
                        free.append(sl)
                        continue
                    active.append({"g": g, "sl": sl, "ready": now, "nxt": nxt})
                if not active:
                    continue
                best = min(active, key=lambda a: max(eng_free.get(a["nxt"][0], 0.0), a["ready"]))
                eng, dur = best["nxt"]
                start = max(eng_free.get(eng, 0.0), best["ready"])
                end = start + dur
                eng_free[eng] = end
                best["ready"] = end + 0.25
                try:
                    best["nxt"] = next(best["g"])
                except StopIteration:
                    active.remove(best)
                    free.append(best["sl"])
                    now = end

        def fox_stream(slot, Q, h):
            nkb = 2 * Q + 2
            ob = o_banks[slot]
            fsc = [(psA, "psA"), (psB, "psB"), (psE, "psE"), (pst_f32, "pstbank"), (psF, "psF"), (psG, "psG")]
            for kb0 in range(0, nkb, 6):
                kbs = list(range(kb0, min(kb0 + 6, nkb)))
                yield ("pe", 0.5)
                for i, kb in enumerate(kbs):
                    qlo = 128 if kb == nkb - 1 else 0
                    scb, scn = fsc[i]
                    S.op("pe", lambda e, kb=kb, qlo=qlo, scb=scb: e.matmul(
                        scb[:, qlo:256], lhsT=kT(h, kb * 128, (kb + 1) * 128), rhs=qT[:, h, qlo:256], start=True, stop=True),
                        reads=["kT%d_%d" % (h, kb), "qT"], writes=[scn])
                yield ("act", 1.8)
                for i, kb in enumerate(kbs):
                    qlo = 128 if kb == nkb - 1 else 0
                    scb, scn = fsc[i]
                    S.op("act", lambda e, kb=kb, qlo=qlo, i=i, scb=scb: e.activation(
                        out=Pb[i][:, qlo:256], in_=scb[:, qlo:256], func=AF.Exp, scale=SCALE, bias=biasT[:, kb, h:h + 1]),
                        reads=[scn, "biasT"], writes=["P%d" % i])
                    if kb >= nkb - 2:
                        dq = 0 if kb == nkb - 2 else 128
                        S.op("pool", lambda e, i=i, dq=dq: e.tensor_tensor(out=Pb[i][:, dq:dq + 128],
                                                                           in0=Pb[i][:, dq:dq + 128], in1=maskLE,
                                                                           op=ALU.mult),
                             reads=["P%d" % i, "cstb"], writes=["P%d" % i])
                yield ("pe", 1.2)
                for i, kb in enumerate(kbs):
                    for sub in range(2):
                        if kb == nkb - 1 and sub == 0:
                            continue
                        last = (kb == nkb - 2) if sub == 0 else (kb == nkb - 1)
                        S.op("pe", lambda e, kb=kb, sub=sub, i=i, last=last: e.matmul(
                            ob[sub][0][:, 0:129], lhsT=Pb[i][:, sub * 128:(sub + 1) * 128], rhs=Vaug(kb, h, 129),
                            start=(kb == 0), stop=last),
                            reads=["P%d" % i, "V%d" % kb], writes=[ob[sub][1]], sig=(sub == 1 or kb == nkb - 2))
            yield ("dve", 0.5)
            for sub in range(2):
                S.op("dve", lambda e, sub=sub: e.reciprocal(out=rden[:, 2 * slot + sub:2 * slot + sub + 1],
                                                            in_=ob[sub][0][:, 128:129]),
                     reads=[ob[sub][1]], writes=["rden%d" % slot])
                S.op("dve", lambda e, sub=sub: e.scalar_tensor_tensor(
                    out=ybuf[sub][:, h * 128:(h + 1) * 128], in0=ob[sub][0][:, 0:128],
                    scalar=rden[:, 2 * slot + sub:2 * slot + sub + 1],
                    in1=gsil[sub][:, h * 128:(h + 1) * 128], op0=ALU.mult, op1=ALU.mult),
                    reads=[ob[sub][1], "rden%d" % slot, "gsil%d" % sub], writes=["y%d_%d" % (sub, h)])

        def fox_tile(Q):
            nkb = 2 * Q + 2
            for h in range(4):
                S.op("dve", lambda e, h=h: e.tensor_scalar(out=biasT[:, 0:nkb, h], in0=cpall[:, 0:nkb, h],
                                                           scalar1=cpall[:, 2 * Q + 1, 4 + h:5 + h], scalar2=None,
                                                           op0=ALU.subtract),
                     reads=["cpall"], writes=["biasT"])
            run_streams([(lambda sl, h=h: fox_stream(sl, Q, h)) for h in range(4)], FOXW)

        def sb_stream(slot, qrows, qT_ap, qT_names, chunks, o_ap, o_name, pv, total_pv, carry_in=None, aT_hook=None,
                      fin=None, out_state=None):
            prev = carry_in
            th, Rx, a_ = thb[slot], Rext[slot], ab[slot]
            tn = ["th0", "th1", "th2", "gtmp"][slot]
            rn = ["Rx0", "Rx1", "Rx2", "kf321"][slot]
            an = ["a0", "a1", "a2", "k_tok"][slot]
            aTnames = [["aT0"], ["aT1"], ["P0", "P1"], ["q_tok"]][slot]
            scb, scn = [(psA, "psA"), (psB, "psB"), (psE, "psE"), (psG, "psG")][slot]
            pst, pn, pso = scb[:, :].bitcast(BF16), scn, 0
            for ci, ch in enumerate(chunks):
                W = ch["W"]
                yield ("pe", 0.45)
                S.op("pe", lambda e, ch=ch, W=W, scb=scb: e.matmul(scb[0:qrows, 0:W], lhsT=qT_ap, rhs=ch["kT"], start=True, stop=True),
                     reads=qT_names + ch["names"], writes=[scn])
                yield ("act", 0.65)
                S.op("act", lambda e, W=W, scb=scb: e.activation(out=th[0:qrows, 0:W], in_=scb[0:qrows, 0:W], func=AF.Tanh,
                                                        scale=0.5 * SCALE),
                     reads=[scn], writes=[tn])
                yield ("pool", 0.65)
                S.op("pool", lambda e, W=W: e.tensor_scalar(th[0:qrows, 0:W], th[0:qrows, 0:W], -0.5, 0.5, ALU.mult, ALU.add),
                     reads=[tn], writes=[tn])
                yield ("dve", 1.5 if slot >= 2 else 2.2)
                if ch.get("mask") is not None:
                    M, Mc, dw = ch["mask"]
                    S.op("dve", lambda e, W=W, dw=dw, M=M: e.tensor_tensor(out=th[0:qrows, W - dw:W], in0=th[0:qrows, W - dw:W],
                                                                           in1=M, op=ALU.mult),
                         reads=[tn, "cstf"], writes=[tn])
                    S.op("dve", lambda e, W=W, dw=dw, Mc=Mc: e.tensor_tensor(out=th[0:qrows, W - dw:W], in0=th[0:qrows, W - dw:W],
                                                                             in1=Mc, op=ALU.add),
                         reads=[tn, "cstf"], writes=[tn])
                if prev is None:
                    S.op("dve", lambda e, W=W: e.memset(Rx[0:qrows, W:W + 1], 1.0), reads=[], writes=[rn])
                else:
                    pR, pname = prev
                    S.op("dve", lambda e, W=W, pR=pR: e.tensor_copy(out=Rx[0:qrows, W:W + 1], in_=pR),
                         reads=[pname, an], writes=[rn])
                S.op("dve", lambda e, W=W: e.tensor_tensor_scan(
                    out=Rx[0:qrows, 0:W][:, ::-1], data0=th[0:qrows, 0:W][:, ::-1], data1=zeros[0:qrows, 0:W],
                    initial=Rx[0:qrows, W:W + 1], op0=ALU.mult, op1=ALU.add),
                    reads=[tn, rn, "zeros"], writes=[rn])
                prev = (Rx[0:qrows, 0:1], rn)
                if slot >= 2:
                    yield ("pool", 1.3)
                S.op("pool" if slot >= 2 else "dve", lambda e, W=W: e.tensor_tensor(out=a_[0:qrows, 0:W], in0=Rx[0:qrows, 1:W + 1],
                                                                                   in1=Rx[0:qrows, 0:W], op=ALU.subtract),
                     reads=[rn], writes=[an])
                yield ("pe", 0.7)
                vbl = ch["vblocks"]
                off = 0
                for j, (rhs, wk, vn) in enumerate(vbl):
                    S.op("pe", lambda e, off=off, wk=wk, j=j: e.transpose(out=pst[0:wk, pso + j * 128:pso + j * 128 + qrows],
                                                                          in_=a_[0:qrows, off:off + wk],
                                                                          identity=ident[0:qrows, 0:qrows]),
                         reads=[an, "cstb"], writes=[pn], sig=(j == len(vbl) - 1))
                    off += wk
                yield ("act", 0.65)
                if aT_hook is None:
                    aT = aTb[slot]
                    nb_ = len(vbl)
                    pvw = pst[:, pso:pso + nb_ * 128].rearrange("p (j t) -> p j t", t=128)
                    S.op("act", lambda e, aT=aT, pvw=pvw, nb_=nb_: e.activation(out=aT[:, 0:nb_, 0:qrows], in_=pvw[:, :, 0:qrows],
                                                                                func=AF.Identity),
                         reads=[pn], writes=aTnames)
                    lhs_list = [(aT[0:wk, j, 0:qrows], aTnames) for j, (_, wk, _) in enumerate(vbl)]
                else:
                    lhs_list = aT_hook(ci, vbl, pso, pn, pst)
                yield ("pe", 0.7)
                for j, (rhs, wk, vn) in enumerate(vbl):
                    lhsT, ln = lhs_list[j]
                    n = pv["n"]
                    S.op("pe", lambda e, lhsT=lhsT, rhs=rhs, n=n: e.matmul(o_ap, lhsT=lhsT, rhs=rhs, start=(n == 0),
                                                                           stop=(n == total_pv - 1)),
                         reads=(ln if isinstance(ln, list) else [ln]) + [vn], writes=[o_name], sig=True)
                    pv["n"] += 1
            if out_state is not None:
                out_state["carry"] = prev
            if fin is not None:
                yield ("dve", 0.3)
                fin()

        sb_obank = {0: [(psC, "psC")], 1: [(psD, "psD")], 2: [(psF, "psF")], 3: [(pst_f32, "pstbank")]}
        sb_ocnt = {0: 0, 1: 0}

        def sb_tile(Q):
            makers = []
            for sub in range(2):
                qb = 2 * Q + sub
                e_ = 128 * (qb + 1)
                for h in range(4):
                    def mk(slot, sub=sub, h=h, e_=e_):
                        chunks = []
                        hi = e_
                        while hi > 0:
                            lo = max(0, hi - 512)
                            vbl = [(Vaug(b, h, 128), 128, "V%d" % b) for b in range(lo // 128, hi // 128)]
                            chunks.append(dict(kT=kT(h, lo, hi), W=hi - lo,
                                               names=["kT%d_%d" % (h, b) for b in range(lo // 128, hi // 128)],
                                               vblocks=vbl,
                                               mask=(cf(C_MLT), cf(C_MLTC), 128) if hi == e_ else None))
                            hi = lo
                        ob, on = sb_obank[slot][0]

                        def fin():
                            S.op("dve", lambda e: e.tensor_tensor(out=ybuf[sub][:, h * 128:(h + 1) * 128], in0=ob[:, 0:128],
                                                                  in1=gsil[sub][:, h * 128:(h + 1) * 128], op=ALU.mult),
                                 reads=[on, "gsil%d" % sub], writes=["y%d_%d" % (sub, h)])
                        return sb_stream(slot, 128, qT[:, h, sub * 128:(sub + 1) * 128], ["qT"], chunks, ob[:, 0:128], on,
                                         {"n": 0}, sum(len(c["vblocks"]) for c in chunks), fin=fin)
                    makers.append(mk)
            run_streams(makers, SBW)

        SEQSZ = 8 * 520 + 4096
        assert 4 * SEQSZ <= 4 * T + 32 * 4 * 130
        kstage_f = actT[1][:, :, :].rearrange("p c t -> p (c t)").bitcast(F32).rearrange("p (j c) -> p j c", c=512)

        def sVc_all(s_):
            b0 = s_ * SEQSZ
            return regB[:, b0:b0 + 8 * 520].rearrange("p (j h e) -> p j h e", h=4, e=130)

        def sVc(s_, j, h, n):
            o = s_ * SEQSZ + (j * 4 + h) * 130
            return regB[:, o:o + n]

        def skT(s_, h, lo, hi):
            o = s_ * SEQSZ + 8 * 520 + h * 1024
            return regB[:, o + lo:o + hi]

        def Vn_v():
            return Vn_t[:, :].rearrange("p (h e) -> p h e", e=130)

        def sample_stage(l):
            kcache, vcache = (cfk, cfv) if l == 0 else (csk, csv)
            for s_ in range(4):
                for j in range(8):
                    S.op("pool", lambda e, s_=s_, j=j: e.dma_start(
                        out=sVc_all(s_)[:, j, :, 0:128],
                        in_=vcache[s_][j * 128:(j + 1) * 128, :].rearrange("p (h d) -> p h d", d=128)),
                         writes=["sVc%d" % s_] + (REGB_ALL + ["w_out"] if j == 0 else []), chan="sVc")
                S.op("pool", lambda e, s_=s_: e.memset(sVc_all(s_)[:, :, :, 128:129], 1.0), writes=["sVc%d" % s_])
                for hf in range(2):
                    S.op("sp", lambda e, s_=s_, hf=hf: e.dma_start(
                        out=kstage_f, in_=kcache[s_][hf * 512:(hf + 1) * 512, :].rearrange("(j p) c -> p j c", p=128)),
                         writes=["actT1"], chan="sKc")
                    for jj in range(4):
                        j = hf * 4 + jj
                        for h in range(4):
                            S.op("pe", lambda e, jj=jj, h=h: e.transpose(out=pst_f32[:, h * 128:(h + 1) * 128],
                                                                         in_=kstage_f[:, jj, h * 128:(h + 1) * 128],
                                                                         identity=cstf[:, C_ID:C_ID + 128]),
                                 reads=["actT1", "cstf"], writes=PST_ALL, sig=(h == 3))
                        kb_ = s_ * SEQSZ + 8 * 520
                        S.op("act", lambda e, kb_=kb_, j=j: e.activation(
                            out=regB[:, kb_:kb_ + 4096].rearrange("p (h t) -> p h t", t=1024)[:, :, j * 128:(j + 1) * 128],
                            in_=pst_f32[:, 0:512].rearrange("p (h t) -> p h t", t=128), func=AF.Identity),
                             reads=PST_ALL, writes=["skT%d" % s_] + (REGB_ALL + ["w_out"] if j == 0 else []))
            if l == 0:
                for s_ in range(4):
                    S.op("sp", lambda e, s_=s_: e.dma_start(out=clf[:, :, s_ * 4:(s_ + 1) * 4],
                                                            in_=cfl[s_].rearrange("(j p) h -> p j h", p=128)),
                         writes=["clf"], chan="clf")
                for j in range(8):
                    S.op("pe", lambda e, j=j: e.matmul(psE[:, 16 + 16 * j:32 + 16 * j], lhsT=cf(C_LS), rhs=clf[:, j, :], start=True,
                                                       stop=(j == 7)), reads=["cstf", "clf"], writes=["psE"], sig=(j == 7))
                    for j2 in range(j + 1, 8):
                        S.op("pe", lambda e, j=j, j2=j2: e.matmul(psE[:, 16 + 16 * j:32 + 16 * j], lhsT=cf(C_ONES), rhs=clf[:, j2, :],
                                                                  start=False, stop=(j2 == 7)), reads=["cstf", "clf"],
                             writes=["psE"], sig=(j2 == 7))
                S.op("dve", lambda e: e.tensor_copy(out=sufb[:].rearrange("p j c -> p (j c)"), in_=psE[:, 16:144]), reads=["psE"],
                     writes=["sufb"])

        def sample_block(l, slot):
            tb = 32
            rows = TS
            Vn = Vn_v()
            project_block(l, slot, 0, tb, rows, lambda h: qT[:, :, 128:128 + rows], ["qT"],
                          Vn_v, ["Vn"], 0)
            if l == 0:
                S.op("pe", lambda e: e.matmul(psE[0:64, 8:12], lhsT=cf(C_UB16, 64, 64), rhs=lps[0:64, 0:4], start=True, stop=True),
                     reads=["cstf", "lps"], writes=["psE"])
                S.op("dve", lambda e: e.tensor_copy(out=cpn[:, :], in_=psE[0:64, 8:12]), reads=["psE"], writes=["cpn"])
            for h in range(4):
                ob, on = o_banks[h % 2][0]
                if l == 0:
                    o_ap = ob[0:64, 0:129]
                    first = True
                    for s_ in range(4):
                        for j in range(8):
                            scb, scn = sc_banks[pcount["sc"] % 2]
                            pcount["sc"] += 1
                            S.op("pe", lambda e, s_=s_, j=j, h=h, scb=scb: e.matmul(
                                scb[:, 0:16], lhsT=skT(s_, h, j * 128, (j + 1) * 128), rhs=qT[:, h, s_ * 16:(s_ + 1) * 16],
                                start=True, stop=True), reads=["skT%d" % s_, "qT"], writes=[scn])
                            S.op("act", lambda e, s_=s_, j=j, h=h, scb=scb: e.activation(
                                out=Pw[s_][:, s_ * 16:(s_ + 1) * 16], in_=scb[:, 0:16], func=AF.Exp, scale=SCALE,
                                bias=sufb[:, j, s_ * 4 + h:s_ * 4 + h + 1]), reads=[scn, "sufb"], writes=["Pw%d" % s_])
                            S.op("pe", lambda e, s_=s_, j=j, h=h, first=first, o_ap=o_ap: e.matmul(
                                o_ap, lhsT=Pw[s_][:, 0:64], rhs=sVc(s_, j, h, 129), start=first, stop=False),
                                reads=["Pw%d" % s_, "sVc%d" % s_], writes=[on], sig=True)
                            first = False
                    scb, scn = sc_banks[pcount["sc"] % 2]
                    pcount["sc"] += 1
                    S.op("pe", lambda e, h=h, scb=scb: e.matmul(scb[0:64, 0:64], lhsT=qT[:, h, 128:192], rhs=qT[:, h, 0:64],
                                                                start=True, stop=True), reads=["qT"], writes=[scn])
                    S.op("act", lambda e, h=h, scb=scb: e.activation(out=Pn[:, :], in_=scb[0:64, 0:64], func=AF.Exp, scale=SCALE,
                                                                     bias=cpn[:, h:h + 1]), reads=[scn, "cpn"], writes=["Pn"])
                    S.op("pool", lambda e: e.tensor_tensor(out=Pn[:, :], in0=Pn[:, :], in1=mask64, op=ALU.mult),
                         reads=["Pn", "cstb"], writes=["Pn"])
                    S.op("pe", lambda e, h=h, o_ap=o_ap: e.matmul(o_ap, lhsT=Pn[:, :], rhs=Vn[0:64, h, 0:129], start=False, stop=True),
                         reads=["Pn", "Vn"], writes=[on])
                    S.op("dve", lambda e, ob=ob: e.reciprocal(out=rden[0:64, 0:1], in_=ob[0:64, 128:129]), reads=[on],
                         writes=["rden0"])
                    S.op("dve", lambda e, h=h, ob=ob: e.scalar_tensor_tensor(
                        out=ybuf[0][0:64, h * 128:(h + 1) * 128], in0=ob[0:64, 0:128], scalar=rden[0:64, 0:1],
                        in1=gsil[0][0:64, h * 128:(h + 1) * 128], op0=ALU.mult, op1=ALU.mult),
                        reads=[on, "rden0", "gsil0"], writes=["y0_%d" % h])
                else:
                    pass
            if l == 1:
                def head_stream(slot, h):
                    ob, on = sb_obank[slot][0]
                    o_ap = ob[0:64, 0:128]
                    pv = {"n": 0}
                    total = 1 + 4 * 8

                    def hook0(ci, vbl, pso, pn, pst_=None):
                        S.op("act", lambda e: e.activation(out=aTn[:, :], in_=pst_[0:64, pso:pso + 64], func=AF.Identity),
                             reads=[pn], writes=["aTn"])
                        return [(aTn[:, :], "aTn")]
                    ch0 = dict(kT=qT[:, h, 128:192], W=64, names=["qT"], vblocks=[(Vn[0:64, h, 0:128], 64, "Vn")],
                               mask=(cf(C_MS, 64, 64), cf(C_MSC, 64, 64), 64))
                    st = {}
                    yield from sb_stream(slot, 64, qT[:, h, 0:64], ["qT"], [ch0], o_ap, on, pv, total, None, hook0, None, st)
                    car = st["carry"]
                    cr = carry0[:, slot:slot + 1]
                    yield ("dve", 0.1)
                    S.op("dve", lambda e: e.tensor_copy(out=cr, in_=car[0]), reads=[car[1]], writes=["carry0_%d" % slot])
                    for s_ in range(4):
                        def hook(ci, vbl, pso, pn, pst_=None, s_=s_):
                            base = 4 if ci == 0 else 0
                            pvw = pst_[:, pso:pso + 512].rearrange("p (j t) -> p j t", t=128)
                            S.op("act", lambda e: e.activation(out=aTw[s_][:, base:base + 4, s_ * 16:(s_ + 1) * 16],
                                                               in_=pvw[:, :, s_ * 16:(s_ + 1) * 16], func=AF.Identity),
                                 reads=[pn], writes=["aTw%d" % s_])
                            return [(aTw[s_][:, base + j, 0:64], "aTw%d" % s_) for j in range(4)]
                        chunks = []
                        for (lo, hi) in ((512, 1024), (0, 512)):
                            chunks.append(dict(kT=skT(s_, h, lo, hi), W=512, names=["skT%d" % s_],
                                               vblocks=[(sVc(s_, b, h, 128), 128, "sVc%d" % s_) for b in range(lo // 128, hi // 128)],
                                               mask=None))
                        yield from sb_stream(slot, 64, qT[:, h, 0:64], ["qT"], chunks, o_ap, on, pv, total,
                                             (cr, "carry0_%d" % slot), hook)
                    yield ("dve", 0.3)
                    S.op("dve", lambda e: e.tensor_tensor(out=ybuf[0][0:64, h * 128:(h + 1) * 128], in0=ob[0:64, 0:128],
                                                          in1=gsil[0][0:64, h * 128:(h + 1) * 128], op=ALU.mult),
                         reads=[on, "gsil0"], writes=["y0_%d" % h])
                run_streams([(lambda sl, h=h: head_stream(sl, h)) for h in range(4)], SBW)
            load_w_out(l)
            y_out([tb], [rows])

        def p_phase(l):
            load_act(0, xT_all, "xT", 0, 256)
            nt = 16
            for Q in range(nt):
                slot = Q % 2
                if Q < 15:
                    load_act(1 - slot, xT_all, "xT", (Q + 1) * 256, 256)
                else:
                    load_act(1 - slot, xT_all, "xT", T, TS)
                for sub in range(2):
                    tb = 2 * Q + sub
                    project_block(l, slot, sub, tb, 128,
                                  lambda h, tb=tb: regB[:, 0:4 * T].rearrange("p (h t) -> p h t", t=T)[:, :, tb * 128:(tb + 1) * 128],
                                  ["kT%d_%d" % (h, tb) for h in range(4)],
                                  lambda tb=tb: Vaug_blk(tb), ["V%d" % tb], sub * 128)
                    if l == 0:
                        cumsum_block(tb)
                if l == 0:
                    fox_tile(Q)
                else:
                    sb_tile(Q)
                y_out([2 * Q, 2 * Q + 1], [128, 128])
            if nt == 16:
                sample_stage(l)
                sample_block(l, 0)

        def o_phase(l):
            wq = load_w_in(1) if l == 0 else iter(())
            wv = w_out()
            load_act(0, yT_all, "yT", 0, 256)
            xsrc = xc0 if l == 0 else xres[1].ap()
            xdst = xres[l + 1].ap()

            def o_load(tb):
                rows = blk_rows(tb)
                i = tb % 2
                S.op("sp", lambda e: e.dma_start(out=xcb[i][0:rows, :], in_=xsrc[tb * 128:tb * 128 + rows, :]),
                     reads=["xres%d" % l], writes=["kf32%d" % i], chan="kf32%d" % i)

            for Q in range(17):
                slot = Q % 2
                if Q < 15:
                    load_act(1 - slot, yT_all, "yT", (Q + 1) * 256, 256)
                elif Q == 15:
                    load_act(1 - slot, yT_all, "yT", T, TS)
                next(wq, None)
                for sub in range(2 if Q < 16 else 1):
                    tb = 2 * Q + sub
                    rows = blk_rows(tb)
                    i = tb % 2
                    if tb == 0:
                        o_load(0)
                    if tb + 1 < NB:
                        o_load(tb + 1)
                    for c in range(16):
                        S.op("pe", lambda e, c=c, slot=slot, sub=sub, rows=rows: e.matmul(
                            psA[0:rows, :], lhsT=actT[slot][:, c, sub * 128:sub * 128 + rows], rhs=wv[:, c, :], start=(c == 0),
                            stop=(c == 15)), reads=["actT%d" % slot, "w_out"], writes=["psA"], sig=(c == 15))
                    S.op("dve", lambda e, i=i, rows=rows: e.tensor_tensor(out=xnb[i][0:rows, :], in0=psA[0:rows, :],
                                                                          in1=xcb[i][0:rows, :], op=ALU.add),
                         reads=["psA", "kf32%d" % i], writes=["vf32%d" % i])
                    S.op("sp", lambda e, i=i, tb=tb, rows=rows: e.dma_start(out=xdst[tb * 128:tb * 128 + rows, :],
                                                                              in_=xnb[i][0:rows, :]),
                         reads=["vf32%d" % i], writes=["xres%d" % (l + 1)], chan="vf32%d" % i)
                    norm_prep(xnb[i], "vf32%d" % i, tb, rows, l + 1, l == 0)
            for _ in wq:
                pass

        stage = 99
        if stage >= 2:
            p_phase(0)
        if stage >= 4:
            load_nw(1)
            o_phase(0)
            rstd_from_ssq("1")
        if stage >= 5:
            p_phase(1)
        if stage >= 7:
            load_nw(2)
            o_phase(1)
            rstd_from_ssq("2")
            x2 = xres[2].ap()
            def f_load(tb):
                rows = blk_rows(tb)
                i = tb % 2
                S.op("sp", lambda e: e.dma_start(out=xcb[i][0:rows, :], in_=x2[tb * 128:tb * 128 + rows, :]),
                     reads=["xres2"], writes=["kf32%d" % i], chan="kf32%d" % i)

            f_load(0)
            for tb in range(NB):
                rows = blk_rows(tb)
                i = tb % 2
                if tb + 1 < NB:
                    f_load(tb + 1)
                S.op("dve", lambda e, i=i, tb=tb, rows=rows: e.scalar_tensor_tensor(
                    out=xnb[i][0:rows, :], in0=xcb[i][0:rows, :], scalar=rstd[0:rows, tb:tb + 1], in1=nw[2][0:rows, :],
                    op0=ALU.mult, op1=ALU.mult), reads=["kf32%d" % i, "rstd", "nw"], writes=["vf32%d" % i])
                S.op("sp", lambda e, i=i, tb=tb, rows=rows: e.dma_start(out=yout[tb * 128:tb * 128 + rows, :], in_=xnb[i][0:rows, :]),
                     reads=["vf32%d" % i], writes=[], chan="vf32%d" % i)
        S.op("sp", lambda e: e.dma_start(out=lfout[0:T, :].rearrange("(b p) h -> p b h", p=128), in_=lfo[:, 0:32, :]),
             reads=["lfo"], writes=[], chan="lfo")
        S.op("sp", lambda e: e.dma_start(out=lfout[T:TT, :], in_=lfo[0:TS, 32, :]), reads=["lfo"], writes=[], chan="lfo")
        for sn, v in list(S.cnt.items()):
            if sn.startswith("d_"):
                S.prog["sp"].append(("wait", sn, v))

        sem_names = sorted(S.cnt.keys())
        sems = {}
        for sn in sem_names:
            sems[sn] = es.enter_context(nc.semaphore(sn))
        block = es.enter_context(nc.Block())

        def emit(eng_obj, name):
            for item in S.prog[name]:
                if item[0] == "wait":
                    eng_obj.wait_ge(sems[item[1]], item[2])
                else:
                    _, fn, sn, inc = item
                    ins = fn(eng_obj)
                    if sn is not None:
                        if inc is None:
                            ins.then_inc(sems[sn])
                        else:
                            ins.then_inc(sems[sn], inc)

        @block.sync
        def _(e):
            emit(e, "sp")

        @block.scalar
        def _(e):
            emit(e, "act")

        @block.vector
        def _(e):
            emit(e, "dve")

        @block.gpsimd
        def _(e):
            emit(e, "pool")

        @block.tensor
        def _(e):
            emit(e, "pe")
    return nc


def _consts():
    c = np.zeros((128, NCST), np.float32)
    k = np.arange(128)[:, None]
    m = np.arange(128)[None, :]
    c[:, C_U:C_U + 128] = (k <= m)
    c[:, C_E127:C_E127 + 128] = (k == 127)
    c[:, C_UB16:C_UB16 + 128] = (k <= m) & (k // 16 == m // 16)
    c[:, C_LS:C_LS + 128] = (k > m)
    c[:, C_ONES:C_ONES + 128] = 1.0
    c[:, C_MS:C_MS + 128] = (m < k) & (k // 16 == m // 16)
    c[:, C_MSC:C_MSC + 128] = 1.0 - ((m < k) & (k // 16 == m // 16))
    c[:, C_MLT:C_MLT + 128] = (m < k)
    c[:, C_MLTC:C_MLTC + 128] = 1.0 - (m < k)
    c[:, C_ID:C_ID + 128] = (k == m)
    c[:, C_MLE:C_MLE + 128] = (k <= m)
    c[:, C_M64:C_M64 + 128] = (k <= m) & (k // 16 == m // 16)
    return c


_NC_CACHE = {}


def kernel(x_prompt, x_sample, cache_fox_k, cache_fox_v, cache_fox_logf, cache_sb_k, cache_sb_v,
           norm_0, w_in_0, b_f_0, w_out_0, norm_1, w_in_1, w_out_1, norm_f):
    f = np.float32
    A = lambda a: np.ascontiguousarray(np.asarray(a, dtype=f))
    x_prompt, x_sample = A(x_prompt), A(x_sample)
    w_in_0, w_in_1, w_out_0, w_out_1 = A(w_in_0), A(w_in_1), A(w_out_0), A(w_out_1)
    caches = [A(cache_fox_k), A(cache_fox_v), A(cache_sb_k), A(cache_sb_v)]
    cache_fox_logf = A(cache_fox_logf)
    norms = [A(norm_0), A(norm_1), A(norm_f)]
    b_f_0 = A(b_f_0)
    if "nc" not in _NC_CACHE:
        _NC_CACHE["nc"] = build_program()
    nc = _NC_CACHE["nc"]
    cst = _consts()
    in_maps = []
    for core in range(8):
        b, g = core // 4, core % 4
        cs = slice(g * 512, (g + 1) * 512)
        xc0 = np.concatenate([x_prompt[b][:, cs], x_sample[4 * b:4 * b + 4].reshape(64, D)[:, cs]], axis=0)
        m = {"xc0": A(xc0), "cst": cst}
        for i, nm in enumerate(["nw0", "nw1", "nwf"]):
            m[nm] = A(np.broadcast_to(norms[i][cs][None, :], (128, 512)))
        m["win0"] = A(np.concatenate([w_in_0[:, j * D + g * 512: j * D + (g + 1) * 512] for j in range(4)]
                                     + [w_in_0[:, 4 * D + 4 * g: 4 * D + 4 * g + 4]], axis=1))
        m["win1"] = A(np.concatenate([w_in_1[:, j * D + g * 512: j * D + (g + 1) * 512] for j in range(4)], axis=1))
        m["wout0"] = A(w_out_0[:, cs])
        m["wout1"] = A(w_out_1[:, cs])
        m["bfb"] = A(np.broadcast_to(b_f_0[4 * g:4 * g + 4][None, :], (128, 4)))
        for nm, cch in zip(["cfk", "cfv", "csk", "csv"], caches):
            m[nm] = A(cch[4 * b:4 * b + 4, :, 4 * g:4 * g + 4, :].reshape(4, PAST, 512))
        m["cfl"] = A(cache_fox_logf[4 * b:4 * b + 4, :, 4 * g:4 * g + 4])
        in_maps.append(m)
    res = run_bass_kernel_spmd(nc, in_maps, core_ids=list(range(8)))
    R = res.results
    y_prompt = np.zeros((2, T, D), f)
    y_sample = np.zeros((8, 16, D), f)
    pk = [np.zeros((2, T, 16, 128), f) for _ in range(4)]
    sk = [np.zeros((8, 16, 16, 128), f) for _ in range(4)]
    plf = np.zeros((2, T, 16), f)
    slf = np.zeros((8, 16, 16), f)
    for core in range(8):
        b, g = core // 4, core % 4
        cs = slice(g * 512, (g + 1) * 512)
        r = R[core]
        y_prompt[b][:, cs] = r["yout"][:T]
        y_sample[4 * b:4 * b + 4][:, :, cs] = r["yout"][T:].reshape(4, 16, 512)
        for i, nm in enumerate(["kf", "vf", "ks", "vs"]):
            pk[i][b][:, 4 * g:4 * g + 4, :] = r[nm][:T].reshape(T, 4, 128)
            sk[i][4 * b:4 * b + 4][:, :, 4 * g:4 * g + 4, :] = r[nm][T:].reshape(4, 16, 4, 128)
        plf[b][:, 4 * g:4 * g + 4] = r["lf"][:T]
        slf[4 * b:4 * b + 4][:, :, 4 * g:4 * g + 4] = r["lf"][T:].reshape(4, 16, 4)
    return (y_prompt, y_sample, pk[0], pk[1], plf, pk[2], pk[3], sk[0], sk[1], slf, sk[2], sk[3])
```

```python
import numpy as np
import concourse.bass as bass
import concourse.mybir as mybir
from concourse.bass_utils import run_bass_kernel_spmd

F32 = mybir.dt.float32
BF16 = mybir.dt.bfloat16
AF = mybir.ActivationFunctionType
ALU = mybir.AluOpType

D = 2048
T = 4096
TS = 64
TT = T + TS
NB = 33
PAST = 1024
SCALE = 128 ** -0.5
EPS = 1e-6
GROUPS = [[0, 1, 2, 3], [4, 5, 6, 7]]
ENG = ("sp", "act", "dve", "pool", "pe")
FOXW = 1
SBW = 4

C_U, C_E127, C_UB16, C_LS, C_ONES, C_MLT, C_MLTC, C_MS, C_MSC, C_ID, C_MLE, C_M64 = [128 * i for i in range(12)]
NCST = 128 * 12


class Sched:
    def __init__(self):
        self.prog = {e: [] for e in ENG}
        self.cnt = {}
        self.waited = {}
        self.lastw = {}
        self.readers = {}

    def _need(self, eng, events):
        for sn, val in events:
            if sn.startswith("d_"):
                val = self.cnt[sn]
            if sn == "pe" and eng == "pe":
                continue
            key = (eng, sn)
            if self.waited.get(key, 0) >= val:
                continue
            self.waited[key] = val
            self.prog[eng].append(("wait", sn, val))

    def op(self, eng, fn, reads=(), writes=(), sig=True, chan=None, cc=None):
        ps_reads = [r for r in reads if r.startswith("ps") and r not in writes]
        if ps_reads:
            writes = list(writes) + ps_reads
        ev = []
        for r in reads:
            if r in self.lastw:
                ev.append(self.lastw[r])
        for w in writes:
            if w in self.lastw:
                ev.append(self.lastw[w])
            ev.extend(self.readers.get(w, {}).items())
        self._need(eng, ev)
        if cc is not None:
            sn = "c_" + cc
            self.cnt[sn] = 1
            myev = (sn, 1)
            self.prog[eng].append(("op", fn, sn, None))
        elif chan is not None:
            sn = "d_" + chan
            self.cnt[sn] = self.cnt.get(sn, 0) + 16
            myev = (sn, self.cnt[sn])
            self.prog[eng].append(("op", fn, sn, 16))
        elif sig:
            self.cnt[eng] = self.cnt.get(eng, 0) + 1
            myev = (eng, self.cnt[eng])
            self.prog[eng].append(("op", fn, eng, 1))
        else:
            myev = (eng, self.cnt.get(eng, 0) + 1)
            self.prog[eng].append(("op", fn, None, 0))
        for r in reads:
            d = self.readers.setdefault(r, {})
            d[myev[0]] = max(d.get(myev[0], 0), myev[1])
        for w in writes:
            self.lastw[w] = myev
            self.readers[w] = {}


def build_program():
    nc = bass.Bass("TRN2", target_bir_lowering=False)
    S = Sched()

    def din(name, shape, dt=F32):
        return nc.dram_tensor(name, shape, dt, kind="ExternalInput").ap()

    def dout(name, shape, dt=F32):
        return nc.dram_tensor(name, shape, dt, kind="ExternalOutput").ap()

    xc0 = din("xc0", [TT, 512])
    nwd = [din("nw0", [128, 512]), din("nw1", [128, 512]), din("nwf", [128, 512])]
    wind = [din("win0", [D, 2052]), din("win1", [D, 2048])]
    woutd = [din("wout0", [D, 512]), din("wout1", [D, 512])]
    bfd = din("bfb", [128, 4])
    cstd = din("cst", [128, NCST])
    cfk = din("cfk", [4, PAST, 512])
    cfv = din("cfv", [4, PAST, 512])
    csk = din("csk", [4, PAST, 512])
    csv = din("csv", [4, PAST, 512])
    cfl = din("cfl", [4, PAST, 4])

    yout = dout("yout", [TT, 512])
    kvout = [[dout("kf", [TT, 512]), dout("vf", [TT, 512])], [dout("ks", [TT, 512]), dout("vs", [TT, 512])]]
    lfout = dout("lf", [TT, 4])

    PW = [1024, 1024, 1024, 1024, TS]
    xT_loc = [nc.dram_tensor("xT_loc%d" % p, [512, PW[p]], BF16) for p in range(5)]
    xT_all = [nc.dram_tensor("xT_all%d" % p, [D, PW[p]], BF16) for p in range(5)]
    yT_loc = [nc.dram_tensor("yT_loc%d" % p, [512, PW[p]], BF16) for p in range(5)]
    yT_all = [nc.dram_tensor("yT_all%d" % p, [D, PW[p]], BF16) for p in range(5)]

    def piece_of(tb):
        return (tb // 8, (tb % 8) * 128) if tb < 32 else (4, 0)
    ssq_loc = nc.dram_tensor("ssq_loc", [128, NB], F32)
    ssq_all = nc.dram_tensor("ssq_all", [512, NB], F32)
    xres = [None, nc.dram_tensor("x1s", [TT, 512], F32), nc.dram_tensor("x2s", [TT, 512], F32)]

    from contextlib import ExitStack
    es = ExitStack()

    def sb(name, shape, dt):
        return es.enter_context(nc.sbuf_tensor(name, shape, dt))

    def ps(name, shape, dt):
        return es.enter_context(nc.psum_tensor(name, shape, dt))

    with es:
        cstf = sb("cstf", [128, C_ID + 128], F32)
        cstb = sb("cstb", [128, 3 * 128], BF16)
        zeros = sb("zeros", [128, 512], BF16)
        nwt = sb("nwt", [128, 512], F32)
        nw = [nwt, nwt, nwt]
        bfb = sb("bfbs", [128, 4], F32)
        w_in = sb("w_in", [128, 16, 2052], BF16)
        regB = sb("regB", [128, 4 * T + 32 * 4 * 130], BF16)
        actT = [sb("actT0", [128, 16, 256], BF16), sb("actT1", [128, 16, 256], BF16)]
        ssq_part = sb("ssq_part", [128, NB], F32)
        ssq4 = sb("ssq4", [128, 4, NB], F32)
        ssum = sb("ssum", [128, NB], F32)
        rstd = sb("rstd", [128, NB], F32)
        nrstd = sb("nrstd", [128, NB], F32)
        hrstd = sb("hrstd", [128, NB], F32)
        cpall = sb("cpall", [128, NB, 8], F32)
        lfo = sb("lfo", [128, NB, 4], F32)
        lps = sb("lps", [128, 4], F32)
        xl = sb("xl", [128, 4], F32)
        el = sb("el", [128, 4], F32)
        q_tok = sb("q_tok", [128, 512], BF16)
        k_tok = sb("k_tok", [128, 512], BF16)
        kf32b_big = sb("kf32b", [128, 516], F32)
        kf32 = [sb("kf32a", [128, 512], F32)[:, :], kf32b_big[:, 0:512]]
        vf32 = [sb("vf32a", [128, 512], F32), sb("vf32b", [128, 512], F32)]
        gsil = [sb("gsil0", [128, 512], F32), sb("gsil1", [128, 512], F32)]
        gtmp = sb("gtmp", [128, 512], F32)
        qT = sb("qT", [128, 4, 256], BF16)
        ybuf = [sb("y0", [128, 512], BF16), sb("y1", [128, 512], BF16)]
        yTs = [sb("yTs0", [128, 4, 128], BF16), sb("yTs1", [128, 4, 128], BF16)]
        Pball = sb("Pball", [128, 6, 256], BF16)
        Pb = [Pball[:, i, :] for i in range(6)]
        biasT = sb("biasT", [128, 32, 4], F32)
        rden = sb("rden", [128, 4], F32)
        thb = [sb("th%d" % i, [128, 512], F32) for i in range(3)] + [gtmp]
        Rext = [sb("Rx%d" % i, [128, 516], F32) for i in range(3)] + [kf32b_big]
        ab = [sb("a%d" % i, [128, 512], BF16) for i in range(3)] + [k_tok]
        aTb = [sb("aT0", [128, 4, 128], BF16), sb("aT1", [128, 4, 128], BF16),
               Pball[:, 0:2, :].rearrange("p a (b t) -> p (a b) t", t=128),
               q_tok[:, :].rearrange("p (j t) -> p j t", t=128)]
        xcb = kf32
        xnb = vf32
        xtb = q_tok
        sqj = gtmp
        xTs = yTs
        Pw = [sb("Pw%d" % i, [128, 64], BF16) for i in range(4)]
        Pn = sb("Pn", [64, 64], BF16)
        clf = sb("clf", [128, 8, 16], F32)
        sufb = sb("sufb", [128, 8, 16], F32)
        cpn = sb("cpn", [64, 4], F32)
        aTw = [sb("aTw%d" % i, [128, 8, 64], BF16) for i in range(4)]
        aTn = sb("aTn", [64, 64], BF16)
        Vn_t = sb("Vn_t", [128, 520], BF16)
        carry0 = sb("carry0", [64, 4], F32)
        cbias = sb("cbias", [128, 2], F32)

        psA = ps("psA", [128, 512], F32)
        psB = ps("psB", [128, 512], F32)
        psC = ps("psC", [128, 512], F32)
        psD = ps("psD", [128, 512], F32)
        psE = ps("psE", [128, 512], F32)
        psF = ps("psF", [128, 512], F32)
        psG = ps("psG", [128, 512], F32)
        pst = ps("pst", [128, 1024], BF16)

        KT_OFF = 0
        V_OFF = 4 * T

        def kT(h, lo, hi):
            return regB[:, KT_OFF + h * T + lo: KT_OFF + h * T + hi]

        def Vaug(blk, h, n):
            o = V_OFF + (blk * 4 + h) * 130
            return regB[:, o:o + n]

        def Vaug_blk(blk):
            o = V_OFF + blk * 4 * 130
            return regB[:, o:o + 520].rearrange("p (h e) -> p h e", e=130)

        def w_out():
            o = 4096 + 8 * 520 + 4096
            return regB[:, o:o + 16 * 512].rearrange("p (c n) -> p c n", n=512)

        PST_ALL = ["pstbank"]
        pst_full = pst
        psE_bf = psE[:, :].bitcast(BF16)
        pst_f32 = pst[:, :].bitcast(F32)
        psG_bf = psG[:, :].bitcast(BF16)
        REGB_ALL = ["kT%d_%d" % (h, b) for h in range(4) for b in range(32)] + ["V%d" % b for b in range(32)]

        ident = cstb[:, 0:128]
        maskLE = cstb[:, 128:256]
        mask64 = cstb[0:64, 256:320]

        def cf(c0, n=128, rows=128):
            return cstf[0:rows, c0:c0 + n]

        S.op("sp", lambda e: e.dma_start(out=cstf[:], in_=cstd[:, 0:C_ID + 128]), writes=["cstf"], chan="cst")
        S.op("pool", lambda e: e.dma_start(out=cstb[:], in_=cstd[:, C_ID:C_ID + 384]), writes=["cstb"], chan="cstb")
        def load_nw(i):
            S.op("sp", lambda e: e.dma_start(out=nwt[:], in_=nwd[i][:, :]), writes=["nw"], chan="cst")
        load_nw(0)
        S.op("sp", lambda e: e.dma_start(out=bfb[:], in_=bfd[:, :]), writes=["bfb"], chan="cst")
        S.op("dve", lambda e: e.memset(zeros[:], 0.0), writes=["zeros"])
        S.op("dve", lambda e: e.memset(cbias[:, 0:1], EPS), writes=["cbias"])
        S.op("dve", lambda e: e.memset(cbias[:, 1:2], 1.0), writes=["cbias"])
        S.op("dve", lambda e: e.memset(ssq_part[:], 1.0), writes=["ssq_part"])
        S.op("dve", lambda e: e.memset(cpall[:], 0.0), writes=["cpall"])
        S.op("dve", lambda e: e.memset(lfo[:], 0.0), writes=["lfo"])
        for i in range(4):
            S.op("pool", lambda e, i=i: e.memset(Pw[i][:], 0.0), writes=["Pw%d" % i])
            S.op("pool", lambda e, i=i: e.memset(aTw[i][:], 0.0), writes=["aTw%d" % i])

        if False:
            for nm_, ap_ in [("win0", wind[0][0:128, 0:512]), ("win1", wind[1][0:128, 0:512]), ("wout0", woutd[0][0:128, :]),
                             ("wout1", woutd[1][0:128, :]), ("cfk", cfk[0][0:128, :]), ("cfv", cfv[0][0:128, :]),
                             ("csk", csk[0][0:128, :]), ("csv", csv[0][0:128, :])]:
                S.op("sp", lambda e, ap_=ap_: e.dma_start(out=gtmp[:], in_=ap_), writes=["gtmp"], chan="dbg")
            S.op("sp", lambda e: e.dma_start(out=gtmp[:, 0:4], in_=cfl[0][0:128, :]), writes=["gtmp"], chan="dbg")

        def load_w_in(l):
            ncol = 2052 if l == 0 else 2048
            src = wind[l].rearrange("(c p) n -> p c n", p=128)
            for c in range(16):
                yield
                si = c % 3
                o = 20544 + si * 4104
                stg = regB[:, o:o + 4104].bitcast(F32)
                S.op("sp", lambda e, c=c, stg=stg: e.dma_start(out=stg[:, 0:ncol], in_=src[:, c, :]),
                     writes=["wst%d" % si] + (REGB_ALL + ["sVc%d" % q for q in range(4)] + ["skT%d" % q for q in range(4)]
                                              if c < 3 else []), chan="wst%d" % si)
                if c % 2 == 0:
                    S.op("act", lambda e, c=c, stg=stg: e.activation(out=w_in[:, c, 0:ncol], in_=stg[:, 0:ncol], func=AF.Identity),
                         reads=["wst%d" % si] + (REGB_ALL if c >= 13 else []), writes=["w_in"])
                else:
                    S.op("dve", lambda e, c=c, stg=stg: e.tensor_copy(out=w_in[:, c, 0:ncol], in_=stg[:, 0:ncol]),
                         reads=["wst%d" % si] + (REGB_ALL if c >= 13 else []), writes=["w_in"])

        def load_w_out(l):
            src = woutd[l].rearrange("(c p) n -> p c n", p=128)
            wv = w_out()
            for c in range(0, 16, 4):
                S.op("pool", lambda e, c=c: e.dma_start(out=wv[:, c:c + 4, :], in_=src[:, c:c + 4, :]),
                     writes=["w_out"] + REGB_ALL + ["sVc%d" % q for q in range(4)] + ["skT%d" % q for q in range(4)], chan="wout")

        def blk_rows(tb):
            return 128 if tb < 32 else TS

        cnt = {"xts": 0, "tr": 0}

        def norm_prep(xblk, xres_name, tb, rows, nwi, want_xT, want_gather=True):
            S.op("act", lambda e: e.activation(out=sqj[0:rows, :], in_=xblk[0:rows, :], func=AF.Square,
                                               accum_out=ssq_part[0:rows, tb:tb + 1]),
                 reads=[xres_name], writes=["gtmp", "ssq_part"])
            if not want_xT:
                return
            S.op("dve", lambda e: e.tensor_tensor(out=xtb[0:rows, :], in0=xblk[0:rows, :], in1=nw[nwi][0:rows, :],
                                                  op=ALU.mult),
                 reads=[xres_name, "nw"], writes=["q_tok"])
            for c in range(4):
                S.op("pe", lambda e, c=c: e.transpose(out=pst[:, c * 128:c * 128 + rows],
                                                      in_=xtb[0:rows, c * 128:(c + 1) * 128],
                                                      identity=ident[0:rows, 0:rows]),
                     reads=["q_tok", "cstb"], writes=PST_ALL, sig=(c == 3))
            i = cnt["tr"] % 2
            cnt["tr"] += 1
            pv = pst[:, 0:512].rearrange("p (c t) -> p c t", t=128)
            S.op("act", lambda e: e.activation(out=xTs[i][:, :, 0:rows], in_=pv[:, :, 0:rows], func=AF.Identity),
                 reads=PST_ALL, writes=["yTs%d" % i])
            pc, c0 = piece_of(tb)
            dst = xT_loc[pc].ap().rearrange("(c p) t -> p c t", p=128)
            S.op("sp", lambda e: e.dma_start(out=dst[:, :, c0:c0 + rows], in_=xTs[i][:, :, 0:rows]),
                 reads=["yTs%d" % i], writes=["xT_loc%d" % pc], chan="yTs%d" % i)
            if want_gather and (tb % 8 == 7 or tb == 32):
                gather(xT_loc[pc], xT_all[pc], "xT", pc)

        def rstd_from_ssq(tag):
            S.op("sp", lambda e: e.dma_start(out=ssq_loc.ap(), in_=ssq_part[:]), reads=["ssq_part"],
                 writes=["ssq_loc"], chan="ssq")
            S.op("pool", lambda e: e.collective_compute("AllGather", ALU.bypass, replica_groups=GROUPS,
                                                        ins=[ssq_loc.ap().opt()], outs=[ssq_all.ap().opt()]),
                 reads=["ssq_loc"], writes=["ssq_all"], cc="ssq" + tag)
            S.op("sp", lambda e: e.dma_start(out=ssq4[:], in_=ssq_all.ap().rearrange("(r p) b -> p r b", p=128)),
                 reads=["ssq_all"], writes=["ssq4"], chan="ssq")
            S.op("dve", lambda e: e.tensor_tensor(out=ssum[:], in0=ssq4[:, 0, :], in1=ssq4[:, 1, :], op=ALU.add),
                 reads=["ssq4"], writes=["ssum"])
            S.op("dve", lambda e: e.tensor_tensor(out=ssum[:], in0=ssum[:], in1=ssq4[:, 2, :], op=ALU.add),
                 reads=["ssq4", "ssum"], writes=["ssum"])
            S.op("dve", lambda e: e.tensor_tensor(out=ssum[:], in0=ssum[:], in1=ssq4[:, 3, :], op=ALU.add),
                 reads=["ssq4", "ssum"], writes=["ssum"])
            S.op("act", lambda e: e.activation(out=ssum[:], in_=ssum[:], func=AF.Ln, scale=1.0 / D, bias=cbias[:, 0:1]),
                 reads=["ssum", "cbias"], writes=["ssum"])
            S.op("act", lambda e: e.activation(out=rstd[:], in_=ssum[:], func=AF.Exp, scale=-0.5),
                 reads=["ssum"], writes=["rstd"])
            S.op("dve", lambda e: e.tensor_scalar(out=nrstd[:], in0=rstd[:], scalar1=-1.0, scalar2=None, op0=ALU.mult),
                 reads=["rstd"], writes=["nrstd"])
            S.op("dve", lambda e: e.tensor_scalar(out=hrstd[:], in0=rstd[:], scalar1=0.5, scalar2=None, op0=ALU.mult),
                 reads=["rstd"], writes=["hrstd"])
            S.op("dve", lambda e: e.memset(ssq_part[:], 1.0), reads=[], writes=["ssq_part"])

        gcount = {"n": 0}

        def gather(loc, allt, rname, pc):
            gcount["n"] += 1
            S.op("pool", lambda e: e.collective_compute("AllGather", ALU.bypass, replica_groups=GROUPS,
                                                        ins=[loc.ap().opt()], outs=[allt.ap().opt()]),
                 reads=["%s_loc%d" % (rname, pc)], writes=["%s_all%d" % (rname, pc)], cc="g%d" % gcount["n"])

        wq = load_w_in(0)

        def n0_load(tb):
            rows = blk_rows(tb)
            i = tb % 2
            S.op("sp", lambda e: e.dma_start(out=xcb[i][0:rows, :], in_=xc0[tb * 128:tb * 128 + rows, :]),
                 writes=["kf32%d" % i], chan="kf32%d" % i)

        for tb in range(NB):
            rows = blk_rows(tb)
            i = tb % 2
            if tb % 2 == 0:
                next(wq, None)
            if tb == 0:
                n0_load(0)
            if tb + 1 < NB:
                n0_load(tb + 1)
            norm_prep(xcb[i], "kf32%d" % i, tb, rows, 0, True)
        for _ in wq:
            pass
        rstd_from_ssq("0")

        def load_act(slot, src_all, rname, c0, ncols):
            pc, lc = (c0 // 1024, c0 % 1024) if c0 < T else (4, 0)
            src = src_all[pc].ap().rearrange("(c p) t -> p c t", p=128)
            S.op("sp", lambda e: e.dma_start(out=actT[slot][:, :, 0:ncols], in_=src[:, :, lc:lc + ncols]),
                 reads=["%s_all%d" % (rname, pc)], writes=["actT%d" % slot], chan="actT%d" % slot)

        def transposes_to(src_tok, src_name, rows, dst_fn, dst_names, evac_eng):
            for h in range(4):
                S.op("pe", lambda e, h=h: e.transpose(out=pst[:, h * 128:h * 128 + rows],
                                                      in_=src_tok[0:rows, h * 128:(h + 1) * 128],
                                                      identity=ident[0:rows, 0:rows]),
                     reads=[src_name, "cstb"], writes=PST_ALL, sig=(h == 3))
            pv4 = pst[:, 0:512].rearrange("p (h t) -> p h t", t=128)
            if evac_eng == "act":
                S.op("act", lambda e: e.activation(out=dst_fn(None), in_=pv4[:, :, 0:rows], func=AF.Identity),
                     reads=PST_ALL, writes=dst_names)
            else:
                S.op("dve", lambda e: e.tensor_copy(out=dst_fn(None), in_=pv4[:, :, 0:rows]), reads=PST_ALL, writes=dst_names)

        def project_block(l, slot, sub, tb, rows, kT_dst, kT_names, v_dst_fn, v_names, qcol0):
            banks = [psA, psB, psC, psD]
            bn = ["psA", "psB", "psC", "psD"]
            for c in range(16):
                lhsT = actT[slot][:, c, sub * 128: sub * 128 + rows]
                for j in range(4):
                    S.op("pe", lambda e, c=c, j=j, lhsT=lhsT: e.matmul(banks[j][0:rows, :], lhsT=lhsT,
                                                                        rhs=w_in[:, c, j * 512:(j + 1) * 512],
                                                                        start=(c == 0), stop=(c == 15)),
                         reads=["actT%d" % slot, "w_in"], writes=[bn[j]], sig=(c == 15))
                if l == 0:
                    S.op("pe", lambda e, c=c, lhsT=lhsT: e.matmul(psE[0:rows, 0:4], lhsT=lhsT,
                                                                   rhs=w_in[:, c, 2048:2052],
                                                                   start=(c == 0), stop=(c == 15)),
                         reads=["actT%d" % slot, "w_in"], writes=["psE"], sig=(c == 15))
            rs = rstd[0:rows, tb:tb + 1]
            nrs = nrstd[0:rows, tb:tb + 1]
            i = tb % 2
            S.op("dve", lambda e: e.tensor_scalar(out=q_tok[0:rows, :], in0=psA[0:rows, :], scalar1=rs, scalar2=None,
                                                  op0=ALU.mult),
                 reads=["psA", "rstd"], writes=["q_tok"])
            S.op("act", lambda e: e.activation(out=kf32[i][0:rows, :], in_=psB[0:rows, :], func=AF.Identity, scale=rs),
                 reads=["psB", "rstd"], writes=["kf32%d" % i])
            S.op("dve", lambda e: e.tensor_scalar(vf32[i][0:rows, :], psC[0:rows, :], rs, None, ALU.mult),
                 reads=["psC", "rstd"], writes=["vf32%d" % i])
            S.op("sp", lambda e: e.dma_start(out=kvout[l][0][tb * 128:tb * 128 + rows, :], in_=kf32[i][0:rows, :]),
                 reads=["kf32%d" % i], writes=[], chan="kf32%d" % i)
            S.op("sp", lambda e: e.dma_start(out=kvout[l][1][tb * 128:tb * 128 + rows, :], in_=vf32[i][0:rows, :]),
                 reads=["vf32%d" % i], writes=[], chan="vf32%d" % i)
            S.op("act", lambda e: e.activation(out=k_tok[0:rows, :], in_=psB[0:rows, :], func=AF.Identity, scale=rs),
                 reads=["psB", "rstd"], writes=["k_tok"])
            vd = v_dst_fn()
            S.op("dve", lambda e: e.tensor_copy(out=vd[0:rows, :, 0:128],
                                                in_=vf32[i][0:rows, :].rearrange("p (h d) -> p h d", d=128)),
                 reads=["vf32%d" % i], writes=v_names)
            S.op("pool", lambda e: e.memset(vd[0:rows, :, 128:129], 1.0), reads=[], writes=v_names)
            if l == 0:
                S.op("act", lambda e: e.activation(out=gtmp[0:rows, :], in_=psD[0:rows, :], func=AF.Exp, scale=nrs),
                     reads=["psD", "nrstd"], writes=["gtmp"])
                S.op("act", lambda e: e.activation(out=gtmp[0:rows, :], in_=gtmp[0:rows, :], func=AF.Ln, bias=cbias[0:rows, 1:2]),
                     reads=["gtmp", "cbias"], writes=["gtmp"])
                S.op("act", lambda e: e.activation(out=gtmp[0:rows, :], in_=gtmp[0:rows, :], func=AF.Exp, scale=-1.0),
                     reads=["gtmp"], writes=["gtmp"])
            else:
                S.op("act", lambda e: e.activation(out=gtmp[0:rows, :], in_=psD[0:rows, :], func=AF.Sigmoid, scale=rs),
                     reads=["psD", "rstd"], writes=["gtmp"])
            S.op("dve", lambda e: e.scalar_tensor_tensor(out=gsil[sub][0:rows, :], in0=psD[0:rows, :], scalar=rs,
                                                         in1=gtmp[0:rows, :], op0=ALU.mult, op1=ALU.mult),
                 reads=["psD", "rstd", "gtmp"], writes=["gsil%d" % sub])
            if l == 0:
                S.op("dve", lambda e: e.scalar_tensor_tensor(out=xl[0:rows, :], in0=psE[0:rows, 0:4], scalar=rs,
                                                             in1=bfb[0:rows, :], op0=ALU.mult, op1=ALU.add),
                     reads=["psE", "rstd", "bfb"], writes=["xl"])
                S.op("act", lambda e: e.activation(out=el[0:rows, :], in_=xl[0:rows, :], func=AF.Exp, scale=-1.0),
                     reads=["xl"], writes=["el"])
                S.op("act", lambda e: e.activation(out=lps[0:rows, :], in_=el[0:rows, :], func=AF.Ln, bias=cbias[0:rows, 1:2]),
                     reads=["el", "cbias"], writes=["lps"])
                S.op("pool", lambda e: e.tensor_scalar(out=lfo[0:rows, tb, :], in0=lps[0:rows, :], scalar1=-1.0,
                                                       scalar2=0.0, op0=ALU.mult, op1=ALU.add),
                     reads=["lps"], writes=["lfo"])
            transposes_to(q_tok, "q_tok", rows, lambda h: qT[:, :, qcol0:qcol0 + rows], ["qT"], "dve")
            transposes_to(k_tok, "k_tok", rows, kT_dst, kT_names, "act")

        def cumsum_block(tb):
            first = (tb == 0)
            S.op("pe", lambda e: e.matmul(psE[:, 8:12], lhsT=cf(C_U), rhs=lps[:, 0:4], start=True, stop=first),
                 reads=["cstf", "lps"], writes=["psE"], sig=first)
            if not first:
                S.op("pe", lambda e: e.matmul(psE[:, 8:12], lhsT=cf(C_E127), rhs=cpall[:, tb - 1, 0:4], start=False,
                                              stop=True),
                     reads=["cstf", "cpall"], writes=["psE"], sig=False)
                S.op("pe", lambda e: e.matmul(psE[:, 12:16], lhsT=cf(C_E127), rhs=cpall[:, tb - 1, 0:4], start=True,
                                              stop=True),
                     reads=["cstf", "cpall"], writes=["psE"])
                S.op("dve", lambda e: e.tensor_copy(out=cpall[:, tb, 0:8], in_=psE[:, 8:16]), reads=["psE"],
                     writes=["cpall"])
            else:
                S.op("dve", lambda e: e.tensor_copy(out=cpall[:, tb, 0:4], in_=psE[:, 8:12]), reads=["psE"],
                     writes=["cpall"])

        sc_banks = [(psA, "psA"), (psB, "psB")]
        o_banks = [((psC, "psC"), (psD, "psD")), ((psF, "psF"), (psG, "psG"))]

        def y_out(tbs, rows_l):
            for sub, tb in enumerate(tbs):
                rows = rows_l[sub]
                for c in range(4):
                    S.op("pe", lambda e, c=c, sub=sub, rows=rows: e.transpose(out=pst[:, c * 128:c * 128 + rows],
                                                                               in_=ybuf[sub][0:rows, c * 128:(c + 1) * 128],
                                                                               identity=ident[0:rows, 0:rows]),
                         reads=["y%d_%d" % (sub, hh) for hh in range(4)] + ["cstb"], writes=PST_ALL, sig=(c == 3))
                i = cnt["tr"] % 2
                cnt["tr"] += 1
                pv = pst[:, 0:512].rearrange("p (c t) -> p c t", t=128)
                S.op("act", lambda e, i=i, rows=rows, pv=pv: e.activation(out=yTs[i][:, :, 0:rows], in_=pv[:, :, 0:rows],
                                                                           func=AF.Identity),
                     reads=PST_ALL, writes=["yTs%d" % i])
                pc, c0 = piece_of(tb)
                dst = yT_loc[pc].ap().rearrange("(c p) t -> p c t", p=128)
                S.op("sp", lambda e, i=i, rows=rows, c0=c0, dst=dst: e.dma_start(out=dst[:, :, c0:c0 + rows],
                                                                                  in_=yTs[i][:, :, 0:rows]),
                     reads=["yTs%d" % i], writes=["yT_loc%d" % pc], chan="yTs%d" % i)
                if tb % 8 == 7 or tb == 32:
                    gather(yT_loc[pc], yT_all[pc], "yT", pc)

        pcount = {"p": 0, "sc": 0, "ch": 0}

        def run_streams(makers, width):
            pending = list(makers)
            active = []
            free = list(range(width))
            eng_free = {}
            now = 0.0
            while pending or active:
                while pending and free:
                    sl = free.pop(0)
                    g = pending.pop(0)(sl)
                    try:
                        nxt = next(g)
                    except StopIteration:
                        free.append(sl)
                        continue
                    active.append({"g": g, "sl": sl, "ready": now, "nxt": nxt})
                if not active:
                    continue
                best = min(active, key=lambda a: max(eng_free.get(a["nxt"][0], 0.0), a["ready"]))
                eng, dur = best["nxt"]
                start = max(eng_free.get(eng, 0.0), best["ready"])
                end = start + dur
                eng_free[eng] = end
                best["ready"] = end + 0.25
                try:
                    best["nxt"] = next(best["g"])
                except StopIteration:
                    active.remove(best)
                    free.append(best["sl"])
                    now = end

        def fox_stream(slot, Q, h):
            nkb = 2 * Q + 2
            ob = o_banks[slot]
            fsc = [(psA, "psA"), (psB, "psB"), (psE, "psE"), (pst_f32, "pstbank"), (psF, "psF"), (psG, "psG")]
            for kb0 in range(0, nkb, 6):
                kbs = list(range(kb0, min(kb0 + 6, nkb)))
                yield ("pe", 0.5)
                for i, kb in enumerate(kbs):
                    qlo = 128 if kb == nkb - 1 else 0
                    scb, scn = fsc[i]
                    S.op("pe", lambda e, kb=kb, qlo=qlo, scb=scb: e.matmul(
                        scb[:, qlo:256], lhsT=kT(h, kb * 128, (kb + 1) * 128), rhs=qT[:, h, qlo:256], start=True, stop=True),
                        reads=["kT%d_%d" % (h, kb), "qT"], writes=[scn])
                yield ("act", 1.8)
                for i, kb in enumerate(kbs):
                    qlo = 128 if kb == nkb - 1 else 0
                    scb, scn = fsc[i]
                    S.op("act", lambda e, kb=kb, qlo=qlo, i=i, scb=scb: e.activation(
                        out=Pb[i][:, qlo:256], in_=scb[:, qlo:256], func=AF.Exp, scale=SCALE, bias=biasT[:, kb, h:h + 1]),
                        reads=[scn, "biasT"], writes=["P%d" % i])
                    if kb >= nkb - 2:
                        dq = 0 if kb == nkb - 2 else 128
                        S.op("pool", lambda e, i=i, dq=dq: e.tensor_tensor(out=Pb[i][:, dq:dq + 128],
                                                                           in0=Pb[i][:, dq:dq + 128], in1=maskLE,
                                                                           op=ALU.mult),
                             reads=["P%d" % i, "cstb"], writes=["P%d" % i])
                yield ("pe", 1.2)
                for i, kb in enumerate(kbs):
                    for sub in range(2):
                        if kb == nkb - 1 and sub == 0:
                            continue
                        last = (kb == nkb - 2) if sub == 0 else (kb == nkb - 1)
                        S.op("pe", lambda e, kb=kb, sub=sub, i=i, last=last: e.matmul(
                            ob[sub][0][:, 0:129], lhsT=Pb[i][:, sub * 128:(sub + 1) * 128], rhs=Vaug(kb, h, 129),
                            start=(kb == 0), stop=last),
                            reads=["P%d" % i, "V%d" % kb], writes=[ob[sub][1]], sig=(sub == 1 or kb == nkb - 2))
            yield ("dve", 0.5)
            for sub in range(2):
                S.op("dve", lambda e, sub=sub: e.reciprocal(out=rden[:, 2 * slot + sub:2 * slot + sub + 1],
                                                            in_=ob[sub][0][:, 128:129]),
                     reads=[ob[sub][1]], writes=["rden%d" % slot])
                S.op("dve", lambda e, sub=sub: e.scalar_tensor_tensor(
                    out=ybuf[sub][:, h * 128:(h + 1) * 128], in0=ob[sub][0][:, 0:128],
                    scalar=rden[:, 2 * slot + sub:2 * slot + sub + 1],
                    in1=gsil[sub][:, h * 128:(h + 1) * 128], op0=ALU.mult, op1=ALU.mult),
                    reads=[ob[sub][1], "rden%d" % slot, "gsil%d" % sub], writes=["y%d_%d" % (sub, h)])

        def fox_tile(Q):
            nkb = 2 * Q + 2
            for h in range(4):
                S.op("dve", lambda e, h=h: e.tensor_scalar(out=biasT[:, 0:nkb, h], in0=cpall[:, 0:nkb, h],
                                                           scalar1=cpall[:, 2 * Q + 1, 4 + h:5 + h], scalar2=None,
                                                           op0=ALU.subtract),
                     reads=["cpall"], writes=["biasT"])
            run_streams([(lambda sl, h=h: fox_stream(sl, Q, h)) for h in range(4)], FOXW)

        def sb_stream(slot, qrows, qT_ap, qT_names, chunks, o_ap, o_name, pv, total_pv, carry_in=None, aT_hook=None,
                      fin=None, out_state=None):
            prev = carry_in
            th, Rx, a_ = thb[slot], Rext[slot], ab[slot]
            tn = ["th0", "th1", "th2", "gtmp"][slot]
            rn = ["Rx0", "Rx1", "Rx2", "kf321"][slot]
            an = ["a0", "a1", "a2", "k_tok"][slot]
            aTnames = [["aT0"], ["aT1"], ["P0", "P1"], ["q_tok"]][slot]
            scb, scn = [(psA, "psA"), (psB, "psB"), (psE, "psE"), (psG, "psG")][slot]
            pst, pn, pso = scb[:, :].bitcast(BF16), scn, 0
            for ci, ch in enumerate(chunks):
                W = ch["W"]
                yield ("pe", 0.45)
                S.op("pe", lambda e, ch=ch, W=W, scb=scb: e.matmul(scb[0:qrows, 0:W], lhsT=qT_ap, rhs=ch["kT"], start=True, stop=True),
                     reads=qT_names + ch["names"], writes=[scn])
                yield ("act", 0.65)
                S.op("act", lambda e, W=W, scb=scb: e.activation(out=th[0:qrows, 0:W], in_=scb[0:qrows, 0:W], func=AF.Sigmoid,
                                                        scale=-SCALE),
                     reads=[scn], writes=[tn])
                yield ("dve", 1.5 if slot >= 2 else 2.2)
                if ch.get("mask") is not None:
                    M, Mc, dw = ch["mask"]
                    S.op("dve", lambda e, W=W, dw=dw, M=M: e.tensor_tensor(out=th[0:qrows, W - dw:W], in0=th[0:qrows, W - dw:W],
                                                                           in1=M, op=ALU.mult),
                         reads=[tn, "cstf"], writes=[tn])
                    S.op("dve", lambda e, W=W, dw=dw, Mc=Mc: e.tensor_tensor(out=th[0:qrows, W - dw:W], in0=th[0:qrows, W - dw:W],
                                                                             in1=Mc, op=ALU.add),
                         reads=[tn, "cstf"], writes=[tn])
                if prev is None:
                    S.op("dve", lambda e, W=W: e.memset(Rx[0:qrows, W:W + 1], 1.0), reads=[], writes=[rn])
                else:
                    pR, pname = prev
                    S.op("dve", lambda e, W=W, pR=pR: e.tensor_copy(out=Rx[0:qrows, W:W + 1], in_=pR),
                         reads=[pname, an], writes=[rn])
                S.op("dve", lambda e, W=W: e.tensor_tensor_scan(
                    out=Rx[0:qrows, 0:W][:, ::-1], data0=th[0:qrows, 0:W][:, ::-1], data1=zeros[0:qrows, 0:W],
                    initial=Rx[0:qrows, W:W + 1], op0=ALU.mult, op1=ALU.add),
                    reads=[tn, rn, "zeros"], writes=[rn])
                prev = (Rx[0:qrows, 0:1], rn)
                if slot >= 2:
                    yield ("pool", 1.3)
                S.op("pool" if slot >= 2 else "dve", lambda e, W=W: e.tensor_tensor(out=a_[0:qrows, 0:W], in0=Rx[0:qrows, 1:W + 1],
                                                                                   in1=Rx[0:qrows, 0:W], op=ALU.subtract),
                     reads=[rn], writes=[an])
                yield ("pe", 0.7)
                vbl = ch["vblocks"]
                off = 0
                for j, (rhs, wk, vn) in enumerate(vbl):
                    S.op("pe", lambda e, off=off, wk=wk, j=j: e.transpose(out=pst[0:wk, pso + j * 128:pso + j * 128 + qrows],
                                                                          in_=a_[0:qrows, off:off + wk],
                                                                          identity=ident[0:qrows, 0:qrows]),
                         reads=[an, "cstb"], writes=[pn], sig=(j == len(vbl) - 1))
                    off += wk
                yield ("act", 0.65)
                if aT_hook is None:
                    aT = aTb[slot]
                    nb_ = len(vbl)
                    pvw = pst[:, pso:pso + nb_ * 128].rearrange("p (j t) -> p j t", t=128)
                    S.op("act", lambda e, aT=aT, pvw=pvw, nb_=nb_: e.activation(out=aT[:, 0:nb_, 0:qrows], in_=pvw[:, :, 0:qrows],
                                                                                func=AF.Identity),
                         reads=[pn], writes=aTnames)
                    lhs_list = [(aT[0:wk, j, 0:qrows], aTnames) for j, (_, wk, _) in enumerate(vbl)]
                else:
                    lhs_list = aT_hook(ci, vbl, pso, pn, pst)
                yield ("pe", 0.7)
                for j, (rhs, wk, vn) in enumerate(vbl):
                    lhsT, ln = lhs_list[j]
                    n = pv["n"]
                    S.op("pe", lambda e, lhsT=lhsT, rhs=rhs, n=n: e.matmul(o_ap, lhsT=lhsT, rhs=rhs, start=(n == 0),
                                                                           stop=(n == total_pv - 1)),
                         reads=(ln if isinstance(ln, list) else [ln]) + [vn], writes=[o_name], sig=True)
                    pv["n"] += 1
            if out_state is not None:
                out_state["carry"] = prev
            if fin is not None:
                yield ("dve", 0.3)
                fin()

        sb_obank = {0: [(psC, "psC")], 1: [(psD, "psD")], 2: [(psF, "psF")], 3: [(pst_f32, "pstbank")]}
        sb_ocnt = {0: 0, 1: 0}

        def sb_tile(Q):
            makers = []
            for sub in range(2):
                qb = 2 * Q + sub
                e_ = 128 * (qb + 1)
                for h in range(4):
                    def mk(slot, sub=sub, h=h, e_=e_):
                        chunks = []
                        hi = e_
                        while hi > 0:
                            lo = max(0, hi - 512)
                            vbl = [(Vaug(b, h, 128), 128, "V%d" % b) for b in range(lo // 128, hi // 128)]
                            chunks.append(dict(kT=kT(h, lo, hi), W=hi - lo,
                                               names=["kT%d_%d" % (h, b) for b in range(lo // 128, hi // 128)],
                                               vblocks=vbl,
                                               mask=(cf(C_MLT), cf(C_MLTC), 128) if hi == e_ else None))
                            hi = lo
                        ob, on = sb_obank[slot][0]

                        def fin():
                            S.op("dve", lambda e: e.tensor_tensor(out=ybuf[sub][:, h * 128:(h + 1) * 128], in0=ob[:, 0:128],
                                                                  in1=gsil[sub][:, h * 128:(h + 1) * 128], op=ALU.mult),
                                 reads=[on, "gsil%d" % sub], writes=["y%d_%d" % (sub, h)])
                        return sb_stream(slot, 128, qT[:, h, sub * 128:(sub + 1) * 128], ["qT"], chunks, ob[:, 0:128], on,
                                         {"n": 0}, sum(len(c["vblocks"]) for c in chunks), fin=fin)
                    makers.append(mk)
            run_streams(makers, SBW)

        SEQSZ = 8 * 520 + 4096
        assert 4 * SEQSZ <= 4 * T + 32 * 4 * 130
        kstage_f = actT[1][:, :, :].rearrange("p c t -> p (c t)").bitcast(F32).rearrange("p (j c) -> p j c", c=512)

        def sVc_all(s_):
            b0 = s_ * SEQSZ
            return regB[:, b0:b0 + 8 * 520].rearrange("p (j h e) -> p j h e", h=4, e=130)

        def sVc(s_, j, h, n):
            o = s_ * SEQSZ + (j * 4 + h) * 130
            return regB[:, o:o + n]

        def skT(s_, h, lo, hi):
            o = s_ * SEQSZ + 8 * 520 + h * 1024
            return regB[:, o + lo:o + hi]

        def Vn_v():
            return Vn_t[:, :].rearrange("p (h e) -> p h e", e=130)

        def sample_stage(l):
            kcache, vcache = (cfk, cfv) if l == 0 else (csk, csv)
            for s_ in range(4):
                for j in range(8):
                    S.op("pool", lambda e, s_=s_, j=j: e.dma_start(
                        out=sVc_all(s_)[:, j, :, 0:128],
                        in_=vcache[s_][j * 128:(j + 1) * 128, :].rearrange("p (h d) -> p h d", d=128)),
                         writes=["sVc%d" % s_] + (REGB_ALL + ["w_out"] if j == 0 else []), chan="sVc")
                S.op("pool", lambda e, s_=s_: e.memset(sVc_all(s_)[:, :, :, 128:129], 1.0), writes=["sVc%d" % s_])
                for hf in range(2):
                    S.op("sp", lambda e, s_=s_, hf=hf: e.dma_start(
                        out=kstage_f, in_=kcache[s_][hf * 512:(hf + 1) * 512, :].rearrange("(j p) c -> p j c", p=128)),
                         writes=["actT1"], chan="sKc")
                    for jj in range(4):
                        j = hf * 4 + jj
                        for h in range(4):
                            S.op("pe", lambda e, jj=jj, h=h: e.transpose(out=pst_f32[:, h * 128:(h + 1) * 128],
                                                                         in_=kstage_f[:, jj, h * 128:(h + 1) * 128],
                                                                         identity=cstf[:, C_ID:C_ID + 128]),
                                 reads=["actT1", "cstf"], writes=PST_ALL, sig=(h == 3))
                        kb_ = s_ * SEQSZ + 8 * 520
                        S.op("act", lambda e, kb_=kb_, j=j: e.activation(
                            out=regB[:, kb_:kb_ + 4096].rearrange("p (h t) -> p h t", t=1024)[:, :, j * 128:(j + 1) * 128],
                            in_=pst_f32[:, 0:512].rearrange("p (h t) -> p h t", t=128), func=AF.Identity),
                             reads=PST_ALL, writes=["skT%d" % s_] + (REGB_ALL + ["w_out"] if j == 0 else []))
            if l == 0:
                for s_ in range(4):
                    S.op("sp", lambda e, s_=s_: e.dma_start(out=clf[:, :, s_ * 4:(s_ + 1) * 4],
                                                            in_=cfl[s_].rearrange("(j p) h -> p j h", p=128)),
                         writes=["clf"], chan="clf")
                for j in range(8):
                    S.op("pe", lambda e, j=j: e.matmul(psE[:, 16 + 16 * j:32 + 16 * j], lhsT=cf(C_LS), rhs=clf[:, j, :], start=True,
                                                       stop=(j == 7)), reads=["cstf", "clf"], writes=["psE"], sig=(j == 7))
                    for j2 in range(j + 1, 8):
                        S.op("pe", lambda e, j=j, j2=j2: e.matmul(psE[:, 16 + 16 * j:32 + 16 * j], lhsT=cf(C_ONES), rhs=clf[:, j2, :],
                                                                  start=False, stop=(j2 == 7)), reads=["cstf", "clf"],
                             writes=["psE"], sig=(j2 == 7))
                S.op("dve", lambda e: e.tensor_copy(out=sufb[:].rearrange("p j c -> p (j c)"), in_=psE[:, 16:144]), reads=["psE"],
                     writes=["sufb"])

        def sample_block(l, slot):
            tb = 32
            rows = TS
            Vn = Vn_v()
            project_block(l, slot, 0, tb, rows, lambda h: qT[:, :, 128:128 + rows], ["qT"],
                          Vn_v, ["Vn"], 0)
            if l == 0:
                S.op("pe", lambda e: e.matmul(psE[0:64, 8:12], lhsT=cf(C_UB16, 64, 64), rhs=lps[0:64, 0:4], start=True, stop=True),
                     reads=["cstf", "lps"], writes=["psE"])
                S.op("dve", lambda e: e.tensor_copy(out=cpn[:, :], in_=psE[0:64, 8:12]), reads=["psE"], writes=["cpn"])
            for h in range(4):
                ob, on = o_banks[h % 2][0]
                if l == 0:
                    o_ap = ob[0:64, 0:129]
                    first = True
                    for s_ in range(4):
                        for j in range(8):
                            scb, scn = sc_banks[pcount["sc"] % 2]
                            pcount["sc"] += 1
                            S.op("pe", lambda e, s_=s_, j=j, h=h, scb=scb: e.matmul(
                                scb[:, 0:16], lhsT=skT(s_, h, j * 128, (j + 1) * 128), rhs=qT[:, h, s_ * 16:(s_ + 1) * 16],
                                start=True, stop=True), reads=["skT%d" % s_, "qT"], writes=[scn])
                            S.op("act", lambda e, s_=s_, j=j, h=h, scb=scb: e.activation(
                                out=Pw[s_][:, s_ * 16:(s_ + 1) * 16], in_=scb[:, 0:16], func=AF.Exp, scale=SCALE,
                                bias=sufb[:, j, s_ * 4 + h:s_ * 4 + h + 1]), reads=[scn, "sufb"], writes=["Pw%d" % s_])
                            S.op("pe", lambda e, s_=s_, j=j, h=h, first=first, o_ap=o_ap: e.matmul(
                                o_ap, lhsT=Pw[s_][:, 0:64], rhs=sVc(s_, j, h, 129), start=first, stop=False),
                                reads=["Pw%d" % s_, "sVc%d" % s_], writes=[on], sig=True)
                            first = False
                    scb, scn = sc_banks[pcount["sc"] % 2]
                    pcount["sc"] += 1
                    S.op("pe", lambda e, h=h, scb=scb: e.matmul(scb[0:64, 0:64], lhsT=qT[:, h, 128:192], rhs=qT[:, h, 0:64],
                                                                start=True, stop=True), reads=["qT"], writes=[scn])
                    S.op("act", lambda e, h=h, scb=scb: e.activation(out=Pn[:, :], in_=scb[0:64, 0:64], func=AF.Exp, scale=SCALE,
                                                                     bias=cpn[:, h:h + 1]), reads=[scn, "cpn"], writes=["Pn"])
                    S.op("pool", lambda e: e.tensor_tensor(out=Pn[:, :], in0=Pn[:, :], in1=mask64, op=ALU.mult),
                         reads=["Pn", "cstb"], writes=["Pn"])
                    S.op("pe", lambda e, h=h, o_ap=o_ap: e.matmul(o_ap, lhsT=Pn[:, :], rhs=Vn[0:64, h, 0:129], start=False, stop=True),
                         reads=["Pn", "Vn"], writes=[on])
                    S.op("dve", lambda e, ob=ob: e.reciprocal(out=rden[0:64, 0:1], in_=ob[0:64, 128:129]), reads=[on],
                         writes=["rden0"])
                    S.op("dve", lambda e, h=h, ob=ob: e.scalar_tensor_tensor(
                        out=ybuf[0][0:64, h * 128:(h + 1) * 128], in0=ob[0:64, 0:128], scalar=rden[0:64, 0:1],
                        in1=gsil[0][0:64, h * 128:(h + 1) * 128], op0=ALU.mult, op1=ALU.mult),
                        reads=[on, "rden0", "gsil0"], writes=["y0_%d" % h])
                else:
                    pass
            if l == 1:
                def head_stream(slot, h):
                    ob, on = sb_obank[slot][0]
                    o_ap = ob[0:64, 0:128]
                    pv = {"n": 0}
                    total = 1 + 4 * 8

                    def hook0(ci, vbl, pso, pn, pst_=None):
                        S.op("act", lambda e: e.activation(out=aTn[:, :], in_=pst_[0:64, pso:pso + 64], func=AF.Identity),
                             reads=[pn], writes=["aTn"])
                        return [(aTn[:, :], "aTn")]
                    ch0 = dict(kT=qT[:, h, 128:192], W=64, names=["qT"], vblocks=[(Vn[0:64, h, 0:128], 64, "Vn")],
                               mask=(cf(C_MS, 64, 64), cf(C_MSC, 64, 64), 64))
                    st = {}
                    yield from sb_stream(slot, 64, qT[:, h, 0:64], ["qT"], [ch0], o_ap, on, pv, total, None, hook0, None, st)
                    car = st["carry"]
                    cr = carry0[:, slot:slot + 1]
                    yield ("dve", 0.1)
                    S.op("dve", lambda e: e.tensor_copy(out=cr, in_=car[0]), reads=[car[1]], writes=["carry0_%d" % slot])
                    for s_ in range(4):
                        def hook(ci, vbl, pso, pn, pst_=None, s_=s_):
                            base = 4 if ci == 0 else 0
                            pvw = pst_[:, pso:pso + 512].rearrange("p (j t) -> p j t", t=128)
                            S.op("act", lambda e: e.activation(out=aTw[s_][:, base:base + 4, s_ * 16:(s_ + 1) * 16],
                                                               in_=pvw[:, :, s_ * 16:(s_ + 1) * 16], func=AF.Identity),
                                 reads=[pn], writes=["aTw%d" % s_])
                            return [(aTw[s_][:, base + j, 0:64], "aTw%d" % s_) for j in range(4)]
                        chunks = []
                        for (lo, hi) in ((512, 1024), (0, 512)):
                            chunks.append(dict(kT=skT(s_, h, lo, hi), W=512, names=["skT%d" % s_],
                                               vblocks=[(sVc(s_, b, h, 128), 128, "sVc%d" % s_) for b in range(lo // 128, hi // 128)],
                                               mask=None))
                        yield from sb_stream(slot, 64, qT[:, h, 0:64], ["qT"], chunks, o_ap, on, pv, total,
                                             (cr, "carry0_%d" % slot), hook)
                    yield ("dve", 0.3)
                    S.op("dve", lambda e: e.tensor_tensor(out=ybuf[0][0:64, h * 128:(h + 1) * 128], in0=ob[0:64, 0:128],
                                                          in1=gsil[0][0:64, h * 128:(h + 1) * 128], op=ALU.mult),
                         reads=[on, "gsil0"], writes=["y0_%d" % h])
                run_streams([(lambda sl, h=h: head_stream(sl, h)) for h in range(4)], SBW)
            load_w_out(l)
            y_out([tb], [rows])

        def p_phase(l):
            load_act(0, xT_all, "xT", 0, 256)
            nt = 16
            for Q in range(nt):
                slot = Q % 2
                if Q < 15:
                    load_act(1 - slot, xT_all, "xT", (Q + 1) * 256, 256)
                else:
                    load_act(1 - slot, xT_all, "xT", T, TS)
                for sub in range(2):
                    tb = 2 * Q + sub
                    project_block(l, slot, sub, tb, 128,
                                  lambda h, tb=tb: regB[:, 0:4 * T].rearrange("p (h t) -> p h t", t=T)[:, :, tb * 128:(tb + 1) * 128],
                                  ["kT%d_%d" % (h, tb) for h in range(4)],
                                  lambda tb=tb: Vaug_blk(tb), ["V%d" % tb], sub * 128)
                    if l == 0:
                        cumsum_block(tb)
                if l == 0:
                    fox_tile(Q)
                else:
                    sb_tile(Q)
                y_out([2 * Q, 2 * Q + 1], [128, 128])
            if nt == 16:
                sample_stage(l)
                sample_block(l, 0)

        def o_phase(l):
            wq = load_w_in(1) if l == 0 else iter(())
            wv = w_out()
            load_act(0, yT_all, "yT", 0, 256)
            xsrc = xc0 if l == 0 else xres[1].ap()
            xdst = xres[l + 1].ap()

            def o_load(tb):
                rows = blk_rows(tb)
                i = tb % 2
                S.op("sp", lambda e: e.dma_start(out=xcb[i][0:rows, :], in_=xsrc[tb * 128:tb * 128 + rows, :]),
                     reads=["xres%d" % l], writes=["kf32%d" % i], chan="kf32%d" % i)

            for Q in range(17):
                slot = Q % 2
                if Q < 15:
                    load_act(1 - slot, yT_all, "yT", (Q + 1) * 256, 256)
                elif Q == 15:
                    load_act(1 - slot, yT_all, "yT", T, TS)
                next(wq, None)
                for sub in range(2 if Q < 16 else 1):
                    tb = 2 * Q + sub
                    rows = blk_rows(tb)
                    i = tb % 2
                    if tb == 0:
                        o_load(0)
                    if tb + 1 < NB:
                        o_load(tb + 1)
                    for c in range(16):
                        S.op("pe", lambda e, c=c, slot=slot, sub=sub, rows=rows: e.matmul(
                            psA[0:rows, :], lhsT=actT[slot][:, c, sub * 128:sub * 128 + rows], rhs=wv[:, c, :], start=(c == 0),
                            stop=(c == 15)), reads=["actT%d" % slot, "w_out"], writes=["psA"], sig=(c == 15))
                    S.op("dve", lambda e, i=i, rows=rows: e.tensor_tensor(out=xnb[i][0:rows, :], in0=psA[0:rows, :],
                                                                          in1=xcb[i][0:rows, :], op=ALU.add),
                         reads=["psA", "kf32%d" % i], writes=["vf32%d" % i])
                    S.op("sp", lambda e, i=i, tb=tb, rows=rows: e.dma_start(out=xdst[tb * 128:tb * 128 + rows, :],
                                                                              in_=xnb[i][0:rows, :]),
                         reads=["vf32%d" % i], writes=["xres%d" % (l + 1)], chan="vf32%d" % i)
                    norm_prep(xnb[i], "vf32%d" % i, tb, rows, l + 1, l == 0)
            for _ in wq:
                pass

        stage = 99
        if stage >= 2:
            p_phase(0)
        if stage >= 4:
            load_nw(1)
            o_phase(0)
            rstd_from_ssq("1")
        if stage >= 5:
            p_phase(1)
        if stage >= 7:
            load_nw(2)
            o_phase(1)
            rstd_from_ssq("2")
            x2 = xres[2].ap()
            def f_load(tb):
                rows = blk_rows(tb)
                i = tb % 2
                S.op("sp", lambda e: e.dma_start(out=xcb[i][0:rows, :], in_=x2[tb * 128:tb * 128 + rows, :]),
                     reads=["xres2"], writes=["kf32%d" % i], chan="kf32%d" % i)

            f_load(0)
            for tb in range(NB):
                rows = blk_rows(tb)
                i = tb % 2
                if tb + 1 < NB:
                    f_load(tb + 1)
                S.op("dve", lambda e, i=i, tb=tb, rows=rows: e.scalar_tensor_tensor(
                    out=xnb[i][0:rows, :], in0=xcb[i][0:rows, :], scalar=rstd[0:rows, tb:tb + 1], in1=nw[2][0:rows, :],
                    op0=ALU.mult, op1=ALU.mult), reads=["kf32%d" % i, "rstd", "nw"], writes=["vf32%d" % i])
                S.op("sp", lambda e, i=i, tb=tb, rows=rows: e.dma_start(out=yout[tb * 128:tb * 128 + rows, :], in_=xnb[i][0:rows, :]),
                     reads=["vf32%d" % i], writes=[], chan="vf32%d" % i)
        S.op("sp", lambda e: e.dma_start(out=lfout[0:T, :].rearrange("(b p) h -> p b h", p=128), in_=lfo[:, 0:32, :]),
             reads=["lfo"], writes=[], chan="lfo")
        S.op("sp", lambda e: e.dma_start(out=lfout[T:TT, :], in_=lfo[0:TS, 32, :]), reads=["lfo"], writes=[], chan="lfo")
        for sn, v in list(S.cnt.items()):
            if sn.startswith("d_"):
                S.prog["sp"].append(("wait", sn, v))

        sem_names = sorted(S.cnt.keys())
        sems = {}
        for sn in sem_names:
            sems[sn] = es.enter_context(nc.semaphore(sn))
        block = es.enter_context(nc.Block())

        def emit(eng_obj, name):
            for item in S.prog[name]:
                if item[0] == "wait":
                    eng_obj.wait_ge(sems[item[1]], item[2])
                else:
                    _, fn, sn, inc = item
                    ins = fn(eng_obj)
                    if sn is not None:
                        if inc is None:
                            ins.then_inc(sems[sn])
                        else:
                            ins.then_inc(sems[sn], inc)

        @block.sync
        def _(e):
            emit(e, "sp")

        @block.scalar
        def _(e):
            emit(e, "act")

        @block.vector
        def _(e):
            emit(e, "dve")

        @block.gpsimd
        def _(e):
            emit(e, "pool")

        @block.tensor
        def _(e):
            emit(e, "pe")
    return nc


def _consts():
    c = np.zeros((128, NCST), np.float32)
    k = np.arange(128)[:, None]
    m = np.arange(128)[None, :]
    c[:, C_U:C_U + 128] = (k <= m)
    c[:, C_E127:C_E127 + 128] = (k == 127)
    c[:, C_UB16:C_UB16 + 128] = (k <= m) & (k // 16 == m // 16)
    c[:, C_LS:C_LS + 128] = (k > m)
    c[:, C_ONES:C_ONES + 128] = 1.0
    c[:, C_MS:C_MS + 128] = (m < k) & (k // 16 == m // 16)
    c[:, C_MSC:C_MSC + 128] = 1.0 - ((m < k) & (k // 16 == m // 16))
    c[:, C_MLT:C_MLT + 128] = (m < k)
    c[:, C_MLTC:C_MLTC + 128] = 1.0 - (m < k)
    c[:, C_ID:C_ID + 128] = (k == m)
    c[:, C_MLE:C_MLE + 128] = (k <= m)
    c[:, C_M64:C_M64 + 128] = (k <= m) & (k // 16 == m // 16)
    return c


_NC_CACHE = {}


def kernel(x_prompt, x_sample, cache_fox_k, cache_fox_v, cache_fox_logf, cache_sb_k, cache_sb_v,
           norm_0, w_in_0, b_f_0, w_out_0, norm_1, w_in_1, w_out_1, norm_f):
    f = np.float32
    A = lambda a: np.ascontiguousarray(np.asarray(a, dtype=f))
    x_prompt, x_sample = A(x_prompt), A(x_sample)
    w_in_0, w_in_1, w_out_0, w_out_1 = A(w_in_0), A(w_in_1), A(w_out_0), A(w_out_1)
    caches = [A(cache_fox_k), A(cache_fox_v), A(cache_sb_k), A(cache_sb_v)]
    cache_fox_logf = A(cache_fox_logf)
    norms = [A(norm_0), A(norm_1), A(norm_f)]
    b_f_0 = A(b_f_0)
    if "nc" not in _NC_CACHE:
        _NC_CACHE["nc"] = build_program()
    nc = _NC_CACHE["nc"]
    cst = _consts()
    in_maps = []
    for core in range(8):
        b, g = core // 4, core % 4
        cs = slice(g * 512, (g + 1) * 512)
        xc0 = np.concatenate([x_prompt[b][:, cs], x_sample[4 * b:4 * b + 4].reshape(64, D)[:, cs]], axis=0)
        m = {"xc0": A(xc0), "cst": cst}
        for i, nm in enumerate(["nw0", "nw1", "nwf"]):
            m[nm] = A(np.broadcast_to(norms[i][cs][None, :], (128, 512)))
        m["win0"] = A(np.concatenate([w_in_0[:, j * D + g * 512: j * D + (g + 1) * 512] for j in range(4)]
                                     + [w_in_0[:, 4 * D + 4 * g: 4 * D + 4 * g + 4]], axis=1))
        m["win1"] = A(np.concatenate([w_in_1[:, j * D + g * 512: j * D + (g + 1) * 512] for j in range(4)], axis=1))
        m["wout0"] = A(w_out_0[:, cs])
        m["wout1"] = A(w_out_1[:, cs])
        m["bfb"] = A(np.broadcast_to(b_f_0[4 * g:4 * g + 4][None, :], (128, 4)))
        for nm, cch in zip(["cfk", "cfv", "csk", "csv"], caches):
            m[nm] = A(cch[4 * b:4 * b + 4, :, 4 * g:4 * g + 4, :].reshape(4, PAST, 512))
        m["cfl"] = A(cache_fox_logf[4 * b:4 * b + 4, :, 4 * g:4 * g + 4])
        in_maps.append(m)
    res = run_bass_kernel_spmd(nc, in_maps, core_ids=list(range(8)))
    R = res.results
    y_prompt = np.zeros((2, T, D), f)
    y_sample = np.zeros((8, 16, D), f)
    pk = [np.zeros((2, T, 16, 128), f) for _ in range(4)]
    sk = [np.zeros((8, 16, 16, 128), f) for _ in range(4)]
    plf = np.zeros((2, T, 16), f)
    slf = np.zeros((8, 16, 16), f)
    for core in range(8):
        b, g = core // 4, core % 4
        cs = slice(g * 512, (g + 1) * 512)
        r = R[core]
        y_prompt[b][:, cs] = r["yout"][:T]
        y_sample[4 * b:4 * b + 4][:, :, cs] = r["yout"][T:].reshape(4, 16, 512)
        for i, nm in enumerate(["kf", "vf", "ks", "vs"]):
            pk[i][b][:, 4 * g:4 * g + 4, :] = r[nm][:T].reshape(T, 4, 128)
            sk[i][4 * b:4 * b + 4][:, :, 4 * g:4 * g + 4, :] = r[nm][T:].reshape(4, 16, 4, 128)
        plf[b][:, 4 * g:4 * g + 4] = r["lf"][:T]
        slf[4 * b:4 * b + 4][:, :, 4 * g:4 * g + 4] = r["lf"][T:].reshape(4, 16, 4)
    return (y_prompt, y_sample, pk[0], pk[1], plf, pk[2], pk[3], sk[0], sk[1], slf, sk[2], sk[3])
```

```python
import numpy as np
import concourse.bass as bass
import concourse.mybir as mybir
from concourse.bass_utils import run_bass_kernel_spmd

F32 = mybir.dt.float32
BF16 = mybir.dt.bfloat16
AF = mybir.ActivationFunctionType
ALU = mybir.AluOpType

D = 2048
T = 4096
TS = 64
TT = T + TS
NB = 33
PAST = 1024
SCALE = 128 ** -0.5
EPS = 1e-6
GROUPS = [[0, 1, 2, 3], [4, 5, 6, 7]]
ENG = ("sp", "act", "dve", "pool", "pe")
FOXW = 1
SBW = 4

C_U, C_E127, C_UB16, C_LS, C_ONES, C_MLT, C_MLTC, C_MS, C_MSC, C_ID, C_MLE, C_M64 = [128 * i for i in range(12)]
NCST = 128 * 12


class Sched:
    def __init__(self):
        self.prog = {e: [] for e in ENG}
        self.cnt = {}
        self.waited = {}
        self.lastw = {}
        self.readers = {}

    def _need(self, eng, events):
        for sn, val in events:
            if sn.startswith("d_"):
                val = self.cnt[sn]
            if sn == "pe" and eng == "pe":
                continue
            key = (eng, sn)
            if self.waited.get(key, 0) >= val:
                continue
            self.waited[key] = val
            self.prog[eng].append(("wait", sn, val))

    def op(self, eng, fn, reads=(), writes=(), sig=True, chan=None, cc=None):
        ps_reads = [r for r in reads if r.startswith("ps") and r not in writes]
        if ps_reads:
            writes = list(writes) + ps_reads
        ev = []
        for r in reads:
            if r in self.lastw:
                ev.append(self.lastw[r])
        for w in writes:
            if w in self.lastw:
                ev.append(self.lastw[w])
            ev.extend(self.readers.get(w, {}).items())
        self._need(eng, ev)
        if cc is not None:
            sn = "c_" + cc
            self.cnt[sn] = 1
            myev = (sn, 1)
            self.prog[eng].append(("op", fn, sn, None))
        elif chan is not None:
            sn = "d_" + chan
            self.cnt[sn] = self.cnt.get(sn, 0) + 16
            myev = (sn, self.cnt[sn])
            self.prog[eng].append(("op", fn, sn, 16))
        elif sig:
            self.cnt[eng] = self.cnt.get(eng, 0) + 1
            myev = (eng, self.cnt[eng])
            self.prog[eng].append(("op", fn, eng, 1))
        else:
            myev = (eng, self.cnt.get(eng, 0) + 1)
            self.prog[eng].append(("op", fn, None, 0))
        for r in reads:
            d = self.readers.setdefault(r, {})
            d[myev[0]] = max(d.get(myev[0], 0), myev[1])
        for w in writes:
            self.lastw[w] = myev
            self.readers[w] = {}


def build_program():
    nc = bass.Bass("TRN2", target_bir_lowering=False)
    S = Sched()

    def din(name, shape, dt=F32):
        return nc.dram_tensor(name, shape, dt, kind="ExternalInput").ap()

    def dout(name, shape, dt=F32):
        return nc.dram_tensor(name, shape, dt, kind="ExternalOutput").ap()

    xc0 = din("xc0", [TT, 512])
    nwd = [din("nw0", [128, 512]), din("nw1", [128, 512]), din("nwf", [128, 512])]
    wind = [din("win0", [D, 2052]), din("win1", [D, 2048])]
    woutd = [din("wout0", [D, 512]), din("wout1", [D, 512])]
    bfd = din("bfb", [128, 4])
    cstd = din("cst", [128, NCST])
    cfk = din("cfk", [4, PAST, 512])
    cfv = din("cfv", [4, PAST, 512])
    csk = din("csk", [4, PAST, 512])
    csv = din("csv", [4, PAST, 512])
    cfl = din("cfl", [4, PAST, 4])

    yout = dout("yout", [TT, 512])
    kvout = [[dout("kf", [TT, 512]), dout("vf", [TT, 512])], [dout("ks", [TT, 512]), dout("vs", [TT, 512])]]
    lfout = dout("lf", [TT, 4])

    PW = [1024, 1024, 1024, 1024, TS]
    xT_loc = [nc.dram_tensor("xT_loc%d" % p, [512, PW[p]], BF16) for p in range(5)]
    xT_all = [nc.dram_tensor("xT_all%d" % p, [D, PW[p]], BF16) for p in range(5)]
    yT_loc = [nc.dram_tensor("yT_loc%d" % p, [512, PW[p]], BF16) for p in range(5)]
    yT_all = [nc.dram_tensor("yT_all%d" % p, [D, PW[p]], BF16) for p in range(5)]

    def piece_of(tb):
        return (tb // 8, (tb % 8) * 128) if tb < 32 else (4, 0)
    ssq_loc = nc.dram_tensor("ssq_loc", [128, NB], F32)
    ssq_all = nc.dram_tensor("ssq_all", [512, NB], F32)
    xres = [None, nc.dram_tensor("x1s", [TT, 512], F32), nc.dram_tensor("x2s", [TT, 512], F32)]

    from contextlib import ExitStack
    es = ExitStack()

    def sb(name, shape, dt):
        return es.enter_context(nc.sbuf_tensor(name, shape, dt))

    def ps(name, shape, dt):
        return es.enter_context(nc.psum_tensor(name, shape, dt))

    with es:
        cstf = sb("cstf", [128, C_ID + 128], F32)
        cstb = sb("cstb", [128, 3 * 128], BF16)
        zeros = sb("zeros", [128, 512], BF16)
        nwt = sb("nwt", [128, 512], F32)
        nw = [nwt, nwt, nwt]
        bfb = sb("bfbs", [128, 4], F32)
        w_in = sb("w_in", [128, 16, 2052], BF16)
        regB = sb("regB", [128, 4 * T + 32 * 4 * 130], BF16)
        actT = [sb("actT0", [128, 16, 256], BF16), sb("actT1", [128, 16, 256], BF16)]
        ssq_part = sb("ssq_part", [128, NB], F32)
        ssq4 = sb("ssq4", [128, 4, NB], F32)
        ssum = sb("ssum", [128, NB], F32)
        rstd = sb("rstd", [128, NB], F32)
        nrstd = sb("nrstd", [128, NB], F32)
        hrstd = sb("hrstd", [128, NB], F32)
        cpall = sb("cpall", [128, NB, 8], F32)
        lfo = sb("lfo", [128, NB, 4], F32)
        lps = sb("lps", [128, 4], F32)
        xl = sb("xl", [128, 4], F32)
        el = sb("el", [128, 4], F32)
        q_tok = sb("q_tok", [128, 512], BF16)
        k_tok = sb("k_tok", [128, 512], BF16)
        kf32b_big = sb("kf32b", [128, 516], F32)
        kf32 = [sb("kf32a", [128, 512], F32)[:, :], kf32b_big[:, 0:512]]
        vf32 = [sb("vf32a", [128, 512], F32), sb("vf32b", [128, 512], F32)]
        gsil = [sb("gsil0", [128, 512], F32), sb("gsil1", [128, 512], F32)]
        gtmp = sb("gtmp", [128, 512], F32)
        qT = sb("qT", [128, 4, 256], BF16)
        ybuf = [sb("y0", [128, 512], BF16), sb("y1", [128, 512], BF16)]
        yTs = [sb("yTs0", [128, 4, 128], BF16), sb("yTs1", [128, 4, 128], BF16)]
        Pball = sb("Pball", [128, 6, 256], BF16)
        Pb = [Pball[:, i, :] for i in range(6)]
        biasT = sb("biasT", [128, 32, 4], F32)
        rden = sb("rden", [128, 4], F32)
        thb = [sb("th%d" % i, [128, 512], F32) for i in range(3)] + [gtmp]
        Rext = [sb("Rx%d" % i, [128, 516], F32) for i in range(3)] + [kf32b_big]
        ab = [sb("a%d" % i, [128, 512], BF16) for i in range(3)] + [k_tok]
        aTb = [sb("aT0", [128, 4, 128], BF16), sb("aT1", [128, 4, 128], BF16),
               Pball[:, 0:2, :].rearrange("p a (b t) -> p (a b) t", t=128),
               q_tok[:, :].rearrange("p (j t) -> p j t", t=128)]
        xcb = kf32
        xnb = vf32
        xtb = q_tok
        sqj = gtmp
        xTs = yTs
        Pw = [sb("Pw%d" % i, [128, 64], BF16) for i in range(4)]
        Pn = sb("Pn", [64, 64], BF16)
        clf = sb("clf", [128, 8, 16], F32)
        sufb = sb("sufb", [128, 8, 16], F32)
        cpn = sb("cpn", [64, 4], F32)
        aTw = [sb("aTw%d" % i, [128, 8, 64], BF16) for i in range(4)]
        aTn = sb("aTn", [64, 64], BF16)
        Vn_t = sb("Vn_t", [128, 520], BF16)
        carry0 = sb("carry0", [64, 4], F32)
        cbias = sb("cbias", [128, 2], F32)

        psA = ps("psA", [128, 512], F32)
        psB = ps("psB", [128, 512], F32)
        psC = ps("psC", [128, 512], F32)
        psD = ps("psD", [128, 512], F32)
        psE = ps("psE", [128, 512], F32)
        psF = ps("psF", [128, 512], F32)
        psG = ps("psG", [128, 512], F32)
        pst = ps("pst", [128, 1024], BF16)

        KT_OFF = 0
        V_OFF = 4 * T

        def kT(h, lo, hi):
            return regB[:, KT_OFF + h * T + lo: KT_OFF + h * T + hi]

        def Vaug(blk, h, n):
            o = V_OFF + (blk * 4 + h) * 130
            return regB[:, o:o + n]

        def Vaug_blk(blk):
            o = V_OFF + blk * 4 * 130
            return regB[:, o:o + 520].rearrange("p (h e) -> p h e", e=130)

        def w_out():
            o = 4096 + 8 * 520 + 4096
            return regB[:, o:o + 16 * 512].rearrange("p (c n) -> p c n", n=512)

        PST_ALL = ["pstbank"]
        pst_full = pst
        psE_bf = psE[:, :].bitcast(BF16)
        pst_f32 = pst[:, :].bitcast(F32)
        psG_bf = psG[:, :].bitcast(BF16)
        REGB_ALL = ["kT%d_%d" % (h, b) for h in range(4) for b in range(32)] + ["V%d" % b for b in range(32)]

        ident = cstb[:, 0:128]
        maskLE = cstb[:, 128:256]
        mask64 = cstb[0:64, 256:320]

        def cf(c0, n=128, rows=128):
            return cstf[0:rows, c0:c0 + n]

        S.op("sp", lambda e: e.dma_start(out=cstf[:], in_=cstd[:, 0:C_ID + 128]), writes=["cstf"], chan="cst")
        S.op("pool", lambda e: e.dma_start(out=cstb[:], in_=cstd[:, C_ID:C_ID + 384]), writes=["cstb"], chan="cstb")
        def load_nw(i):
            S.op("sp", lambda e: e.dma_start(out=nwt[:], in_=nwd[i][:, :]), writes=["nw"], chan="cst")
        load_nw(0)
        S.op("sp", lambda e: e.dma_start(out=bfb[:], in_=bfd[:, :]), writes=["bfb"], chan="cst")
        S.op("dve", lambda e: e.memset(zeros[:], 0.0), writes=["zeros"])
        S.op("dve", lambda e: e.memset(cbias[:, 0:1], EPS), writes=["cbias"])
        S.op("dve", lambda e: e.memset(cbias[:, 1:2], 1.0), writes=["cbias"])
        S.op("dve", lambda e: e.memset(ssq_part[:], 1.0), writes=["ssq_part"])
        S.op("dve", lambda e: e.memset(cpall[:], 0.0), writes=["cpall"])
        S.op("dve", lambda e: e.memset(lfo[:], 0.0), writes=["lfo"])
        for i in range(4):
            S.op("pool", lambda e, i=i: e.memset(Pw[i][:], 0.0), writes=["Pw%d" % i])
            S.op("pool", lambda e, i=i: e.memset(aTw[i][:], 0.0), writes=["aTw%d" % i])

        if False:
            for nm_, ap_ in [("win0", wind[0][0:128, 0:512]), ("win1", wind[1][0:128, 0:512]), ("wout0", woutd[0][0:128, :]),
                             ("wout1", woutd[1][0:128, :]), ("cfk", cfk[0][0:128, :]), ("cfv", cfv[0][0:128, :]),
                             ("csk", csk[0][0:128, :]), ("csv", csv[0][0:128, :])]:
                S.op("sp", lambda e, ap_=ap_: e.dma_start(out=gtmp[:], in_=ap_), writes=["gtmp"], chan="dbg")
            S.op("sp", lambda e: e.dma_start(out=gtmp[:, 0:4], in_=cfl[0][0:128, :]), writes=["gtmp"], chan="dbg")

        def load_w_in(l):
            ncol = 2052 if l == 0 else 2048
            src = wind[l].rearrange("(c p) n -> p c n", p=128)
            for c in range(16):
                yield
                si = c % 3
                o = 20544 + si * 4104
                stg = regB[:, o:o + 4104].bitcast(F32)
                S.op("sp", lambda e, c=c, stg=stg: e.dma_start(out=stg[:, 0:ncol], in_=src[:, c, :]),
                     writes=["wst%d" % si] + (REGB_ALL + ["sVc%d" % q for q in range(4)] + ["skT%d" % q for q in range(4)]
                                              if c < 3 else []), chan="wst%d" % si)
                if c % 2 == 0:
                    S.op("act", lambda e, c=c, stg=stg: e.activation(out=w_in[:, c, 0:ncol], in_=stg[:, 0:ncol], func=AF.Identity),
                         reads=["wst%d" % si] + (REGB_ALL if c >= 13 else []), writes=["w_in"])
                else:
                    S.op("dve", lambda e, c=c, stg=stg: e.tensor_copy(out=w_in[:, c, 0:ncol], in_=stg[:, 0:ncol]),
                         reads=["wst%d" % si] + (REGB_ALL if c >= 13 else []), writes=["w_in"])

        def load_w_out(l):
            src = woutd[l].rearrange("(c p) n -> p c n", p=128)
            wv = w_out()
            for c in range(0, 16, 4):
                S.op("pool", lambda e, c=c: e.dma_start(out=wv[:, c:c + 4, :], in_=src[:, c:c + 4, :]),
                     writes=["w_out"] + REGB_ALL + ["sVc%d" % q for q in range(4)] + ["skT%d" % q for q in range(4)], chan="wout")

        def blk_rows(tb):
            return 128 if tb < 32 else TS

        cnt = {"xts": 0, "tr": 0}

        def norm_prep(xblk, xres_name, tb, rows, nwi, want_xT, want_gather=True):
            S.op("act", lambda e: e.activation(out=sqj[0:rows, :], in_=xblk[0:rows, :], func=AF.Square,
                                               accum_out=ssq_part[0:rows, tb:tb + 1]),
                 reads=[xres_name], writes=["gtmp", "ssq_part"])
            if not want_xT:
                return
            S.op("dve", lambda e: e.tensor_tensor(out=xtb[0:rows, :], in0=xblk[0:rows, :], in1=nw[nwi][0:rows, :],
                                                  op=ALU.mult),
                 reads=[xres_name, "nw"], writes=["q_tok"])
            for c in range(4):
                S.op("pe", lambda e, c=c: e.transpose(out=pst[:, c * 128:c * 128 + rows],
                                                      in_=xtb[0:rows, c * 128:(c + 1) * 128],
                                                      identity=ident[0:rows, 0:rows]),
                     reads=["q_tok", "cstb"], writes=PST_ALL, sig=(c == 3))
            i = cnt["tr"] % 2
            cnt["tr"] += 1
            pv = pst[:, 0:512].rearrange("p (c t) -> p c t", t=128)
            S.op("act", lambda e: e.activation(out=xTs[i][:, :, 0:rows], in_=pv[:, :, 0:rows], func=AF.Identity),
                 reads=PST_ALL, writes=["yTs%d" % i])
            pc, c0 = piece_of(tb)
            dst = xT_loc[pc].ap().rearrange("(c p) t -> p c t", p=128)
            S.op("sp", lambda e: e.dma_start(out=dst[:, :, c0:c0 + rows], in_=xTs[i][:, :, 0:rows]),
                 reads=["yTs%d" % i], writes=["xT_loc%d" % pc], chan="yTs%d" % i)
            if want_gather and (tb % 8 == 7 or tb == 32):
                gather(xT_loc[pc], xT_all[pc], "xT", pc)

        def rstd_from_ssq(tag):
            S.op("sp", lambda e: e.dma_start(out=ssq_loc.ap(), in_=ssq_part[:]), reads=["ssq_part"],
                 writes=["ssq_loc"], chan="ssq")
            S.op("pool", lambda e: e.collective_compute("AllGather", ALU.bypass, replica_groups=GROUPS,
                                                        ins=[ssq_loc.ap().opt()], outs=[ssq_all.ap().opt()]),
                 reads=["ssq_loc"], writes=["ssq_all"], cc="ssq" + tag)
            S.op("sp", lambda e: e.dma_start(out=ssq4[:], in_=ssq_all.ap().rearrange("(r p) b -> p r b", p=128)),
                 reads=["ssq_all"], writes=["ssq4"], chan="ssq")
            S.op("dve", lambda e: e.tensor_tensor(out=ssum[:], in0=ssq4[:, 0, :], in1=ssq4[:, 1, :], op=ALU.add),
                 reads=["ssq4"], writes=["ssum"])
            S.op("dve", lambda e: e.tensor_tensor(out=ssum[:], in0=ssum[:], in1=ssq4[:, 2, :], op=ALU.add),
                 reads=["ssq4", "ssum"], writes=["ssum"])
            S.op("dve", lambda e: e.tensor_tensor(out=ssum[:], in0=ssum[:], in1=ssq4[:, 3, :], op=ALU.add),
                 reads=["ssq4", "ssum"], writes=["ssum"])
            S.op("act", lambda e: e.activation(out=ssum[:], in_=ssum[:], func=AF.Ln, scale=1.0 / D, bias=cbias[:, 0:1]),
                 reads=["ssum", "cbias"], writes=["ssum"])
            S.op("act", lambda e: e.activation(out=rstd[:], in_=ssum[:], func=AF.Exp, scale=-0.5),
                 reads=["ssum"], writes=["rstd"])
            S.op("dve", lambda e: e.tensor_scalar(out=nrstd[:], in0=rstd[:], scalar1=-1.0, scalar2=None, op0=ALU.mult),
                 reads=["rstd"], writes=["nrstd"])
            S.op("dve", lambda e: e.tensor_scalar(out=hrstd[:], in0=rstd[:], scalar1=0.5, scalar2=None, op0=ALU.mult),
                 reads=["rstd"], writes=["hrstd"])
            S.op("dve", lambda e: e.memset(ssq_part[:], 1.0), reads=[], writes=["ssq_part"])

        gcount = {"n": 0}

        def gather(loc, allt, rname, pc):
            gcount["n"] += 1
            S.op("pool", lambda e: e.collective_compute("AllGather", ALU.bypass, replica_groups=GROUPS,
                                                        ins=[loc.ap().opt()], outs=[allt.ap().opt()]),
                 reads=["%s_loc%d" % (rname, pc)], writes=["%s_all%d" % (rname, pc)], cc="g%d" % gcount["n"])

        wq = load_w_in(0)

        def n0_load(tb):
            rows = blk_rows(tb)
            i = tb % 2
            S.op("sp", lambda e: e.dma_start(out=xcb[i][0:rows, :], in_=xc0[tb * 128:tb * 128 + rows, :]),
                 writes=["kf32%d" % i], chan="kf32%d" % i)

        for tb in range(NB):
            rows = blk_rows(tb)
            i = tb % 2
            if tb % 2 == 0:
                next(wq, None)
            if tb == 0:
                n0_load(0)
            if tb + 1 < NB:
                n0_load(tb + 1)
            norm_prep(xcb[i], "kf32%d" % i, tb, rows, 0, True)
        for _ in wq:
            pass
        rstd_from_ssq("0")

        def load_act(slot, src_all, rname, c0, ncols):
            pc, lc = (c0 // 1024, c0 % 1024) if c0 < T else (4, 0)
            src = src_all[pc].ap().rearrange("(c p) t -> p c t", p=128)
            S.op("sp", lambda e: e.dma_start(out=actT[slot][:, :, 0:ncols], in_=src[:, :, lc:lc + ncols]),
                 reads=["%s_all%d" % (rname, pc)], writes=["actT%d" % slot], chan="actT%d" % slot)

        def transposes_to(src_tok, src_name, rows, dst_fn, dst_names, evac_eng):
            for h in range(4):
                S.op("pe", lambda e, h=h: e.transpose(out=pst[:, h * 128:h * 128 + rows],
                                                      in_=src_tok[0:rows, h * 128:(h + 1) * 128],
                                                      identity=ident[0:rows, 0:rows]),
                     reads=[src_name, "cstb"], writes=PST_ALL, sig=(h == 3))
            pv4 = pst[:, 0:512].rearrange("p (h t) -> p h t", t=128)
            if evac_eng == "act":
                S.op("act", lambda e: e.activation(out=dst_fn(None), in_=pv4[:, :, 0:rows], func=AF.Identity),
                     reads=PST_ALL, writes=dst_names)
            else:
                S.op("dve", lambda e: e.tensor_copy(out=dst_fn(None), in_=pv4[:, :, 0:rows]), reads=PST_ALL, writes=dst_names)

        def project_block(l, slot, sub, tb, rows, kT_dst, kT_names, v_dst_fn, v_names, qcol0):
            banks = [psA, psB, psC, psD]
            bn = ["psA", "psB", "psC", "psD"]
            for c in range(16):
                lhsT = actT[slot][:, c, sub * 128: sub * 128 + rows]
                for j in range(4):
                    S.op("pe", lambda e, c=c, j=j, lhsT=lhsT: e.matmul(banks[j][0:rows, :], lhsT=lhsT,
                                                                        rhs=w_in[:, c, j * 512:(j + 1) * 512],
                                                                        start=(c == 0), stop=(c == 15)),
                         reads=["actT%d" % slot, "w_in"], writes=[bn[j]], sig=(c == 15))
                if l == 0:
                    S.op("pe", lambda e, c=c, lhsT=lhsT: e.matmul(psE[0:rows, 0:4], lhsT=lhsT,
                                                                   rhs=w_in[:, c, 2048:2052],
                                                                   start=(c == 0), stop=(c == 15)),
                         reads=["actT%d" % slot, "w_in"], writes=["psE"], sig=(c == 15))
            rs = rstd[0:rows, tb:tb + 1]
            nrs = nrstd[0:rows, tb:tb + 1]
            i = tb % 2
            S.op("dve", lambda e: e.tensor_scalar(out=q_tok[0:rows, :], in0=psA[0:rows, :], scalar1=rs, scalar2=None,
                                                  op0=ALU.mult),
                 reads=["psA", "rstd"], writes=["q_tok"])
            S.op("act", lambda e: e.activation(out=kf32[i][0:rows, :], in_=psB[0:rows, :], func=AF.Identity, scale=rs),
                 reads=["psB", "rstd"], writes=["kf32%d" % i])
            S.op("dve", lambda e: e.tensor_scalar(vf32[i][0:rows, :], psC[0:rows, :], rs, None, ALU.mult),
                 reads=["psC", "rstd"], writes=["vf32%d" % i])
            S.op("sp", lambda e: e.dma_start(out=kvout[l][0][tb * 128:tb * 128 + rows, :], in_=kf32[i][0:rows, :]),
                 reads=["kf32%d" % i], writes=[], chan="kf32%d" % i)
            S.op("sp", lambda e: e.dma_start(out=kvout[l][1][tb * 128:tb * 128 + rows, :], in_=vf32[i][0:rows, :]),
                 reads=["vf32%d" % i], writes=[], chan="vf32%d" % i)
            S.op("act", lambda e: e.activation(out=k_tok[0:rows, :], in_=psB[0:rows, :], func=AF.Identity, scale=rs),
                 reads=["psB", "rstd"], writes=["k_tok"])
            vd = v_dst_fn()
            S.op("dve", lambda e: e.tensor_copy(out=vd[0:rows, :, 0:128],
                                                in_=vf32[i][0:rows, :].rearrange("p (h d) -> p h d", d=128)),
                 reads=["vf32%d" % i], writes=v_names)
            S.op("pool", lambda e: e.memset(vd[0:rows, :, 128:129], 1.0), reads=[], writes=v_names)
            if l == 0:
                S.op("act", lambda e: e.activation(out=gtmp[0:rows, :], in_=psD[0:rows, :], func=AF.Exp, scale=nrs),
                     reads=["psD", "nrstd"], writes=["gtmp"])
                S.op("act", lambda e: e.activation(out=gtmp[0:rows, :], in_=gtmp[0:rows, :], func=AF.Ln, bias=cbias[0:rows, 1:2]),
                     reads=["gtmp", "cbias"], writes=["gtmp"])
                S.op("act", lambda e: e.activation(out=gtmp[0:rows, :], in_=gtmp[0:rows, :], func=AF.Exp, scale=-1.0),
                     reads=["gtmp"], writes=["gtmp"])
            else:
                S.op("act", lambda e: e.activation(out=gtmp[0:rows, :], in_=psD[0:rows, :], func=AF.Sigmoid, scale=rs),
                     reads=["psD", "rstd"], writes=["gtmp"])
            S.op("dve", lambda e: e.scalar_tensor_tensor(out=gsil[sub][0:rows, :], in0=psD[0:rows, :], scalar=rs,
                                                         in1=gtmp[0:rows, :], op0=ALU.mult, op1=ALU.mult),
                 reads=["psD", "rstd", "gtmp"], writes=["gsil%d" % sub])
            if l == 0:
                S.op("dve", lambda e: e.scalar_tensor_tensor(out=xl[0:rows, :], in0=psE[0:rows, 0:4], scalar=rs,
                                                             in1=bfb[0:rows, :], op0=ALU.mult, op1=ALU.add),
                     reads=["psE", "rstd", "bfb"], writes=["xl"])
                S.op("act", lambda e: e.activation(out=el[0:rows, :], in_=xl[0:rows, :], func=AF.Exp, scale=-1.0),
                     reads=["xl"], writes=["el"])
                S.op("act", lambda e: e.activation(out=lps[0:rows, :], in_=el[0:rows, :], func=AF.Ln, bias=cbias[0:rows, 1:2]),
                     reads=["el", "cbias"], writes=["lps"])
                S.op("pool", lambda e: e.tensor_scalar(out=lfo[0:rows, tb, :], in0=lps[0:rows, :], scalar1=-1.0,
                                                       scalar2=0.0, op0=ALU.mult, op1=ALU.add),
                     reads=["lps"], writes=["lfo"])
            transposes_to(q_tok, "q_tok", rows, lambda h: qT[:, :, qcol0:qcol0 + rows], ["qT"], "dve")
            transposes_to(k_tok, "k_tok", rows, kT_dst, kT_names, "act")

        def cumsum_block(tb):
            first = (tb == 0)
            S.op("pe", lambda e: e.matmul(psE[:, 8:12], lhsT=cf(C_U), rhs=lps[:, 0:4], start=True, stop=first),
                 reads=["cstf", "lps"], writes=["psE"], sig=first)
            if not first:
                S.op("pe", lambda e: e.matmul(psE[:, 8:12], lhsT=cf(C_E127), rhs=cpall[:, tb - 1, 0:4], start=False,
                                              stop=True),
                     reads=["cstf", "cpall"], writes=["psE"], sig=False)
                S.op("pe", lambda e: e.matmul(psE[:, 12:16], lhsT=cf(C_E127), rhs=cpall[:, tb - 1, 0:4], start=True,
                                              stop=True),
                     reads=["cstf", "cpall"], writes=["psE"])
                S.op("dve", lambda e: e.tensor_copy(out=cpall[:, tb, 0:8], in_=psE[:, 8:16]), reads=["psE"],
                     writes=["cpall"])
            else:
                S.op("dve", lambda e: e.tensor_copy(out=cpall[:, tb, 0:4], in_=psE[:, 8:12]), reads=["psE"],
                     writes=["cpall"])

        sc_banks = [(psA, "psA"), (psB, "psB")]
        o_banks = [((psC, "psC"), (psD, "psD")), ((psF, "psF"), (psG, "psG"))]

        def y_out(tbs, rows_l):
            for sub, tb in enumerate(tbs):
                rows = rows_l[sub]
                for c in range(4):
                    S.op("pe", lambda e, c=c, sub=sub, rows=rows: e.transpose(out=pst[:, c * 128:c * 128 + rows],
                                                                               in_=ybuf[sub][0:rows, c * 128:(c + 1) * 128],
                                                                               identity=ident[0:rows, 0:rows]),
                         reads=["y%d_%d" % (sub, hh) for hh in range(4)] + ["cstb"], writes=PST_ALL, sig=(c == 3))
                i = cnt["tr"] % 2
                cnt["tr"] += 1
                pv = pst[:, 0:512].rearrange("p (c t) -> p c t", t=128)
                S.op("act", lambda e, i=i, rows=rows, pv=pv: e.activation(out=yTs[i][:, :, 0:rows], in_=pv[:, :, 0:rows],
                                                                           func=AF.Identity),
                     reads=PST_ALL, writes=["yTs%d" % i])
                pc, c0 = piece_of(tb)
                dst = yT_loc[pc].ap().rearrange("(c p) t -> p c t", p=128)
                S.op("sp", lambda e, i=i, rows=rows, c0=c0, dst=dst: e.dma_start(out=dst[:, :, c0:c0 + rows],
                                                                                  in_=yTs[i][:, :, 0:rows]),
                     reads=["yTs%d" % i], writes=["yT_loc%d" % pc], chan="yTs%d" % i)
                if tb % 8 == 7 or tb == 32:
                    gather(yT_loc[pc], yT_all[pc], "yT", pc)

        pcount = {"p": 0, "sc": 0, "ch": 0}

        def run_streams(makers, width):
            pending = list(makers)
            active = []
            free = list(range(width))
            eng_free = {}
            now = 0.0
            while pending or active:
                while pending and free:
                    sl = free.pop(0)
                    g = pending.pop(0)(sl)
                    try:
                        nxt = next(g)
                    except StopIteration:
                        free.append(sl)
                        continue
                    active.append({"g": g, "sl": sl, "ready": now, "nxt": nxt})
                if not active:
                    continue
                best = min(active, key=lambda a: max(eng_free.get(a["nxt"][0], 0.0), a["ready"]))
                eng, dur = best["nxt"]
                start = max(eng_free.get(eng, 0.0), best["ready"])
                end = start + dur
                eng_free[eng] = end
                best["ready"] = end + 0.25
                try:
                    best["nxt"] = next(best["g"])
                except StopIteration:
                    active.remove(best)
                    free.append(best["sl"])
                    now = end

        def fox_stream(slot, Q, h):
            nkb = 2 * Q + 2
            ob = o_banks[slot]
            fsc = [(psA, "psA"), (psB, "psB"), (psE, "psE"), (pst_f32, "pstbank"), (psF, "psF"), (psG, "psG")]
            for kb0 in range(0, nkb, 6):
                kbs = list(range(kb0, min(kb0 + 6, nkb)))
                yield ("pe", 0.5)
                for i, kb in enumerate(kbs):
                    qlo = 128 if kb == nkb - 1 else 0
                    scb, scn = fsc[i]
                    S.op("pe", lambda e, kb=kb, qlo=qlo, scb=scb: e.matmul(
                        scb[:, qlo:256], lhsT=kT(h, kb * 128, (kb + 1) * 128), rhs=qT[:, h, qlo:256], start=True, stop=True),
                        reads=["kT%d_%d" % (h, kb), "qT"], writes=[scn])
                yield ("act", 1.8)
                for i, kb in enumerate(kbs):
                    qlo = 128 if kb == nkb - 1 else 0
                    scb, scn = fsc[i]
                    S.op("act", lambda e, kb=kb, qlo=qlo, i=i, scb=scb: e.activation(
                        out=Pb[i][:, qlo:256], in_=scb[:, qlo:256], func=AF.Exp, scale=SCALE, bias=biasT[:, kb, h:h + 1]),
                        reads=[scn, "biasT"], writes=["P%d" % i])
                    if kb >= nkb - 2:
                        dq = 0 if kb == nkb - 2 else 128
                        S.op("pool", lambda e, i=i, dq=dq: e.tensor_tensor(out=Pb[i][:, dq:dq + 128],
                                                                           in0=Pb[i][:, dq:dq + 128], in1=maskLE,
                                                                           op=ALU.mult),
                             reads=["P%d" % i, "cstb"], writes=["P%d" % i])
                yield ("pe", 1.2)
                for i, kb in enumerate(kbs):
                    for sub in range(2):
                        if kb == nkb - 1 and sub == 0:
                            continue
                        last = (kb == nkb - 2) if sub == 0 else (kb == nkb - 1)
                        S.op("pe", lambda e, kb=kb, sub=sub, i=i, last=last: e.matmul(
                            ob[sub][0][:, 0:129], lhsT=Pb[i][:, sub * 128:(sub + 1) * 128], rhs=Vaug(kb, h, 129),
                            start=(kb == 0), stop=last),
                            reads=["P%d" % i, "V%d" % kb], writes=[ob[sub][1]], sig=(sub == 1 or kb == nkb - 2))
            yield ("dve", 0.5)
            for sub in range(2):
                S.op("dve", lambda e, sub=sub: e.reciprocal(out=rden[:, 2 * slot + sub:2 * slot + sub + 1],
                                                            in_=ob[sub][0][:, 128:129]),
                     reads=[ob[sub][1]], writes=["rden%d" % slot])
                S.op("dve", lambda e, sub=sub: e.scalar_tensor_tensor(
                    out=ybuf[sub][:, h * 128:(h + 1) * 128], in0=ob[sub][0][:, 0:128],
                    scalar=rden[:, 2 * slot + sub:2 * slot + sub + 1],
                    in1=gsil[sub][:, h * 128:(h + 1) * 128], op0=ALU.mult, op1=ALU.mult),
                    reads=[ob[sub][1], "rden%d" % slot, "gsil%d" % sub], writes=["y%d_%d" % (sub, h)])

        def fox_tile(Q):
            nkb = 2 * Q + 2
            for h in range(4):
                S.op("dve", lambda e, h=h: e.tensor_scalar(out=biasT[:, 0:nkb, h], in0=cpall[:, 0:nkb, h],
                                                           scalar1=cpall[:, 2 * Q + 1, 4 + h:5 + h], scalar2=None,
                                                           op0=ALU.subtract),
                     reads=["cpall"], writes=["biasT"])
            run_streams([(lambda sl, h=h: fox_stream(sl, Q, h)) for h in range(4)], FOXW)

        def sb_stream(slot, qrows, qT_ap, qT_names, chunks, o_ap, o_name, pv, total_pv, carry_in=None, aT_hook=None,
                      fin=None, out_state=None):
            prev = carry_in
            th, Rx, a_ = thb[slot], Rext[slot], ab[slot]
            tn = ["th0", "th1", "th2", "gtmp"][slot]
            rn = ["Rx0", "Rx1", "Rx2", "kf321"][slot]
            an = ["a0", "a1", "a2", "k_tok"][slot]
            aTnames = [["aT0"], ["aT1"], ["P0", "P1"], ["q_tok"]][slot]
            scb, scn = [(psA, "psA"), (psB, "psB"), (psE, "psE"), (psG, "psG")][slot]
            pst, pn, pso = scb[:, :].bitcast(BF16), scn, 0
            for ci, ch in enumerate(chunks):
                W = ch["W"]
                yield ("pe", 0.45)
                S.op("pe", lambda e, ch=ch, W=W, scb=scb: e.matmul(scb[0:qrows, 0:W], lhsT=qT_ap, rhs=ch["kT"], start=True, stop=True),
                     reads=qT_names + ch["names"], writes=[scn])
                yield ("act", 0.65)
                S.op("act", lambda e, W=W, scb=scb: e.activation(out=th[0:qrows, 0:W], in_=scb[0:qrows, 0:W], func=AF.Sigmoid,
                                                        scale=-SCALE),
                     reads=[scn], writes=[tn])
                yield ("dve", 1.5)
                if ch.get("mask") is not None:
                    M, Mc, dw = ch["mask"]
                    S.op("dve", lambda e, W=W, dw=dw, M=M: e.tensor_tensor(out=th[0:qrows, W - dw:W], in0=th[0:qrows, W - dw:W],
                                                                           in1=M, op=ALU.mult),
                         reads=[tn, "cstf"], writes=[tn])
                    S.op("dve", lambda e, W=W, dw=dw, Mc=Mc: e.tensor_tensor(out=th[0:qrows, W - dw:W], in0=th[0:qrows, W - dw:W],
                                                                             in1=Mc, op=ALU.add),
                         reads=[tn, "cstf"], writes=[tn])
                if prev is None:
                    S.op("dve", lambda e, W=W: e.memset(Rx[0:qrows, W:W + 1], 1.0), reads=[], writes=[rn])
                else:
                    pR, pname = prev
                    S.op("dve", lambda e, W=W, pR=pR: e.tensor_copy(out=Rx[0:qrows, W:W + 1], in_=pR),
                         reads=[pname, an], writes=[rn])
                S.op("dve", lambda e, W=W: e.tensor_tensor_scan(
                    out=Rx[0:qrows, 0:W][:, ::-1], data0=th[0:qrows, 0:W][:, ::-1], data1=zeros[0:qrows, 0:W],
                    initial=Rx[0:qrows, W:W + 1], op0=ALU.mult, op1=ALU.add),
                    reads=[tn, rn, "zeros"], writes=[rn])
                prev = (Rx[0:qrows, 0:1], rn)
                if True:
                    yield ("pool", 1.3)
                S.op("pool", lambda e, W=W: e.tensor_tensor(out=a_[0:qrows, 0:W], in0=Rx[0:qrows, 1:W + 1],
                                                                                   in1=Rx[0:qrows, 0:W], op=ALU.subtract),
                     reads=[rn], writes=[an])
                yield ("pe", 0.7)
                vbl = ch["vblocks"]
                off = 0
                for j, (rhs, wk, vn) in enumerate(vbl):
                    S.op("pe", lambda e, off=off, wk=wk, j=j: e.transpose(out=pst[0:wk, pso + j * 128:pso + j * 128 + qrows],
                                                                          in_=a_[0:qrows, off:off + wk],
                                                                          identity=ident[0:qrows, 0:qrows]),
                         reads=[an, "cstb"], writes=[pn], sig=(j == len(vbl) - 1))
                    off += wk
                yield ("act", 0.65)
                if aT_hook is None:
                    aT = aTb[slot]
                    nb_ = len(vbl)
                    pvw = pst[:, pso:pso + nb_ * 128].rearrange("p (j t) -> p j t", t=128)
                    S.op("act", lambda e, aT=aT, pvw=pvw, nb_=nb_: e.activation(out=aT[:, 0:nb_, 0:qrows], in_=pvw[:, :, 0:qrows],
                                                                                func=AF.Identity),
                         reads=[pn], writes=aTnames)
                    lhs_list = [(aT[0:wk, j, 0:qrows], aTnames) for j, (_, wk, _) in enumerate(vbl)]
                else:
                    lhs_list = aT_hook(ci, vbl, pso, pn, pst)
                if aT_hook is None:
                    yield ("pe", 0.7)
                for j, (rhs, wk, vn) in enumerate(vbl):
                    lhsT, ln = lhs_list[j]
                    n = pv["n"]
                    S.op("pe", lambda e, lhsT=lhsT, rhs=rhs, n=n: e.matmul(o_ap, lhsT=lhsT, rhs=rhs, start=(n == 0),
                                                                           stop=(n == total_pv - 1)),
                         reads=(ln if isinstance(ln, list) else [ln]) + [vn], writes=[o_name], sig=True)
                    pv["n"] += 1
            if out_state is not None:
                out_state["carry"] = prev
            if fin is not None:
                yield ("dve", 0.3)
                fin()

        sb_obank = {0: [(psC, "psC")], 1: [(psD, "psD")], 2: [(psF, "psF")], 3: [(pst_f32, "pstbank")]}
        sb_ocnt = {0: 0, 1: 0}

        def sb_tile(Q):
            makers = []
            for sub in range(2):
                qb = 2 * Q + sub
                e_ = 128 * (qb + 1)
                for h in range(4):
                    def mk(slot, sub=sub, h=h, e_=e_):
                        chunks = []
                        hi = e_
                        while hi > 0:
                            lo = max(0, hi - 512)
                            vbl = [(Vaug(b, h, 128), 128, "V%d" % b) for b in range(lo // 128, hi // 128)]
                            chunks.append(dict(kT=kT(h, lo, hi), W=hi - lo,
                                               names=["kT%d_%d" % (h, b) for b in range(lo // 128, hi // 128)],
                                               vblocks=vbl,
                                               mask=(cf(C_MLT), cf(C_MLTC), 128) if hi == e_ else None))
                            hi = lo
                        ob, on = sb_obank[slot][0]

                        def fin():
                            S.op("dve", lambda e: e.tensor_tensor(out=ybuf[sub][:, h * 128:(h + 1) * 128], in0=ob[:, 0:128],
                                                                  in1=gsil[sub][:, h * 128:(h + 1) * 128], op=ALU.mult),
                                 reads=[on, "gsil%d" % sub], writes=["y%d_%d" % (sub, h)])
                        return sb_stream(slot, 128, qT[:, h, sub * 128:(sub + 1) * 128], ["qT"], chunks, ob[:, 0:128], on,
                                         {"n": 0}, sum(len(c["vblocks"]) for c in chunks), fin=fin)
                    makers.append(mk)
            run_streams(makers, SBW)

        SEQSZ = 8 * 520 + 4096
        assert 4 * SEQSZ <= 4 * T + 32 * 4 * 130
        kstage_f = actT[1][:, :, :].rearrange("p c t -> p (c t)").bitcast(F32).rearrange("p (j c) -> p j c", c=512)

        def sVc_all(s_):
            b0 = s_ * SEQSZ
            return regB[:, b0:b0 + 8 * 520].rearrange("p (j h e) -> p j h e", h=4, e=130)

        def sVc(s_, j, h, n):
            o = s_ * SEQSZ + (j * 4 + h) * 130
            return regB[:, o:o + n]

        def skT(s_, h, lo, hi):
            o = s_ * SEQSZ + 8 * 520 + h * 1024
            return regB[:, o + lo:o + hi]

        def Vn_v():
            return Vn_t[:, :].rearrange("p (h e) -> p h e", e=130)

        def sample_stage(l):
            kcache, vcache = (cfk, cfv) if l == 0 else (csk, csv)
            for s_ in range(4):
                for j in range(8):
                    S.op("pool", lambda e, s_=s_, j=j: e.dma_start(
                        out=sVc_all(s_)[:, j, :, 0:128],
                        in_=vcache[s_][j * 128:(j + 1) * 128, :].rearrange("p (h d) -> p h d", d=128)),
                         writes=["sVc%d" % s_] + (REGB_ALL + ["w_out"] if j == 0 else []), chan="sVc")
                S.op("pool", lambda e, s_=s_: e.memset(sVc_all(s_)[:, :, :, 128:129], 1.0), writes=["sVc%d" % s_])
                for hf in range(2):
                    S.op("sp", lambda e, s_=s_, hf=hf: e.dma_start(
                        out=kstage_f, in_=kcache[s_][hf * 512:(hf + 1) * 512, :].rearrange("(j p) c -> p j c", p=128)),
                         writes=["actT1"], chan="sKc")
                    for jj in range(4):
                        j = hf * 4 + jj
                        for h in range(4):
                            S.op("pe", lambda e, jj=jj, h=h: e.transpose(out=pst_f32[:, h * 128:(h + 1) * 128],
                                                                         in_=kstage_f[:, jj, h * 128:(h + 1) * 128],
                                                                         identity=cstf[:, C_ID:C_ID + 128]),
                                 reads=["actT1", "cstf"], writes=PST_ALL, sig=(h == 3))
                        kb_ = s_ * SEQSZ + 8 * 520
                        S.op("act", lambda e, kb_=kb_, j=j: e.activation(
                            out=regB[:, kb_:kb_ + 4096].rearrange("p (h t) -> p h t", t=1024)[:, :, j * 128:(j + 1) * 128],
                            in_=pst_f32[:, 0:512].rearrange("p (h t) -> p h t", t=128), func=AF.Identity),
                             reads=PST_ALL, writes=["skT%d" % s_] + (REGB_ALL + ["w_out"] if j == 0 else []))
            if l == 0:
                for s_ in range(4):
                    S.op("sp", lambda e, s_=s_: e.dma_start(out=clf[:, :, s_ * 4:(s_ + 1) * 4],
                                                            in_=cfl[s_].rearrange("(j p) h -> p j h", p=128)),
                         writes=["clf"], chan="clf")
                for j in range(8):
                    S.op("pe", lambda e, j=j: e.matmul(psE[:, 16 + 16 * j:32 + 16 * j], lhsT=cf(C_LS), rhs=clf[:, j, :], start=True,
                                                       stop=(j == 7)), reads=["cstf", "clf"], writes=["psE"], sig=(j == 7))
                    for j2 in range(j + 1, 8):
                        S.op("pe", lambda e, j=j, j2=j2: e.matmul(psE[:, 16 + 16 * j:32 + 16 * j], lhsT=cf(C_ONES), rhs=clf[:, j2, :],
                                                                  start=False, stop=(j2 == 7)), reads=["cstf", "clf"],
                             writes=["psE"], sig=(j2 == 7))
                S.op("dve", lambda e: e.tensor_copy(out=sufb[:].rearrange("p j c -> p (j c)"), in_=psE[:, 16:144]), reads=["psE"],
                     writes=["sufb"])

        def sample_block(l, slot):
            tb = 32
            rows = TS
            Vn = Vn_v()
            project_block(l, slot, 0, tb, rows, lambda h: qT[:, :, 128:128 + rows], ["qT"],
                          Vn_v, ["Vn"], 0)
            if l == 0:
                S.op("pe", lambda e: e.matmul(psE[0:64, 8:12], lhsT=cf(C_UB16, 64, 64), rhs=lps[0:64, 0:4], start=True, stop=True),
                     reads=["cstf", "lps"], writes=["psE"])
                S.op("dve", lambda e: e.tensor_copy(out=cpn[:, :], in_=psE[0:64, 8:12]), reads=["psE"], writes=["cpn"])
            for h in range(4):
                ob, on = o_banks[h % 2][0]
                if l == 0:
                    o_ap = ob[0:64, 0:129]
                    first = True
                    for s_ in range(4):
                        for j in range(8):
                            scb, scn = sc_banks[pcount["sc"] % 2]
                            pcount["sc"] += 1
                            S.op("pe", lambda e, s_=s_, j=j, h=h, scb=scb: e.matmul(
                                scb[:, 0:16], lhsT=skT(s_, h, j * 128, (j + 1) * 128), rhs=qT[:, h, s_ * 16:(s_ + 1) * 16],
                                start=True, stop=True), reads=["skT%d" % s_, "qT"], writes=[scn])
                            S.op("act", lambda e, s_=s_, j=j, h=h, scb=scb: e.activation(
                                out=Pw[s_][:, s_ * 16:(s_ + 1) * 16], in_=scb[:, 0:16], func=AF.Exp, scale=SCALE,
                                bias=sufb[:, j, s_ * 4 + h:s_ * 4 + h + 1]), reads=[scn, "sufb"], writes=["Pw%d" % s_])
                            S.op("pe", lambda e, s_=s_, j=j, h=h, first=first, o_ap=o_ap: e.matmul(
                                o_ap, lhsT=Pw[s_][:, 0:64], rhs=sVc(s_, j, h, 129), start=first, stop=False),
                                reads=["Pw%d" % s_, "sVc%d" % s_], writes=[on], sig=True)
                            first = False
                    scb, scn = sc_banks[pcount["sc"] % 2]
                    pcount["sc"] += 1
                    S.op("pe", lambda e, h=h, scb=scb: e.matmul(scb[0:64, 0:64], lhsT=qT[:, h, 128:192], rhs=qT[:, h, 0:64],
                                                                start=True, stop=True), reads=["qT"], writes=[scn])
                    S.op("act", lambda e, h=h, scb=scb: e.activation(out=Pn[:, :], in_=scb[0:64, 0:64], func=AF.Exp, scale=SCALE,
                                                                     bias=cpn[:, h:h + 1]), reads=[scn, "cpn"], writes=["Pn"])
                    S.op("pool", lambda e: e.tensor_tensor(out=Pn[:, :], in0=Pn[:, :], in1=mask64, op=ALU.mult),
                         reads=["Pn", "cstb"], writes=["Pn"])
                    S.op("pe", lambda e, h=h, o_ap=o_ap: e.matmul(o_ap, lhsT=Pn[:, :], rhs=Vn[0:64, h, 0:129], start=False, stop=True),
                         reads=["Pn", "Vn"], writes=[on])
                    S.op("dve", lambda e, ob=ob: e.reciprocal(out=rden[0:64, 0:1], in_=ob[0:64, 128:129]), reads=[on],
                         writes=["rden0"])
                    S.op("dve", lambda e, h=h, ob=ob: e.scalar_tensor_tensor(
                        out=ybuf[0][0:64, h * 128:(h + 1) * 128], in0=ob[0:64, 0:128], scalar=rden[0:64, 0:1],
                        in1=gsil[0][0:64, h * 128:(h + 1) * 128], op0=ALU.mult, op1=ALU.mult),
                        reads=[on, "rden0", "gsil0"], writes=["y0_%d" % h])
                else:
                    pass
            if l == 1:
                def head_stream(slot, h):
                    ob, on = sb_obank[slot][0]
                    o_ap = ob[0:64, 0:128]
                    pv = {"n": 0}
                    total = 1 + 4 * 8

                    def hook0(ci, vbl, pso, pn, pst_=None):
                        S.op("act", lambda e: e.activation(out=aTn[:, :], in_=pst_[0:64, pso:pso + 64], func=AF.Identity),
                             reads=[pn], writes=["aTn"])
                        return [(aTn[:, :], "aTn")]
                    ch0 = dict(kT=qT[:, h, 128:192], W=64, names=["qT"], vblocks=[(Vn[0:64, h, 0:128], 64, "Vn")],
                               mask=(cf(C_MS, 64, 64), cf(C_MSC, 64, 64), 64))
                    st = {}
                    yield from sb_stream(slot, 64, qT[:, h, 0:64], ["qT"], [ch0], o_ap, on, pv, total, None, hook0, None, st)
                    car = st["carry"]
                    cr = carry0[:, slot:slot + 1]
                    yield ("dve", 0.1)
                    S.op("dve", lambda e: e.tensor_copy(out=cr, in_=car[0]), reads=[car[1]], writes=["carry0_%d" % slot])
                    for s_ in range(4):
                        def hook(ci, vbl, pso, pn, pst_=None, s_=s_):
                            base = 4 if ci == 0 else 0
                            pvw = pst_[:, pso:pso + 512].rearrange("p (j t) -> p j t", t=128)
                            S.op("act", lambda e: e.activation(out=aTw[s_][:, base:base + 4, s_ * 16:(s_ + 1) * 16],
                                                               in_=pvw[:, :, s_ * 16:(s_ + 1) * 16], func=AF.Identity),
                                 reads=[pn], writes=["aTw%d" % s_])
                            return [(aTw[s_][:, base + j, 0:64], "aTw%d" % s_) for j in range(4)]
                        chunks = []
                        for (lo, hi) in ((512, 1024), (0, 512)):
                            chunks.append(dict(kT=skT(s_, h, lo, hi), W=512, names=["skT%d" % s_],
                                               vblocks=[(sVc(s_, b, h, 128), 128, "sVc%d" % s_) for b in range(lo // 128, hi // 128)],
                                               mask=None))
                        yield from sb_stream(slot, 64, qT[:, h, 0:64], ["qT"], chunks, o_ap, on, pv, total,
                                             (cr, "carry0_%d" % slot), hook)
                    yield ("dve", 0.3)
                    S.op("dve", lambda e: e.tensor_tensor(out=ybuf[0][0:64, h * 128:(h + 1) * 128], in0=ob[0:64, 0:128],
                                                          in1=gsil[0][0:64, h * 128:(h + 1) * 128], op=ALU.mult),
                         reads=[on, "gsil0"], writes=["y0_%d" % h])
                run_streams([(lambda sl, h=h: head_stream(sl, h)) for h in range(4)], SBW)
            load_w_out(l)
            y_out([tb], [rows])

        def p_phase(l):
            load_act(0, xT_all, "xT", 0, 256)
            nt = 16
            for Q in range(nt):
                slot = Q % 2
                if Q < 15:
                    load_act(1 - slot, xT_all, "xT", (Q + 1) * 256, 256)
                else:
                    load_act(1 - slot, xT_all, "xT", T, TS)
                for sub in range(2):
                    tb = 2 * Q + sub
                    project_block(l, slot, sub, tb, 128,
                                  lambda h, tb=tb: regB[:, 0:4 * T].rearrange("p (h t) -> p h t", t=T)[:, :, tb * 128:(tb + 1) * 128],
                                  ["kT%d_%d" % (h, tb) for h in range(4)],
                                  lambda tb=tb: Vaug_blk(tb), ["V%d" % tb], sub * 128)
                    if l == 0:
                        cumsum_block(tb)
                if l == 0:
                    fox_tile(Q)
                else:
                    sb_tile(Q)
                y_out([2 * Q, 2 * Q + 1], [128, 128])
            if nt == 16:
                sample_stage(l)
                sample_block(l, 0)

        def o_phase(l):
            wq = load_w_in(1) if l == 0 else iter(())
            wv = w_out()
            load_act(0, yT_all, "yT", 0, 256)
            xsrc = xc0 if l == 0 else xres[1].ap()
            xdst = xres[l + 1].ap()

            def o_load(tb):
                rows = blk_rows(tb)
                i = tb % 2
                S.op("sp", lambda e: e.dma_start(out=xcb[i][0:rows, :], in_=xsrc[tb * 128:tb * 128 + rows, :]),
                     reads=["xres%d" % l], writes=["kf32%d" % i], chan="kf32%d" % i)

            for Q in range(17):
                slot = Q % 2
                if Q < 15:
                    load_act(1 - slot, yT_all, "yT", (Q + 1) * 256, 256)
                elif Q == 15:
                    load_act(1 - slot, yT_all, "yT", T, TS)
                next(wq, None)
                for sub in range(2 if Q < 16 else 1):
                    tb = 2 * Q + sub
                    rows = blk_rows(tb)
                    i = tb % 2
                    if tb == 0:
                        o_load(0)
                    if tb + 1 < NB:
                        o_load(tb + 1)
                    for c in range(16):
                        S.op("pe", lambda e, c=c, slot=slot, sub=sub, rows=rows: e.matmul(
                            psA[0:rows, :], lhsT=actT[slot][:, c, sub * 128:sub * 128 + rows], rhs=wv[:, c, :], start=(c == 0),
                            stop=(c == 15)), reads=["actT%d" % slot, "w_out"], writes=["psA"], sig=(c == 15))
                    S.op("dve", lambda e, i=i, rows=rows: e.tensor_tensor(out=xnb[i][0:rows, :], in0=psA[0:rows, :],
                                                                          in1=xcb[i][0:rows, :], op=ALU.add),
                         reads=["psA", "kf32%d" % i], writes=["vf32%d" % i])
                    S.op("sp", lambda e, i=i, tb=tb, rows=rows: e.dma_start(out=xdst[tb * 128:tb * 128 + rows, :],
                                                                              in_=xnb[i][0:rows, :]),
                         reads=["vf32%d" % i], writes=["xres%d" % (l + 1)], chan="vf32%d" % i)
                    norm_prep(xnb[i], "vf32%d" % i, tb, rows, l + 1, l == 0)
            for _ in wq:
                pass

        stage = 99
        if stage >= 2:
            p_phase(0)
        if stage >= 4:
            load_nw(1)
            o_phase(0)
            rstd_from_ssq("1")
        if stage >= 5:
            p_phase(1)
        if stage >= 7:
            load_nw(2)
            o_phase(1)
            rstd_from_ssq("2")
            x2 = xres[2].ap()
            def f_load(tb):
                rows = blk_rows(tb)
                i = tb % 2
                S.op("sp", lambda e: e.dma_start(out=xcb[i][0:rows, :], in_=x2[tb * 128:tb * 128 + rows, :]),
                     reads=["xres2"], writes=["kf32%d" % i], chan="kf32%d" % i)

            f_load(0)
            for tb in range(NB):
                rows = blk_rows(tb)
                i = tb % 2
                if tb + 1 < NB:
                    f_load(tb + 1)
                S.op("dve", lambda e, i=i, tb=tb, rows=rows: e.scalar_tensor_tensor(
                    out=xnb[i][0:rows, :], in0=xcb[i][0:rows, :], scalar=rstd[0:rows, tb:tb + 1], in1=nw[2][0:rows, :],
                    op0=ALU.mult, op1=ALU.mult), reads=["kf32%d" % i, "rstd", "nw"], writes=["vf32%d" % i])
                S.op("sp", lambda e, i=i, tb=tb, rows=rows: e.dma_start(out=yout[tb * 128:tb * 128 + rows, :], in_=xnb[i][0:rows, :]),
                     reads=["vf32%d" % i], writes=[], chan="vf32%d" % i)
        S.op("sp", lambda e: e.dma_start(out=lfout[0:T, :].rearrange("(b p) h -> p b h", p=128), in_=lfo[:, 0:32, :]),
             reads=["lfo"], writes=[], chan="lfo")
        S.op("sp", lambda e: e.dma_start(out=lfout[T:TT, :], in_=lfo[0:TS, 32, :]), reads=["lfo"], writes=[], chan="lfo")
        for sn, v in list(S.cnt.items()):
            if sn.startswith("d_"):
                S.prog["sp"].append(("wait", sn, v))

        sem_names = sorted(S.cnt.keys())
        sems = {}
        for sn in sem_names:
            sems[sn] = es.enter_context(nc.semaphore(sn))
        block = es.enter_context(nc.Block())

        def emit(eng_obj, name):
            for item in S.prog[name]:
                if item[0] == "wait":
                    eng_obj.wait_ge(sems[item[1]], item[2])
                else:
                    _, fn, sn, inc = item
                    ins = fn(eng_obj)
                    if sn is not None:
                        if inc is None:
                            ins.then_inc(sems[sn])
                        else:
                            ins.then_inc(sems[sn], inc)

        @block.sync
        def _(e):
            emit(e, "sp")

        @block.scalar
        def _(e):
            emit(e, "act")

        @block.vector
        def _(e):
            emit(e, "dve")

        @block.gpsimd
        def _(e):
            emit(e, "pool")

        @block.tensor
        def _(e):
            emit(e, "pe")
    return nc


def _consts():
    c = np.zeros((128, NCST), np.float32)
    k = np.arange(128)[:, None]
    m = np.arange(128)[None, :]
    c[:, C_U:C_U + 128] = (k <= m)
    c[:, C_E127:C_E127 + 128] = (k == 127)
    c[:, C_UB16:C_UB16 + 128] = (k <= m) & (k // 16 == m // 16)
    c[:, C_LS:C_LS + 128] = (k > m)
    c[:, C_ONES:C_ONES + 128] = 1.0
    c[:, C_MS:C_MS + 128] = (m < k) & (k // 16 == m // 16)
    c[:, C_MSC:C_MSC + 128] = 1.0 - ((m < k) & (k // 16 == m // 16))
    c[:, C_MLT:C_MLT + 128] = (m < k)
    c[:, C_MLTC:C_MLTC + 128] = 1.0 - (m < k)
    c[:, C_ID:C_ID + 128] = (k == m)
    c[:, C_MLE:C_MLE + 128] = (k <= m)
    c[:, C_M64:C_M64 + 128] = (k <= m) & (k // 16 == m // 16)
    return c


_NC_CACHE = {}


def kernel(x_prompt, x_sample, cache_fox_k, cache_fox_v, cache_fox_logf, cache_sb_k, cache_sb_v,
           norm_0, w_in_0, b_f_0, w_out_0, norm_1, w_in_1, w_out_1, norm_f):
    f = np.float32
    A = lambda a: np.ascontiguousarray(np.asarray(a, dtype=f))
    x_prompt, x_sample = A(x_prompt), A(x_sample)
    w_in_0, w_in_1, w_out_0, w_out_1 = A(w_in_0), A(w_in_1), A(w_out_0), A(w_out_1)
    caches = [A(cache_fox_k), A(cache_fox_v), A(cache_sb_k), A(cache_sb_v)]
    cache_fox_logf = A(cache_fox_logf)
    norms = [A(norm_0), A(norm_1), A(norm_f)]
    b_f_0 = A(b_f_0)
    if "nc" not in _NC_CACHE:
        _NC_CACHE["nc"] = build_program()
    nc = _NC_CACHE["nc"]
    cst = _consts()
    in_maps = []
    for core in range(8):
        b, g = core // 4, core % 4
        cs = slice(g * 512, (g + 1) * 512)
        xc0 = np.concatenate([x_prompt[b][:, cs], x_sample[4 * b:4 * b + 4].reshape(64, D)[:, cs]], axis=0)
        m = {"xc0": A(xc0), "cst": cst}
        for i, nm in enumerate(["nw0", "nw1", "nwf"]):
            m[nm] = A(np.broadcast_to(norms[i][cs][None, :], (128, 512)))
        m["win0"] = A(np.concatenate([w_in_0[:, j * D + g * 512: j * D + (g + 1) * 512] for j in range(4)]
                                     + [w_in_0[:, 4 * D + 4 * g: 4 * D + 4 * g + 4]], axis=1))
        m["win1"] = A(np.concatenate([w_in_1[:, j * D + g * 512: j * D + (g + 1) * 512] for j in range(4)], axis=1))
        m["wout0"] = A(w_out_0[:, cs])
        m["wout1"] = A(w_out_1[:, cs])
        m["bfb"] = A(np.broadcast_to(b_f_0[4 * g:4 * g + 4][None, :], (128, 4)))
        for nm, cch in zip(["cfk", "cfv", "csk", "csv"], caches):
            m[nm] = A(cch[4 * b:4 * b + 4, :, 4 * g:4 * g + 4, :].reshape(4, PAST, 512))
        m["cfl"] = A(cache_fox_logf[4 * b:4 * b + 4, :, 4 * g:4 * g + 4])
        in_maps.append(m)
    res = run_bass_kernel_spmd(nc, in_maps, core_ids=list(range(8)))
    R = res.results
    y_prompt = np.zeros((2, T, D), f)
    y_sample = np.zeros((8, 16, D), f)
    pk = [np.zeros((2, T, 16, 128), f) for _ in range(4)]
    sk = [np.zeros((8, 16, 16, 128), f) for _ in range(4)]
    plf = np.zeros((2, T, 16), f)
    slf = np.zeros((8, 16, 16), f)
    for core in range(8):
        b, g = core // 4, core % 4
        cs = slice(g * 512, (g + 1) * 512)
        r = R[core]
        y_prompt[b][:, cs] = r["yout"][:T]
        y_sample[4 * b:4 * b + 4][:, :, cs] = r["yout"][T:].reshape(4, 16, 512)
        for i, nm in enumerate(["kf", "vf", "ks", "vs"]):
            pk[i][b][:, 4 * g:4 * g + 4, :] = r[nm][:T].reshape(T, 4, 128)
            sk[i][4 * b:4 * b + 4][:, :, 4 * g:4 * g + 4, :] = r[nm][T:].reshape(4, 16, 4, 128)
        plf[b][:, 4 * g:4 * g + 4] = r["lf"][:T]
        slf[4 * b:4 * b + 4][:, :, 4 * g:4 * g + 4] = r["lf"][T:].reshape(4, 16, 4)
    return (y_prompt, y_sample, pk[0], pk[1], plf, pk[2], pk[3], sk[0], sk[1], slf, sk[2], sk[3])
```

```python
import numpy as np
import concourse.bass as bass
import concourse.mybir as mybir
from concourse.bass_utils import run_bass_kernel_spmd

F32 = mybir.dt.float32
BF16 = mybir.dt.bfloat16
AF = mybir.ActivationFunctionType
ALU = mybir.AluOpType

D = 2048
T = 4096
TS = 64
TT = T + TS
NB = 33
PAST = 1024
SCALE = 128 ** -0.5
EPS = 1e-6
GROUPS = [[0, 1, 2, 3], [4, 5, 6, 7]]
ENG = ("sp", "act", "dve", "pool", "pe")
FOXW = 1
SBW = 4

C_U, C_E127, C_UB16, C_LS, C_ONES, C_MLT, C_MLTC, C_MS, C_MSC, C_ID, C_MLE, C_M64 = [128 * i for i in range(12)]
NCST = 128 * 12


class Sched:
    def __init__(self):
        self.prog = {e: [] for e in ENG}
        self.cnt = {}
        self.waited = {}
        self.lastw = {}
        self.readers = {}

    def _need(self, eng, events):
        for sn, val in events:
            if sn.startswith("d_"):
                val = self.cnt[sn]
            if sn == "pe" and eng == "pe":
                continue
            key = (eng, sn)
            if self.waited.get(key, 0) >= val:
                continue
            self.waited[key] = val
            self.prog[eng].append(("wait", sn, val))

    def op(self, eng, fn, reads=(), writes=(), sig=True, chan=None, cc=None):
        ps_reads = [r for r in reads if r.startswith("ps") and r not in writes]
        if ps_reads:
            writes = list(writes) + ps_reads
        ev = []
        for r in reads:
            if r in self.lastw:
                ev.append(self.lastw[r])
        for w in writes:
            if w in self.lastw:
                ev.append(self.lastw[w])
            ev.extend(self.readers.get(w, {}).items())
        self._need(eng, ev)
        if cc is not None:
            sn = "c_" + cc
            self.cnt[sn] = 1
            myev = (sn, 1)
            self.prog[eng].append(("op", fn, sn, None))
        elif chan is not None:
            sn = "d_" + chan
            self.cnt[sn] = self.cnt.get(sn, 0) + 16
            myev = (sn, self.cnt[sn])
            self.prog[eng].append(("op", fn, sn, 16))
        elif sig:
            self.cnt[eng] = self.cnt.get(eng, 0) + 1
            myev = (eng, self.cnt[eng])
            self.prog[eng].append(("op", fn, eng, 1))
        else:
            myev = (eng, self.cnt.get(eng, 0) + 1)
            self.prog[eng].append(("op", fn, None, 0))
        for r in reads:
            d = self.readers.setdefault(r, {})
            d[myev[0]] = max(d.get(myev[0], 0), myev[1])
        for w in writes:
            self.lastw[w] = myev
            self.readers[w] = {}


def build_program():
    nc = bass.Bass("TRN2", target_bir_lowering=False)
    S = Sched()

    def din(name, shape, dt=F32):
        return nc.dram_tensor(name, shape, dt, kind="ExternalInput").ap()

    def dout(name, shape, dt=F32):
        return nc.dram_tensor(name, shape, dt, kind="ExternalOutput").ap()

    xc0 = din("xc0", [TT, 512])
    nwd = [din("nw0", [128, 512]), din("nw1", [128, 512]), din("nwf", [128, 512])]
    wind = [din("win0", [D, 2052]), din("win1", [D, 2048])]
    woutd = [din("wout0", [D, 512]), din("wout1", [D, 512])]
    bfd = din("bfb", [128, 4])
    cstd = din("cst", [128, NCST])
    cfk = din("cfk", [4, PAST, 512])
    cfv = din("cfv", [4, PAST, 512])
    csk = din("csk", [4, PAST, 512])
    csv = din("csv", [4, PAST, 512])
    cfl = din("cfl", [4, PAST, 4])

    yout = dout("yout", [TT, 512])
    kvout = [[dout("kf", [TT, 512]), dout("vf", [TT, 512])], [dout("ks", [TT, 512]), dout("vs", [TT, 512])]]
    lfout = dout("lf", [TT, 4])

    PW = [1024, 1024, 1024, 1024, TS]
    xT_loc = [nc.dram_tensor("xT_loc%d" % p, [512, PW[p]], BF16) for p in range(5)]
    xT_all = [nc.dram_tensor("xT_all%d" % p, [D, PW[p]], BF16) for p in range(5)]
    yT_loc = [nc.dram_tensor("yT_loc%d" % p, [512, PW[p]], BF16) for p in range(5)]
    yT_all = [nc.dram_tensor("yT_all%d" % p, [D, PW[p]], BF16) for p in range(5)]

    def piece_of(tb):
        return (tb // 8, (tb % 8) * 128) if tb < 32 else (4, 0)
    ssq_loc = nc.dram_tensor("ssq_loc", [128, NB], F32)
    ssq_all = nc.dram_tensor("ssq_all", [512, NB], F32)
    xres = [None, nc.dram_tensor("x1s", [TT, 512], F32), nc.dram_tensor("x2s", [TT, 512], F32)]

    from contextlib import ExitStack
    es = ExitStack()

    def sb(name, shape, dt):
        return es.enter_context(nc.sbuf_tensor(name, shape, dt))

    def ps(name, shape, dt):
        return es.enter_context(nc.psum_tensor(name, shape, dt))

    with es:
        cstf = sb("cstf", [128, C_ID + 128], F32)
        cstb = sb("cstb", [128, 3 * 128], BF16)
        zeros = sb("zeros", [128, 512], BF16)
        nwt = sb("nwt", [128, 512], F32)
        nw = [nwt, nwt, nwt]
        bfb = sb("bfbs", [128, 4], F32)
        w_in = sb("w_in", [128, 16, 2052], BF16)
        regB = sb("regB", [128, 4 * T + 32 * 4 * 130], BF16)
        actT = [sb("actT0", [128, 16, 256], BF16), sb("actT1", [128, 16, 256], BF16)]
        ssq_part = sb("ssq_part", [128, NB], F32)
        ssq4 = sb("ssq4", [128, 4, NB], F32)
        ssum = sb("ssum", [128, NB], F32)
        rstd = sb("rstd", [128, NB], F32)
        nrstd = sb("nrstd", [128, NB], F32)
        hrstd = sb("hrstd", [128, NB], F32)
        cpall = sb("cpall", [128, NB, 8], F32)
        lfo = sb("lfo", [128, NB, 4], F32)
        lps = sb("lps", [128, 4], F32)
        xl = sb("xl", [128, 4], F32)
        el = sb("el", [128, 4], F32)
        q_tok = sb("q_tok", [128, 512], BF16)
        k_tok = sb("k_tok", [128, 512], BF16)
        kf32b_big = sb("kf32b", [128, 516], F32)
        kf32 = [sb("kf32a", [128, 512], F32)[:, :], kf32b_big[:, 0:512]]
        vf32 = [sb("vf32a", [128, 512], F32), sb("vf32b", [128, 512], F32)]
        gsil = [sb("gsil0", [128, 512], F32), sb("gsil1", [128, 512], F32)]
        gtmp = sb("gtmp", [128, 512], F32)
        qT = sb("qT", [128, 4, 256], BF16)
        ybuf = [sb("y0", [128, 512], BF16), sb("y1", [128, 512], BF16)]
        yTs = [sb("yTs0", [128, 4, 128], BF16), sb("yTs1", [128, 4, 128], BF16)]
        Pball = sb("Pball", [128, 6, 256], BF16)
        Pb = [Pball[:, i, :] for i in range(6)]
        biasT = sb("biasT", [128, 32, 4], F32)
        rden = sb("rden", [128, 4], F32)
        thb = [sb("th%d" % i, [128, 512], F32) for i in range(3)] + [gtmp]
        Rext = [sb("Rx%d" % i, [128, 516], F32) for i in range(3)] + [kf32b_big]
        ab = [sb("a%d" % i, [128, 512], BF16) for i in range(3)] + [k_tok]
        aTb = [sb("aT0", [128, 4, 128], BF16), sb("aT1", [128, 4, 128], BF16),
               Pball[:, 0:2, :].rearrange("p a (b t) -> p (a b) t", t=128),
               q_tok[:, :].rearrange("p (j t) -> p j t", t=128)]
        xcb = kf32
        xnb = vf32
        xtb = q_tok
        sqj = gtmp
        xTs = yTs
        Pw = [sb("Pw%d" % i, [128, 64], BF16) for i in range(4)]
        Pn = sb("Pn", [64, 64], BF16)
        clf = sb("clf", [128, 8, 16], F32)
        sufb = sb("sufb", [128, 8, 16], F32)
        cpn = sb("cpn", [64, 4], F32)
        aTw = [sb("aTw%d" % i, [128, 8, 64], BF16) for i in range(4)]
        aTn = sb("aTn", [64, 64], BF16)
        Vn_t = sb("Vn_t", [128, 520], BF16)
        carry0 = sb("carry0", [64, 4], F32)
        cbias = sb("cbias", [128, 2], F32)

        psA = ps("psA", [128, 512], F32)
        psB = ps("psB", [128, 512], F32)
        psC = ps("psC", [128, 512], F32)
        psD = ps("psD", [128, 512], F32)
        psE = ps("psE", [128, 512], F32)
        psF = ps("psF", [128, 512], F32)
        psG = ps("psG", [128, 512], F32)
        pst = ps("pst", [128, 1024], BF16)

        KT_OFF = 0
        V_OFF = 4 * T

        def kT(h, lo, hi):
            return regB[:, KT_OFF + h * T + lo: KT_OFF + h * T + hi]

        def Vaug(blk, h, n):
            o = V_OFF + (blk * 4 + h) * 130
            return regB[:, o:o + n]

        def Vaug_blk(blk):
            o = V_OFF + blk * 4 * 130
            return regB[:, o:o + 520].rearrange("p (h e) -> p h e", e=130)

        def w_out():
            o = 4096 + 8 * 520 + 4096
            return regB[:, o:o + 16 * 512].rearrange("p (c n) -> p c n", n=512)

        PST_ALL = ["pstbank"]
        pst_full = pst
        psE_bf = psE[:, :].bitcast(BF16)
        pst_f32 = pst[:, :].bitcast(F32)
        psG_bf = psG[:, :].bitcast(BF16)
        REGB_ALL = ["kT%d_%d" % (h, b) for h in range(4) for b in range(32)] + ["V%d" % b for b in range(32)]

        ident = cstb[:, 0:128]
        maskLE = cstb[:, 128:256]
        mask64 = cstb[0:64, 256:320]

        def cf(c0, n=128, rows=128):
            return cstf[0:rows, c0:c0 + n]

        S.op("sp", lambda e: e.dma_start(out=cstf[:], in_=cstd[:, 0:C_ID + 128]), writes=["cstf"], chan="cst")
        S.op("pool", lambda e: e.dma_start(out=cstb[:], in_=cstd[:, C_ID:C_ID + 384]), writes=["cstb"], chan="cstb")
        def load_nw(i):
            S.op("sp", lambda e: e.dma_start(out=nwt[:], in_=nwd[i][:, :]), writes=["nw"], chan="cst")
        load_nw(0)
        S.op("sp", lambda e: e.dma_start(out=bfb[:], in_=bfd[:, :]), writes=["bfb"], chan="cst")
        S.op("dve", lambda e: e.memset(zeros[:], 0.0), writes=["zeros"])
        S.op("dve", lambda e: e.memset(cbias[:, 0:1], EPS), writes=["cbias"])
        S.op("dve", lambda e: e.memset(cbias[:, 1:2], 1.0), writes=["cbias"])
        S.op("dve", lambda e: e.memset(ssq_part[:], 1.0), writes=["ssq_part"])
        S.op("dve", lambda e: e.memset(cpall[:], 0.0), writes=["cpall"])
        S.op("dve", lambda e: e.memset(lfo[:], 0.0), writes=["lfo"])
        for i in range(2):
            S.op("pool", lambda e, i=i: e.memset(thb[i][:], 0.0), writes=["th%d" % i])
        for i in range(4):
            S.op("pool", lambda e, i=i: e.memset(aTw[i][:], 0.0), writes=["aTw%d" % i])

        if False:
            for nm_, ap_ in [("win0", wind[0][0:128, 0:512]), ("win1", wind[1][0:128, 0:512]), ("wout0", woutd[0][0:128, :]),
                             ("wout1", woutd[1][0:128, :]), ("cfk", cfk[0][0:128, :]), ("cfv", cfv[0][0:128, :]),
                             ("csk", csk[0][0:128, :]), ("csv", csv[0][0:128, :])]:
                S.op("sp", lambda e, ap_=ap_: e.dma_start(out=gtmp[:], in_=ap_), writes=["gtmp"], chan="dbg")
            S.op("sp", lambda e: e.dma_start(out=gtmp[:, 0:4], in_=cfl[0][0:128, :]), writes=["gtmp"], chan="dbg")

        def load_w_in(l):
            ncol = 2052 if l == 0 else 2048
            src = wind[l].rearrange("(c p) n -> p c n", p=128)
            for c in range(16):
                yield
                si = c % 3
                o = 20544 + si * 4104
                stg = regB[:, o:o + 4104].bitcast(F32)
                S.op("sp", lambda e, c=c, stg=stg: e.dma_start(out=stg[:, 0:ncol], in_=src[:, c, :]),
                     writes=["wst%d" % si] + (REGB_ALL + ["sVc%d" % q for q in range(4)] + ["skT%d" % q for q in range(4)]
                                              if c < 3 else []), chan="wst%d" % si)
                if c % 2 == 0:
                    S.op("act", lambda e, c=c, stg=stg: e.activation(out=w_in[:, c, 0:ncol], in_=stg[:, 0:ncol], func=AF.Identity),
                         reads=["wst%d" % si] + (REGB_ALL if c >= 13 else []), writes=["w_in"])
                else:
                    S.op("dve", lambda e, c=c, stg=stg: e.tensor_copy(out=w_in[:, c, 0:ncol], in_=stg[:, 0:ncol]),
                         reads=["wst%d" % si] + (REGB_ALL if c >= 13 else []), writes=["w_in"])

        def load_w_out(l):
            src = woutd[l].rearrange("(c p) n -> p c n", p=128)
            wv = w_out()
            for c in range(0, 16, 4):
                S.op("pool", lambda e, c=c: e.dma_start(out=wv[:, c:c + 4, :], in_=src[:, c:c + 4, :]),
                     writes=["w_out"] + REGB_ALL + ["sVc%d" % q for q in range(4)] + ["skT%d" % q for q in range(4)], chan="wout")

        def blk_rows(tb):
            return 128 if tb < 32 else TS

        cnt = {"xts": 0, "tr": 0}

        def norm_prep(xblk, xres_name, tb, rows, nwi, want_xT, want_gather=True):
            S.op("act", lambda e: e.activation(out=sqj[0:rows, :], in_=xblk[0:rows, :], func=AF.Square,
                                               accum_out=ssq_part[0:rows, tb:tb + 1]),
                 reads=[xres_name], writes=["gtmp", "ssq_part"])
            if not want_xT:
                return
            S.op("dve", lambda e: e.tensor_tensor(out=xtb[0:rows, :], in0=xblk[0:rows, :], in1=nw[nwi][0:rows, :],
                                                  op=ALU.mult),
                 reads=[xres_name, "nw"], writes=["q_tok"])
            for c in range(4):
                S.op("pe", lambda e, c=c: e.transpose(out=pst[:, c * 128:c * 128 + rows],
                                                      in_=xtb[0:rows, c * 128:(c + 1) * 128],
                                                      identity=ident[0:rows, 0:rows]),
                     reads=["q_tok", "cstb"], writes=PST_ALL, sig=(c == 3))
            i = cnt["tr"] % 2
            cnt["tr"] += 1
            pv = pst[:, 0:512].rearrange("p (c t) -> p c t", t=128)
            S.op("act", lambda e: e.activation(out=xTs[i][:, :, 0:rows], in_=pv[:, :, 0:rows], func=AF.Identity),
                 reads=PST_ALL, writes=["yTs%d" % i])
            pc, c0 = piece_of(tb)
            dst = xT_loc[pc].ap().rearrange("(c p) t -> p c t", p=128)
            S.op("sp", lambda e: e.dma_start(out=dst[:, :, c0:c0 + rows], in_=xTs[i][:, :, 0:rows]),
                 reads=["yTs%d" % i], writes=["xT_loc%d" % pc], chan="yTs%d" % i)
            if want_gather and (tb % 8 == 7 or tb == 32):
                gather(xT_loc[pc], xT_all[pc], "xT", pc)

        def rstd_from_ssq(tag):
            S.op("sp", lambda e: e.dma_start(out=ssq_loc.ap(), in_=ssq_part[:]), reads=["ssq_part"],
                 writes=["ssq_loc"], chan="ssq")
            S.op("pool", lambda e: e.collective_compute("AllGather", ALU.bypass, replica_groups=GROUPS,
                                                        ins=[ssq_loc.ap().opt()], outs=[ssq_all.ap().opt()]),
                 reads=["ssq_loc"], writes=["ssq_all"], cc="ssq" + tag)
            S.op("sp", lambda e: e.dma_start(out=ssq4[:], in_=ssq_all.ap().rearrange("(r p) b -> p r b", p=128)),
                 reads=["ssq_all"], writes=["ssq4"], chan="ssq")
            S.op("dve", lambda e: e.tensor_tensor(out=ssum[:], in0=ssq4[:, 0, :], in1=ssq4[:, 1, :], op=ALU.add),
                 reads=["ssq4"], writes=["ssum"])
            S.op("dve", lambda e: e.tensor_tensor(out=ssum[:], in0=ssum[:], in1=ssq4[:, 2, :], op=ALU.add),
                 reads=["ssq4", "ssum"], writes=["ssum"])
            S.op("dve", lambda e: e.tensor_tensor(out=ssum[:], in0=ssum[:], in1=ssq4[:, 3, :], op=ALU.add),
                 reads=["ssq4", "ssum"], writes=["ssum"])
            S.op("act", lambda e: e.activation(out=ssum[:], in_=ssum[:], func=AF.Ln, scale=1.0 / D, bias=cbias[:, 0:1]),
                 reads=["ssum", "cbias"], writes=["ssum"])
            S.op("act", lambda e: e.activation(out=rstd[:], in_=ssum[:], func=AF.Exp, scale=-0.5),
                 reads=["ssum"], writes=["rstd"])
            S.op("dve", lambda e: e.tensor_scalar(out=nrstd[:], in0=rstd[:], scalar1=-1.0, scalar2=None, op0=ALU.mult),
                 reads=["rstd"], writes=["nrstd"])
            S.op("dve", lambda e: e.tensor_scalar(out=hrstd[:], in0=rstd[:], scalar1=0.5, scalar2=None, op0=ALU.mult),
                 reads=["rstd"], writes=["hrstd"])
            S.op("dve", lambda e: e.memset(ssq_part[:], 1.0), reads=[], writes=["ssq_part"])

        gcount = {"n": 0}

        def gather(loc, allt, rname, pc):
            gcount["n"] += 1
            S.op("pool", lambda e: e.collective_compute("AllGather", ALU.bypass, replica_groups=GROUPS,
                                                        ins=[loc.ap().opt()], outs=[allt.ap().opt()]),
                 reads=["%s_loc%d" % (rname, pc)], writes=["%s_all%d" % (rname, pc)], cc="g%d" % gcount["n"])

        wq = load_w_in(0)

        def n0_load(tb):
            rows = blk_rows(tb)
            i = tb % 2
            S.op("sp", lambda e: e.dma_start(out=xcb[i][0:rows, :], in_=xc0[tb * 128:tb * 128 + rows, :]),
                 writes=["kf32%d" % i], chan="kf32%d" % i)

        for tb in range(NB):
            rows = blk_rows(tb)
            i = tb % 2
            if tb % 2 == 0:
                next(wq, None)
            if tb == 0:
                n0_load(0)
            if tb + 1 < NB:
                n0_load(tb + 1)
            norm_prep(xcb[i], "kf32%d" % i, tb, rows, 0, True)
        for _ in wq:
            pass
        rstd_from_ssq("0")

        def load_act(slot, src_all, rname, c0, ncols):
            pc, lc = (c0 // 1024, c0 % 1024) if c0 < T else (4, 0)
            src = src_all[pc].ap().rearrange("(c p) t -> p c t", p=128)
            S.op("sp", lambda e: e.dma_start(out=actT[slot][:, :, 0:ncols], in_=src[:, :, lc:lc + ncols]),
                 reads=["%s_all%d" % (rname, pc)], writes=["actT%d" % slot], chan="actT%d" % slot)

        def transposes_to(src_tok, src_name, rows, dst_fn, dst_names, evac_eng):
            for h in range(4):
                S.op("pe", lambda e, h=h: e.transpose(out=pst[:, h * 128:h * 128 + rows],
                                                      in_=src_tok[0:rows, h * 128:(h + 1) * 128],
                                                      identity=ident[0:rows, 0:rows]),
                     reads=[src_name, "cstb"], writes=PST_ALL, sig=(h == 3))
            pv4 = pst[:, 0:512].rearrange("p (h t) -> p h t", t=128)
            if evac_eng == "act":
                S.op("act", lambda e: e.activation(out=dst_fn(None), in_=pv4[:, :, 0:rows], func=AF.Identity),
                     reads=PST_ALL, writes=dst_names)
            else:
                S.op("dve", lambda e: e.tensor_copy(out=dst_fn(None), in_=pv4[:, :, 0:rows]), reads=PST_ALL, writes=dst_names)

        def project_block(l, slot, sub, tb, rows, kT_dst, kT_names, v_dst_fn, v_names, qcol0):
            banks = [psA, psB, psC, psD]
            bn = ["psA", "psB", "psC", "psD"]
            for c in range(16):
                lhsT = actT[slot][:, c, sub * 128: sub * 128 + rows]
                for j in range(4):
                    S.op("pe", lambda e, c=c, j=j, lhsT=lhsT: e.matmul(banks[j][0:rows, :], lhsT=lhsT,
                                                                        rhs=w_in[:, c, j * 512:(j + 1) * 512],
                                                                        start=(c == 0), stop=(c == 15)),
                         reads=["actT%d" % slot, "w_in"], writes=[bn[j]], sig=(c == 15))
                if l == 0:
                    S.op("pe", lambda e, c=c, lhsT=lhsT: e.matmul(psE[0:rows, 0:4], lhsT=lhsT,
                                                                   rhs=w_in[:, c, 2048:2052],
                                                                   start=(c == 0), stop=(c == 15)),
                         reads=["actT%d" % slot, "w_in"], writes=["psE"], sig=(c == 15))
            rs = rstd[0:rows, tb:tb + 1]
            nrs = nrstd[0:rows, tb:tb + 1]
            i = tb % 2
            S.op("dve", lambda e: e.tensor_scalar(out=q_tok[0:rows, :], in0=psA[0:rows, :], scalar1=rs, scalar2=None,
                                                  op0=ALU.mult),
                 reads=["psA", "rstd"], writes=["q_tok"])
            S.op("act", lambda e: e.activation(out=kf32[i][0:rows, :], in_=psB[0:rows, :], func=AF.Identity, scale=rs),
                 reads=["psB", "rstd"], writes=["kf32%d" % i])
            S.op("dve", lambda e: e.tensor_scalar(vf32[i][0:rows, :], psC[0:rows, :], rs, None, ALU.mult),
                 reads=["psC", "rstd"], writes=["vf32%d" % i])
            S.op("sp", lambda e: e.dma_start(out=kvout[l][0][tb * 128:tb * 128 + rows, :], in_=kf32[i][0:rows, :]),
                 reads=["kf32%d" % i], writes=[], chan="kf32%d" % i)
            S.op("sp", lambda e: e.dma_start(out=kvout[l][1][tb * 128:tb * 128 + rows, :], in_=vf32[i][0:rows, :]),
                 reads=["vf32%d" % i], writes=[], chan="vf32%d" % i)
            S.op("act", lambda e: e.activation(out=k_tok[0:rows, :], in_=psB[0:rows, :], func=AF.Identity, scale=rs),
                 reads=["psB", "rstd"], writes=["k_tok"])
            vd = v_dst_fn()
            S.op("dve", lambda e: e.tensor_copy(out=vd[0:rows, :, 0:128],
                                                in_=vf32[i][0:rows, :].rearrange("p (h d) -> p h d", d=128)),
                 reads=["vf32%d" % i], writes=v_names)
            S.op("pool", lambda e: e.memset(vd[0:rows, :, 128:129], 1.0), reads=[], writes=v_names)
            if l == 0:
                S.op("act", lambda e: e.activation(out=gtmp[0:rows, :], in_=psD[0:rows, :], func=AF.Exp, scale=nrs),
                     reads=["psD", "nrstd"], writes=["gtmp"])
                S.op("act", lambda e: e.activation(out=gtmp[0:rows, :], in_=gtmp[0:rows, :], func=AF.Ln, bias=cbias[0:rows, 1:2]),
                     reads=["gtmp", "cbias"], writes=["gtmp"])
                S.op("act", lambda e: e.activation(out=gtmp[0:rows, :], in_=gtmp[0:rows, :], func=AF.Exp, scale=-1.0),
                     reads=["gtmp"], writes=["gtmp"])
            else:
                S.op("act", lambda e: e.activation(out=gtmp[0:rows, :], in_=psD[0:rows, :], func=AF.Sigmoid, scale=rs),
                     reads=["psD", "rstd"], writes=["gtmp"])
            S.op("dve", lambda e: e.scalar_tensor_tensor(out=gsil[sub][0:rows, :], in0=psD[0:rows, :], scalar=rs,
                                                         in1=gtmp[0:rows, :], op0=ALU.mult, op1=ALU.mult),
                 reads=["psD", "rstd", "gtmp"], writes=["gsil%d" % sub])
            if l == 0:
                S.op("dve", lambda e: e.scalar_tensor_tensor(out=xl[0:rows, :], in0=psE[0:rows, 0:4], scalar=rs,
                                                             in1=bfb[0:rows, :], op0=ALU.mult, op1=ALU.add),
                     reads=["psE", "rstd", "bfb"], writes=["xl"])
                S.op("act", lambda e: e.activation(out=el[0:rows, :], in_=xl[0:rows, :], func=AF.Exp, scale=-1.0),
                     reads=["xl"], writes=["el"])
                S.op("act", lambda e: e.activation(out=lps[0:rows, :], in_=el[0:rows, :], func=AF.Ln, bias=cbias[0:rows, 1:2]),
                     reads=["el", "cbias"], writes=["lps"])
                S.op("pool", lambda e: e.tensor_scalar(out=lfo[0:rows, tb, :], in0=lps[0:rows, :], scalar1=-1.0,
                                                       scalar2=0.0, op0=ALU.mult, op1=ALU.add),
                     reads=["lps"], writes=["lfo"])
            transposes_to(q_tok, "q_tok", rows, lambda h: qT[:, :, qcol0:qcol0 + rows], ["qT"], "dve")
            transposes_to(k_tok, "k_tok", rows, kT_dst, kT_names, "act")

        def cumsum_block(tb):
            first = (tb == 0)
            S.op("pe", lambda e: e.matmul(psE[:, 8:12], lhsT=cf(C_U), rhs=lps[:, 0:4], start=True, stop=first),
                 reads=["cstf", "lps"], writes=["psE"], sig=first)
            if not first:
                S.op("pe", lambda e: e.matmul(psE[:, 8:12], lhsT=cf(C_E127), rhs=cpall[:, tb - 1, 0:4], start=False,
                                              stop=True),
                     reads=["cstf", "cpall"], writes=["psE"], sig=False)
                S.op("pe", lambda e: e.matmul(psE[:, 12:16], lhsT=cf(C_E127), rhs=cpall[:, tb - 1, 0:4], start=True,
                                              stop=True),
                     reads=["cstf", "cpall"], writes=["psE"])
                S.op("dve", lambda e: e.tensor_copy(out=cpall[:, tb, 0:8], in_=psE[:, 8:16]), reads=["psE"],
                     writes=["cpall"])
            else:
                S.op("dve", lambda e: e.tensor_copy(out=cpall[:, tb, 0:4], in_=psE[:, 8:12]), reads=["psE"],
                     writes=["cpall"])

        sc_banks = [(psA, "psA"), (psB, "psB")]
        o_banks = [((psC, "psC"), (psD, "psD")), ((psF, "psF"), (psG, "psG"))]

        def y_out(tbs, rows_l):
            for sub, tb in enumerate(tbs):
                rows = rows_l[sub]
                for c in range(4):
                    S.op("pe", lambda e, c=c, sub=sub, rows=rows: e.transpose(out=pst[:, c * 128:c * 128 + rows],
                                                                               in_=ybuf[sub][0:rows, c * 128:(c + 1) * 128],
                                                                               identity=ident[0:rows, 0:rows]),
                         reads=["y%d_%d" % (sub, hh) for hh in range(4)] + ["cstb"], writes=PST_ALL, sig=(c == 3))
                i = cnt["tr"] % 2
                cnt["tr"] += 1
                pv = pst[:, 0:512].rearrange("p (c t) -> p c t", t=128)
                S.op("act", lambda e, i=i, rows=rows, pv=pv: e.activation(out=yTs[i][:, :, 0:rows], in_=pv[:, :, 0:rows],
                                                                           func=AF.Identity),
                     reads=PST_ALL, writes=["yTs%d" % i])
                pc, c0 = piece_of(tb)
                dst = yT_loc[pc].ap().rearrange("(c p) t -> p c t", p=128)
                S.op("sp", lambda e, i=i, rows=rows, c0=c0, dst=dst: e.dma_start(out=dst[:, :, c0:c0 + rows],
                                                                                  in_=yTs[i][:, :, 0:rows]),
                     reads=["yTs%d" % i], writes=["yT_loc%d" % pc], chan="yTs%d" % i)
                if tb % 8 == 7 or tb == 32:
                    gather(yT_loc[pc], yT_all[pc], "yT", pc)

        pcount = {"p": 0, "sc": 0, "ch": 0}

        def run_streams(makers, width):
            pending = list(makers)
            active = []
            free = list(range(width))
            eng_free = {}
            now = 0.0
            while pending or active:
                while pending and free:
                    sl = free.pop(0)
                    g = pending.pop(0)(sl)
                    try:
                        nxt = next(g)
                    except StopIteration:
                        free.append(sl)
                        continue
                    active.append({"g": g, "sl": sl, "ready": now, "nxt": nxt})
                if not active:
                    continue
                best = min(active, key=lambda a: max(eng_free.get(a["nxt"][0], 0.0), a["ready"]))
                eng, dur = best["nxt"]
                start = max(eng_free.get(eng, 0.0), best["ready"])
                end = start + dur
                eng_free[eng] = end
                best["ready"] = end + 0.25
                try:
                    best["nxt"] = next(best["g"])
                except StopIteration:
                    active.remove(best)
                    free.append(best["sl"])
                    now = end

        def fox_stream(slot, Q, h):
            nkb = 2 * Q + 2
            ob = o_banks[slot]
            fsc = [(psA, "psA"), (psB, "psB"), (psE, "psE"), (pst_f32, "pstbank"), (psF, "psF"), (psG, "psG")]
            for kb0 in range(0, nkb, 6):
                kbs = list(range(kb0, min(kb0 + 6, nkb)))
                yield ("pe", 0.5)
                for i, kb in enumerate(kbs):
                    qlo = 128 if kb == nkb - 1 else 0
                    scb, scn = fsc[i]
                    S.op("pe", lambda e, kb=kb, qlo=qlo, scb=scb: e.matmul(
                        scb[:, qlo:256], lhsT=kT(h, kb * 128, (kb + 1) * 128), rhs=qT[:, h, qlo:256], start=True, stop=True),
                        reads=["kT%d_%d" % (h, kb), "qT"], writes=[scn])
                yield ("act", 1.8)
                for i, kb in enumerate(kbs):
                    qlo = 128 if kb == nkb - 1 else 0
                    scb, scn = fsc[i]
                    S.op("act", lambda e, kb=kb, qlo=qlo, i=i, scb=scb: e.activation(
                        out=Pb[i][:, qlo:256], in_=scb[:, qlo:256], func=AF.Exp, scale=SCALE, bias=biasT[:, kb, h:h + 1]),
                        reads=[scn, "biasT"], writes=["P%d" % i])
                    if kb >= nkb - 2:
                        dq = 0 if kb == nkb - 2 else 128
                        S.op("pool", lambda e, i=i, dq=dq: e.tensor_tensor(out=Pb[i][:, dq:dq + 128],
                                                                           in0=Pb[i][:, dq:dq + 128], in1=maskLE,
                                                                           op=ALU.mult),
                             reads=["P%d" % i, "cstb"], writes=["P%d" % i])
                yield ("pe", 1.2)
                for i, kb in enumerate(kbs):
                    for sub in range(2):
                        if kb == nkb - 1 and sub == 0:
                            continue
                        last = (kb == nkb - 2) if sub == 0 else (kb == nkb - 1)
                        S.op("pe", lambda e, kb=kb, sub=sub, i=i, last=last: e.matmul(
                            ob[sub][0][:, 0:129], lhsT=Pb[i][:, sub * 128:(sub + 1) * 128], rhs=Vaug(kb, h, 129),
                            start=(kb == 0), stop=last),
                            reads=["P%d" % i, "V%d" % kb], writes=[ob[sub][1]], sig=(sub == 1 or kb == nkb - 2))
            yield ("dve", 0.5)
            for sub in range(2):
                S.op("dve", lambda e, sub=sub: e.reciprocal(out=rden[:, 2 * slot + sub:2 * slot + sub + 1],
                                                            in_=ob[sub][0][:, 128:129]),
                     reads=[ob[sub][1]], writes=["rden%d" % slot])
                S.op("dve", lambda e, sub=sub: e.scalar_tensor_tensor(
                    out=ybuf[sub][:, h * 128:(h + 1) * 128], in0=ob[sub][0][:, 0:128],
                    scalar=rden[:, 2 * slot + sub:2 * slot + sub + 1],
                    in1=gsil[sub][:, h * 128:(h + 1) * 128], op0=ALU.mult, op1=ALU.mult),
                    reads=[ob[sub][1], "rden%d" % slot, "gsil%d" % sub], writes=["y%d_%d" % (sub, h)])

        def fox_tile(Q):
            nkb = 2 * Q + 2
            for h in range(4):
                S.op("dve", lambda e, h=h: e.tensor_scalar(out=biasT[:, 0:nkb, h], in0=cpall[:, 0:nkb, h],
                                                           scalar1=cpall[:, 2 * Q + 1, 4 + h:5 + h], scalar2=None,
                                                           op0=ALU.subtract),
                     reads=["cpall"], writes=["biasT"])
            run_streams([(lambda sl, h=h: fox_stream(sl, Q, h)) for h in range(4)], FOXW)

        def sb_stream(slot, qrows, qT_ap, qT_names, chunks, o_ap, o_name, pv, total_pv, carry_in=None, aT_hook=None,
                      fin=None, out_state=None):
            prev = carry_in
            th, Rx, a_ = thb[slot], Rext[slot], ab[slot]
            tn = ["th0", "th1", "th2", "gtmp"][slot]
            rn = ["Rx0", "Rx1", "Rx2", "kf321"][slot]
            an = ["a0", "a1", "a2", "k_tok"][slot]
            aTnames = [["aT0"], ["aT1"], ["P0", "P1"], ["q_tok"]][slot]
            scb, scn = [(psA, "psA"), (psB, "psB"), (psE, "psE"), (psG, "psG")][slot]
            pst, pn, pso = scb[:, :].bitcast(BF16), scn, 0
            for ci, ch in enumerate(chunks):
                W = ch["W"]
                yield ("pe", 0.45)
                S.op("pe", lambda e, ch=ch, W=W, scb=scb: e.matmul(scb[0:qrows, 0:W], lhsT=qT_ap, rhs=ch["kT"], start=True, stop=True),
                     reads=qT_names + ch["names"], writes=[scn])
                yield ("act", 0.65)
                S.op("act", lambda e, W=W, scb=scb: e.activation(out=th[0:qrows, 0:W], in_=scb[0:qrows, 0:W], func=AF.Sigmoid,
                                                        scale=-SCALE),
                     reads=[scn], writes=[tn])
                yield ("dve", 1.5)
                if ch.get("mask") is not None:
                    M, Mc, dw = ch["mask"]
                    S.op("dve", lambda e, W=W, dw=dw, M=M: e.tensor_tensor(out=th[0:qrows, W - dw:W], in0=th[0:qrows, W - dw:W],
                                                                           in1=M, op=ALU.mult),
                         reads=[tn, "cstf"], writes=[tn])
                    S.op("dve", lambda e, W=W, dw=dw, Mc=Mc: e.tensor_tensor(out=th[0:qrows, W - dw:W], in0=th[0:qrows, W - dw:W],
                                                                             in1=Mc, op=ALU.add),
                         reads=[tn, "cstf"], writes=[tn])
                if prev is None:
                    S.op("dve", lambda e, W=W: e.memset(Rx[0:qrows, W:W + 1], 1.0), reads=[], writes=[rn])
                else:
                    pR, pname = prev
                    S.op("dve", lambda e, W=W, pR=pR: e.tensor_copy(out=Rx[0:qrows, W:W + 1], in_=pR),
                         reads=[pname, an], writes=[rn])
                S.op("dve", lambda e, W=W: e.tensor_tensor_scan(
                    out=Rx[0:qrows, 0:W][:, ::-1], data0=th[0:qrows, 0:W][:, ::-1], data1=zeros[0:qrows, 0:W],
                    initial=Rx[0:qrows, W:W + 1], op0=ALU.mult, op1=ALU.add),
                    reads=[tn, rn, "zeros"], writes=[rn])
                prev = (Rx[0:qrows, 0:1], rn)
                if True:
                    yield ("pool", 1.3)
                S.op("pool", lambda e, W=W: e.tensor_tensor(out=a_[0:qrows, 0:W], in0=Rx[0:qrows, 1:W + 1],
                                                                                   in1=Rx[0:qrows, 0:W], op=ALU.subtract),
                     reads=[rn], writes=[an])
                yield ("pe", 0.7)
                vbl = ch["vblocks"]
                off = 0
                for j, (rhs, wk, vn) in enumerate(vbl):
                    S.op("pe", lambda e, off=off, wk=wk, j=j: e.transpose(out=pst[0:wk, pso + j * 128:pso + j * 128 + qrows],
                                                                          in_=a_[0:qrows, off:off + wk],
                                                                          identity=ident[0:qrows, 0:qrows]),
                         reads=[an, "cstb"], writes=[pn], sig=(j == len(vbl) - 1))
                    off += wk
                yield ("act", 0.65)
                if aT_hook is None:
                    aT = aTb[slot]
                    nb_ = len(vbl)
                    pvw = pst[:, pso:pso + nb_ * 128].rearrange("p (j t) -> p j t", t=128)
                    S.op("act", lambda e, aT=aT, pvw=pvw, nb_=nb_: e.activation(out=aT[:, 0:nb_, 0:qrows], in_=pvw[:, :, 0:qrows],
                                                                                func=AF.Identity),
                         reads=[pn], writes=aTnames)
                    lhs_list = [(aT[0:wk, j, 0:qrows], aTnames) for j, (_, wk, _) in enumerate(vbl)]
                else:
                    lhs_list = aT_hook(ci, vbl, pso, pn, pst)
                if aT_hook is None:
                    yield ("pe", 0.7)
                for j, (rhs, wk, vn) in enumerate(vbl):
                    lhsT, ln = lhs_list[j]
                    n = pv["n"]
                    S.op("pe", lambda e, lhsT=lhsT, rhs=rhs, n=n: e.matmul(o_ap, lhsT=lhsT, rhs=rhs, start=(n == 0),
                                                                           stop=(n == total_pv - 1)),
                         reads=(ln if isinstance(ln, list) else [ln]) + [vn], writes=[o_name], sig=True)
                    pv["n"] += 1
            if out_state is not None:
                out_state["carry"] = prev
            if fin is not None:
                yield ("dve", 0.3)
                fin()

        sb_obank = {0: [(psC, "psC")], 1: [(psD, "psD")], 2: [(psF, "psF")], 3: [(pst_f32, "pstbank")]}
        sb_ocnt = {0: 0, 1: 0}

        def sb_tile(Q):
            makers = []
            for sub in range(2):
                qb = 2 * Q + sub
                e_ = 128 * (qb + 1)
                for h in range(4):
                    def mk(slot, sub=sub, h=h, e_=e_):
                        chunks = []
                        hi = e_
                        while hi > 0:
                            lo = max(0, hi - 512)
                            vbl = [(Vaug(b, h, 128), 128, "V%d" % b) for b in range(lo // 128, hi // 128)]
                            chunks.append(dict(kT=kT(h, lo, hi), W=hi - lo,
                                               names=["kT%d_%d" % (h, b) for b in range(lo // 128, hi // 128)],
                                               vblocks=vbl,
                                               mask=(cf(C_MLT), cf(C_MLTC), 128) if hi == e_ else None))
                            hi = lo
                        ob, on = sb_obank[slot][0]

                        def fin():
                            S.op("dve", lambda e: e.tensor_tensor(out=ybuf[sub][:, h * 128:(h + 1) * 128], in0=ob[:, 0:128],
                                                                  in1=gsil[sub][:, h * 128:(h + 1) * 128], op=ALU.mult),
                                 reads=[on, "gsil%d" % sub], writes=["y%d_%d" % (sub, h)])
                        return sb_stream(slot, 128, qT[:, h, sub * 128:(sub + 1) * 128], ["qT"], chunks, ob[:, 0:128], on,
                                         {"n": 0}, sum(len(c["vblocks"]) for c in chunks), fin=fin)
                    makers.append(mk)
            run_streams(makers, SBW)

        SEQSZ = 8 * 520 + 4096
        assert 4 * SEQSZ <= 4 * T + 32 * 4 * 130
        kstage_f = actT[1][:, :, :].rearrange("p c t -> p (c t)").bitcast(F32).rearrange("p (j c) -> p j c", c=512)

        def sVc_all(s_):
            b0 = s_ * SEQSZ
            return regB[:, b0:b0 + 8 * 520].rearrange("p (j h e) -> p j h e", h=4, e=130)

        def sVc(s_, j, h, n):
            o = s_ * SEQSZ + (j * 4 + h) * 130
            return regB[:, o:o + n]

        def skT(s_, h, lo, hi):
            o = s_ * SEQSZ + 8 * 520 + h * 1024
            return regB[:, o + lo:o + hi]

        def Pw8(s_):
            v = thb[s_ // 2][:, :].bitcast(BF16)
            return v[:, (s_ % 2) * 512:(s_ % 2 + 1) * 512].rearrange("p (j c) -> p j c", c=64)

        def Vn_v():
            return Vn_t[:, :].rearrange("p (h e) -> p h e", e=130)

        def sample_stage(l):
            kcache, vcache = (cfk, cfv) if l == 0 else (csk, csv)
            for s_ in range(4):
                for j in range(8):
                    S.op("pool", lambda e, s_=s_, j=j: e.dma_start(
                        out=sVc_all(s_)[:, j, :, 0:128],
                        in_=vcache[s_][j * 128:(j + 1) * 128, :].rearrange("p (h d) -> p h d", d=128)),
                         writes=["sVc%d" % s_] + (REGB_ALL + ["w_out"] if j == 0 else []), chan="sVc")
                S.op("pool", lambda e, s_=s_: e.memset(sVc_all(s_)[:, :, :, 128:129], 1.0), writes=["sVc%d" % s_])
                for hf in range(2):
                    S.op("sp", lambda e, s_=s_, hf=hf: e.dma_start(
                        out=kstage_f, in_=kcache[s_][hf * 512:(hf + 1) * 512, :].rearrange("(j p) c -> p j c", p=128)),
                         writes=["actT1"], chan="sKc")
                    for jj in range(4):
                        j = hf * 4 + jj
                        for h in range(4):
                            S.op("pe", lambda e, jj=jj, h=h: e.transpose(out=pst_f32[:, h * 128:(h + 1) * 128],
                                                                         in_=kstage_f[:, jj, h * 128:(h + 1) * 128],
                                                                         identity=cstf[:, C_ID:C_ID + 128]),
                                 reads=["actT1", "cstf"], writes=PST_ALL, sig=(h == 3))
                        kb_ = s_ * SEQSZ + 8 * 520
                        S.op("act", lambda e, kb_=kb_, j=j: e.activation(
                            out=regB[:, kb_:kb_ + 4096].rearrange("p (h t) -> p h t", t=1024)[:, :, j * 128:(j + 1) * 128],
                            in_=pst_f32[:, 0:512].rearrange("p (h t) -> p h t", t=128), func=AF.Identity),
                             reads=PST_ALL, writes=["skT%d" % s_] + (REGB_ALL + ["w_out"] if j == 0 else []))
            if l == 0:
                for s_ in range(4):
                    S.op("sp", lambda e, s_=s_: e.dma_start(out=clf[:, :, s_ * 4:(s_ + 1) * 4],
                                                            in_=cfl[s_].rearrange("(j p) h -> p j h", p=128)),
                         writes=["clf"], chan="clf")
                for j in range(8):
                    S.op("pe", lambda e, j=j: e.matmul(psE[:, 16 + 16 * j:32 + 16 * j], lhsT=cf(C_LS), rhs=clf[:, j, :], start=True,
                                                       stop=(j == 7)), reads=["cstf", "clf"], writes=["psE"], sig=(j == 7))
                    for j2 in range(j + 1, 8):
                        S.op("pe", lambda e, j=j, j2=j2: e.matmul(psE[:, 16 + 16 * j:32 + 16 * j], lhsT=cf(C_ONES), rhs=clf[:, j2, :],
                                                                  start=False, stop=(j2 == 7)), reads=["cstf", "clf"],
                             writes=["psE"], sig=(j2 == 7))
                S.op("dve", lambda e: e.tensor_copy(out=sufb[:].rearrange("p j c -> p (j c)"), in_=psE[:, 16:144]), reads=["psE"],
                     writes=["sufb"])

        def sample_block(l, slot):
            tb = 32
            rows = TS
            Vn = Vn_v()
            project_block(l, slot, 0, tb, rows, lambda h: qT[:, :, 128:128 + rows], ["qT"],
                          Vn_v, ["Vn"], 0)
            if l == 0:
                S.op("pe", lambda e: e.matmul(psE[0:64, 8:12], lhsT=cf(C_UB16, 64, 64), rhs=lps[0:64, 0:4], start=True, stop=True),
                     reads=["cstf", "lps"], writes=["psE"])
                S.op("dve", lambda e: e.tensor_copy(out=cpn[:, :], in_=psE[0:64, 8:12]), reads=["psE"], writes=["cpn"])
            for h in range(4):
                ob, on = o_banks[h % 2][0]
                if l == 0:
                    o_ap = ob[0:64, 0:129]
                    first = True
                    for s_ in range(4):
                        scb, scn = sc_banks[pcount["sc"] % 2]
                        pcount["sc"] += 1
                        pw = Pw8(s_)
                        pwn = "th%d" % (s_ // 2)
                        for j in range(8):
                            S.op("pe", lambda e, s_=s_, j=j, h=h, scb=scb: e.matmul(
                                scb[:, j * 16:(j + 1) * 16], lhsT=skT(s_, h, j * 128, (j + 1) * 128),
                                rhs=qT[:, h, s_ * 16:(s_ + 1) * 16], start=True, stop=True),
                                reads=["skT%d" % s_, "qT"], writes=[scn], sig=(j == 7))
                        for j in range(8):
                            S.op("act", lambda e, s_=s_, j=j, h=h, scb=scb, pw=pw: e.activation(
                                out=pw[:, j, s_ * 16:(s_ + 1) * 16], in_=scb[:, j * 16:(j + 1) * 16], func=AF.Exp, scale=SCALE,
                                bias=sufb[:, j, s_ * 4 + h:s_ * 4 + h + 1]), reads=[scn, "sufb"], writes=[pwn])
                        for j in range(8):
                            S.op("pe", lambda e, s_=s_, j=j, h=h, first=first, o_ap=o_ap, pw=pw: e.matmul(
                                o_ap, lhsT=pw[:, j, 0:64], rhs=sVc(s_, j, h, 129), start=first, stop=False),
                                reads=[pwn, "sVc%d" % s_], writes=[on], sig=(j == 7))
                            first = False
                    scb, scn = sc_banks[pcount["sc"] % 2]
                    pcount["sc"] += 1
                    S.op("pe", lambda e, h=h, scb=scb: e.matmul(scb[0:64, 0:64], lhsT=qT[:, h, 128:192], rhs=qT[:, h, 0:64],
                                                                start=True, stop=True), reads=["qT"], writes=[scn])
                    S.op("act", lambda e, h=h, scb=scb: e.activation(out=Pn[:, :], in_=scb[0:64, 0:64], func=AF.Exp, scale=SCALE,
                                                                     bias=cpn[:, h:h + 1]), reads=[scn, "cpn"], writes=["Pn"])
                    S.op("pool", lambda e: e.tensor_tensor(out=Pn[:, :], in0=Pn[:, :], in1=mask64, op=ALU.mult),
                         reads=["Pn", "cstb"], writes=["Pn"])
                    S.op("pe", lambda e, h=h, o_ap=o_ap: e.matmul(o_ap, lhsT=Pn[:, :], rhs=Vn[0:64, h, 0:129], start=False, stop=True),
                         reads=["Pn", "Vn"], writes=[on])
                    S.op("dve", lambda e, ob=ob: e.reciprocal(out=rden[0:64, 0:1], in_=ob[0:64, 128:129]), reads=[on],
                         writes=["rden0"])
                    S.op("dve", lambda e, h=h, ob=ob: e.scalar_tensor_tensor(
                        out=ybuf[0][0:64, h * 128:(h + 1) * 128], in0=ob[0:64, 0:128], scalar=rden[0:64, 0:1],
                        in1=gsil[0][0:64, h * 128:(h + 1) * 128], op0=ALU.mult, op1=ALU.mult),
                        reads=[on, "rden0", "gsil0"], writes=["y0_%d" % h])
                else:
                    pass
            if l == 1:
                def head_stream(slot, h):
                    ob, on = sb_obank[slot][0]
                    o_ap = ob[0:64, 0:128]
                    pv = {"n": 0}
                    total = 1 + 4 * 8

                    def hook0(ci, vbl, pso, pn, pst_=None):
                        S.op("act", lambda e: e.activation(out=aTn[:, :], in_=pst_[0:64, pso:pso + 64], func=AF.Identity),
                             reads=[pn], writes=["aTn"])
                        return [(aTn[:, :], "aTn")]
                    ch0 = dict(kT=qT[:, h, 128:192], W=64, names=["qT"], vblocks=[(Vn[0:64, h, 0:128], 64, "Vn")],
                               mask=(cf(C_MS, 64, 64), cf(C_MSC, 64, 64), 64))
                    st = {}
                    yield from sb_stream(slot, 64, qT[:, h, 0:64], ["qT"], [ch0], o_ap, on, pv, total, None, hook0, None, st)
                    car = st["carry"]
                    cr = carry0[:, slot:slot + 1]
                    yield ("dve", 0.1)
                    S.op("dve", lambda e: e.tensor_copy(out=cr, in_=car[0]), reads=[car[1]], writes=["carry0_%d" % slot])
                    for s_ in range(4):
                        def hook(ci, vbl, pso, pn, pst_=None, s_=s_):
                            base = 4 if ci == 0 else 0
                            pvw = pst_[:, pso:pso + 512].rearrange("p (j t) -> p j t", t=128)
                            S.op("act", lambda e: e.activation(out=aTw[s_][:, base:base + 4, s_ * 16:(s_ + 1) * 16],
                                                               in_=pvw[:, :, s_ * 16:(s_ + 1) * 16], func=AF.Identity),
                                 reads=[pn], writes=["aTw%d" % s_])
                            return [(aTw[s_][:, base + j, 0:64], "aTw%d" % s_) for j in range(4)]
                        chunks = []
                        for (lo, hi) in ((512, 1024), (0, 512)):
                            chunks.append(dict(kT=skT(s_, h, lo, hi), W=512, names=["skT%d" % s_],
                                               vblocks=[(sVc(s_, b, h, 128), 128, "sVc%d" % s_) for b in range(lo // 128, hi // 128)],
                                               mask=None))
                        yield from sb_stream(slot, 64, qT[:, h, 0:64], ["qT"], chunks, o_ap, on, pv, total,
                                             (cr, "carry0_%d" % slot), hook)
                    yield ("dve", 0.3)
                    S.op("dve", lambda e: e.tensor_tensor(out=ybuf[0][0:64, h * 128:(h + 1) * 128], in0=ob[0:64, 0:128],
                                                          in1=gsil[0][0:64, h * 128:(h + 1) * 128], op=ALU.mult),
                         reads=[on, "gsil0"], writes=["y0_%d" % h])
                run_streams([(lambda sl, h=h: head_stream(sl, h)) for h in range(4)], SBW)
            load_w_out(l)
            y_out([tb], [rows])

        def p_phase(l):
            load_act(0, xT_all, "xT", 0, 256)
            nt = 16
            for Q in range(nt):
                slot = Q % 2
                if Q < 15:
                    load_act(1 - slot, xT_all, "xT", (Q + 1) * 256, 256)
                else:
                    load_act(1 - slot, xT_all, "xT", T, TS)
                for sub in range(2):
                    tb = 2 * Q + sub
                    project_block(l, slot, sub, tb, 128,
                                  lambda h, tb=tb: regB[:, 0:4 * T].rearrange("p (h t) -> p h t", t=T)[:, :, tb * 128:(tb + 1) * 128],
                                  ["kT%d_%d" % (h, tb) for h in range(4)],
                                  lambda tb=tb: Vaug_blk(tb), ["V%d" % tb], sub * 128)
                    if l == 0:
                        cumsum_block(tb)
                if l == 0:
                    fox_tile(Q)
                else:
                    sb_tile(Q)
                y_out([2 * Q, 2 * Q + 1], [128, 128])
            if nt == 16:
                sample_stage(l)
                sample_block(l, 0)

        def o_phase(l):
            wq = load_w_in(1) if l == 0 else iter(())
            wv = w_out()
            load_act(0, yT_all, "yT", 0, 256)
            xsrc = xc0 if l == 0 else xres[1].ap()
            xdst = xres[l + 1].ap()

            def o_load(tb):
                rows = blk_rows(tb)
                i = tb % 2
                S.op("sp", lambda e: e.dma_start(out=xcb[i][0:rows, :], in_=xsrc[tb * 128:tb * 128 + rows, :]),
                     reads=["xres%d" % l], writes=["kf32%d" % i], chan="kf32%d" % i)

            for Q in range(17):
                slot = Q % 2
                if Q < 15:
                    load_act(1 - slot, yT_all, "yT", (Q + 1) * 256, 256)
                elif Q == 15:
                    load_act(1 - slot, yT_all, "yT", T, TS)
                next(wq, None)
                for sub in range(2 if Q < 16 else 1):
                    tb = 2 * Q + sub
                    rows = blk_rows(tb)
                    i = tb % 2
                    if tb == 0:
                        o_load(0)
                    if tb + 1 < NB:
                        o_load(tb + 1)
                    for c in range(16):
                        S.op("pe", lambda e, c=c, slot=slot, sub=sub, rows=rows: e.matmul(
                            psA[0:rows, :], lhsT=actT[slot][:, c, sub * 128:sub * 128 + rows], rhs=wv[:, c, :], start=(c == 0),
                            stop=(c == 15)), reads=["actT%d" % slot, "w_out"], writes=["psA"], sig=(c == 15))
                    S.op("dve", lambda e, i=i, rows=rows: e.tensor_tensor(out=xnb[i][0:rows, :], in0=psA[0:rows, :],
                                                                          in1=xcb[i][0:rows, :], op=ALU.add),
                         reads=["psA", "kf32%d" % i], writes=["vf32%d" % i])
                    S.op("sp", lambda e, i=i, tb=tb, rows=rows: e.dma_start(out=xdst[tb * 128:tb * 128 + rows, :],
                                                                              in_=xnb[i][0:rows, :]),
                         reads=["vf32%d" % i], writes=["xres%d" % (l + 1)], chan="vf32%d" % i)
                    norm_prep(xnb[i], "vf32%d" % i, tb, rows, l + 1, l == 0)
            for _ in wq:
                pass

        stage = 99
        if stage >= 2:
            p_phase(0)
        if stage >= 4:
            load_nw(1)
            o_phase(0)
            rstd_from_ssq("1")
        if stage >= 5:
            p_phase(1)
        if stage >= 7:
            load_nw(2)
            o_phase(1)
            rstd_from_ssq("2")
            x2 = xres[2].ap()
            def f_load(tb):
                rows = blk_rows(tb)
                i = tb % 2
                S.op("sp", lambda e: e.dma_start(out=xcb[i][0:rows, :], in_=x2[tb * 128:tb * 128 + rows, :]),
                     reads=["xres2"], writes=["kf32%d" % i], chan="kf32%d" % i)

            f_load(0)
            for tb in range(NB):
                rows = blk_rows(tb)
                i = tb % 2
                if tb + 1 < NB:
                    f_load(tb + 1)
                S.op("dve", lambda e, i=i, tb=tb, rows=rows: e.scalar_tensor_tensor(
                    out=xnb[i][0:rows, :], in0=xcb[i][0:rows, :], scalar=rstd[0:rows, tb:tb + 1], in1=nw[2][0:rows, :],
                    op0=ALU.mult, op1=ALU.mult), reads=["kf32%d" % i, "rstd", "nw"], writes=["vf32%d" % i])
                S.op("sp", lambda e, i=i, tb=tb, rows=rows: e.dma_start(out=yout[tb * 128:tb * 128 + rows, :], in_=xnb[i][0:rows, :]),
                     reads=["vf32%d" % i], writes=[], chan="vf32%d" % i)
        S.op("sp", lambda e: e.dma_start(out=lfout[0:T, :].rearrange("(b p) h -> p b h", p=128), in_=lfo[:, 0:32, :]),
             reads=["lfo"], writes=[], chan="lfo")
        S.op("sp", lambda e: e.dma_start(out=lfout[T:TT, :], in_=lfo[0:TS, 32, :]), reads=["lfo"], writes=[], chan="lfo")
        for sn, v in list(S.cnt.items()):
            if sn.startswith("d_"):
                S.prog["sp"].append(("wait", sn, v))

        sem_names = sorted(S.cnt.keys())
        sems = {}
        for sn in sem_names:
            sems[sn] = es.enter_context(nc.semaphore(sn))
        block = es.enter_context(nc.Block())

        def emit(eng_obj, name):
            for item in S.prog[name]:
                if item[0] == "wait":
                    eng_obj.wait_ge(sems[item[1]], item[2])
                else:
                    _, fn, sn, inc = item
                    ins = fn(eng_obj)
                    if sn is not None:
                        if inc is None:
                            ins.then_inc(sems[sn])
                        else:
                            ins.then_inc(sems[sn], inc)

        @block.sync
        def _(e):
            emit(e, "sp")

        @block.scalar
        def _(e):
            emit(e, "act")

        @block.vector
        def _(e):
            emit(e, "dve")

        @block.gpsimd
        def _(e):
            emit(e, "pool")

        @block.tensor
        def _(e):
            emit(e, "pe")
    return nc


def _consts():
    c = np.zeros((128, NCST), np.float32)
    k = np.arange(128)[:, None]
    m = np.arange(128)[None, :]
    c[:, C_U:C_U + 128] = (k <= m)
    c[:, C_E127:C_E127 + 128] = (k == 127)
    c[:, C_UB16:C_UB16 + 128] = (k <= m) & (k // 16 == m // 16)
    c[:, C_LS:C_LS + 128] = (k > m)
    c[:, C_ONES:C_ONES + 128] = 1.0
    c[:, C_MS:C_MS + 128] = (m < k) & (k // 16 == m // 16)
    c[:, C_MSC:C_MSC + 128] = 1.0 - ((m < k) & (k // 16 == m // 16))
    c[:, C_MLT:C_MLT + 128] = (m < k)
    c[:, C_MLTC:C_MLTC + 128] = 1.0 - (m < k)
    c[:, C_ID:C_ID + 128] = (k == m)
    c[:, C_MLE:C_MLE + 128] = (k <= m)
    c[:, C_M64:C_M64 + 128] = (k <= m) & (k // 16 == m // 16)
    return c


_NC_CACHE = {}


def kernel(x_prompt, x_sample, cache_fox_k, cache_fox_v, cache_fox_logf, cache_sb_k, cache_sb_v,
           norm_0, w_in_0, b_f_0, w_out_0, norm_1, w_in_1, w_out_1, norm_f):
    f = np.float32
    A = lambda a: np.ascontiguousarray(np.asarray(a, dtype=f))
    x_prompt, x_sample = A(x_prompt), A(x_sample)
    w_in_0, w_in_1, w_out_0, w_out_1 = A(w_in_0), A(w_in_1), A(w_out_0), A(w_out_1)
    caches = [A(cache_fox_k), A(cache_fox_v), A(cache_sb_k), A(cache_sb_v)]
    cache_fox_logf = A(cache_fox_logf)
    norms = [A(norm_0), A(norm_1), A(norm_f)]
    b_f_0 = A(b_f_0)
    if "nc" not in _NC_CACHE:
        _NC_CACHE["nc"] = build_program()
    nc = _NC_CACHE["nc"]
    cst = _consts()
    in_maps = []
    for core in range(8):
        b, g = core // 4, core % 4
        cs = slice(g * 512, (g + 1) * 512)
        xc0 = np.concatenate([x_prompt[b][:, cs], x_sample[4 * b:4 * b + 4].reshape(64, D)[:, cs]], axis=0)
        m = {"xc0": A(xc0), "cst": cst}
        for i, nm in enumerate(["nw0", "nw1", "nwf"]):
            m[nm] = A(np.broadcast_to(norms[i][cs][None, :], (128, 512)))
        m["win0"] = A(np.concatenate([w_in_0[:, j * D + g * 512: j * D + (g + 1) * 512] for j in range(4)]
                                     + [w_in_0[:, 4 * D + 4 * g: 4 * D + 4 * g + 4]], axis=1))
        m["win1"] = A(np.concatenate([w_in_1[:, j * D + g * 512: j * D + (g + 1) * 512] for j in range(4)], axis=1))
        m["wout0"] = A(w_out_0[:, cs])
        m["wout1"] = A(w_out_1[:, cs])
        m["bfb"] = A(np.broadcast_to(b_f_0[4 * g:4 * g + 4][None, :], (128, 4)))
        for nm, cch in zip(["cfk", "cfv", "csk", "csv"], caches):
            m[nm] = A(cch[4 * b:4 * b + 4, :, 4 * g:4 * g + 4, :].reshape(4, PAST, 512))
        m["cfl"] = A(cache_fox_logf[4 * b:4 * b + 4, :, 4 * g:4 * g + 4])
        in_maps.append(m)
    res = run_bass_kernel_spmd(nc, in_maps, core_ids=list(range(8)))
    R = res.results
    y_prompt = np.zeros((2, T, D), f)
    y_sample = np.zeros((8, 16, D), f)
    pk = [np.zeros((2, T, 16, 128), f) for _ in range(4)]
    sk = [np.zeros((8, 16, 16, 128), f) for _ in range(4)]
    plf = np.zeros((2, T, 16), f)
    slf = np.zeros((8, 16, 16), f)
    for core in range(8):
        b, g = core // 4, core % 4
        cs = slice(g * 512, (g + 1) * 512)
        r = R[core]
        y_prompt[b][:, cs] = r["yout"][:T]
        y_sample[4 * b:4 * b + 4][:, :, cs] = r["yout"][T:].reshape(4, 16, 512)
        for i, nm in enumerate(["kf", "vf", "ks", "vs"]):
            pk[i][b][:, 4 * g:4 * g + 4, :] = r[nm][:T].reshape(T, 4, 128)
            sk[i][4 * b:4 * b + 4][:, :, 4 * g:4 * g + 4, :] = r[nm][T:].reshape(4, 16, 4, 128)
        plf[b][:, 4 * g:4 * g + 4] = r["lf"][:T]
        slf[4 * b:4 * b + 4][:, :, 4 * g:4 * g + 4] = r["lf"][T:].reshape(4, 16, 4)
    return (y_prompt, y_sample, pk[0], pk[1], plf, pk[2], pk[3], sk[0], sk[1], slf, sk[2], sk[3])
```

```python
import numpy as np
import concourse.bass as bass
import concourse.mybir as mybir
from concourse.bass_utils import run_bass_kernel_spmd

F32 = mybir.dt.float32
BF16 = mybir.dt.bfloat16
AF = mybir.ActivationFunctionType
ALU = mybir.AluOpType

D = 2048
T = 4096
TS = 64
TT = T + TS
NB = 33
PAST = 1024
SCALE = 128 ** -0.5
EPS = 1e-6
GROUPS = [[0, 1, 2, 3], [4, 5, 6, 7]]
ENG = ("sp", "act", "dve", "pool", "pe")
FOXW = 1
SBW = 4

C_U, C_E127, C_UB16, C_LS, C_ONES, C_MLT, C_MLTC, C_MS, C_MSC, C_ID, C_MLE, C_M64 = [128 * i for i in range(12)]
NCST = 128 * 12


class Sched:
    def __init__(self):
        self.prog = {e: [] for e in ENG}
        self.cnt = {}
        self.waited = {}
        self.lastw = {}
        self.readers = {}

    def _need(self, eng, events):
        for sn, val in events:
            if sn.startswith("d_"):
                val = self.cnt[sn]
            if sn == "pe" and eng == "pe":
                continue
            key = (eng, sn)
            if self.waited.get(key, 0) >= val:
                continue
            self.waited[key] = val
            self.prog[eng].append(("wait", sn, val))

    def op(self, eng, fn, reads=(), writes=(), sig=True, chan=None, cc=None):
        ps_reads = [r for r in reads if r.startswith("ps") and r not in writes]
        if ps_reads:
            writes = list(writes) + ps_reads
        ev = []
        for r in reads:
            if r in self.lastw:
                ev.append(self.lastw[r])
        for w in writes:
            if w in self.lastw:
                ev.append(self.lastw[w])
            ev.extend(self.readers.get(w, {}).items())
        self._need(eng, ev)
        if cc is not None:
            sn = "c_" + cc
            self.cnt[sn] = 1
            myev = (sn, 1)
            self.prog[eng].append(("op", fn, sn, None))
        elif chan is not None:
            sn = "d_" + chan
            self.cnt[sn] = self.cnt.get(sn, 0) + 16
            myev = (sn, self.cnt[sn])
            self.prog[eng].append(("op", fn, sn, 16))
        elif sig:
            self.cnt[eng] = self.cnt.get(eng, 0) + 1
            myev = (eng, self.cnt[eng])
            self.prog[eng].append(("op", fn, eng, 1))
        else:
            myev = (eng, self.cnt.get(eng, 0) + 1)
            self.prog[eng].append(("op", fn, None, 0))
        for r in reads:
            d = self.readers.setdefault(r, {})
            d[myev[0]] = max(d.get(myev[0], 0), myev[1])
        for w in writes:
            self.lastw[w] = myev
            self.readers[w] = {}


def build_program():
    nc = bass.Bass("TRN2", target_bir_lowering=False)
    S = Sched()

    def din(name, shape, dt=F32):
        return nc.dram_tensor(name, shape, dt, kind="ExternalInput").ap()

    def dout(name, shape, dt=F32):
        return nc.dram_tensor(name, shape, dt, kind="ExternalOutput").ap()

    xc0 = din("xc0", [TT, 512])
    nwd = [din("nw0", [128, 512]), din("nw1", [128, 512]), din("nwf", [128, 512])]
    wind = [din("win0", [D, 2052]), din("win1", [D, 2048])]
    woutd = [din("wout0", [D, 512]), din("wout1", [D, 512])]
    bfd = din("bfb", [128, 4])
    cstd = din("cst", [128, NCST])
    cfk = din("cfk", [4, PAST, 512])
    cfv = din("cfv", [4, PAST, 512])
    csk = din("csk", [4, PAST, 512])
    csv = din("csv", [4, PAST, 512])
    cfl = din("cfl", [4, PAST, 4])

    yout = dout("yout", [TT, 512])
    kvout = [[dout("kf", [TT, 512]), dout("vf", [TT, 512])], [dout("ks", [TT, 512]), dout("vs", [TT, 512])]]
    lfout = dout("lf", [TT, 4])

    PW = [1024, 1024, 1024, 1024, TS]
    xT_loc = [nc.dram_tensor("xT_loc%d" % p, [512, PW[p]], BF16) for p in range(5)]
    xT_all = [nc.dram_tensor("xT_all%d" % p, [D, PW[p]], BF16) for p in range(5)]
    yT_loc = [nc.dram_tensor("yT_loc%d" % p, [512, PW[p]], BF16) for p in range(5)]
    yT_all = [nc.dram_tensor("yT_all%d" % p, [D, PW[p]], BF16) for p in range(5)]

    def piece_of(tb):
        return (tb // 8, (tb % 8) * 128) if tb < 32 else (4, 0)
    ssq_loc = nc.dram_tensor("ssq_loc", [128, NB], F32)
    ssq_all = nc.dram_tensor("ssq_all", [512, NB], F32)
    xres = [None, nc.dram_tensor("x1s", [TT, 512], F32), nc.dram_tensor("x2s", [TT, 512], F32)]

    from contextlib import ExitStack
    es = ExitStack()

    def sb(name, shape, dt):
        return es.enter_context(nc.sbuf_tensor(name, shape, dt))

    def ps(name, shape, dt):
        return es.enter_context(nc.psum_tensor(name, shape, dt))

    with es:
        cstf = sb("cstf", [128, C_ID + 128], F32)
        cstb = sb("cstb", [128, 3 * 128], BF16)
        zeros = sb("zeros", [128, 512], BF16)
        nwt = sb("nwt", [128, 512], F32)
        nw = [nwt, nwt, nwt]
        bfb = sb("bfbs", [128, 4], F32)
        w_in = sb("w_in", [128, 16, 2052], BF16)
        regB = sb("regB", [128, 4 * T + 32 * 4 * 130], BF16)
        actT = [sb("actT0", [128, 16, 256], BF16), sb("actT1", [128, 16, 256], BF16)]
        ssq_part = sb("ssq_part", [128, NB], F32)
        ssq4 = sb("ssq4", [128, 4, NB], F32)
        ssum = sb("ssum", [128, NB], F32)
        rstd = sb("rstd", [128, NB], F32)
        nrstd = sb("nrstd", [128, NB], F32)
        hrstd = sb("hrstd", [128, NB], F32)
        cpall = sb("cpall", [128, NB, 8], F32)
        lfo = sb("lfo", [128, NB, 4], F32)
        lps = sb("lps", [128, 4], F32)
        xl = sb("xl", [128, 4], F32)
        el = sb("el", [128, 4], F32)
        q_tok = sb("q_tok", [128, 512], BF16)
        k_tok = sb("k_tok", [128, 512], BF16)
        kf32b_big = sb("kf32b", [128, 516], F32)
        kf32 = [sb("kf32a", [128, 512], F32)[:, :], kf32b_big[:, 0:512]]
        vf32 = [sb("vf32a", [128, 512], F32), sb("vf32b", [128, 512], F32)]
        gsil = [sb("gsil0", [128, 512], F32), sb("gsil1", [128, 512], F32)]
        gtmp = sb("gtmp", [128, 512], F32)
        qT = sb("qT", [128, 4, 256], BF16)
        ybuf = [sb("y0", [128, 512], BF16), sb("y1", [128, 512], BF16)]
        yTs = [sb("yTs0", [128, 4, 128], BF16), sb("yTs1", [128, 4, 128], BF16)]
        Pball = sb("Pball", [128, 6, 256], BF16)
        Pb = [Pball[:, i, :] for i in range(6)]
        biasT = sb("biasT", [128, 32, 4], F32)
        rden = sb("rden", [128, 4], F32)
        thb = [sb("th%d" % i, [128, 512], F32) for i in range(3)] + [gtmp]
        Rext = [sb("Rx%d" % i, [128, 516], F32) for i in range(3)] + [kf32b_big]
        ab = [sb("a%d" % i, [128, 512], BF16) for i in range(3)] + [k_tok]
        aTb = [sb("aT0", [128, 4, 128], BF16), sb("aT1", [128, 4, 128], BF16),
               Pball[:, 0:2, :].rearrange("p a (b t) -> p (a b) t", t=128),
               q_tok[:, :].rearrange("p (j t) -> p j t", t=128)]
        xcb = kf32
        xnb = vf32
        xtb = q_tok
        sqj = gtmp
        xTs = yTs
        Pw = [sb("Pw%d" % i, [128, 64], BF16) for i in range(4)]
        Pn = sb("Pn", [64, 64], BF16)
        clf = sb("clf", [128, 8, 16], F32)
        sufb = sb("sufb", [128, 8, 16], F32)
        cpn = sb("cpn", [64, 4], F32)
        aTw = [sb("aTw%d" % i, [128, 8, 64], BF16) for i in range(4)]
        aTn = sb("aTn", [64, 64], BF16)
        Vn_t = sb("Vn_t", [128, 520], BF16)
        carry0 = sb("carry0", [64, 4], F32)
        cbias = sb("cbias", [128, 2], F32)

        psA = ps("psA", [128, 512], F32)
        psB = ps("psB", [128, 512], F32)
        psC = ps("psC", [128, 512], F32)
        psD = ps("psD", [128, 512], F32)
        psE = ps("psE", [128, 512], F32)
        psF = ps("psF", [128, 512], F32)
        psG = ps("psG", [128, 512], F32)
        pst = ps("pst", [128, 1024], BF16)

        KT_OFF = 0
        V_OFF = 4 * T

        def kT(h, lo, hi):
            return regB[:, KT_OFF + h * T + lo: KT_OFF + h * T + hi]

        def Vaug(blk, h, n):
            o = V_OFF + (blk * 4 + h) * 130
            return regB[:, o:o + n]

        def Vaug_blk(blk):
            o = V_OFF + blk * 4 * 130
            return regB[:, o:o + 520].rearrange("p (h e) -> p h e", e=130)

        def w_out():
            o = 4096 + 8 * 520 + 4096
            return regB[:, o:o + 16 * 512].rearrange("p (c n) -> p c n", n=512)

        PST_ALL = ["pstbank"]
        pst_full = pst
        psE_bf = psE[:, :].bitcast(BF16)
        pst_f32 = pst[:, :].bitcast(F32)
        psG_bf = psG[:, :].bitcast(BF16)
        REGB_ALL = ["kT%d_%d" % (h, b) for h in range(4) for b in range(32)] + ["V%d" % b for b in range(32)]

        ident = cstb[:, 0:128]
        maskLE = cstb[:, 128:256]
        mask64 = cstb[0:64, 256:320]

        def cf(c0, n=128, rows=128):
            return cstf[0:rows, c0:c0 + n]

        S.op("sp", lambda e: e.dma_start(out=cstf[:], in_=cstd[:, 0:C_ID + 128]), writes=["cstf"], chan="cst")
        S.op("pool", lambda e: e.dma_start(out=cstb[:], in_=cstd[:, C_ID:C_ID + 384]), writes=["cstb"], chan="cstb")
        def load_nw(i):
            S.op("sp", lambda e: e.dma_start(out=nwt[:], in_=nwd[i][:, :]), writes=["nw"], chan="cst")
        load_nw(0)
        S.op("sp", lambda e: e.dma_start(out=bfb[:], in_=bfd[:, :]), writes=["bfb"], chan="cst")
        S.op("dve", lambda e: e.memset(zeros[:], 0.0), writes=["zeros"])
        S.op("dve", lambda e: e.memset(cbias[:, 0:1], EPS), writes=["cbias"])
        S.op("dve", lambda e: e.memset(cbias[:, 1:2], 1.0), writes=["cbias"])
        S.op("dve", lambda e: e.memset(ssq_part[:], 1.0), writes=["ssq_part"])
        S.op("dve", lambda e: e.memset(cpall[:], 0.0), writes=["cpall"])
        S.op("dve", lambda e: e.memset(lfo[:], 0.0), writes=["lfo"])
        for i in range(2):
            S.op("pool", lambda e, i=i: e.memset(thb[i][:], 0.0), writes=["th%d" % i])
        for i in range(4):
            S.op("pool", lambda e, i=i: e.memset(aTw[i][:], 0.0), writes=["aTw%d" % i])

        if False:
            for nm_, ap_ in [("win0", wind[0][0:128, 0:512]), ("win1", wind[1][0:128, 0:512]), ("wout0", woutd[0][0:128, :]),
                             ("wout1", woutd[1][0:128, :]), ("cfk", cfk[0][0:128, :]), ("cfv", cfv[0][0:128, :]),
                             ("csk", csk[0][0:128, :]), ("csv", csv[0][0:128, :])]:
                S.op("sp", lambda e, ap_=ap_: e.dma_start(out=gtmp[:], in_=ap_), writes=["gtmp"], chan="dbg")
            S.op("sp", lambda e: e.dma_start(out=gtmp[:, 0:4], in_=cfl[0][0:128, :]), writes=["gtmp"], chan="dbg")

        def load_w_in(l):
            ncol = 2052 if l == 0 else 2048
            src = wind[l].rearrange("(c p) n -> p c n", p=128)
            for c in range(16):
                yield
                si = c % 3
                o = 20544 + si * 4104
                stg = regB[:, o:o + 4104].bitcast(F32)
                S.op("sp", lambda e, c=c, stg=stg: e.dma_start(out=stg[:, 0:ncol], in_=src[:, c, :]),
                     writes=["wst%d" % si] + (REGB_ALL + ["sVc%d" % q for q in range(4)] + ["skT%d" % q for q in range(4)]
                                              if c < 3 else []), chan="wst%d" % si)
                if c % 2 == 0:
                    S.op("act", lambda e, c=c, stg=stg: e.activation(out=w_in[:, c, 0:ncol], in_=stg[:, 0:ncol], func=AF.Identity),
                         reads=["wst%d" % si] + (REGB_ALL if c >= 13 else []), writes=["w_in"])
                else:
                    S.op("dve", lambda e, c=c, stg=stg: e.tensor_copy(out=w_in[:, c, 0:ncol], in_=stg[:, 0:ncol]),
                         reads=["wst%d" % si] + (REGB_ALL if c >= 13 else []), writes=["w_in"])

        def load_w_out(l):
            src = woutd[l].rearrange("(c p) n -> p c n", p=128)
            wv = w_out()
            for c in range(0, 16, 4):
                S.op("pool", lambda e, c=c: e.dma_start(out=wv[:, c:c + 4, :], in_=src[:, c:c + 4, :]),
                     writes=["w_out"] + REGB_ALL + ["sVc%d" % q for q in range(4)] + ["skT%d" % q for q in range(4)], chan="wout")

        def blk_rows(tb):
            return 128 if tb < 32 else TS

        cnt = {"xts": 0, "tr": 0}

        def norm_prep(xblk, xres_name, tb, rows, nwi, want_xT, want_gather=True):
            S.op("act", lambda e: e.activation(out=sqj[0:rows, :], in_=xblk[0:rows, :], func=AF.Square,
                                               accum_out=ssq_part[0:rows, tb:tb + 1]),
                 reads=[xres_name], writes=["gtmp", "ssq_part"])
            if not want_xT:
                return
            S.op("dve", lambda e: e.tensor_tensor(out=xtb[0:rows, :], in0=xblk[0:rows, :], in1=nw[nwi][0:rows, :],
                                                  op=ALU.mult),
                 reads=[xres_name, "nw"], writes=["q_tok"])
            for c in range(4):
                S.op("pe", lambda e, c=c: e.transpose(out=pst[:, c * 128:c * 128 + rows],
                                                      in_=xtb[0:rows, c * 128:(c + 1) * 128],
                                                      identity=ident[0:rows, 0:rows]),
                     reads=["q_tok", "cstb"], writes=PST_ALL, sig=(c == 3))
            i = cnt["tr"] % 2
            cnt["tr"] += 1
            pv = pst[:, 0:512].rearrange("p (c t) -> p c t", t=128)
            S.op("act", lambda e: e.activation(out=xTs[i][:, :, 0:rows], in_=pv[:, :, 0:rows], func=AF.Identity),
                 reads=PST_ALL, writes=["yTs%d" % i])
            pc, c0 = piece_of(tb)
            dst = xT_loc[pc].ap().rearrange("(c p) t -> p c t", p=128)
            S.op("sp", lambda e: e.dma_start(out=dst[:, :, c0:c0 + rows], in_=xTs[i][:, :, 0:rows]),
                 reads=["yTs%d" % i], writes=["xT_loc%d" % pc], chan="yTs%d" % i)
            if want_gather and (tb % 8 == 7 or tb == 32):
                gather(xT_loc[pc], xT_all[pc], "xT", pc)

        def rstd_from_ssq(tag):
            S.op("sp", lambda e: e.dma_start(out=ssq_loc.ap(), in_=ssq_part[:]), reads=["ssq_part"],
                 writes=["ssq_loc"], chan="ssq")
            S.op("pool", lambda e: e.collective_compute("AllGather", ALU.bypass, replica_groups=GROUPS,
                                                        ins=[ssq_loc.ap().opt()], outs=[ssq_all.ap().opt()]),
                 reads=["ssq_loc"], writes=["ssq_all"], cc="ssq" + tag)
            S.op("sp", lambda e: e.dma_start(out=ssq4[:], in_=ssq_all.ap().rearrange("(r p) b -> p r b", p=128)),
                 reads=["ssq_all"], writes=["ssq4"], chan="ssq")
            S.op("dve", lambda e: e.tensor_tensor(out=ssum[:], in0=ssq4[:, 0, :], in1=ssq4[:, 1, :], op=ALU.add),
                 reads=["ssq4"], writes=["ssum"])
            S.op("dve", lambda e: e.tensor_tensor(out=ssum[:], in0=ssum[:], in1=ssq4[:, 2, :], op=ALU.add),
                 reads=["ssq4", "ssum"], writes=["ssum"])
            S.op("dve", lambda e: e.tensor_tensor(out=ssum[:], in0=ssum[:], in1=ssq4[:, 3, :], op=ALU.add),
                 reads=["ssq4", "ssum"], writes=["ssum"])
            S.op("act", lambda e: e.activation(out=ssum[:], in_=ssum[:], func=AF.Ln, scale=1.0 / D, bias=cbias[:, 0:1]),
                 reads=["ssum", "cbias"], writes=["ssum"])
            S.op("act", lambda e: e.activation(out=rstd[:], in_=ssum[:], func=AF.Exp, scale=-0.5),
                 reads=["ssum"], writes=["rstd"])
            S.op("dve", lambda e: e.tensor_scalar(out=nrstd[:], in0=rstd[:], scalar1=-1.0, scalar2=None, op0=ALU.mult),
                 reads=["rstd"], writes=["nrstd"])
            S.op("dve", lambda e: e.tensor_scalar(out=hrstd[:], in0=rstd[:], scalar1=0.5, scalar2=None, op0=ALU.mult),
                 reads=["rstd"], writes=["hrstd"])
            S.op("dve", lambda e: e.memset(ssq_part[:], 1.0), reads=[], writes=["ssq_part"])

        gcount = {"n": 0}

        def gather(loc, allt, rname, pc):
            gcount["n"] += 1
            S.op("pool", lambda e: e.collective_compute("AllGather", ALU.bypass, replica_groups=GROUPS,
                                                        ins=[loc.ap().opt()], outs=[allt.ap().opt()]),
                 reads=["%s_loc%d" % (rname, pc)], writes=["%s_all%d" % (rname, pc)], cc="g%d" % gcount["n"])

        wq = load_w_in(0)

        def n0_load(tb):
            rows = blk_rows(tb)
            i = tb % 2
            S.op("sp", lambda e: e.dma_start(out=xcb[i][0:rows, :], in_=xc0[tb * 128:tb * 128 + rows, :]),
                 writes=["kf32%d" % i], chan="kf32%d" % i)

        for tb in range(NB):
            rows = blk_rows(tb)
            i = tb % 2
            if tb % 2 == 0:
                next(wq, None)
            if tb == 0:
                n0_load(0)
            if tb + 1 < NB:
                n0_load(tb + 1)
            norm_prep(xcb[i], "kf32%d" % i, tb, rows, 0, True)
        for _ in wq:
            pass
        rstd_from_ssq("0")

        def load_act(slot, src_all, rname, c0, ncols):
            pc, lc = (c0 // 1024, c0 % 1024) if c0 < T else (4, 0)
            src = src_all[pc].ap().rearrange("(c p) t -> p c t", p=128)
            S.op("sp", lambda e: e.dma_start(out=actT[slot][:, :, 0:ncols], in_=src[:, :, lc:lc + ncols]),
                 reads=["%s_all%d" % (rname, pc)], writes=["actT%d" % slot], chan="actT%d" % slot)

        def transposes_to(src_tok, src_name, rows, dst_fn, dst_names, evac_eng):
            for h in range(4):
                S.op("pe", lambda e, h=h: e.transpose(out=pst[:, h * 128:h * 128 + rows],
                                                      in_=src_tok[0:rows, h * 128:(h + 1) * 128],
                                                      identity=ident[0:rows, 0:rows]),
                     reads=[src_name, "cstb"], writes=PST_ALL, sig=(h == 3))
            pv4 = pst[:, 0:512].rearrange("p (h t) -> p h t", t=128)
            if evac_eng == "act":
                S.op("act", lambda e: e.activation(out=dst_fn(None), in_=pv4[:, :, 0:rows], func=AF.Identity),
                     reads=PST_ALL, writes=dst_names)
            else:
                S.op("dve", lambda e: e.tensor_copy(out=dst_fn(None), in_=pv4[:, :, 0:rows]), reads=PST_ALL, writes=dst_names)

        def project_block(l, slot, sub, tb, rows, kT_dst, kT_names, v_dst_fn, v_names, qcol0):
            banks = [psA, psB, psC, psD]
            bn = ["psA", "psB", "psC", "psD"]
            for c in range(16):
                lhsT = actT[slot][:, c, sub * 128: sub * 128 + rows]
                for j in range(4):
                    S.op("pe", lambda e, c=c, j=j, lhsT=lhsT: e.matmul(banks[j][0:rows, :], lhsT=lhsT,
                                                                        rhs=w_in[:, c, j * 512:(j + 1) * 512],
                                                                        start=(c == 0), stop=(c == 15)),
                         reads=["actT%d" % slot, "w_in"], writes=[bn[j]], sig=(c == 15))
                if l == 0:
                    S.op("pe", lambda e, c=c, lhsT=lhsT: e.matmul(psE[0:rows, 0:4], lhsT=lhsT,
                                                                   rhs=w_in[:, c, 2048:2052],
                                                                   start=(c == 0), stop=(c == 15)),
                         reads=["actT%d" % slot, "w_in"], writes=["psE"], sig=(c == 15))
            rs = rstd[0:rows, tb:tb + 1]
            nrs = nrstd[0:rows, tb:tb + 1]
            i = tb % 2
            S.op("dve", lambda e: e.tensor_scalar(out=q_tok[0:rows, :], in0=psA[0:rows, :], scalar1=rs, scalar2=None,
                                                  op0=ALU.mult),
                 reads=["psA", "rstd"], writes=["q_tok"])
            S.op("act", lambda e: e.activation(out=kf32[i][0:rows, :], in_=psB[0:rows, :], func=AF.Identity, scale=rs),
                 reads=["psB", "rstd"], writes=["kf32%d" % i])
            S.op("dve", lambda e: e.tensor_scalar(vf32[i][0:rows, :], psC[0:rows, :], rs, None, ALU.mult),
                 reads=["psC", "rstd"], writes=["vf32%d" % i])
            S.op("sp", lambda e: e.dma_start(out=kvout[l][0][tb * 128:tb * 128 + rows, :], in_=kf32[i][0:rows, :]),
                 reads=["kf32%d" % i], writes=[], chan="kf32%d" % i)
            S.op("sp", lambda e: e.dma_start(out=kvout[l][1][tb * 128:tb * 128 + rows, :], in_=vf32[i][0:rows, :]),
                 reads=["vf32%d" % i], writes=[], chan="vf32%d" % i)
            S.op("act", lambda e: e.activation(out=k_tok[0:rows, :], in_=psB[0:rows, :], func=AF.Identity, scale=rs),
                 reads=["psB", "rstd"], writes=["k_tok"])
            vd = v_dst_fn()
            S.op("dve", lambda e: e.tensor_copy(out=vd[0:rows, :, 0:128],
                                                in_=vf32[i][0:rows, :].rearrange("p (h d) -> p h d", d=128)),
                 reads=["vf32%d" % i], writes=v_names)
            S.op("pool", lambda e: e.memset(vd[0:rows, :, 128:129], 1.0), reads=[], writes=v_names)
            if l == 0:
                S.op("act", lambda e: e.activation(out=gtmp[0:rows, :], in_=psD[0:rows, :], func=AF.Exp, scale=nrs),
                     reads=["psD", "nrstd"], writes=["gtmp"])
                S.op("act", lambda e: e.activation(out=gtmp[0:rows, :], in_=gtmp[0:rows, :], func=AF.Ln, bias=cbias[0:rows, 1:2]),
                     reads=["gtmp", "cbias"], writes=["gtmp"])
                S.op("act", lambda e: e.activation(out=gtmp[0:rows, :], in_=gtmp[0:rows, :], func=AF.Exp, scale=-1.0),
                     reads=["gtmp"], writes=["gtmp"])
            else:
                S.op("act", lambda e: e.activation(out=gtmp[0:rows, :], in_=psD[0:rows, :], func=AF.Sigmoid, scale=rs),
                     reads=["psD", "rstd"], writes=["gtmp"])
            S.op("dve", lambda e: e.scalar_tensor_tensor(out=gsil[sub][0:rows, :], in0=psD[0:rows, :], scalar=rs,
                                                         in1=gtmp[0:rows, :], op0=ALU.mult, op1=ALU.mult),
                 reads=["psD", "rstd", "gtmp"], writes=["gsil%d" % sub])
            if l == 0:
                S.op("dve", lambda e: e.scalar_tensor_tensor(out=xl[0:rows, :], in0=psE[0:rows, 0:4], scalar=rs,
                                                             in1=bfb[0:rows, :], op0=ALU.mult, op1=ALU.add),
                     reads=["psE", "rstd", "bfb"], writes=["xl"])
                S.op("act", lambda e: e.activation(out=el[0:rows, :], in_=xl[0:rows, :], func=AF.Exp, scale=-1.0),
                     reads=["xl"], writes=["el"])
                S.op("act", lambda e: e.activation(out=lps[0:rows, :], in_=el[0:rows, :], func=AF.Ln, bias=cbias[0:rows, 1:2]),
                     reads=["el", "cbias"], writes=["lps"])
                S.op("pool", lambda e: e.tensor_scalar(out=lfo[0:rows, tb, :], in0=lps[0:rows, :], scalar1=-1.0,
                                                       scalar2=0.0, op0=ALU.mult, op1=ALU.add),
                     reads=["lps"], writes=["lfo"])
            transposes_to(q_tok, "q_tok", rows, lambda h: qT[:, :, qcol0:qcol0 + rows], ["qT"], "dve")
            transposes_to(k_tok, "k_tok", rows, kT_dst, kT_names, "act")

        def cumsum_block(tb):
            first = (tb == 0)
            S.op("pe", lambda e: e.matmul(psE[:, 8:12], lhsT=cf(C_U), rhs=lps[:, 0:4], start=True, stop=first),
                 reads=["cstf", "lps"], writes=["psE"], sig=first)
            if not first:
                S.op("pe", lambda e: e.matmul(psE[:, 8:12], lhsT=cf(C_E127), rhs=cpall[:, tb - 1, 0:4], start=False,
                                              stop=True),
                     reads=["cstf", "cpall"], writes=["psE"], sig=False)
                S.op("pe", lambda e: e.matmul(psE[:, 12:16], lhsT=cf(C_E127), rhs=cpall[:, tb - 1, 0:4], start=True,
                                              stop=True),
                     reads=["cstf", "cpall"], writes=["psE"])
                S.op("dve", lambda e: e.tensor_copy(out=cpall[:, tb, 0:8], in_=psE[:, 8:16]), reads=["psE"],
                     writes=["cpall"])
            else:
                S.op("dve", lambda e: e.tensor_copy(out=cpall[:, tb, 0:4], in_=psE[:, 8:12]), reads=["psE"],
                     writes=["cpall"])

        sc_banks = [(psA, "psA"), (psB, "psB")]
        o_banks = [((psC, "psC"), (psD, "psD")), ((psF, "psF"), (psG, "psG"))]

        def y_out(tbs, rows_l):
            for sub, tb in enumerate(tbs):
                rows = rows_l[sub]
                for c in range(4):
                    S.op("pe", lambda e, c=c, sub=sub, rows=rows: e.transpose(out=pst[:, c * 128:c * 128 + rows],
                                                                               in_=ybuf[sub][0:rows, c * 128:(c + 1) * 128],
                                                                               identity=ident[0:rows, 0:rows]),
                         reads=["y%d_%d" % (sub, hh) for hh in range(4)] + ["cstb"], writes=PST_ALL, sig=(c == 3))
                i = cnt["tr"] % 2
                cnt["tr"] += 1
                pv = pst[:, 0:512].rearrange("p (c t) -> p c t", t=128)
                S.op("act", lambda e, i=i, rows=rows, pv=pv: e.activation(out=yTs[i][:, :, 0:rows], in_=pv[:, :, 0:rows],
                                                                           func=AF.Identity),
                     reads=PST_ALL, writes=["yTs%d" % i])
                pc, c0 = piece_of(tb)
                dst = yT_loc[pc].ap().rearrange("(c p) t -> p c t", p=128)
                S.op("sp", lambda e, i=i, rows=rows, c0=c0, dst=dst: e.dma_start(out=dst[:, :, c0:c0 + rows],
                                                                                  in_=yTs[i][:, :, 0:rows]),
                     reads=["yTs%d" % i], writes=["yT_loc%d" % pc], chan="yTs%d" % i)
                if tb % 8 == 7 or tb == 32:
                    gather(yT_loc[pc], yT_all[pc], "yT", pc)

        pcount = {"p": 0, "sc": 0, "ch": 0}

        def run_streams(makers, width):
            pending = list(makers)
            active = []
            free = list(range(width))
            eng_free = {}
            now = 0.0
            while pending or active:
                while pending and free:
                    sl = free.pop(0)
                    g = pending.pop(0)(sl)
                    try:
                        nxt = next(g)
                    except StopIteration:
                        free.append(sl)
                        continue
                    active.append({"g": g, "sl": sl, "ready": now, "nxt": nxt})
                if not active:
                    continue
                best = min(active, key=lambda a: max(eng_free.get(a["nxt"][0], 0.0), a["ready"]))
                eng, dur = best["nxt"]
                start = max(eng_free.get(eng, 0.0), best["ready"])
                end = start + dur
                eng_free[eng] = end
                best["ready"] = end + 0.25
                try:
                    best["nxt"] = next(best["g"])
                except StopIteration:
                    active.remove(best)
                    free.append(best["sl"])
                    now = end

        def fox_stream(slot, Q, h):
            nkb = 2 * Q + 2
            ob = o_banks[slot]
            fsc = [(psA, "psA"), (psB, "psB"), (psE, "psE"), (pst_f32, "pstbank"), (psF, "psF"), (psG, "psG")]
            for kb0 in range(0, nkb, 6):
                kbs = list(range(kb0, min(kb0 + 6, nkb)))
                yield ("pe", 0.5)
                for i, kb in enumerate(kbs):
                    qlo = 128 if kb == nkb - 1 else 0
                    scb, scn = fsc[i]
                    S.op("pe", lambda e, kb=kb, qlo=qlo, scb=scb: e.matmul(
                        scb[:, qlo:256], lhsT=kT(h, kb * 128, (kb + 1) * 128), rhs=qT[:, h, qlo:256], start=True, stop=True),
                        reads=["kT%d_%d" % (h, kb), "qT"], writes=[scn])
                yield ("act", 1.8)
                for i, kb in enumerate(kbs):
                    qlo = 128 if kb == nkb - 1 else 0
                    scb, scn = fsc[i]
                    S.op("act", lambda e, kb=kb, qlo=qlo, i=i, scb=scb: e.activation(
                        out=Pb[i][:, qlo:256], in_=scb[:, qlo:256], func=AF.Exp, scale=SCALE, bias=biasT[:, kb, h:h + 1]),
                        reads=[scn, "biasT"], writes=["P%d" % i])
                    if kb >= nkb - 2:
                        dq = 0 if kb == nkb - 2 else 128
                        S.op("pool", lambda e, i=i, dq=dq: e.tensor_tensor(out=Pb[i][:, dq:dq + 128],
                                                                           in0=Pb[i][:, dq:dq + 128], in1=maskLE,
                                                                           op=ALU.mult),
                             reads=["P%d" % i, "cstb"], writes=["P%d" % i])
                yield ("pe", 1.2)
                for i, kb in enumerate(kbs):
                    for sub in range(2):
                        if kb == nkb - 1 and sub == 0:
                            continue
                        last = (kb == nkb - 2) if sub == 0 else (kb == nkb - 1)
                        S.op("pe", lambda e, kb=kb, sub=sub, i=i, last=last: e.matmul(
                            ob[sub][0][:, 0:129], lhsT=Pb[i][:, sub * 128:(sub + 1) * 128], rhs=Vaug(kb, h, 129),
                            start=(kb == 0), stop=last),
                            reads=["P%d" % i, "V%d" % kb], writes=[ob[sub][1]], sig=(sub == 1 or kb == nkb - 2))
            yield ("dve", 0.5)
            for sub in range(2):
                S.op("dve", lambda e, sub=sub: e.reciprocal(out=rden[:, 2 * slot + sub:2 * slot + sub + 1],
                                                            in_=ob[sub][0][:, 128:129]),
                     reads=[ob[sub][1]], writes=["rden%d" % slot])
                S.op("dve", lambda e, sub=sub: e.scalar_tensor_tensor(
                    out=ybuf[sub][:, h * 128:(h + 1) * 128], in0=ob[sub][0][:, 0:128],
                    scalar=rden[:, 2 * slot + sub:2 * slot + sub + 1],
                    in1=gsil[sub][:, h * 128:(h + 1) * 128], op0=ALU.mult, op1=ALU.mult),
                    reads=[ob[sub][1], "rden%d" % slot, "gsil%d" % sub], writes=["y%d_%d" % (sub, h)])

        def fox_tile(Q):
            nkb = 2 * Q + 2
            for h in range(4):
                S.op("dve", lambda e, h=h: e.tensor_scalar(out=biasT[:, 0:nkb, h], in0=cpall[:, 0:nkb, h],
                                                           scalar1=cpall[:, 2 * Q + 1, 4 + h:5 + h], scalar2=None,
                                                           op0=ALU.subtract),
                     reads=["cpall"], writes=["biasT"])
            run_streams([(lambda sl, h=h: fox_stream(sl, Q, h)) for h in range(4)], FOXW)

        def sb_stream(slot, qrows, qT_ap, qT_names, chunks, o_ap, o_name, pv, total_pv, carry_in=None, aT_hook=None,
                      fin=None, out_state=None):
            prev = carry_in
            th, Rx, a_ = thb[slot], Rext[slot], ab[slot]
            tn = ["th0", "th1", "th2", "gtmp"][slot]
            rn = ["Rx0", "Rx1", "Rx2", "kf321"][slot]
            an = ["a0", "a1", "a2", "k_tok"][slot]
            aTnames = [["aT0"], ["aT1"], ["P0", "P1"], ["q_tok"]][slot]
            scb, scn = [(psA, "psA"), (psB, "psB"), (psE, "psE"), (psG, "psG")][slot]
            pst, pn, pso = scb[:, :].bitcast(BF16), scn, 0
            for ci, ch in enumerate(chunks):
                W = ch["W"]
                yield ("pe", 0.45)
                S.op("pe", lambda e, ch=ch, W=W, scb=scb: e.matmul(scb[0:qrows, 0:W], lhsT=qT_ap, rhs=ch["kT"], start=True, stop=True),
                     reads=qT_names + ch["names"], writes=[scn])
                yield ("act", 0.65)
                S.op("act", lambda e, W=W, scb=scb: e.activation(out=th[0:qrows, 0:W], in_=scb[0:qrows, 0:W], func=AF.Sigmoid,
                                                        scale=-SCALE),
                     reads=[scn], writes=[tn])
                yield ("dve", 1.5)
                if ch.get("mask") is not None:
                    M, Mc, dw = ch["mask"]
                    S.op("dve", lambda e, W=W, dw=dw, M=M: e.tensor_tensor(out=th[0:qrows, W - dw:W], in0=th[0:qrows, W - dw:W],
                                                                           in1=M, op=ALU.mult),
                         reads=[tn, "cstf"], writes=[tn])
                    S.op("dve", lambda e, W=W, dw=dw, Mc=Mc: e.tensor_tensor(out=th[0:qrows, W - dw:W], in0=th[0:qrows, W - dw:W],
                                                                             in1=Mc, op=ALU.add),
                         reads=[tn, "cstf"], writes=[tn])
                if prev is None:
                    S.op("dve", lambda e, W=W: e.memset(Rx[0:qrows, W:W + 1], 1.0), reads=[], writes=[rn])
                else:
                    pR, pname = prev
                    S.op("dve", lambda e, W=W, pR=pR: e.tensor_copy(out=Rx[0:qrows, W:W + 1], in_=pR),
                         reads=[pname, an], writes=[rn])
                S.op("dve", lambda e, W=W: e.tensor_tensor_scan(
                    out=Rx[0:qrows, 0:W][:, ::-1], data0=th[0:qrows, 0:W][:, ::-1], data1=zeros[0:qrows, 0:W],
                    initial=Rx[0:qrows, W:W + 1], op0=ALU.mult, op1=ALU.add),
                    reads=[tn, rn, "zeros"], writes=[rn])
                prev = (Rx[0:qrows, 0:1], rn)
                if True:
                    yield ("pool", 1.3)
                S.op("pool", lambda e, W=W: e.tensor_tensor(out=a_[0:qrows, 0:W], in0=Rx[0:qrows, 1:W + 1],
                                                                                   in1=Rx[0:qrows, 0:W], op=ALU.subtract),
                     reads=[rn], writes=[an])
                yield ("pe", 0.7)
                vbl = ch["vblocks"]
                off = 0
                for j, (rhs, wk, vn) in enumerate(vbl):
                    S.op("pe", lambda e, off=off, wk=wk, j=j: e.transpose(out=pst[0:wk, pso + j * 128:pso + j * 128 + qrows],
                                                                          in_=a_[0:qrows, off:off + wk],
                                                                          identity=ident[0:qrows, 0:qrows]),
                         reads=[an, "cstb"], writes=[pn], sig=(j == len(vbl) - 1))
                    off += wk
                yield ("act", 0.65)
                if aT_hook is None:
                    aT = aTb[slot]
                    nb_ = len(vbl)
                    pvw = pst[:, pso:pso + nb_ * 128].rearrange("p (j t) -> p j t", t=128)
                    S.op("act", lambda e, aT=aT, pvw=pvw, nb_=nb_: e.activation(out=aT[:, 0:nb_, 0:qrows], in_=pvw[:, :, 0:qrows],
                                                                                func=AF.Identity),
                         reads=[pn], writes=aTnames)
                    lhs_list = [(aT[0:wk, j, 0:qrows], aTnames) for j, (_, wk, _) in enumerate(vbl)]
                else:
                    lhs_list = aT_hook(ci, vbl, pso, pn, pst)
                if aT_hook is None:
                    yield ("pe", 0.7)
                for j, (rhs, wk, vn) in enumerate(vbl):
                    lhsT, ln = lhs_list[j]
                    n = pv["n"]
                    S.op("pe", lambda e, lhsT=lhsT, rhs=rhs, n=n: e.matmul(o_ap, lhsT=lhsT, rhs=rhs, start=(n == 0),
                                                                           stop=(n == total_pv - 1)),
                         reads=(ln if isinstance(ln, list) else [ln]) + [vn], writes=[o_name], sig=True)
                    pv["n"] += 1
            if out_state is not None:
                out_state["carry"] = prev
            if fin is not None:
                yield ("dve", 0.3)
                fin()

        sb_obank = {0: [(psC, "psC")], 1: [(psD, "psD")], 2: [(psF, "psF")], 3: [(pst_f32, "pstbank")]}
        sb_ocnt = {0: 0, 1: 0}

        def sb_tile(Q):
            makers = []
            for sub in range(2):
                qb = 2 * Q + sub
                e_ = 128 * (qb + 1)
                for h in range(4):
                    def mk(slot, sub=sub, h=h, e_=e_):
                        chunks = []
                        hi = e_
                        while hi > 0:
                            lo = max(0, hi - 512)
                            vbl = [(Vaug(b, h, 128), 128, "V%d" % b) for b in range(lo // 128, hi // 128)]
                            chunks.append(dict(kT=kT(h, lo, hi), W=hi - lo,
                                               names=["kT%d_%d" % (h, b) for b in range(lo // 128, hi // 128)],
                                               vblocks=vbl,
                                               mask=(cf(C_MLT), cf(C_MLTC), 128) if hi == e_ else None))
                            hi = lo
                        ob, on = sb_obank[slot][0]

                        def fin():
                            S.op("dve", lambda e: e.tensor_tensor(out=ybuf[sub][:, h * 128:(h + 1) * 128], in0=ob[:, 0:128],
                                                                  in1=gsil[sub][:, h * 128:(h + 1) * 128], op=ALU.mult),
                                 reads=[on, "gsil%d" % sub], writes=["y%d_%d" % (sub, h)])
                        return sb_stream(slot, 128, qT[:, h, sub * 128:(sub + 1) * 128], ["qT"], chunks, ob[:, 0:128], on,
                                         {"n": 0}, sum(len(c["vblocks"]) for c in chunks), fin=fin)
                    makers.append(mk)
            run_streams(makers, SBW)

        SEQSZ = 8 * 520 + 4096
        assert 4 * SEQSZ <= 4 * T + 32 * 4 * 130
        kstage_f = actT[1][:, :, :].rearrange("p c t -> p (c t)").bitcast(F32).rearrange("p (j c) -> p j c", c=512)

        def sVc_all(s_):
            b0 = s_ * SEQSZ
            return regB[:, b0:b0 + 8 * 520].rearrange("p (j h e) -> p j h e", h=4, e=130)

        def sVc(s_, j, h, n):
            o = s_ * SEQSZ + (j * 4 + h) * 130
            return regB[:, o:o + n]

        def skT(s_, h, lo, hi):
            o = s_ * SEQSZ + 8 * 520 + h * 1024
            return regB[:, o + lo:o + hi]

        def Pw8(s_):
            v = thb[s_ // 2][:, :].bitcast(BF16)
            return v[:, (s_ % 2) * 512:(s_ % 2 + 1) * 512].rearrange("p (j c) -> p j c", c=64)

        def Vn_v():
            return Vn_t[:, :].rearrange("p (h e) -> p h e", e=130)

        def sample_stage(l):
            kcache, vcache = (cfk, cfv) if l == 0 else (csk, csv)
            for s_ in range(4):
                for j in range(8):
                    S.op("pool", lambda e, s_=s_, j=j: e.dma_start(
                        out=sVc_all(s_)[:, j, :, 0:128],
                        in_=vcache[s_][j * 128:(j + 1) * 128, :].rearrange("p (h d) -> p h d", d=128)),
                         writes=["sVc%d" % s_] + (REGB_ALL + ["w_out"] if j == 0 else []), chan="sVc")
                S.op("pool", lambda e, s_=s_: e.memset(sVc_all(s_)[:, :, :, 128:129], 1.0), writes=["sVc%d" % s_])
                for hf in range(2):
                    S.op("sp", lambda e, s_=s_, hf=hf: e.dma_start(
                        out=kstage_f, in_=kcache[s_][hf * 512:(hf + 1) * 512, :].rearrange("(j p) c -> p j c", p=128)),
                         writes=["actT1"], chan="sKc")
                    for jj in range(4):
                        j = hf * 4 + jj
                        for h in range(4):
                            S.op("pe", lambda e, jj=jj, h=h: e.transpose(out=pst_f32[:, h * 128:(h + 1) * 128],
                                                                         in_=kstage_f[:, jj, h * 128:(h + 1) * 128],
                                                                         identity=cstf[:, C_ID:C_ID + 128]),
                                 reads=["actT1", "cstf"], writes=PST_ALL, sig=(h == 3))
                        kb_ = s_ * SEQSZ + 8 * 520
                        S.op("act", lambda e, kb_=kb_, j=j: e.activation(
                            out=regB[:, kb_:kb_ + 4096].rearrange("p (h t) -> p h t", t=1024)[:, :, j * 128:(j + 1) * 128],
                            in_=pst_f32[:, 0:512].rearrange("p (h t) -> p h t", t=128), func=AF.Identity),
                             reads=PST_ALL, writes=["skT%d" % s_] + (REGB_ALL + ["w_out"] if j == 0 else []))
            if l == 0:
                for s_ in range(4):
                    S.op("sp", lambda e, s_=s_: e.dma_start(out=clf[:, :, s_ * 4:(s_ + 1) * 4],
                                                            in_=cfl[s_].rearrange("(j p) h -> p j h", p=128)),
                         writes=["clf"], chan="clf")
                for j in range(8):
                    S.op("pe", lambda e, j=j: e.matmul(psE[:, 16 + 16 * j:32 + 16 * j], lhsT=cf(C_LS), rhs=clf[:, j, :], start=True,
                                                       stop=(j == 7)), reads=["cstf", "clf"], writes=["psE"], sig=(j == 7))
                    for j2 in range(j + 1, 8):
                        S.op("pe", lambda e, j=j, j2=j2: e.matmul(psE[:, 16 + 16 * j:32 + 16 * j], lhsT=cf(C_ONES), rhs=clf[:, j2, :],
                                                                  start=False, stop=(j2 == 7)), reads=["cstf", "clf"],
                             writes=["psE"], sig=(j2 == 7))
                S.op("dve", lambda e: e.tensor_copy(out=sufb[:].rearrange("p j c -> p (j c)"), in_=psE[:, 16:144]), reads=["psE"],
                     writes=["sufb"])

        def sample_block(l, slot):
            tb = 32
            rows = TS
            Vn = Vn_v()
            project_block(l, slot, 0, tb, rows, lambda h: qT[:, :, 128:128 + rows], ["qT"],
                          Vn_v, ["Vn"], 0)
            if l == 0:
                S.op("pe", lambda e: e.matmul(psE[0:64, 8:12], lhsT=cf(C_UB16, 64, 64), rhs=lps[0:64, 0:4], start=True, stop=True),
                     reads=["cstf", "lps"], writes=["psE"])
                S.op("dve", lambda e: e.tensor_copy(out=cpn[:, :], in_=psE[0:64, 8:12]), reads=["psE"], writes=["cpn"])
            for h in range(4):
                ob, on = o_banks[h % 2][0]
                if l == 0:
                    o_ap = ob[0:64, 0:129]
                    first = True
                    for s_ in range(4):
                        scb, scn = sc_banks[pcount["sc"] % 2]
                        pcount["sc"] += 1
                        pw = Pw8(s_)
                        pwn = "th%d" % (s_ // 2)
                        for j in range(8):
                            S.op("pe", lambda e, s_=s_, j=j, h=h, scb=scb: e.matmul(
                                scb[:, j * 16:(j + 1) * 16], lhsT=skT(s_, h, j * 128, (j + 1) * 128),
                                rhs=qT[:, h, s_ * 16:(s_ + 1) * 16], start=True, stop=True),
                                reads=["skT%d" % s_, "qT"], writes=[scn], sig=(j == 7))
                        for j in range(8):
                            S.op("act", lambda e, s_=s_, j=j, h=h, scb=scb, pw=pw: e.activation(
                                out=pw[:, j, s_ * 16:(s_ + 1) * 16], in_=scb[:, j * 16:(j + 1) * 16], func=AF.Exp, scale=SCALE,
                                bias=sufb[:, j, s_ * 4 + h:s_ * 4 + h + 1]), reads=[scn, "sufb"], writes=[pwn])
                        for j in range(8):
                            S.op("pe", lambda e, s_=s_, j=j, h=h, first=first, o_ap=o_ap, pw=pw: e.matmul(
                                o_ap, lhsT=pw[:, j, 0:64], rhs=sVc(s_, j, h, 129), start=first, stop=False),
                                reads=[pwn, "sVc%d" % s_], writes=[on], sig=(j == 7))
                            first = False
                    scb, scn = sc_banks[pcount["sc"] % 2]
                    pcount["sc"] += 1
                    S.op("pe", lambda e, h=h, scb=scb: e.matmul(scb[0:64, 0:64], lhsT=qT[:, h, 128:192], rhs=qT[:, h, 0:64],
                                                                start=True, stop=True), reads=["qT"], writes=[scn])
                    S.op("act", lambda e, h=h, scb=scb: e.activation(out=Pn[:, :], in_=scb[0:64, 0:64], func=AF.Exp, scale=SCALE,
                                                                     bias=cpn[:, h:h + 1]), reads=[scn, "cpn"], writes=["Pn"])
                    S.op("pool", lambda e: e.tensor_tensor(out=Pn[:, :], in0=Pn[:, :], in1=mask64, op=ALU.mult),
                         reads=["Pn", "cstb"], writes=["Pn"])
                    S.op("pe", lambda e, h=h, o_ap=o_ap: e.matmul(o_ap, lhsT=Pn[:, :], rhs=Vn[0:64, h, 0:129], start=False, stop=True),
                         reads=["Pn", "Vn"], writes=[on])
                    S.op("dve", lambda e, ob=ob: e.reciprocal(out=rden[0:64, 0:1], in_=ob[0:64, 128:129]), reads=[on],
                         writes=["rden0"])
                    S.op("dve", lambda e, h=h, ob=ob: e.scalar_tensor_tensor(
                        out=ybuf[0][0:64, h * 128:(h + 1) * 128], in0=ob[0:64, 0:128], scalar=rden[0:64, 0:1],
                        in1=gsil[0][0:64, h * 128:(h + 1) * 128], op0=ALU.mult, op1=ALU.mult),
                        reads=[on, "rden0", "gsil0"], writes=["y0_%d" % h])
                else:
                    pass
            if l == 1:
                def head_stream(slot, h):
                    ob, on = sb_obank[slot][0]
                    o_ap = ob[0:64, 0:128]
                    pv = {"n": 0}
                    total = 1 + 4 * 8

                    def hook0(ci, vbl, pso, pn, pst_=None):
                        S.op("act", lambda e: e.activation(out=aTn[:, :], in_=pst_[0:64, pso:pso + 64], func=AF.Identity),
                             reads=[pn], writes=["aTn"])
                        return [(aTn[:, :], "aTn")]
                    ch0 = dict(kT=qT[:, h, 128:192], W=64, names=["qT"], vblocks=[(Vn[0:64, h, 0:128], 64, "Vn")],
                               mask=(cf(C_MS, 64, 64), cf(C_MSC, 64, 64), 64))
                    st = {}
                    yield from sb_stream(slot, 64, qT[:, h, 0:64], ["qT"], [ch0], o_ap, on, pv, total, None, hook0, None, st)
                    car = st["carry"]
                    cr = carry0[:, slot:slot + 1]
                    yield ("dve", 0.1)
                    S.op("dve", lambda e: e.tensor_copy(out=cr, in_=car[0]), reads=[car[1]], writes=["carry0_%d" % slot])
                    for s_ in range(4):
                        def hook(ci, vbl, pso, pn, pst_=None, s_=s_):
                            base = 4 if ci == 0 else 0
                            pvw = pst_[:, pso:pso + 512].rearrange("p (j t) -> p j t", t=128)
                            S.op("act", lambda e: e.activation(out=aTw[s_][:, base:base + 4, s_ * 16:(s_ + 1) * 16],
                                                               in_=pvw[:, :, s_ * 16:(s_ + 1) * 16], func=AF.Identity),
                                 reads=[pn], writes=["aTw%d" % s_])
                            return [(aTw[s_][:, base + j, 0:64], "aTw%d" % s_) for j in range(4)]
                        chunks = []
                        for (lo, hi) in ((512, 1024), (0, 512)):
                            chunks.append(dict(kT=skT(s_, h, lo, hi), W=512, names=["skT%d" % s_],
                                               vblocks=[(sVc(s_, b, h, 128), 128, "sVc%d" % s_) for b in range(lo // 128, hi // 128)],
                                               mask=None))
                        yield from sb_stream(slot, 64, qT[:, h, 0:64], ["qT"], chunks, o_ap, on, pv, total,
                                             (cr, "carry0_%d" % slot), hook)
                    yield ("dve", 0.3)
                    S.op("dve", lambda e: e.tensor_tensor(out=ybuf[0][0:64, h * 128:(h + 1) * 128], in0=ob[0:64, 0:128],
                                                          in1=gsil[0][0:64, h * 128:(h + 1) * 128], op=ALU.mult),
                         reads=[on, "gsil0"], writes=["y0_%d" % h])
                run_streams([(lambda sl, h=h: head_stream(sl, h)) for h in range(4)], SBW)
            load_w_out(l)
            y_out([tb], [rows])

        def p_phase(l):
            load_act(0, xT_all, "xT", 0, 256)
            nt = 16
            for Q in range(nt):
                slot = Q % 2
                if Q < 15:
                    load_act(1 - slot, xT_all, "xT", (Q + 1) * 256, 256)
                else:
                    load_act(1 - slot, xT_all, "xT", T, TS)
                for sub in range(2):
                    tb = 2 * Q + sub
                    project_block(l, slot, sub, tb, 128,
                                  lambda h, tb=tb: regB[:, 0:4 * T].rearrange("p (h t) -> p h t", t=T)[:, :, tb * 128:(tb + 1) * 128],
                                  ["kT%d_%d" % (h, tb) for h in range(4)],
                                  lambda tb=tb: Vaug_blk(tb), ["V%d" % tb], sub * 128)
                    if l == 0:
                        cumsum_block(tb)
                if l == 0:
                    fox_tile(Q)
                else:
                    sb_tile(Q)
                y_out([2 * Q, 2 * Q + 1], [128, 128])
            if nt == 16:
                sample_stage(l)
                sample_block(l, 0)

        def o_phase(l):
            wq = load_w_in(1) if l == 0 else iter(())
            wv = w_out()
            load_act(0, yT_all, "yT", 0, 256)
            xsrc = xc0 if l == 0 else xres[1].ap()
            xdst = xres[l + 1].ap()

            def o_load(tb):
                rows = blk_rows(tb)
                i = tb % 2
                S.op("sp", lambda e: e.dma_start(out=xcb[i][0:rows, :], in_=xsrc[tb * 128:tb * 128 + rows, :]),
                     reads=["xres%d" % l], writes=["kf32%d" % i], chan="kf32%d" % i)

            for Q in range(17):
                slot = Q % 2
                if Q < 15:
                    load_act(1 - slot, yT_all, "yT", (Q + 1) * 256, 256)
                elif Q == 15:
                    load_act(1 - slot, yT_all, "yT", T, TS)
                next(wq, None)
                for sub in range(2 if Q < 16 else 1):
                    tb = 2 * Q + sub
                    rows = blk_rows(tb)
                    i = tb % 2
                    if tb == 0:
                        o_load(0)
                    if tb + 1 < NB:
                        o_load(tb + 1)
                    for c in range(16):
                        S.op("pe", lambda e, c=c, slot=slot, sub=sub, rows=rows: e.matmul(
                            psA[0:rows, :], lhsT=actT[slot][:, c, sub * 128:sub * 128 + rows], rhs=wv[:, c, :], start=(c == 0),
                            stop=(c == 15)), reads=["actT%d" % slot, "w_out"], writes=["psA"], sig=(c == 15))
                    S.op("dve", lambda e, i=i, rows=rows: e.tensor_tensor(out=xnb[i][0:rows, :], in0=psA[0:rows, :],
                                                                          in1=xcb[i][0:rows, :], op=ALU.add),
                         reads=["psA", "kf32%d" % i], writes=["vf32%d" % i])
                    S.op("sp", lambda e, i=i, tb=tb, rows=rows: e.dma_start(out=xdst[tb * 128:tb * 128 + rows, :],
                                                                              in_=xnb[i][0:rows, :]),
                         reads=["vf32%d" % i], writes=["xres%d" % (l + 1)], chan="vf32%d" % i)
                    norm_prep(xnb[i], "vf32%d" % i, tb, rows, l + 1, l == 0)
            for _ in wq:
                pass

        stage = 99
        if stage >= 2:
            p_phase(0)
        if stage >= 4:
            load_nw(1)
            o_phase(0)
            rstd_from_ssq("1")
        if stage >= 5:
            p_phase(1)
        if stage >= 7:
            load_nw(2)
            o_phase(1)
            rstd_from_ssq("2")
            x2 = xres[2].ap()
            fin_in = [(kf32[0], "kf320"), (kf32[1], "kf321"), (thb[0], "th0"), (thb[1], "th1")]
            fin_out = [(vf32[0], "vf320"), (vf32[1], "vf321"), (thb[2], "th2"), (gtmp, "gtmp")]

            def f_load(tb):
                rows = blk_rows(tb)
                buf, nm = fin_in[tb % 4]
                S.op("sp", lambda e: e.dma_start(out=buf[0:rows, 0:512], in_=x2[tb * 128:tb * 128 + rows, :]),
                     reads=["xres2"], writes=[nm], chan=nm)

            for t_ in range(3):
                f_load(t_)
            for tb in range(NB):
                rows = blk_rows(tb)
                if tb + 3 < NB:
                    f_load(tb + 3)
                ib, inm = fin_in[tb % 4]
                ob_, onm = fin_out[tb % 4]
                S.op("dve", lambda e, tb=tb, rows=rows, ib=ib, ob_=ob_: e.scalar_tensor_tensor(
                    out=ob_[0:rows, 0:512], in0=ib[0:rows, 0:512], scalar=rstd[0:rows, tb:tb + 1], in1=nw[2][0:rows, :],
                    op0=ALU.mult, op1=ALU.mult), reads=[inm, "rstd", "nw"], writes=[onm])
                S.op("sp", lambda e, tb=tb, rows=rows, ob_=ob_: e.dma_start(out=yout[tb * 128:tb * 128 + rows, :],
                                                                             in_=ob_[0:rows, 0:512]),
                     reads=[onm], writes=[], chan=onm)
        S.op("sp", lambda e: e.dma_start(out=lfout[0:T, :].rearrange("(b p) h -> p b h", p=128), in_=lfo[:, 0:32, :]),
             reads=["lfo"], writes=[], chan="lfo")
        S.op("sp", lambda e: e.dma_start(out=lfout[T:TT, :], in_=lfo[0:TS, 32, :]), reads=["lfo"], writes=[], chan="lfo")
        for sn, v in list(S.cnt.items()):
            if sn.startswith("d_"):
                S.prog["sp"].append(("wait", sn, v))

        sem_names = sorted(S.cnt.keys())
        sems = {}
        for sn in sem_names:
            sems[sn] = es.enter_context(nc.semaphore(sn))
        block = es.enter_context(nc.Block())

        def emit(eng_obj, name):
            for item in S.prog[name]:
                if item[0] == "wait":
                    eng_obj.wait_ge(sems[item[1]], item[2])
                else:
                    _, fn, sn, inc = item
                    ins = fn(eng_obj)
                    if sn is not None:
                        if inc is None:
                            ins.then_inc(sems[sn])
                        else:
                            ins.then_inc(sems[sn], inc)

        @block.sync
        def _(e):
            emit(e, "sp")

        @block.scalar
        def _(e):
            emit(e, "act")

        @block.vector
        def _(e):
            emit(e, "dve")

        @block.gpsimd
        def _(e):
            emit(e, "pool")

        @block.tensor
        def _(e):
            emit(e, "pe")
    return nc


def _consts():
    c = np.zeros((128, NCST), np.float32)
    k = np.arange(128)[:, None]
    m = np.arange(128)[None, :]
    c[:, C_U:C_U + 128] = (k <= m)
    c[:, C_E127:C_E127 + 128] = (k == 127)
    c[:, C_UB16:C_UB16 + 128] = (k <= m) & (k // 16 == m // 16)
    c[:, C_LS:C_LS + 128] = (k > m)
    c[:, C_ONES:C_ONES + 128] = 1.0
    c[:, C_MS:C_MS + 128] = (m < k) & (k // 16 == m // 16)
    c[:, C_MSC:C_MSC + 128] = 1.0 - ((m < k) & (k // 16 == m // 16))
    c[:, C_MLT:C_MLT + 128] = (m < k)
    c[:, C_MLTC:C_MLTC + 128] = 1.0 - (m < k)
    c[:, C_ID:C_ID + 128] = (k == m)
    c[:, C_MLE:C_MLE + 128] = (k <= m)
    c[:, C_M64:C_M64 + 128] = (k <= m) & (k // 16 == m // 16)
    return c


_NC_CACHE = {}


def kernel(x_prompt, x_sample, cache_fox_k, cache_fox_v, cache_fox_logf, cache_sb_k, cache_sb_v,
           norm_0, w_in_0, b_f_0, w_out_0, norm_1, w_in_1, w_out_1, norm_f):
    f = np.float32
    A = lambda a: np.ascontiguousarray(np.asarray(a, dtype=f))
    x_prompt, x_sample = A(x_prompt), A(x_sample)
    w_in_0, w_in_1, w_out_0, w_out_1 = A(w_in_0), A(w_in_1), A(w_out_0), A(w_out_1)
    caches = [A(cache_fox_k), A(cache_fox_v), A(cache_sb_k), A(cache_sb_v)]
    cache_fox_logf = A(cache_fox_logf)
    norms = [A(norm_0), A(norm_1), A(norm_f)]
    b_f_0 = A(b_f_0)
    if "nc" not in _NC_CACHE:
        _NC_CACHE["nc"] = build_program()
    nc = _NC_CACHE["nc"]
    cst = _consts()
    in_maps = []
    for core in range(8):
        b, g = core // 4, core % 4
        cs = slice(g * 512, (g + 1) * 512)
        xc0 = np.concatenate([x_prompt[b][:, cs], x_sample[4 * b:4 * b + 4].reshape(64, D)[:, cs]], axis=0)
        m = {"xc0": A(xc0), "cst": cst}
        for i, nm in enumerate(["nw0", "nw1", "nwf"]):
            m[nm] = A(np.broadcast_to(norms[i][cs][None, :], (128, 512)))
        m["win0"] = A(np.concatenate([w_in_0[:, j * D + g * 512: j * D + (g + 1) * 512] for j in range(4)]
                                     + [w_in_0[:, 4 * D + 4 * g: 4 * D + 4 * g + 4]], axis=1))
        m["win1"] = A(np.concatenate([w_in_1[:, j * D + g * 512: j * D + (g + 1) * 512] for j in range(4)], axis=1))
        m["wout0"] = A(w_out_0[:, cs])
        m["wout1"] = A(w_out_1[:, cs])
        m["bfb"] = A(np.broadcast_to(b_f_0[4 * g:4 * g + 4][None, :], (128, 4)))
        for nm, cch in zip(["cfk", "cfv", "csk", "csv"], caches):
            m[nm] = A(cch[4 * b:4 * b + 4, :, 4 * g:4 * g + 4, :].reshape(4, PAST, 512))
        m["cfl"] = A(cache_fox_logf[4 * b:4 * b + 4, :, 4 * g:4 * g + 4])
        in_maps.append(m)
    res = run_bass_kernel_spmd(nc, in_maps, core_ids=list(range(8)))
    R = res.results
    y_prompt = np.zeros((2, T, D), f)
    y_sample = np.zeros((8, 16, D), f)
    pk = [np.zeros((2, T, 16, 128), f) for _ in range(4)]
    sk = [np.zeros((8, 16, 16, 128), f) for _ in range(4)]
    plf = np.zeros((2, T, 16), f)
    slf = np.zeros((8, 16, 16), f)
    for core in range(8):
        b, g = core // 4, core % 4
        cs = slice(g * 512, (g + 1) * 512)
        r = R[core]
        y_prompt[b][:, cs] = r["yout"][:T]
        y_sample[4 * b:4 * b + 4][:, :, cs] = r["yout"][T:].reshape(4, 16, 512)
        for i, nm in enumerate(["kf", "vf", "ks", "vs"]):
            pk[i][b][:, 4 * g:4 * g + 4, :] = r[nm][:T].reshape(T, 4, 128)
            sk[i][4 * b:4 * b + 4][:, :, 4 * g:4 * g + 4, :] = r[nm][T:].reshape(4, 16, 4, 128)
        plf[b][:, 4 * g:4 * g + 4] = r["lf"][:T]
        slf[4 * b:4 * b + 4][:, :, 4 * g:4 * g + 4] = r["lf"][T:].reshape(4, 16, 4)
    return (y_prompt, y_sample, pk[0], pk[1], plf, pk[2], pk[3], sk[0], sk[1], slf, sk[2], sk[3])
```

```python
import numpy as np
import concourse.bass as bass
import concourse.mybir as mybir
from concourse.bass_utils import run_bass_kernel_spmd

F32 = mybir.dt.float32
BF16 = mybir.dt.bfloat16
AF = mybir.ActivationFunctionType
ALU = mybir.AluOpType

D = 2048
T = 4096
TS = 64
TT = T + TS
NB = 33
PAST = 1024
SCALE = 128 ** -0.5
EPS = 1e-6
GROUPS = [[0, 1, 2, 3], [4, 5, 6, 7]]
ENG = ("sp", "act", "dve", "pool", "pe")
FOXW = 1
SBW = 4

C_U, C_E127, C_UB16, C_LS, C_ONES, C_MLT, C_MLTC, C_MS, C_MSC, C_ID, C_MLE, C_M64 = [128 * i for i in range(12)]
NCST = 128 * 12


class Sched:
    def __init__(self):
        self.prog = {e: [] for e in ENG}
        self.cnt = {}
        self.waited = {}
        self.lastw = {}
        self.readers = {}

    def _need(self, eng, events):
        for sn, val in events:
            if sn.startswith("d_"):
                val = self.cnt[sn]
            if sn == "pe" and eng == "pe":
                continue
            key = (eng, sn)
            if self.waited.get(key, 0) >= val:
                continue
            self.waited[key] = val
            self.prog[eng].append(("wait", sn, val))

    def op(self, eng, fn, reads=(), writes=(), sig=True, chan=None, cc=None):
        ps_reads = [r for r in reads if r.startswith("ps") and r not in writes]
        if ps_reads:
            writes = list(writes) + ps_reads
        ev = []
        for r in reads:
            if r in self.lastw:
                ev.append(self.lastw[r])
        for w in writes:
            if w in self.lastw:
                ev.append(self.lastw[w])
            ev.extend(self.readers.get(w, {}).items())
        self._need(eng, ev)
        if cc is not None:
            sn = "c_" + cc
            self.cnt[sn] = 1
            myev = (sn, 1)
            self.prog[eng].append(("op", fn, sn, None))
        elif chan is not None:
            sn = "d_" + chan
            self.cnt[sn] = self.cnt.get(sn, 0) + 16
            myev = (sn, self.cnt[sn])
            self.prog[eng].append(("op", fn, sn, 16))
        elif sig:
            self.cnt[eng] = self.cnt.get(eng, 0) + 1
            myev = (eng, self.cnt[eng])
            self.prog[eng].append(("op", fn, eng, 1))
        else:
            myev = (eng, self.cnt.get(eng, 0) + 1)
            self.prog[eng].append(("op", fn, None, 0))
        for r in reads:
            d = self.readers.setdefault(r, {})
            d[myev[0]] = max(d.get(myev[0], 0), myev[1])
        for w in writes:
            self.lastw[w] = myev
            self.readers[w] = {}


def build_program():
    nc = bass.Bass("TRN2", target_bir_lowering=False)
    S = Sched()

    def din(name, shape, dt=F32):
        return nc.dram_tensor(name, shape, dt, kind="ExternalInput").ap()

    def dout(name, shape, dt=F32):
        return nc.dram_tensor(name, shape, dt, kind="ExternalOutput").ap()

    xc0 = din("xc0", [TT, 512])
    nwd = [din("nw0", [128, 512]), din("nw1", [128, 512]), din("nwf", [128, 512])]
    wind = [din("win0", [D, 2052]), din("win1", [D, 2048])]
    woutd = [din("wout0", [D, 512]), din("wout1", [D, 512])]
    bfd = din("bfb", [128, 4])
    cstd = din("cst", [128, NCST])
    cfk = din("cfk", [4, PAST, 512])
    cfv = din("cfv", [4, PAST, 512])
    csk = din("csk", [4, PAST, 512])
    csv = din("csv", [4, PAST, 512])
    cfl = din("cfl", [4, PAST, 4])

    yout = dout("yout", [TT, 512])
    kvout = [[dout("kf", [TT, 512]), dout("vf", [TT, 512])], [dout("ks", [TT, 512]), dout("vs", [TT, 512])]]
    lfout = dout("lf", [TT, 4])

    PW = [1024, 1024, 1024, 1024, TS]
    xT_loc = [nc.dram_tensor("xT_loc%d" % p, [512, PW[p]], BF16) for p in range(5)]
    xT_all = [nc.dram_tensor("xT_all%d" % p, [D, PW[p]], BF16) for p in range(5)]
    yT_loc = [nc.dram_tensor("yT_loc%d" % p, [512, PW[p]], BF16) for p in range(5)]
    yT_all = [nc.dram_tensor("yT_all%d" % p, [D, PW[p]], BF16) for p in range(5)]

    def piece_of(tb):
        return (tb // 8, (tb % 8) * 128) if tb < 32 else (4, 0)
    ssq_loc = nc.dram_tensor("ssq_loc", [128, NB], F32)
    ssq_all = nc.dram_tensor("ssq_all", [512, NB], F32)
    xres = [None, nc.dram_tensor("x1s", [TT, 512], F32), nc.dram_tensor("x2s", [TT, 512], F32)]

    from contextlib import ExitStack
    es = ExitStack()

    def sb(name, shape, dt):
        return es.enter_context(nc.sbuf_tensor(name, shape, dt))

    def ps(name, shape, dt):
        return es.enter_context(nc.psum_tensor(name, shape, dt))

    with es:
        cstf = sb("cstf", [128, C_ID + 128], F32)
        cstb = sb("cstb", [128, 3 * 128], BF16)
        zeros = sb("zeros", [128, 512], BF16)
        nwt = sb("nwt", [128, 512], F32)
        nw = [nwt, nwt, nwt]
        bfb = sb("bfbs", [128, 4], F32)
        w_in = sb("w_in", [128, 16, 2052], BF16)
        regB = sb("regB", [128, 4 * T + 32 * 4 * 130], BF16)
        actT = [sb("actT0", [128, 16, 256], BF16), sb("actT1", [128, 16, 256], BF16)]
        ssq_part = sb("ssq_part", [128, NB], F32)
        ssq4 = sb("ssq4", [128, 4, NB], F32)
        ssum = sb("ssum", [128, NB], F32)
        rstd = sb("rstd", [128, NB], F32)
        nrstd = sb("nrstd", [128, NB], F32)
        hrstd = sb("hrstd", [128, NB], F32)
        cpall = sb("cpall", [128, NB, 8], F32)
        lfo = sb("lfo", [128, NB, 4], F32)
        lps = sb("lps", [128, 4], F32)
        xl = sb("xl", [128, 4], F32)
        el = sb("el", [128, 4], F32)
        q_tok = sb("q_tok", [128, 512], BF16)
        k_tok = sb("k_tok", [128, 512], BF16)
        kf32b_big = sb("kf32b", [128, 516], F32)
        kf32 = [sb("kf32a", [128, 512], F32)[:, :], kf32b_big[:, 0:512]]
        vf32 = [sb("vf32a", [128, 512], F32), sb("vf32b", [128, 512], F32)]
        gsil = [sb("gsil0", [128, 512], F32), sb("gsil1", [128, 512], F32)]
        gtmp = sb("gtmp", [128, 512], F32)
        qT = sb("qT", [128, 4, 256], BF16)
        ybuf = [sb("y0", [128, 512], BF16), sb("y1", [128, 512], BF16)]
        yTs = [sb("yTs0", [128, 4, 128], BF16), sb("yTs1", [128, 4, 128], BF16)]
        Pball = sb("Pball", [128, 6, 256], BF16)
        Pb = [Pball[:, i, :] for i in range(6)]
        biasT = sb("biasT", [128, 32, 4], F32)
        rden = sb("rden", [128, 4], F32)
        thb = [sb("th%d" % i, [128, 512], F32) for i in range(3)] + [gtmp]
        Rext = [sb("Rx%d" % i, [128, 516], F32) for i in range(3)] + [kf32b_big]
        ab = [sb("a%d" % i, [128, 512], BF16) for i in range(3)] + [k_tok]
        aTb = [sb("aT0", [128, 4, 128], BF16), sb("aT1", [128, 4, 128], BF16),
               Pball[:, 0:2, :].rearrange("p a (b t) -> p (a b) t", t=128),
               q_tok[:, :].rearrange("p (j t) -> p j t", t=128)]
        xcb = kf32
        xnb = vf32
        xtb = q_tok
        sqj = gtmp
        xTs = yTs
        Pw = [sb("Pw%d" % i, [128, 64], BF16) for i in range(4)]
        Pn = sb("Pn", [64, 64], BF16)
        clf = sb("clf", [128, 8, 16], F32)
        sufb = sb("sufb", [128, 8, 16], F32)
        cpn = sb("cpn", [64, 4], F32)
        aTw = [sb("aTw%d" % i, [128, 8, 64], BF16) for i in range(4)]
        aTn = sb("aTn", [64, 64], BF16)
        Vn_t = sb("Vn_t", [128, 520], BF16)
        carry0 = sb("carry0", [64, 4], F32)
        cbias = sb("cbias", [128, 2], F32)

        psA = ps("psA", [128, 512], F32)
        psB = ps("psB", [128, 512], F32)
        psC = ps("psC", [128, 512], F32)
        psD = ps("psD", [128, 512], F32)
        psE = ps("psE", [128, 512], F32)
        psF = ps("psF", [128, 512], F32)
        psG = ps("psG", [128, 512], F32)
        pst = ps("pst", [128, 1024], BF16)

        KT_OFF = 0
        V_OFF = 4 * T

        def kT(h, lo, hi):
            return regB[:, KT_OFF + h * T + lo: KT_OFF + h * T + hi]

        def Vaug(blk, h, n):
            o = V_OFF + (blk * 4 + h) * 130
            return regB[:, o:o + n]

        def Vaug_blk(blk):
            o = V_OFF + blk * 4 * 130
            return regB[:, o:o + 520].rearrange("p (h e) -> p h e", e=130)

        def w_out():
            o = 4096 + 8 * 520 + 4096
            return regB[:, o:o + 16 * 512].rearrange("p (c n) -> p c n", n=512)

        PST_ALL = ["pstbank"]
        pst_full = pst
        psE_bf = psE[:, :].bitcast(BF16)
        pst_f32 = pst[:, :].bitcast(F32)
        psG_bf = psG[:, :].bitcast(BF16)
        REGB_ALL = ["kT%d_%d" % (h, b) for h in range(4) for b in range(32)] + ["V%d" % b for b in range(32)]

        ident = cstb[:, 0:128]
        maskLE = cstb[:, 128:256]
        mask64 = cstb[0:64, 256:320]

        def cf(c0, n=128, rows=128):
            return cstf[0:rows, c0:c0 + n]

        S.op("sp", lambda e: e.dma_start(out=cstf[:], in_=cstd[:, 0:C_ID + 128]), writes=["cstf"], chan="cst")
        S.op("pool", lambda e: e.dma_start(out=cstb[:], in_=cstd[:, C_ID:C_ID + 384]), writes=["cstb"], chan="cstb")
        def load_nw(i):
            S.op("sp", lambda e: e.dma_start(out=nwt[:], in_=nwd[i][:, :]), writes=["nw"], chan="cst")
        load_nw(0)
        S.op("sp", lambda e: e.dma_start(out=bfb[:], in_=bfd[:, :]), writes=["bfb"], chan="cst")
        S.op("dve", lambda e: e.memset(zeros[:], 0.0), writes=["zeros"])
        S.op("dve", lambda e: e.memset(cbias[:, 0:1], EPS), writes=["cbias"])
        S.op("dve", lambda e: e.memset(cbias[:, 1:2], 1.0), writes=["cbias"])
        S.op("dve", lambda e: e.memset(ssq_part[:], 1.0), writes=["ssq_part"])
        S.op("dve", lambda e: e.memset(cpall[:], 0.0), writes=["cpall"])
        S.op("dve", lambda e: e.memset(lfo[:], 0.0), writes=["lfo"])
        for i in range(2):
            S.op("pool", lambda e, i=i: e.memset(thb[i][:], 0.0), writes=["th%d" % i])
        for i in range(4):
            S.op("pool", lambda e, i=i: e.memset(aTw[i][:], 0.0), writes=["aTw%d" % i])

        if False:
            for nm_, ap_ in [("win0", wind[0][0:128, 0:512]), ("win1", wind[1][0:128, 0:512]), ("wout0", woutd[0][0:128, :]),
                             ("wout1", woutd[1][0:128, :]), ("cfk", cfk[0][0:128, :]), ("cfv", cfv[0][0:128, :]),
                             ("csk", csk[0][0:128, :]), ("csv", csv[0][0:128, :])]:
                S.op("sp", lambda e, ap_=ap_: e.dma_start(out=gtmp[:], in_=ap_), writes=["gtmp"], chan="dbg")
            S.op("sp", lambda e: e.dma_start(out=gtmp[:, 0:4], in_=cfl[0][0:128, :]), writes=["gtmp"], chan="dbg")

        def load_w_in(l):
            ncol = 2052 if l == 0 else 2048
            src = wind[l].rearrange("(c p) n -> p c n", p=128)
            for c in range(16):
                yield
                si = c % 3
                o = 20544 + si * 4104
                stg = regB[:, o:o + 4104].bitcast(F32)
                S.op("sp", lambda e, c=c, stg=stg: e.dma_start(out=stg[:, 0:ncol], in_=src[:, c, :]),
                     writes=["wst%d" % si] + (REGB_ALL + ["sVc%d" % q for q in range(4)] + ["skT%d" % q for q in range(4)]
                                              if c < 3 else []), chan="wst%d" % si)
                if c % 2 == 0:
                    S.op("act", lambda e, c=c, stg=stg: e.activation(out=w_in[:, c, 0:ncol], in_=stg[:, 0:ncol], func=AF.Identity),
                         reads=["wst%d" % si] + (REGB_ALL if c >= 13 else []), writes=["w_in"])
                else:
                    S.op("dve", lambda e, c=c, stg=stg: e.tensor_copy(out=w_in[:, c, 0:ncol], in_=stg[:, 0:ncol]),
                         reads=["wst%d" % si] + (REGB_ALL if c >= 13 else []), writes=["w_in"])

        def load_w_out(l):
            src = woutd[l].rearrange("(c p) n -> p c n", p=128)
            wv = w_out()
            for c in range(0, 16, 4):
                S.op("pool", lambda e, c=c: e.dma_start(out=wv[:, c:c + 4, :], in_=src[:, c:c + 4, :]),
                     writes=["w_out"] + REGB_ALL + ["sVc%d" % q for q in range(4)] + ["skT%d" % q for q in range(4)], chan="wout")

        def blk_rows(tb):
            return 128 if tb < 32 else TS

        cnt = {"xts": 0, "tr": 0}

        def norm_prep(xblk, xres_name, tb, rows, nwi, want_xT, want_gather=True):
            S.op("act", lambda e: e.activation(out=sqj[0:rows, :], in_=xblk[0:rows, :], func=AF.Square,
                                               accum_out=ssq_part[0:rows, tb:tb + 1]),
                 reads=[xres_name], writes=["gtmp", "ssq_part"])
            if not want_xT:
                return
            S.op("dve", lambda e: e.tensor_tensor(out=xtb[0:rows, :], in0=xblk[0:rows, :], in1=nw[nwi][0:rows, :],
                                                  op=ALU.mult),
                 reads=[xres_name, "nw"], writes=["q_tok"])
            for c in range(4):
                S.op("pe", lambda e, c=c: e.transpose(out=pst[:, c * 128:c * 128 + rows],
                                                      in_=xtb[0:rows, c * 128:(c + 1) * 128],
                                                      identity=ident[0:rows, 0:rows]),
                     reads=["q_tok", "cstb"], writes=PST_ALL, sig=(c == 3))
            i = cnt["tr"] % 2
            cnt["tr"] += 1
            pv = pst[:, 0:512].rearrange("p (c t) -> p c t", t=128)
            S.op("act", lambda e: e.activation(out=xTs[i][:, :, 0:rows], in_=pv[:, :, 0:rows], func=AF.Identity),
                 reads=PST_ALL, writes=["yTs%d" % i])
            pc, c0 = piece_of(tb)
            dst = xT_loc[pc].ap().rearrange("(c p) t -> p c t", p=128)
            S.op("sp", lambda e: e.dma_start(out=dst[:, :, c0:c0 + rows], in_=xTs[i][:, :, 0:rows]),
                 reads=["yTs%d" % i], writes=["xT_loc%d" % pc], chan="yTs%d" % i)
            if want_gather and (tb % 8 == 7 or tb == 32):
                gather(xT_loc[pc], xT_all[pc], "xT", pc)

        def rstd_from_ssq(tag):
            S.op("sp", lambda e: e.dma_start(out=ssq_loc.ap(), in_=ssq_part[:]), reads=["ssq_part"],
                 writes=["ssq_loc"], chan="ssq")
            S.op("pool", lambda e: e.collective_compute("AllGather", ALU.bypass, replica_groups=GROUPS,
                                                        ins=[ssq_loc.ap().opt()], outs=[ssq_all.ap().opt()]),
                 reads=["ssq_loc"], writes=["ssq_all"], cc="ssq" + tag)
            S.op("sp", lambda e: e.dma_start(out=ssq4[:], in_=ssq_all.ap().rearrange("(r p) b -> p r b", p=128)),
                 reads=["ssq_all"], writes=["ssq4"], chan="ssq")
            S.op("dve", lambda e: e.tensor_tensor(out=ssum[:], in0=ssq4[:, 0, :], in1=ssq4[:, 1, :], op=ALU.add),
                 reads=["ssq4"], writes=["ssum"])
            S.op("dve", lambda e: e.tensor_tensor(out=ssum[:], in0=ssum[:], in1=ssq4[:, 2, :], op=ALU.add),
                 reads=["ssq4", "ssum"], writes=["ssum"])
            S.op("dve", lambda e: e.tensor_tensor(out=ssum[:], in0=ssum[:], in1=ssq4[:, 3, :], op=ALU.add),
                 reads=["ssq4", "ssum"], writes=["ssum"])
            S.op("act", lambda e: e.activation(out=ssum[:], in_=ssum[:], func=AF.Ln, scale=1.0 / D, bias=cbias[:, 0:1]),
                 reads=["ssum", "cbias"], writes=["ssum"])
            S.op("act", lambda e: e.activation(out=rstd[:], in_=ssum[:], func=AF.Exp, scale=-0.5),
                 reads=["ssum"], writes=["rstd"])
            S.op("dve", lambda e: e.tensor_scalar(out=nrstd[:], in0=rstd[:], scalar1=-1.0, scalar2=None, op0=ALU.mult),
                 reads=["rstd"], writes=["nrstd"])
            S.op("dve", lambda e: e.tensor_scalar(out=hrstd[:], in0=rstd[:], scalar1=0.5, scalar2=None, op0=ALU.mult),
                 reads=["rstd"], writes=["hrstd"])
            S.op("dve", lambda e: e.memset(ssq_part[:], 1.0), reads=[], writes=["ssq_part"])

        gcount = {"n": 0}

        def gather(loc, allt, rname, pc):
            gcount["n"] += 1
            S.op("pool", lambda e: e.collective_compute("AllGather", ALU.bypass, replica_groups=GROUPS,
                                                        ins=[loc.ap().opt()], outs=[allt.ap().opt()]),
                 reads=["%s_loc%d" % (rname, pc)], writes=["%s_all%d" % (rname, pc)], cc="g%d" % gcount["n"])

        wq = load_w_in(0)

        def n0_load(tb):
            rows = blk_rows(tb)
            i = tb % 2
            S.op("sp", lambda e: e.dma_start(out=xcb[i][0:rows, :], in_=xc0[tb * 128:tb * 128 + rows, :]),
                 writes=["kf32%d" % i], chan="kf32%d" % i)

        for tb in range(NB):
            rows = blk_rows(tb)
            i = tb % 2
            if tb % 2 == 0:
                next(wq, None)
            if tb == 0:
                n0_load(0)
            if tb + 1 < NB:
                n0_load(tb + 1)
            norm_prep(xcb[i], "kf32%d" % i, tb, rows, 0, True)
        for _ in wq:
            pass
        rstd_from_ssq("0")

        def load_act(slot, src_all, rname, c0, ncols):
            pc, lc = (c0 // 1024, c0 % 1024) if c0 < T else (4, 0)
            src = src_all[pc].ap().rearrange("(c p) t -> p c t", p=128)
            S.op("sp", lambda e: e.dma_start(out=actT[slot][:, :, 0:ncols], in_=src[:, :, lc:lc + ncols]),
                 reads=["%s_all%d" % (rname, pc)], writes=["actT%d" % slot], chan="actT%d" % slot)

        def transposes_to(src_tok, src_name, rows, dst_fn, dst_names, evac_eng):
            for h in range(4):
                S.op("pe", lambda e, h=h: e.transpose(out=pst[:, h * 128:h * 128 + rows],
                                                      in_=src_tok[0:rows, h * 128:(h + 1) * 128],
                                                      identity=ident[0:rows, 0:rows]),
                     reads=[src_name, "cstb"], writes=PST_ALL, sig=(h == 3))
            pv4 = pst[:, 0:512].rearrange("p (h t) -> p h t", t=128)
            if evac_eng == "act":
                S.op("act", lambda e: e.activation(out=dst_fn(None), in_=pv4[:, :, 0:rows], func=AF.Identity),
                     reads=PST_ALL, writes=dst_names)
            else:
                S.op("dve", lambda e: e.tensor_copy(out=dst_fn(None), in_=pv4[:, :, 0:rows]), reads=PST_ALL, writes=dst_names)

        def project_block(l, slot, sub, tb, rows, kT_dst, kT_names, v_dst_fn, v_names, qcol0):
            banks = [psA, psB, psC, psD]
            bn = ["psA", "psB", "psC", "psD"]
            for c in range(16):
                lhsT = actT[slot][:, c, sub * 128: sub * 128 + rows]
                for j in range(4):
                    S.op("pe", lambda e, c=c, j=j, lhsT=lhsT: e.matmul(banks[j][0:rows, :], lhsT=lhsT,
                                                                        rhs=w_in[:, c, j * 512:(j + 1) * 512],
                                                                        start=(c == 0), stop=(c == 15)),
                         reads=["actT%d" % slot, "w_in"], writes=[bn[j]], sig=(c == 15))
                if l == 0:
                    S.op("pe", lambda e, c=c, lhsT=lhsT: e.matmul(psE[0:rows, 0:4], lhsT=lhsT,
                                                                   rhs=w_in[:, c, 2048:2052],
                                                                   start=(c == 0), stop=(c == 15)),
                         reads=["actT%d" % slot, "w_in"], writes=["psE"], sig=(c == 15))
            rs = rstd[0:rows, tb:tb + 1]
            nrs = nrstd[0:rows, tb:tb + 1]
            i = tb % 2
            S.op("dve", lambda e: e.tensor_scalar(out=q_tok[0:rows, :], in0=psA[0:rows, :], scalar1=rs, scalar2=None,
                                                  op0=ALU.mult),
                 reads=["psA", "rstd"], writes=["q_tok"])
            S.op("act", lambda e: e.activation(out=kf32[i][0:rows, :], in_=psB[0:rows, :], func=AF.Identity, scale=rs),
                 reads=["psB", "rstd"], writes=["kf32%d" % i])
            S.op("dve", lambda e: e.tensor_scalar(vf32[i][0:rows, :], psC[0:rows, :], rs, None, ALU.mult),
                 reads=["psC", "rstd"], writes=["vf32%d" % i])
            S.op("sp", lambda e: e.dma_start(out=kvout[l][0][tb * 128:tb * 128 + rows, :], in_=kf32[i][0:rows, :]),
                 reads=["kf32%d" % i], writes=[], chan="kf32%d" % i)
            S.op("sp", lambda e: e.dma_start(out=kvout[l][1][tb * 128:tb * 128 + rows, :], in_=vf32[i][0:rows, :]),
                 reads=["vf32%d" % i], writes=[], chan="vf32%d" % i)
            S.op("act", lambda e: e.activation(out=k_tok[0:rows, :], in_=psB[0:rows, :], func=AF.Identity, scale=rs),
                 reads=["psB", "rstd"], writes=["k_tok"])
            vd = v_dst_fn()
            S.op("dve", lambda e: e.tensor_copy(out=vd[0:rows, :, 0:128],
                                                in_=vf32[i][0:rows, :].rearrange("p (h d) -> p h d", d=128)),
                 reads=["vf32%d" % i], writes=v_names)
            S.op("pool", lambda e: e.memset(vd[0:rows, :, 128:129], 1.0), reads=[], writes=v_names)
            if l == 0:
                S.op("act", lambda e: e.activation(out=gtmp[0:rows, :], in_=psD[0:rows, :], func=AF.Exp, scale=nrs),
                     reads=["psD", "nrstd"], writes=["gtmp"])
                S.op("act", lambda e: e.activation(out=gtmp[0:rows, :], in_=gtmp[0:rows, :], func=AF.Ln, bias=cbias[0:rows, 1:2]),
                     reads=["gtmp", "cbias"], writes=["gtmp"])
                S.op("act", lambda e: e.activation(out=gtmp[0:rows, :], in_=gtmp[0:rows, :], func=AF.Exp, scale=-1.0),
                     reads=["gtmp"], writes=["gtmp"])
            else:
                S.op("act", lambda e: e.activation(out=gtmp[0:rows, :], in_=psD[0:rows, :], func=AF.Sigmoid, scale=rs),
                     reads=["psD", "rstd"], writes=["gtmp"])
            S.op("dve", lambda e: e.scalar_tensor_tensor(out=gsil[sub][0:rows, :], in0=psD[0:rows, :], scalar=rs,
                                                         in1=gtmp[0:rows, :], op0=ALU.mult, op1=ALU.mult),
                 reads=["psD", "rstd", "gtmp"], writes=["gsil%d" % sub])
            if l == 0:
                S.op("dve", lambda e: e.scalar_tensor_tensor(out=xl[0:rows, :], in0=psE[0:rows, 0:4], scalar=rs,
                                                             in1=bfb[0:rows, :], op0=ALU.mult, op1=ALU.add),
                     reads=["psE", "rstd", "bfb"], writes=["xl"])
                S.op("act", lambda e: e.activation(out=el[0:rows, :], in_=xl[0:rows, :], func=AF.Exp, scale=-1.0),
                     reads=["xl"], writes=["el"])
                S.op("act", lambda e: e.activation(out=lps[0:rows, :], in_=el[0:rows, :], func=AF.Ln, bias=cbias[0:rows, 1:2]),
                     reads=["el", "cbias"], writes=["lps"])
                S.op("pool", lambda e: e.tensor_scalar(out=lfo[0:rows, tb, :], in0=lps[0:rows, :], scalar1=-1.0,
                                                       scalar2=0.0, op0=ALU.mult, op1=ALU.add),
                     reads=["lps"], writes=["lfo"])
            transposes_to(q_tok, "q_tok", rows, lambda h: qT[:, :, qcol0:qcol0 + rows], ["qT"], "dve")
            transposes_to(k_tok, "k_tok", rows, kT_dst, kT_names, "act")

        def cumsum_block(tb):
            first = (tb == 0)
            S.op("pe", lambda e: e.matmul(psE[:, 8:12], lhsT=cf(C_U), rhs=lps[:, 0:4], start=True, stop=first),
                 reads=["cstf", "lps"], writes=["psE"], sig=first)
            if not first:
                S.op("pe", lambda e: e.matmul(psE[:, 8:12], lhsT=cf(C_E127), rhs=cpall[:, tb - 1, 0:4], start=False,
                                              stop=True),
                     reads=["cstf", "cpall"], writes=["psE"], sig=False)
                S.op("pe", lambda e: e.matmul(psE[:, 12:16], lhsT=cf(C_E127), rhs=cpall[:, tb - 1, 0:4], start=True,
                                              stop=True),
                     reads=["cstf", "cpall"], writes=["psE"])
                S.op("dve", lambda e: e.tensor_copy(out=cpall[:, tb, 0:8], in_=psE[:, 8:16]), reads=["psE"],
                     writes=["cpall"])
            else:
                S.op("dve", lambda e: e.tensor_copy(out=cpall[:, tb, 0:4], in_=psE[:, 8:12]), reads=["psE"],
                     writes=["cpall"])

        sc_banks = [(psA, "psA"), (psB, "psB")]
        o_banks = [((psC, "psC"), (psD, "psD")), ((psF, "psF"), (psG, "psG"))]

        def y_out(tbs, rows_l):
            for sub, tb in enumerate(tbs):
                rows = rows_l[sub]
                for c in range(4):
                    S.op("pe", lambda e, c=c, sub=sub, rows=rows: e.transpose(out=pst[:, c * 128:c * 128 + rows],
                                                                               in_=ybuf[sub][0:rows, c * 128:(c + 1) * 128],
                                                                               identity=ident[0:rows, 0:rows]),
                         reads=["y%d_%d" % (sub, hh) for hh in range(4)] + ["cstb"], writes=PST_ALL, sig=(c == 3))
                i = cnt["tr"] % 2
                cnt["tr"] += 1
                pv = pst[:, 0:512].rearrange("p (c t) -> p c t", t=128)
                S.op("act", lambda e, i=i, rows=rows, pv=pv: e.activation(out=yTs[i][:, :, 0:rows], in_=pv[:, :, 0:rows],
                                                                           func=AF.Identity),
                     reads=PST_ALL, writes=["yTs%d" % i])
                pc, c0 = piece_of(tb)
                dst = yT_loc[pc].ap().rearrange("(c p) t -> p c t", p=128)
                S.op("sp", lambda e, i=i, rows=rows, c0=c0, dst=dst: e.dma_start(out=dst[:, :, c0:c0 + rows],
                                                                                  in_=yTs[i][:, :, 0:rows]),
                     reads=["yTs%d" % i], writes=["yT_loc%d" % pc], chan="yTs%d" % i)
                if tb % 8 == 7 or tb == 32:
                    gather(yT_loc[pc], yT_all[pc], "yT", pc)

        pcount = {"p": 0, "sc": 0, "ch": 0}

        def run_streams(makers, width):
            pending = list(makers)
            active = []
            free = list(range(width))
            eng_free = {}
            now = 0.0
            while pending or active:
                while pending and free:
                    sl = free.pop(0)
                    g = pending.pop(0)(sl)
                    try:
                        nxt = next(g)
                    except StopIteration:
                        free.append(sl)
                        continue
                    active.append({"g": g, "sl": sl, "ready": now, "nxt": nxt})
                if not active:
                    continue
                best = min(active, key=lambda a: max(eng_free.get(a["nxt"][0], 0.0), a["ready"]))
                eng, dur = best["nxt"]
                start = max(eng_free.get(eng, 0.0), best["ready"])
                end = start + dur
                eng_free[eng] = end
                best["ready"] = end + 0.25
                try:
                    best["nxt"] = next(best["g"])
                except StopIteration:
                    active.remove(best)
                    free.append(best["sl"])
                    now = end

        def fox_stream(slot, Q, h):
            nkb = 2 * Q + 2
            ob = o_banks[slot]
            fsets = [[(psA, "psA"), (psB, "psB"), (psE, "psE")], [(pst_f32, "pstbank"), (psF, "psF"), (psG, "psG")]]
            batches = [list(range(k0, min(k0 + 3, nkb))) for k0 in range(0, nkb, 3)]

            def emit_qk(b):
                for i, kb in enumerate(batches[b]):
                    qlo = 128 if kb == nkb - 1 else 0
                    scb, scn = fsets[b % 2][i]
                    S.op("pe", lambda e, kb=kb, qlo=qlo, scb=scb: e.matmul(
                        scb[:, qlo:256], lhsT=kT(h, kb * 128, (kb + 1) * 128), rhs=qT[:, h, qlo:256], start=True, stop=True),
                        reads=["kT%d_%d" % (h, kb), "qT"], writes=[scn])

            yield ("pe", 0.4)
            emit_qk(0)
            for b, kbs in enumerate(batches):
                if b + 1 < len(batches):
                    emit_qk(b + 1)
                for i, kb in enumerate(kbs):
                    qlo = 128 if kb == nkb - 1 else 0
                    scb, scn = fsets[b % 2][i]
                    pi = 3 * (b % 2) + i
                    S.op("act", lambda e, kb=kb, qlo=qlo, pi=pi, scb=scb: e.activation(
                        out=Pb[pi][:, qlo:256], in_=scb[:, qlo:256], func=AF.Exp, scale=SCALE, bias=biasT[:, kb, h:h + 1]),
                        reads=[scn, "biasT"], writes=["P%d" % pi])
                    if kb >= nkb - 2:
                        dq = 0 if kb == nkb - 2 else 128
                        S.op("pool", lambda e, pi=pi, dq=dq: e.tensor_tensor(out=Pb[pi][:, dq:dq + 128],
                                                                             in0=Pb[pi][:, dq:dq + 128], in1=maskLE,
                                                                             op=ALU.mult),
                             reads=["P%d" % pi, "cstb"], writes=["P%d" % pi])
                for i, kb in enumerate(kbs):
                    pi = 3 * (b % 2) + i
                    for sub in range(2):
                        if kb == nkb - 1 and sub == 0:
                            continue
                        last = (kb == nkb - 2) if sub == 0 else (kb == nkb - 1)
                        S.op("pe", lambda e, kb=kb, sub=sub, pi=pi, last=last: e.matmul(
                            ob[sub][0][:, 0:129], lhsT=Pb[pi][:, sub * 128:(sub + 1) * 128], rhs=Vaug(kb, h, 129),
                            start=(kb == 0), stop=last),
                            reads=["P%d" % pi, "V%d" % kb], writes=[ob[sub][1]], sig=(sub == 1 or kb == nkb - 2))
            for sub in range(2):
                S.op("dve", lambda e, sub=sub: e.reciprocal(out=rden[:, 2 * slot + sub:2 * slot + sub + 1],
                                                            in_=ob[sub][0][:, 128:129]),
                     reads=[ob[sub][1]], writes=["rden%d" % slot])
                S.op("dve", lambda e, sub=sub: e.scalar_tensor_tensor(
                    out=ybuf[sub][:, h * 128:(h + 1) * 128], in0=ob[sub][0][:, 0:128],
                    scalar=rden[:, 2 * slot + sub:2 * slot + sub + 1],
                    in1=gsil[sub][:, h * 128:(h + 1) * 128], op0=ALU.mult, op1=ALU.mult),
                    reads=[ob[sub][1], "rden%d" % slot, "gsil%d" % sub], writes=["y%d_%d" % (sub, h)])

        def fox_tile(Q):
            nkb = 2 * Q + 2
            for h in range(4):
                S.op("dve", lambda e, h=h: e.tensor_scalar(out=biasT[:, 0:nkb, h], in0=cpall[:, 0:nkb, h],
                                                           scalar1=cpall[:, 2 * Q + 1, 4 + h:5 + h], scalar2=None,
                                                           op0=ALU.subtract),
                     reads=["cpall"], writes=["biasT"])
            run_streams([(lambda sl, h=h: fox_stream(sl, Q, h)) for h in range(4)], FOXW)

        def sb_stream(slot, qrows, qT_ap, qT_names, chunks, o_ap, o_name, pv, total_pv, carry_in=None, aT_hook=None,
                      fin=None, out_state=None):
            prev = carry_in
            th, Rx, a_ = thb[slot], Rext[slot], ab[slot]
            tn = ["th0", "th1", "th2", "gtmp"][slot]
            rn = ["Rx0", "Rx1", "Rx2", "kf321"][slot]
            an = ["a0", "a1", "a2", "k_tok"][slot]
            aTnames = [["aT0"], ["aT1"], ["P0", "P1"], ["q_tok"]][slot]
            scb, scn = [(psA, "psA"), (psB, "psB"), (psE, "psE"), (psG, "psG")][slot]
            pst, pn, pso = scb[:, :].bitcast(BF16), scn, 0
            for ci, ch in enumerate(chunks):
                W = ch["W"]
                yield ("pe", 0.45)
                S.op("pe", lambda e, ch=ch, W=W, scb=scb: e.matmul(scb[0:qrows, 0:W], lhsT=qT_ap, rhs=ch["kT"], start=True, stop=True),
                     reads=qT_names + ch["names"], writes=[scn])
                yield ("act", 0.65)
                S.op("act", lambda e, W=W, scb=scb: e.activation(out=th[0:qrows, 0:W], in_=scb[0:qrows, 0:W], func=AF.Sigmoid,
                                                        scale=-SCALE),
                     reads=[scn], writes=[tn])
                yield ("dve", 1.5)
                if ch.get("mask") is not None:
                    M, Mc, dw = ch["mask"]
                    S.op("dve", lambda e, W=W, dw=dw, M=M: e.tensor_tensor(out=th[0:qrows, W - dw:W], in0=th[0:qrows, W - dw:W],
                                                                           in1=M, op=ALU.mult),
                         reads=[tn, "cstf"], writes=[tn])
                    S.op("dve", lambda e, W=W, dw=dw, Mc=Mc: e.tensor_tensor(out=th[0:qrows, W - dw:W], in0=th[0:qrows, W - dw:W],
                                                                             in1=Mc, op=ALU.add),
                         reads=[tn, "cstf"], writes=[tn])
                if prev is None:
                    S.op("dve", lambda e, W=W: e.memset(Rx[0:qrows, W:W + 1], 1.0), reads=[], writes=[rn])
                else:
                    pR, pname = prev
                    S.op("dve", lambda e, W=W, pR=pR: e.tensor_copy(out=Rx[0:qrows, W:W + 1], in_=pR),
                         reads=[pname, an], writes=[rn])
                S.op("dve", lambda e, W=W: e.tensor_tensor_scan(
                    out=Rx[0:qrows, 0:W][:, ::-1], data0=th[0:qrows, 0:W][:, ::-1], data1=zeros[0:qrows, 0:W],
                    initial=Rx[0:qrows, W:W + 1], op0=ALU.mult, op1=ALU.add),
                    reads=[tn, rn, "zeros"], writes=[rn])
                prev = (Rx[0:qrows, 0:1], rn)
                if True:
                    yield ("pool", 1.3)
                S.op("pool", lambda e, W=W: e.tensor_tensor(out=a_[0:qrows, 0:W], in0=Rx[0:qrows, 1:W + 1],
                                                                                   in1=Rx[0:qrows, 0:W], op=ALU.subtract),
                     reads=[rn], writes=[an])
                yield ("pe", 0.7)
                vbl = ch["vblocks"]
                off = 0
                for j, (rhs, wk, vn) in enumerate(vbl):
                    S.op("pe", lambda e, off=off, wk=wk, j=j: e.transpose(out=pst[0:wk, pso + j * 128:pso + j * 128 + qrows],
                                                                          in_=a_[0:qrows, off:off + wk],
                                                                          identity=ident[0:qrows, 0:qrows]),
                         reads=[an, "cstb"], writes=[pn], sig=(j == len(vbl) - 1))
                    off += wk
                yield ("act", 0.65)
                if aT_hook is None:
                    aT = aTb[slot]
                    nb_ = len(vbl)
                    pvw = pst[:, pso:pso + nb_ * 128].rearrange("p (j t) -> p j t", t=128)
                    S.op("act", lambda e, aT=aT, pvw=pvw, nb_=nb_: e.activation(out=aT[:, 0:nb_, 0:qrows], in_=pvw[:, :, 0:qrows],
                                                                                func=AF.Identity),
                         reads=[pn], writes=aTnames)
                    lhs_list = [(aT[0:wk, j, 0:qrows], aTnames) for j, (_, wk, _) in enumerate(vbl)]
                else:
                    lhs_list = aT_hook(ci, vbl, pso, pn, pst)
                if aT_hook is None:
                    yield ("pe", 0.7)
                for j, (rhs, wk, vn) in enumerate(vbl):
                    lhsT, ln = lhs_list[j]
                    n = pv["n"]
                    S.op("pe", lambda e, lhsT=lhsT, rhs=rhs, n=n: e.matmul(o_ap, lhsT=lhsT, rhs=rhs, start=(n == 0),
                                                                           stop=(n == total_pv - 1)),
                         reads=(ln if isinstance(ln, list) else [ln]) + [vn], writes=[o_name], sig=True)
                    pv["n"] += 1
            if out_state is not None:
                out_state["carry"] = prev
            if fin is not None:
                yield ("dve", 0.3)
                fin()

        sb_obank = {0: [(psC, "psC")], 1: [(psD, "psD")], 2: [(psF, "psF")], 3: [(pst_f32, "pstbank")]}
        sb_ocnt = {0: 0, 1: 0}

        def sb_tile(Q):
            makers = []
            for sub in range(2):
                qb = 2 * Q + sub
                e_ = 128 * (qb + 1)
                for h in range(4):
                    def mk(slot, sub=sub, h=h, e_=e_):
                        chunks = []
                        hi = e_
                        while hi > 0:
                            lo = max(0, hi - 512)
                            vbl = [(Vaug(b, h, 128), 128, "V%d" % b) for b in range(lo // 128, hi // 128)]
                            chunks.append(dict(kT=kT(h, lo, hi), W=hi - lo,
                                               names=["kT%d_%d" % (h, b) for b in range(lo // 128, hi // 128)],
                                               vblocks=vbl,
                                               mask=(cf(C_MLT), cf(C_MLTC), 128) if hi == e_ else None))
                            hi = lo
                        ob, on = sb_obank[slot][0]

                        def fin():
                            S.op("dve", lambda e: e.tensor_tensor(out=ybuf[sub][:, h * 128:(h + 1) * 128], in0=ob[:, 0:128],
                                                                  in1=gsil[sub][:, h * 128:(h + 1) * 128], op=ALU.mult),
                                 reads=[on, "gsil%d" % sub], writes=["y%d_%d" % (sub, h)])
                        return sb_stream(slot, 128, qT[:, h, sub * 128:(sub + 1) * 128], ["qT"], chunks, ob[:, 0:128], on,
                                         {"n": 0}, sum(len(c["vblocks"]) for c in chunks), fin=fin)
                    makers.append(mk)
            run_streams(makers, SBW)

        SEQSZ = 8 * 520 + 4096
        assert 4 * SEQSZ <= 4 * T + 32 * 4 * 130
        kstage_f = actT[1][:, :, :].rearrange("p c t -> p (c t)").bitcast(F32).rearrange("p (j c) -> p j c", c=512)

        def sVc_all(s_):
            b0 = s_ * SEQSZ
            return regB[:, b0:b0 + 8 * 520].rearrange("p (j h e) -> p j h e", h=4, e=130)

        def sVc(s_, j, h, n):
            o = s_ * SEQSZ + (j * 4 + h) * 130
            return regB[:, o:o + n]

        def skT(s_, h, lo, hi):
            o = s_ * SEQSZ + 8 * 520 + h * 1024
            return regB[:, o + lo:o + hi]

        def Pw8(s_):
            v = thb[s_ // 2][:, :].bitcast(BF16)
            return v[:, (s_ % 2) * 512:(s_ % 2 + 1) * 512].rearrange("p (j c) -> p j c", c=64)

        def Vn_v():
            return Vn_t[:, :].rearrange("p (h e) -> p h e", e=130)

        def sample_stage(l):
            kcache, vcache = (cfk, cfv) if l == 0 else (csk, csv)
            for s_ in range(4):
                for j in range(8):
                    S.op("pool", lambda e, s_=s_, j=j: e.dma_start(
                        out=sVc_all(s_)[:, j, :, 0:128],
                        in_=vcache[s_][j * 128:(j + 1) * 128, :].rearrange("p (h d) -> p h d", d=128)),
                         writes=["sVc%d" % s_] + (REGB_ALL + ["w_out"] if j == 0 else []), chan="sVc")
                S.op("pool", lambda e, s_=s_: e.memset(sVc_all(s_)[:, :, :, 128:129], 1.0), writes=["sVc%d" % s_])
                for hf in range(2):
                    S.op("sp", lambda e, s_=s_, hf=hf: e.dma_start(
                        out=kstage_f, in_=kcache[s_][hf * 512:(hf + 1) * 512, :].rearrange("(j p) c -> p j c", p=128)),
                         writes=["actT1"], chan="sKc")
                    for jj in range(4):
                        j = hf * 4 + jj
                        for h in range(4):
                            S.op("pe", lambda e, jj=jj, h=h: e.transpose(out=pst_f32[:, h * 128:(h + 1) * 128],
                                                                         in_=kstage_f[:, jj, h * 128:(h + 1) * 128],
                                                                         identity=cstf[:, C_ID:C_ID + 128]),
                                 reads=["actT1", "cstf"], writes=PST_ALL, sig=(h == 3))
                        kb_ = s_ * SEQSZ + 8 * 520
                        S.op("act", lambda e, kb_=kb_, j=j: e.activation(
                            out=regB[:, kb_:kb_ + 4096].rearrange("p (h t) -> p h t", t=1024)[:, :, j * 128:(j + 1) * 128],
                            in_=pst_f32[:, 0:512].rearrange("p (h t) -> p h t", t=128), func=AF.Identity),
                             reads=PST_ALL, writes=["skT%d" % s_] + (REGB_ALL + ["w_out"] if j == 0 else []))
            if l == 0:
                for s_ in range(4):
                    S.op("sp", lambda e, s_=s_: e.dma_start(out=clf[:, :, s_ * 4:(s_ + 1) * 4],
                                                            in_=cfl[s_].rearrange("(j p) h -> p j h", p=128)),
                         writes=["clf"], chan="clf")
                for j in range(8):
                    S.op("pe", lambda e, j=j: e.matmul(psE[:, 16 + 16 * j:32 + 16 * j], lhsT=cf(C_LS), rhs=clf[:, j, :], start=True,
                                                       stop=(j == 7)), reads=["cstf", "clf"], writes=["psE"], sig=(j == 7))
                    for j2 in range(j + 1, 8):
                        S.op("pe", lambda e, j=j, j2=j2: e.matmul(psE[:, 16 + 16 * j:32 + 16 * j], lhsT=cf(C_ONES), rhs=clf[:, j2, :],
                                                                  start=False, stop=(j2 == 7)), reads=["cstf", "clf"],
                             writes=["psE"], sig=(j2 == 7))
                S.op("dve", lambda e: e.tensor_copy(out=sufb[:].rearrange("p j c -> p (j c)"), in_=psE[:, 16:144]), reads=["psE"],
                     writes=["sufb"])

        def sample_block(l, slot):
            tb = 32
            rows = TS
            Vn = Vn_v()
            project_block(l, slot, 0, tb, rows, lambda h: qT[:, :, 128:128 + rows], ["qT"],
                          Vn_v, ["Vn"], 0)
            if l == 0:
                S.op("pe", lambda e: e.matmul(psE[0:64, 8:12], lhsT=cf(C_UB16, 64, 64), rhs=lps[0:64, 0:4], start=True, stop=True),
                     reads=["cstf", "lps"], writes=["psE"])
                S.op("dve", lambda e: e.tensor_copy(out=cpn[:, :], in_=psE[0:64, 8:12]), reads=["psE"], writes=["cpn"])
            for h in range(4):
                ob, on = o_banks[h % 2][0]
                if l == 0:
                    o_ap = ob[0:64, 0:129]
                    first = True
                    for s_ in range(4):
                        scb, scn = sc_banks[pcount["sc"] % 2]
                        pcount["sc"] += 1
                        pw = Pw8(s_)
                        pwn = "th%d" % (s_ // 2)
                        for j in range(8):
                            S.op("pe", lambda e, s_=s_, j=j, h=h, scb=scb: e.matmul(
                                scb[:, j * 16:(j + 1) * 16], lhsT=skT(s_, h, j * 128, (j + 1) * 128),
                                rhs=qT[:, h, s_ * 16:(s_ + 1) * 16], start=True, stop=True),
                                reads=["skT%d" % s_, "qT"], writes=[scn], sig=(j == 7))
                        for j in range(8):
                            S.op("act", lambda e, s_=s_, j=j, h=h, scb=scb, pw=pw: e.activation(
                                out=pw[:, j, s_ * 16:(s_ + 1) * 16], in_=scb[:, j * 16:(j + 1) * 16], func=AF.Exp, scale=SCALE,
                                bias=sufb[:, j, s_ * 4 + h:s_ * 4 + h + 1]), reads=[scn, "sufb"], writes=[pwn])
                        for j in range(8):
                            S.op("pe", lambda e, s_=s_, j=j, h=h, first=first, o_ap=o_ap, pw=pw: e.matmul(
                                o_ap, lhsT=pw[:, j, 0:64], rhs=sVc(s_, j, h, 129), start=first, stop=False),
                                reads=[pwn, "sVc%d" % s_], writes=[on], sig=(j == 7))
                            first = False
                    scb, scn = sc_banks[pcount["sc"] % 2]
                    pcount["sc"] += 1
                    S.op("pe", lambda e, h=h, scb=scb: e.matmul(scb[0:64, 0:64], lhsT=qT[:, h, 128:192], rhs=qT[:, h, 0:64],
                                                                start=True, stop=True), reads=["qT"], writes=[scn])
                    S.op("act", lambda e, h=h, scb=scb: e.activation(out=Pn[:, :], in_=scb[0:64, 0:64], func=AF.Exp, scale=SCALE,
                                                                     bias=cpn[:, h:h + 1]), reads=[scn, "cpn"], writes=["Pn"])
                    S.op("pool", lambda e: e.tensor_tensor(out=Pn[:, :], in0=Pn[:, :], in1=mask64, op=ALU.mult),
                         reads=["Pn", "cstb"], writes=["Pn"])
                    S.op("pe", lambda e, h=h, o_ap=o_ap: e.matmul(o_ap, lhsT=Pn[:, :], rhs=Vn[0:64, h, 0:129], start=False, stop=True),
                         reads=["Pn", "Vn"], writes=[on])
                    S.op("dve", lambda e, ob=ob: e.reciprocal(out=rden[0:64, 0:1], in_=ob[0:64, 128:129]), reads=[on],
                         writes=["rden0"])
                    S.op("dve", lambda e, h=h, ob=ob: e.scalar_tensor_tensor(
                        out=ybuf[0][0:64, h * 128:(h + 1) * 128], in0=ob[0:64, 0:128], scalar=rden[0:64, 0:1],
                        in1=gsil[0][0:64, h * 128:(h + 1) * 128], op0=ALU.mult, op1=ALU.mult),
                        reads=[on, "rden0", "gsil0"], writes=["y0_%d" % h])
                else:
                    pass
            if l == 1:
                def head_stream(slot, h):
                    ob, on = sb_obank[slot][0]
                    o_ap = ob[0:64, 0:128]
                    pv = {"n": 0}
                    total = 1 + 4 * 8

                    def hook0(ci, vbl, pso, pn, pst_=None):
                        S.op("act", lambda e: e.activation(out=aTn[:, :], in_=pst_[0:64, pso:pso + 64], func=AF.Identity),
                             reads=[pn], writes=["aTn"])
                        return [(aTn[:, :], "aTn")]
                    ch0 = dict(kT=qT[:, h, 128:192], W=64, names=["qT"], vblocks=[(Vn[0:64, h, 0:128], 64, "Vn")],
                               mask=(cf(C_MS, 64, 64), cf(C_MSC, 64, 64), 64))
                    st = {}
                    yield from sb_stream(slot, 64, qT[:, h, 0:64], ["qT"], [ch0], o_ap, on, pv, total, None, hook0, None, st)
                    car = st["carry"]
                    cr = carry0[:, slot:slot + 1]
                    yield ("dve", 0.1)
                    S.op("dve", lambda e: e.tensor_copy(out=cr, in_=car[0]), reads=[car[1]], writes=["carry0_%d" % slot])
                    for s_ in range(4):
                        def hook(ci, vbl, pso, pn, pst_=None, s_=s_):
                            base = 4 if ci == 0 else 0
                            pvw = pst_[:, pso:pso + 512].rearrange("p (j t) -> p j t", t=128)
                            S.op("act", lambda e: e.activation(out=aTw[s_][:, base:base + 4, s_ * 16:(s_ + 1) * 16],
                                                               in_=pvw[:, :, s_ * 16:(s_ + 1) * 16], func=AF.Identity),
                                 reads=[pn], writes=["aTw%d" % s_])
                            return [(aTw[s_][:, base + j, 0:64], "aTw%d" % s_) for j in range(4)]
                        chunks = []
                        for (lo, hi) in ((512, 1024), (0, 512)):
                            chunks.append(dict(kT=skT(s_, h, lo, hi), W=512, names=["skT%d" % s_],
                                               vblocks=[(sVc(s_, b, h, 128), 128, "sVc%d" % s_) for b in range(lo // 128, hi // 128)],
                                               mask=None))
                        yield from sb_stream(slot, 64, qT[:, h, 0:64], ["qT"], chunks, o_ap, on, pv, total,
                                             (cr, "carry0_%d" % slot), hook)
                    yield ("dve", 0.3)
                    S.op("dve", lambda e: e.tensor_tensor(out=ybuf[0][0:64, h * 128:(h + 1) * 128], in0=ob[0:64, 0:128],
                                                          in1=gsil[0][0:64, h * 128:(h + 1) * 128], op=ALU.mult),
                         reads=[on, "gsil0"], writes=["y0_%d" % h])
                run_streams([(lambda sl, h=h: head_stream(sl, h)) for h in range(4)], SBW)
            load_w_out(l)
            y_out([tb], [rows])

        def p_phase(l):
            load_act(0, xT_all, "xT", 0, 256)
            nt = 16
            for Q in range(nt):
                slot = Q % 2
                if Q < 15:
                    load_act(1 - slot, xT_all, "xT", (Q + 1) * 256, 256)
                else:
                    load_act(1 - slot, xT_all, "xT", T, TS)
                for sub in range(2):
                    tb = 2 * Q + sub
                    project_block(l, slot, sub, tb, 128,
                                  lambda h, tb=tb: regB[:, 0:4 * T].rearrange("p (h t) -> p h t", t=T)[:, :, tb * 128:(tb + 1) * 128],
                                  ["kT%d_%d" % (h, tb) for h in range(4)],
                                  lambda tb=tb: Vaug_blk(tb), ["V%d" % tb], sub * 128)
                    if l == 0:
                        cumsum_block(tb)
                if l == 0:
                    fox_tile(Q)
                else:
                    sb_tile(Q)
                y_out([2 * Q, 2 * Q + 1], [128, 128])
            if nt == 16:
                sample_stage(l)
                sample_block(l, 0)

        def o_phase(l):
            wq = load_w_in(1) if l == 0 else iter(())
            wv = w_out()
            load_act(0, yT_all, "yT", 0, 256)
            xsrc = xc0 if l == 0 else xres[1].ap()
            xdst = xres[l + 1].ap()

            def o_load(tb):
                rows = blk_rows(tb)
                i = tb % 2
                S.op("sp", lambda e: e.dma_start(out=xcb[i][0:rows, :], in_=xsrc[tb * 128:tb * 128 + rows, :]),
                     reads=["xres%d" % l], writes=["kf32%d" % i], chan="kf32%d" % i)

            for Q in range(17):
                slot = Q % 2
                if Q < 15:
                    load_act(1 - slot, yT_all, "yT", (Q + 1) * 256, 256)
                elif Q == 15:
                    load_act(1 - slot, yT_all, "yT", T, TS)
                next(wq, None)
                for sub in range(2 if Q < 16 else 1):
                    tb = 2 * Q + sub
                    rows = blk_rows(tb)
                    i = tb % 2
                    if tb == 0:
                        o_load(0)
                    if tb + 1 < NB:
                        o_load(tb + 1)
                    for c in range(16):
                        S.op("pe", lambda e, c=c, slot=slot, sub=sub, rows=rows: e.matmul(
                            psA[0:rows, :], lhsT=actT[slot][:, c, sub * 128:sub * 128 + rows], rhs=wv[:, c, :], start=(c == 0),
                            stop=(c == 15)), reads=["actT%d" % slot, "w_out"], writes=["psA"], sig=(c == 15))
                    S.op("dve", lambda e, i=i, rows=rows: e.tensor_tensor(out=xnb[i][0:rows, :], in0=psA[0:rows, :],
                                                                          in1=xcb[i][0:rows, :], op=ALU.add),
                         reads=["psA", "kf32%d" % i], writes=["vf32%d" % i])
                    S.op("sp", lambda e, i=i, tb=tb, rows=rows: e.dma_start(out=xdst[tb * 128:tb * 128 + rows, :],
                                                                              in_=xnb[i][0:rows, :]),
                         reads=["vf32%d" % i], writes=["xres%d" % (l + 1)], chan="vf32%d" % i)
                    norm_prep(xnb[i], "vf32%d" % i, tb, rows, l + 1, l == 0)
            for _ in wq:
                pass

        stage = 99
        if stage >= 2:
            p_phase(0)
        if stage >= 4:
            load_nw(1)
            o_phase(0)
            rstd_from_ssq("1")
        if stage >= 5:
            p_phase(1)
        if stage >= 7:
            load_nw(2)
            o_phase(1)
            rstd_from_ssq("2")
            x2 = xres[2].ap()
            fin_in = [(kf32[0], "kf320"), (kf32[1], "kf321"), (thb[0], "th0"), (thb[1], "th1")]
            fin_out = [(vf32[0], "vf320"), (vf32[1], "vf321"), (thb[2], "th2"), (gtmp, "gtmp")]

            def f_load(tb):
                rows = blk_rows(tb)
                buf, nm = fin_in[tb % 4]
                S.op("sp", lambda e: e.dma_start(out=buf[0:rows, 0:512], in_=x2[tb * 128:tb * 128 + rows, :]),
                     reads=["xres2"], writes=[nm], chan=nm)

            for t_ in range(3):
                f_load(t_)
            for tb in range(NB):
                rows = blk_rows(tb)
                if tb + 3 < NB:
                    f_load(tb + 3)
                ib, inm = fin_in[tb % 4]
                ob_, onm = fin_out[tb % 4]
                S.op("dve", lambda e, tb=tb, rows=rows, ib=ib, ob_=ob_: e.scalar_tensor_tensor(
                    out=ob_[0:rows, 0:512], in0=ib[0:rows, 0:512], scalar=rstd[0:rows, tb:tb + 1], in1=nw[2][0:rows, :],
                    op0=ALU.mult, op1=ALU.mult), reads=[inm, "rstd", "nw"], writes=[onm])
                S.op("sp", lambda e, tb=tb, rows=rows, ob_=ob_: e.dma_start(out=yout[tb * 128:tb * 128 + rows, :],
                                                                             in_=ob_[0:rows, 0:512]),
                     reads=[onm], writes=[], chan=onm)
        S.op("sp", lambda e: e.dma_start(out=lfout[0:T, :].rearrange("(b p) h -> p b h", p=128), in_=lfo[:, 0:32, :]),
             reads=["lfo"], writes=[], chan="lfo")
        S.op("sp", lambda e: e.dma_start(out=lfout[T:TT, :], in_=lfo[0:TS, 32, :]), reads=["lfo"], writes=[], chan="lfo")
        for sn, v in list(S.cnt.items()):
            if sn.startswith("d_"):
                S.prog["sp"].append(("wait", sn, v))

        sem_names = sorted(S.cnt.keys())
        sems = {}
        for sn in sem_names:
            sems[sn] = es.enter_context(nc.semaphore(sn))
        block = es.enter_context(nc.Block())

        def emit(eng_obj, name):
            for item in S.prog[name]:
                if item[0] == "wait":
                    eng_obj.wait_ge(sems[item[1]], item[2])
                else:
                    _, fn, sn, inc = item
                    ins = fn(eng_obj)
                    if sn is not None:
                        if inc is None:
                            ins.then_inc(sems[sn])
                        else:
                            ins.then_inc(sems[sn], inc)

        @block.sync
        def _(e):
            emit(e, "sp")

        @block.scalar
        def _(e):
            emit(e, "act")

        @block.vector
        def _(e):
            emit(e, "dve")

        @block.gpsimd
        def _(e):
            emit(e, "pool")

        @block.tensor
        def _(e):
            emit(e, "pe")
    return nc


def _consts():
    c = np.zeros((128, NCST), np.float32)
    k = np.arange(128)[:, None]
    m = np.arange(128)[None, :]
    c[:, C_U:C_U + 128] = (k <= m)
    c[:, C_E127:C_E127 + 128] = (k == 127)
    c[:, C_UB16:C_UB16 + 128] = (k <= m) & (k // 16 == m // 16)
    c[:, C_LS:C_LS + 128] = (k > m)
    c[:, C_ONES:C_ONES + 128] = 1.0
    c[:, C_MS:C_MS + 128] = (m < k) & (k // 16 == m // 16)
    c[:, C_MSC:C_MSC + 128] = 1.0 - ((m < k) & (k // 16 == m // 16))
    c[:, C_MLT:C_MLT + 128] = (m < k)
    c[:, C_MLTC:C_MLTC + 128] = 1.0 - (m < k)
    c[:, C_ID:C_ID + 128] = (k == m)
    c[:, C_MLE:C_MLE + 128] = (k <= m)
    c[:, C_M64:C_M64 + 128] = (k <= m) & (k // 16 == m // 16)
    return c


_NC_CACHE = {}


def kernel(x_prompt, x_sample, cache_fox_k, cache_fox_v, cache_fox_logf, cache_sb_k, cache_sb_v,
           norm_0, w_in_0, b_f_0, w_out_0, norm_1, w_in_1, w_out_1, norm_f):
    f = np.float32
    A = lambda a: np.ascontiguousarray(np.asarray(a, dtype=f))
    x_prompt, x_sample = A(x_prompt), A(x_sample)
    w_in_0, w_in_1, w_out_0, w_out_1 = A(w_in_0), A(w_in_1), A(w_out_0), A(w_out_1)
    caches = [A(cache_fox_k), A(cache_fox_v), A(cache_sb_k), A(cache_sb_v)]
    cache_fox_logf = A(cache_fox_logf)
    norms = [A(norm_0), A(norm_1), A(norm_f)]
    b_f_0 = A(b_f_0)
    if "nc" not in _NC_CACHE:
        _NC_CACHE["nc"] = build_program()
    nc = _NC_CACHE["nc"]
    cst = _consts()
    in_maps = []
    for core in range(8):
        b, g = core // 4, core % 4
        cs = slice(g * 512, (g + 1) * 512)
        xc0 = np.concatenate([x_prompt[b][:, cs], x_sample[4 * b:4 * b + 4].reshape(64, D)[:, cs]], axis=0)
        m = {"xc0": A(xc0), "cst": cst}
        for i, nm in enumerate(["nw0", "nw1", "nwf"]):
            m[nm] = A(np.broadcast_to(norms[i][cs][None, :], (128, 512)))
        m["win0"] = A(np.concatenate([w_in_0[:, j * D + g * 512: j * D + (g + 1) * 512] for j in range(4)]
                                     + [w_in_0[:, 4 * D + 4 * g: 4 * D + 4 * g + 4]], axis=1))
        m["win1"] = A(np.concatenate([w_in_1[:, j * D + g * 512: j * D + (g + 1) * 512] for j in range(4)], axis=1))
        m["wout0"] = A(w_out_0[:, cs])
        m["wout1"] = A(w_out_1[:, cs])
        m["bfb"] = A(np.broadcast_to(b_f_0[4 * g:4 * g + 4][None, :], (128, 4)))
        for nm, cch in zip(["cfk", "cfv", "csk", "csv"], caches):
            m[nm] = A(cch[4 * b:4 * b + 4, :, 4 * g:4 * g + 4, :].reshape(4, PAST, 512))
        m["cfl"] = A(cache_fox_logf[4 * b:4 * b + 4, :, 4 * g:4 * g + 4])
        in_maps.append(m)
    res = run_bass_kernel_spmd(nc, in_maps, core_ids=list(range(8)))
    R = res.results
    y_prompt = np.zeros((2, T, D), f)
    y_sample = np.zeros((8, 16, D), f)
    pk = [np.zeros((2, T, 16, 128), f) for _ in range(4)]
    sk = [np.zeros((8, 16, 16, 128), f) for _ in range(4)]
    plf = np.zeros((2, T, 16), f)
    slf = np.zeros((8, 16, 16), f)
    for core in range(8):
        b, g = core // 4, core % 4
        cs = slice(g * 512, (g + 1) * 512)
        r = R[core]
        y_prompt[b][:, cs] = r["yout"][:T]
        y_sample[4 * b:4 * b + 4][:, :, cs] = r["yout"][T:].reshape(4, 16, 512)
        for i, nm in enumerate(["kf", "vf", "ks", "vs"]):
            pk[i][b][:, 4 * g:4 * g + 4, :] = r[nm][:T].reshape(T, 4, 128)
            sk[i][4 * b:4 * b + 4][:, :, 4 * g:4 * g + 4, :] = r[nm][T:].reshape(4, 16, 4, 128)
        plf[b][:, 4 * g:4 * g + 4] = r["lf"][:T]
        slf[4 * b:4 * b + 4][:, :, 4 * g:4 * g + 4] = r["lf"][T:].reshape(4, 16, 4)
    return (y_prompt, y_sample, pk[0], pk[1], plf, pk[2], pk[3], sk[0], sk[1], slf, sk[2], sk[3])
```

```python
import numpy as np
import concourse.bass as bass
import concourse.mybir as mybir
from concourse.bass_utils import run_bass_kernel_spmd

F32 = mybir.dt.float32
BF16 = mybir.dt.bfloat16
AF = mybir.ActivationFunctionType
ALU = mybir.AluOpType

D = 2048
T = 4096
TS = 64
TT = T + TS
NB = 33
PAST = 1024
SCALE = 128 ** -0.5
EPS = 1e-6
GROUPS = [[0, 1, 2, 3], [4, 5, 6, 7]]
ENG = ("sp", "act", "dve", "pool", "pe")
FOXW = 1
SBW = 4

C_U, C_E127, C_UB16, C_LS, C_ONES, C_MLT, C_MLTC, C_MS, C_MSC, C_ID, C_MLE, C_M64 = [128 * i for i in range(12)]
NCST = 128 * 12


class Sched:
    def __init__(self):
        self.prog = {e: [] for e in ENG}
        self.cnt = {}
        self.waited = {}
        self.lastw = {}
        self.readers = {}

    def _need(self, eng, events):
        for sn, val in events:
            if sn.startswith("d_"):
                val = self.cnt[sn]
            if sn == "pe" and eng == "pe":
                continue
            key = (eng, sn)
            if self.waited.get(key, 0) >= val:
                continue
            self.waited[key] = val
            self.prog[eng].append(("wait", sn, val))

    def op(self, eng, fn, reads=(), writes=(), sig=True, chan=None, cc=None):
        ps_reads = [r for r in reads if r.startswith("ps") and r not in writes]
        if ps_reads:
            writes = list(writes) + ps_reads
        ev = []
        for r in reads:
            if r in self.lastw:
                ev.append(self.lastw[r])
        for w in writes:
            if w in self.lastw:
                ev.append(self.lastw[w])
            ev.extend(self.readers.get(w, {}).items())
        self._need(eng, ev)
        if cc is not None:
            sn = "c_" + cc
            self.cnt[sn] = 1
            myev = (sn, 1)
            self.prog[eng].append(("op", fn, sn, None))
        elif chan is not None:
            sn = "d_" + chan
            self.cnt[sn] = self.cnt.get(sn, 0) + 16
            myev = (sn, self.cnt[sn])
            self.prog[eng].append(("op", fn, sn, 16))
        elif sig:
            self.cnt[eng] = self.cnt.get(eng, 0) + 1
            myev = (eng, self.cnt[eng])
            self.prog[eng].append(("op", fn, eng, 1))
        else:
            myev = (eng, self.cnt.get(eng, 0) + 1)
            self.prog[eng].append(("op", fn, None, 0))
        for r in reads:
            d = self.readers.setdefault(r, {})
            d[myev[0]] = max(d.get(myev[0], 0), myev[1])
        for w in writes:
            self.lastw[w] = myev
            self.readers[w] = {}


def build_program():
    nc = bass.Bass("TRN2", target_bir_lowering=False)
    S = Sched()

    def din(name, shape, dt=F32):
        return nc.dram_tensor(name, shape, dt, kind="ExternalInput").ap()

    def dout(name, shape, dt=F32):
        return nc.dram_tensor(name, shape, dt, kind="ExternalOutput").ap()

    xc0 = din("xc0", [TT, 512])
    nwd = [din("nw0", [128, 512]), din("nw1", [128, 512]), din("nwf", [128, 512])]
    wind = [din("win0", [D, 2052]), din("win1", [D, 2048])]
    woutd = [din("wout0", [D, 512]), din("wout1", [D, 512])]
    bfd = din("bfb", [128, 4])
    cstd = din("cst", [128, NCST])
    cfk = din("cfk", [4, PAST, 512])
    cfv = din("cfv", [4, PAST, 512])
    csk = din("csk", [4, PAST, 512])
    csv = din("csv", [4, PAST, 512])
    cfl = din("cfl", [4, PAST, 4])

    yout = dout("yout", [TT, 512])
    kvout = [[dout("kf", [TT, 512]), dout("vf", [TT, 512])], [dout("ks", [TT, 512]), dout("vs", [TT, 512])]]
    lfout = dout("lf", [TT, 4])

    PW = [1024, 1024, 1024, 1024, TS]
    xT_loc = [nc.dram_tensor("xT_loc%d" % p, [512, PW[p]], BF16) for p in range(5)]
    xT_all = [nc.dram_tensor("xT_all%d" % p, [D, PW[p]], BF16) for p in range(5)]
    yT_loc = [nc.dram_tensor("yT_loc%d" % p, [512, PW[p]], BF16) for p in range(5)]
    yT_all = [nc.dram_tensor("yT_all%d" % p, [D, PW[p]], BF16) for p in range(5)]

    def piece_of(tb):
        return (tb // 8, (tb % 8) * 128) if tb < 32 else (4, 0)
    ssq_loc = nc.dram_tensor("ssq_loc", [128, NB], F32)
    ssq_all = nc.dram_tensor("ssq_all", [512, NB], F32)
    xres = [None, nc.dram_tensor("x1s", [TT, 512], F32), nc.dram_tensor("x2s", [TT, 512], F32)]

    from contextlib import ExitStack
    es = ExitStack()

    def sb(name, shape, dt):
        return es.enter_context(nc.sbuf_tensor(name, shape, dt))

    def ps(name, shape, dt):
        return es.enter_context(nc.psum_tensor(name, shape, dt))

    with es:
        cstf = sb("cstf", [128, C_ID + 128], F32)
        cstb = sb("cstb", [128, 3 * 128], BF16)
        zeros = sb("zeros", [128, 512], BF16)
        nwt = sb("nwt", [128, 512], F32)
        nw = [nwt, nwt, nwt]
        bfb = sb("bfbs", [128, 4], F32)
        w_in = sb("w_in", [128, 16, 2052], BF16)
        regB = sb("regB", [128, 4 * T + 32 * 4 * 130], BF16)
        actT = [sb("actT0", [128, 16, 256], BF16), sb("actT1", [128, 16, 256], BF16)]
        ssq_part = sb("ssq_part", [128, NB], F32)
        ssq4 = sb("ssq4", [128, 4, NB], F32)
        ssum = sb("ssum", [128, NB], F32)
        rstd = sb("rstd", [128, NB], F32)
        nrstd = sb("nrstd", [128, NB], F32)
        hrstd = sb("hrstd", [128, NB], F32)
        cpall = sb("cpall", [128, NB, 8], F32)
        lfo = sb("lfo", [128, NB, 4], F32)
        lps = sb("lps", [128, 4], F32)
        xl = sb("xl", [128, 4], F32)
        el = sb("el", [128, 4], F32)
        q_tok = sb("q_tok", [128, 512], BF16)
        k_tok = sb("k_tok", [128, 512], BF16)
        kf32b_big = sb("kf32b", [128, 516], F32)
        kf32 = [sb("kf32a", [128, 512], F32)[:, :], kf32b_big[:, 0:512]]
        vf32 = [sb("vf32a", [128, 512], F32), sb("vf32b", [128, 512], F32)]
        gsil = [sb("gsil0", [128, 512], F32), sb("gsil1", [128, 512], F32)]
        gtmp = sb("gtmp", [128, 512], F32)
        qT = sb("qT", [128, 4, 256], BF16)
        ybuf = [sb("y0", [128, 512], BF16), sb("y1", [128, 512], BF16)]
        yTs = [sb("yTs0", [128, 4, 128], BF16), sb("yTs1", [128, 4, 128], BF16)]
        Pball = sb("Pball", [128, 6, 256], BF16)
        Pb = [Pball[:, i, :] for i in range(6)]
        biasT = sb("biasT", [128, 32, 4], F32)
        rden = sb("rden", [128, 4], F32)
        thb = [sb("th%d" % i, [128, 512], F32) for i in range(3)] + [gtmp]
        Rext = [sb("Rx%d" % i, [128, 516], F32) for i in range(3)] + [kf32b_big]
        ab = [sb("a%d" % i, [128, 512], BF16) for i in range(3)] + [k_tok]
        aTb = [sb("aT0", [128, 4, 128], BF16), sb("aT1", [128, 4, 128], BF16),
               Pball[:, 0:2, :].rearrange("p a (b t) -> p (a b) t", t=128),
               q_tok[:, :].rearrange("p (j t) -> p j t", t=128)]
        xcb = kf32
        xnb = vf32
        xtb = q_tok
        sqj = gtmp
        xTs = yTs
        Pw = [sb("Pw%d" % i, [128, 64], BF16) for i in range(4)]
        Pn = sb("Pn", [64, 64], BF16)
        clf = sb("clf", [128, 8, 16], F32)
        sufb = sb("sufb", [128, 8, 16], F32)
        cpn = sb("cpn", [64, 4], F32)
        aTw = [sb("aTw%d" % i, [128, 8, 64], BF16) for i in range(4)]
        aTn = sb("aTn", [64, 64], BF16)
        Vn_t = sb("Vn_t", [128, 520], BF16)
        carry0 = sb("carry0", [64, 4], F32)
        cbias = sb("cbias", [128, 2], F32)

        psA = ps("psA", [128, 512], F32)
        psB = ps("psB", [128, 512], F32)
        psC = ps("psC", [128, 512], F32)
        psD = ps("psD", [128, 512], F32)
        psE = ps("psE", [128, 512], F32)
        psF = ps("psF", [128, 512], F32)
        psG = ps("psG", [128, 512], F32)
        pst = ps("pst", [128, 1024], BF16)

        KT_OFF = 0
        V_OFF = 4 * T

        def kT(h, lo, hi):
            return regB[:, KT_OFF + h * T + lo: KT_OFF + h * T + hi]

        def Vaug(blk, h, n):
            o = V_OFF + (blk * 4 + h) * 130
            return regB[:, o:o + n]

        def Vaug_blk(blk):
            o = V_OFF + blk * 4 * 130
            return regB[:, o:o + 520].rearrange("p (h e) -> p h e", e=130)

        def w_out():
            o = 4096 + 8 * 520 + 4096
            return regB[:, o:o + 16 * 512].rearrange("p (c n) -> p c n", n=512)

        PST_ALL = ["pstbank"]
        pst_full = pst
        psE_bf = psE[:, :].bitcast(BF16)
        pst_f32 = pst[:, :].bitcast(F32)
        psG_bf = psG[:, :].bitcast(BF16)
        REGB_ALL = ["kT%d_%d" % (h, b) for h in range(4) for b in range(32)] + ["V%d" % b for b in range(32)]

        ident = cstb[:, 0:128]
        maskLE = cstb[:, 128:256]
        mask64 = cstb[0:64, 256:320]

        def cf(c0, n=128, rows=128):
            return cstf[0:rows, c0:c0 + n]

        S.op("sp", lambda e: e.dma_start(out=cstf[:], in_=cstd[:, 0:C_ID + 128]), writes=["cstf"], chan="cst")
        S.op("pool", lambda e: e.dma_start(out=cstb[:], in_=cstd[:, C_ID:C_ID + 384]), writes=["cstb"], chan="cstb")
        def load_nw(i):
            S.op("sp", lambda e: e.dma_start(out=nwt[:], in_=nwd[i][:, :]), writes=["nw"], chan="cst")
        load_nw(0)
        S.op("sp", lambda e: e.dma_start(out=bfb[:], in_=bfd[:, :]), writes=["bfb"], chan="cst")
        S.op("dve", lambda e: e.memset(zeros[:], 0.0), writes=["zeros"])
        S.op("dve", lambda e: e.memset(cbias[:, 0:1], EPS), writes=["cbias"])
        S.op("dve", lambda e: e.memset(cbias[:, 1:2], 1.0), writes=["cbias"])
        S.op("dve", lambda e: e.memset(ssq_part[:], 1.0), writes=["ssq_part"])
        S.op("dve", lambda e: e.memset(cpall[:], 0.0), writes=["cpall"])
        S.op("dve", lambda e: e.memset(lfo[:], 0.0), writes=["lfo"])
        for i in range(2):
            S.op("pool", lambda e, i=i: e.memset(thb[i][:], 0.0), writes=["th%d" % i])
        for i in range(4):
            S.op("pool", lambda e, i=i: e.memset(aTw[i][:], 0.0), writes=["aTw%d" % i])

        if False:
            for nm_, ap_ in [("win0", wind[0][0:128, 0:512]), ("win1", wind[1][0:128, 0:512]), ("wout0", woutd[0][0:128, :]),
                             ("wout1", woutd[1][0:128, :]), ("cfk", cfk[0][0:128, :]), ("cfv", cfv[0][0:128, :]),
                             ("csk", csk[0][0:128, :]), ("csv", csv[0][0:128, :])]:
                S.op("sp", lambda e, ap_=ap_: e.dma_start(out=gtmp[:], in_=ap_), writes=["gtmp"], chan="dbg")
            S.op("sp", lambda e: e.dma_start(out=gtmp[:, 0:4], in_=cfl[0][0:128, :]), writes=["gtmp"], chan="dbg")

        def load_w_in(l):
            ncol = 2052 if l == 0 else 2048
            src = wind[l].rearrange("(c p) n -> p c n", p=128)
            for c in range(16):
                yield
                si = c % 3
                o = 20544 + si * 4104
                stg = regB[:, o:o + 4104].bitcast(F32)
                S.op("sp", lambda e, c=c, stg=stg: e.dma_start(out=stg[:, 0:ncol], in_=src[:, c, :]),
                     writes=["wst%d" % si] + (REGB_ALL + ["sVc%d" % q for q in range(4)] + ["skT%d" % q for q in range(4)]
                                              if c < 3 else []), chan="wst%d" % si)
                if c % 2 == 0:
                    S.op("act", lambda e, c=c, stg=stg: e.activation(out=w_in[:, c, 0:ncol], in_=stg[:, 0:ncol], func=AF.Identity),
                         reads=["wst%d" % si] + (REGB_ALL if c >= 13 else []), writes=["w_in"])
                else:
                    S.op("dve", lambda e, c=c, stg=stg: e.tensor_copy(out=w_in[:, c, 0:ncol], in_=stg[:, 0:ncol]),
                         reads=["wst%d" % si] + (REGB_ALL if c >= 13 else []), writes=["w_in"])

        def load_w_out(l):
            src = woutd[l].rearrange("(c p) n -> p c n", p=128)
            wv = w_out()
            for c in range(0, 16, 4):
                S.op("pool", lambda e, c=c: e.dma_start(out=wv[:, c:c + 4, :], in_=src[:, c:c + 4, :]),
                     writes=["w_out"] + REGB_ALL + ["sVc%d" % q for q in range(4)] + ["skT%d" % q for q in range(4)], chan="wout")

        def blk_rows(tb):
            return 128 if tb < 32 else TS

        cnt = {"xts": 0, "tr": 0}

        def norm_prep(xblk, xres_name, tb, rows, nwi, want_xT, want_gather=True):
            S.op("act", lambda e: e.activation(out=sqj[0:rows, :], in_=xblk[0:rows, :], func=AF.Square,
                                               accum_out=ssq_part[0:rows, tb:tb + 1]),
                 reads=[xres_name], writes=["gtmp", "ssq_part"])
            if not want_xT:
                return
            S.op("dve", lambda e: e.tensor_tensor(out=xtb[0:rows, :], in0=xblk[0:rows, :], in1=nw[nwi][0:rows, :],
                                                  op=ALU.mult),
                 reads=[xres_name, "nw"], writes=["q_tok"])
            for c in range(4):
                S.op("pe", lambda e, c=c: e.transpose(out=pst[:, c * 128:c * 128 + rows],
                                                      in_=xtb[0:rows, c * 128:(c + 1) * 128],
                                                      identity=ident[0:rows, 0:rows]),
                     reads=["q_tok", "cstb"], writes=PST_ALL, sig=(c == 3))
            i = cnt["tr"] % 2
            cnt["tr"] += 1
            pv = pst[:, 0:512].rearrange("p (c t) -> p c t", t=128)
            S.op("act", lambda e: e.activation(out=xTs[i][:, :, 0:rows], in_=pv[:, :, 0:rows], func=AF.Identity),
                 reads=PST_ALL, writes=["yTs%d" % i])
            pc, c0 = piece_of(tb)
            dst = xT_loc[pc].ap().rearrange("(c p) t -> p c t", p=128)
            S.op("sp", lambda e: e.dma_start(out=dst[:, :, c0:c0 + rows], in_=xTs[i][:, :, 0:rows]),
                 reads=["yTs%d" % i], writes=["xT_loc%d" % pc], chan="yTs%d" % i)
            if want_gather and (tb % 8 == 7 or tb == 32):
                gather(xT_loc[pc], xT_all[pc], "xT", pc)

        def rstd_from_ssq(tag):
            S.op("sp", lambda e: e.dma_start(out=ssq_loc.ap(), in_=ssq_part[:]), reads=["ssq_part"],
                 writes=["ssq_loc"], chan="ssq")
            S.op("pool", lambda e: e.collective_compute("AllGather", ALU.bypass, replica_groups=GROUPS,
                                                        ins=[ssq_loc.ap().opt()], outs=[ssq_all.ap().opt()]),
                 reads=["ssq_loc"], writes=["ssq_all"], cc="ssq" + tag)
            S.op("sp", lambda e: e.dma_start(out=ssq4[:], in_=ssq_all.ap().rearrange("(r p) b -> p r b", p=128)),
                 reads=["ssq_all"], writes=["ssq4"], chan="ssq")
            S.op("dve", lambda e: e.tensor_tensor(out=ssum[:], in0=ssq4[:, 0, :], in1=ssq4[:, 1, :], op=ALU.add),
                 reads=["ssq4"], writes=["ssum"])
            S.op("dve", lambda e: e.tensor_tensor(out=ssum[:], in0=ssum[:], in1=ssq4[:, 2, :], op=ALU.add),
                 reads=["ssq4", "ssum"], writes=["ssum"])
            S.op("dve", lambda e: e.tensor_tensor(out=ssum[:], in0=ssum[:], in1=ssq4[:, 3, :], op=ALU.add),
                 reads=["ssq4", "ssum"], writes=["ssum"])
            S.op("act", lambda e: e.activation(out=ssum[:], in_=ssum[:], func=AF.Ln, scale=1.0 / D, bias=cbias[:, 0:1]),
                 reads=["ssum", "cbias"], writes=["ssum"])
            S.op("act", lambda e: e.activation(out=rstd[:], in_=ssum[:], func=AF.Exp, scale=-0.5),
                 reads=["ssum"], writes=["rstd"])
            S.op("dve", lambda e: e.tensor_scalar(out=nrstd[:], in0=rstd[:], scalar1=-1.0, scalar2=None, op0=ALU.mult),
                 reads=["rstd"], writes=["nrstd"])
            S.op("dve", lambda e: e.tensor_scalar(out=hrstd[:], in0=rstd[:], scalar1=0.5, scalar2=None, op0=ALU.mult),
                 reads=["rstd"], writes=["hrstd"])
            S.op("dve", lambda e: e.memset(ssq_part[:], 1.0), reads=[], writes=["ssq_part"])

        gcount = {"n": 0}

        def gather(loc, allt, rname, pc):
            gcount["n"] += 1
            S.op("pool", lambda e: e.collective_compute("AllGather", ALU.bypass, replica_groups=GROUPS,
                                                        ins=[loc.ap().opt()], outs=[allt.ap().opt()]),
                 reads=["%s_loc%d" % (rname, pc)], writes=["%s_all%d" % (rname, pc)], cc="g%d" % gcount["n"])

        wq = load_w_in(0)

        def n0_load(tb):
            rows = blk_rows(tb)
            i = tb % 2
            S.op("sp", lambda e: e.dma_start(out=xcb[i][0:rows, :], in_=xc0[tb * 128:tb * 128 + rows, :]),
                 writes=["kf32%d" % i], chan="kf32%d" % i)

        for tb in range(NB):
            rows = blk_rows(tb)
            i = tb % 2
            if tb % 2 == 0:
                next(wq, None)
            if tb == 0:
                n0_load(0)
            if tb + 1 < NB:
                n0_load(tb + 1)
            norm_prep(xcb[i], "kf32%d" % i, tb, rows, 0, True)
        for _ in wq:
            pass
        rstd_from_ssq("0")

        def load_act(slot, src_all, rname, c0, ncols):
            pc, lc = (c0 // 1024, c0 % 1024) if c0 < T else (4, 0)
            src = src_all[pc].ap().rearrange("(c p) t -> p c t", p=128)
            S.op("sp", lambda e: e.dma_start(out=actT[slot][:, :, 0:ncols], in_=src[:, :, lc:lc + ncols]),
                 reads=["%s_all%d" % (rname, pc)], writes=["actT%d" % slot], chan="actT%d" % slot)

        def transposes_to(src_tok, src_name, rows, dst_fn, dst_names, evac_eng):
            for h in range(4):
                S.op("pe", lambda e, h=h: e.transpose(out=pst[:, h * 128:h * 128 + rows],
                                                      in_=src_tok[0:rows, h * 128:(h + 1) * 128],
                                                      identity=ident[0:rows, 0:rows]),
                     reads=[src_name, "cstb"], writes=PST_ALL, sig=(h == 3))
            pv4 = pst[:, 0:512].rearrange("p (h t) -> p h t", t=128)
            if evac_eng == "act":
                S.op("act", lambda e: e.activation(out=dst_fn(None), in_=pv4[:, :, 0:rows], func=AF.Identity),
                     reads=PST_ALL, writes=dst_names)
            else:
                S.op("dve", lambda e: e.tensor_copy(out=dst_fn(None), in_=pv4[:, :, 0:rows]), reads=PST_ALL, writes=dst_names)

        def project_block(l, slot, sub, tb, rows, kT_dst, kT_names, v_dst_fn, v_names, qcol0):
            banks = [psA, psB, psC, psD]
            bn = ["psA", "psB", "psC", "psD"]
            for c in range(16):
                lhsT = actT[slot][:, c, sub * 128: sub * 128 + rows]
                for j in range(2):
                    S.op("pe", lambda e, c=c, j=j, lhsT=lhsT: e.matmul(banks[j][0:rows, :], lhsT=lhsT,
                                                                        rhs=w_in[:, c, j * 512:(j + 1) * 512],
                                                                        start=(c == 0), stop=(c == 15)),
                         reads=["actT%d" % slot, "w_in"], writes=[bn[j]], sig=(c == 15))
            for c in range(16):
                lhsT = actT[slot][:, c, sub * 128: sub * 128 + rows]
                for j in range(2, 4):
                    S.op("pe", lambda e, c=c, j=j, lhsT=lhsT: e.matmul(banks[j][0:rows, :], lhsT=lhsT,
                                                                        rhs=w_in[:, c, j * 512:(j + 1) * 512],
                                                                        start=(c == 0), stop=(c == 15)),
                         reads=["actT%d" % slot, "w_in"], writes=[bn[j]], sig=(c == 15))
                if l == 0:
                    S.op("pe", lambda e, c=c, lhsT=lhsT: e.matmul(psE[0:rows, 0:4], lhsT=lhsT,
                                                                   rhs=w_in[:, c, 2048:2052],
                                                                   start=(c == 0), stop=(c == 15)),
                         reads=["actT%d" % slot, "w_in"], writes=["psE"], sig=(c == 15))
            rs = rstd[0:rows, tb:tb + 1]
            nrs = nrstd[0:rows, tb:tb + 1]
            i = tb % 2
            S.op("dve", lambda e: e.tensor_scalar(out=q_tok[0:rows, :], in0=psA[0:rows, :], scalar1=rs, scalar2=None,
                                                  op0=ALU.mult),
                 reads=["psA", "rstd"], writes=["q_tok"])
            S.op("act", lambda e: e.activation(out=kf32[i][0:rows, :], in_=psB[0:rows, :], func=AF.Identity, scale=rs),
                 reads=["psB", "rstd"], writes=["kf32%d" % i])
            S.op("dve", lambda e: e.tensor_scalar(vf32[i][0:rows, :], psC[0:rows, :], rs, None, ALU.mult),
                 reads=["psC", "rstd"], writes=["vf32%d" % i])
            S.op("sp", lambda e: e.dma_start(out=kvout[l][0][tb * 128:tb * 128 + rows, :], in_=kf32[i][0:rows, :]),
                 reads=["kf32%d" % i], writes=[], chan="kf32%d" % i)
            S.op("sp", lambda e: e.dma_start(out=kvout[l][1][tb * 128:tb * 128 + rows, :], in_=vf32[i][0:rows, :]),
                 reads=["vf32%d" % i], writes=[], chan="vf32%d" % i)
            S.op("act", lambda e: e.activation(out=k_tok[0:rows, :], in_=psB[0:rows, :], func=AF.Identity, scale=rs),
                 reads=["psB", "rstd"], writes=["k_tok"])
            vd = v_dst_fn()
            S.op("dve", lambda e: e.tensor_copy(out=vd[0:rows, :, 0:128],
                                                in_=vf32[i][0:rows, :].rearrange("p (h d) -> p h d", d=128)),
                 reads=["vf32%d" % i], writes=v_names)
            S.op("pool", lambda e: e.memset(vd[0:rows, :, 128:129], 1.0), reads=[], writes=v_names)
            if l == 0:
                S.op("act", lambda e: e.activation(out=gtmp[0:rows, :], in_=psD[0:rows, :], func=AF.Exp, scale=nrs),
                     reads=["psD", "nrstd"], writes=["gtmp"])
                S.op("act", lambda e: e.activation(out=gtmp[0:rows, :], in_=gtmp[0:rows, :], func=AF.Ln, bias=cbias[0:rows, 1:2]),
                     reads=["gtmp", "cbias"], writes=["gtmp"])
                S.op("act", lambda e: e.activation(out=gtmp[0:rows, :], in_=gtmp[0:rows, :], func=AF.Exp, scale=-1.0),
                     reads=["gtmp"], writes=["gtmp"])
            else:
                S.op("act", lambda e: e.activation(out=gtmp[0:rows, :], in_=psD[0:rows, :], func=AF.Sigmoid, scale=rs),
                     reads=["psD", "rstd"], writes=["gtmp"])
            S.op("dve", lambda e: e.scalar_tensor_tensor(out=gsil[sub][0:rows, :], in0=psD[0:rows, :], scalar=rs,
                                                         in1=gtmp[0:rows, :], op0=ALU.mult, op1=ALU.mult),
                 reads=["psD", "rstd", "gtmp"], writes=["gsil%d" % sub])
            if l == 0:
                S.op("dve", lambda e: e.scalar_tensor_tensor(out=xl[0:rows, :], in0=psE[0:rows, 0:4], scalar=rs,
                                                             in1=bfb[0:rows, :], op0=ALU.mult, op1=ALU.add),
                     reads=["psE", "rstd", "bfb"], writes=["xl"])
                S.op("act", lambda e: e.activation(out=el[0:rows, :], in_=xl[0:rows, :], func=AF.Exp, scale=-1.0),
                     reads=["xl"], writes=["el"])
                S.op("act", lambda e: e.activation(out=lps[0:rows, :], in_=el[0:rows, :], func=AF.Ln, bias=cbias[0:rows, 1:2]),
                     reads=["el", "cbias"], writes=["lps"])
                S.op("pool", lambda e: e.tensor_scalar(out=lfo[0:rows, tb, :], in0=lps[0:rows, :], scalar1=-1.0,
                                                       scalar2=0.0, op0=ALU.mult, op1=ALU.add),
                     reads=["lps"], writes=["lfo"])
            transposes_to(q_tok, "q_tok", rows, lambda h: qT[:, :, qcol0:qcol0 + rows], ["qT"], "dve")
            transposes_to(k_tok, "k_tok", rows, kT_dst, kT_names, "act")

        def cumsum_block(tb):
            first = (tb == 0)
            S.op("pe", lambda e: e.matmul(psE[:, 8:12], lhsT=cf(C_U), rhs=lps[:, 0:4], start=True, stop=first),
                 reads=["cstf", "lps"], writes=["psE"], sig=first)
            if not first:
                S.op("pe", lambda e: e.matmul(psE[:, 8:12], lhsT=cf(C_E127), rhs=cpall[:, tb - 1, 0:4], start=False,
                                              stop=True),
                     reads=["cstf", "cpall"], writes=["psE"], sig=False)
                S.op("pe", lambda e: e.matmul(psE[:, 12:16], lhsT=cf(C_E127), rhs=cpall[:, tb - 1, 0:4], start=True,
                                              stop=True),
                     reads=["cstf", "cpall"], writes=["psE"])
                S.op("dve", lambda e: e.tensor_copy(out=cpall[:, tb, 0:8], in_=psE[:, 8:16]), reads=["psE"],
                     writes=["cpall"])
            else:
                S.op("dve", lambda e: e.tensor_copy(out=cpall[:, tb, 0:4], in_=psE[:, 8:12]), reads=["psE"],
                     writes=["cpall"])

        sc_banks = [(psA, "psA"), (psB, "psB")]
        o_banks = [((psC, "psC"), (psD, "psD")), ((psF, "psF"), (psG, "psG"))]

        def y_out(tbs, rows_l):
            for sub, tb in enumerate(tbs):
                rows = rows_l[sub]
                for c in range(4):
                    S.op("pe", lambda e, c=c, sub=sub, rows=rows: e.transpose(out=pst[:, c * 128:c * 128 + rows],
                                                                               in_=ybuf[sub][0:rows, c * 128:(c + 1) * 128],
                                                                               identity=ident[0:rows, 0:rows]),
                         reads=["y%d_%d" % (sub, hh) for hh in range(4)] + ["cstb"], writes=PST_ALL, sig=(c == 3))
                i = cnt["tr"] % 2
                cnt["tr"] += 1
                pv = pst[:, 0:512].rearrange("p (c t) -> p c t", t=128)
                S.op("act", lambda e, i=i, rows=rows, pv=pv: e.activation(out=yTs[i][:, :, 0:rows], in_=pv[:, :, 0:rows],
                                                                           func=AF.Identity),
                     reads=PST_ALL, writes=["yTs%d" % i])
                pc, c0 = piece_of(tb)
                dst = yT_loc[pc].ap().rearrange("(c p) t -> p c t", p=128)
                S.op("sp", lambda e, i=i, rows=rows, c0=c0, dst=dst: e.dma_start(out=dst[:, :, c0:c0 + rows],
                                                                                  in_=yTs[i][:, :, 0:rows]),
                     reads=["yTs%d" % i], writes=["yT_loc%d" % pc], chan="yTs%d" % i)
                if tb % 8 == 7 or tb == 32:
                    gather(yT_loc[pc], yT_all[pc], "yT", pc)

        pcount = {"p": 0, "sc": 0, "ch": 0}

        def run_streams(makers, width):
            pending = list(makers)
            active = []
            free = list(range(width))
            eng_free = {}
            now = 0.0
            while pending or active:
                while pending and free:
                    sl = free.pop(0)
                    g = pending.pop(0)(sl)
                    try:
                        nxt = next(g)
                    except StopIteration:
                        free.append(sl)
                        continue
                    active.append({"g": g, "sl": sl, "ready": now, "nxt": nxt})
                if not active:
                    continue
                best = min(active, key=lambda a: max(eng_free.get(a["nxt"][0], 0.0), a["ready"]))
                eng, dur = best["nxt"]
                start = max(eng_free.get(eng, 0.0), best["ready"])
                end = start + dur
                eng_free[eng] = end
                best["ready"] = end + 0.25
                try:
                    best["nxt"] = next(best["g"])
                except StopIteration:
                    active.remove(best)
                    free.append(best["sl"])
                    now = end

        def fox_stream(slot, Q, h):
            nkb = 2 * Q + 2
            ob = o_banks[slot]
            fsets = [[(psA, "psA"), (psB, "psB"), (psE, "psE")], [(pst_f32, "pstbank"), (psF, "psF"), (psG, "psG")]]
            batches = [list(range(k0, min(k0 + 3, nkb))) for k0 in range(0, nkb, 3)]

            def emit_qk(b):
                for i, kb in enumerate(batches[b]):
                    qlo = 128 if kb == nkb - 1 else 0
                    scb, scn = fsets[b % 2][i]
                    S.op("pe", lambda e, kb=kb, qlo=qlo, scb=scb: e.matmul(
                        scb[:, qlo:256], lhsT=kT(h, kb * 128, (kb + 1) * 128), rhs=qT[:, h, qlo:256], start=True, stop=True),
                        reads=["kT%d_%d" % (h, kb), "qT"], writes=[scn])

            yield ("pe", 0.4)
            emit_qk(0)
            for b, kbs in enumerate(batches):
                if b + 1 < len(batches):
                    emit_qk(b + 1)
                for i, kb in enumerate(kbs):
                    qlo = 128 if kb == nkb - 1 else 0
                    scb, scn = fsets[b % 2][i]
                    pi = 3 * (b % 2) + i
                    S.op("act", lambda e, kb=kb, qlo=qlo, pi=pi, scb=scb: e.activation(
                        out=Pb[pi][:, qlo:256], in_=scb[:, qlo:256], func=AF.Exp, scale=SCALE, bias=biasT[:, kb, h:h + 1]),
                        reads=[scn, "biasT"], writes=["P%d" % pi])
                    if kb >= nkb - 2:
                        dq = 0 if kb == nkb - 2 else 128
                        S.op("pool", lambda e, pi=pi, dq=dq: e.tensor_tensor(out=Pb[pi][:, dq:dq + 128],
                                                                             in0=Pb[pi][:, dq:dq + 128], in1=maskLE,
                                                                             op=ALU.mult),
                             reads=["P%d" % pi, "cstb"], writes=["P%d" % pi])
                for i, kb in enumerate(kbs):
                    pi = 3 * (b % 2) + i
                    for sub in range(2):
                        if kb == nkb - 1 and sub == 0:
                            continue
                        last = (kb == nkb - 2) if sub == 0 else (kb == nkb - 1)
                        S.op("pe", lambda e, kb=kb, sub=sub, pi=pi, last=last: e.matmul(
                            ob[sub][0][:, 0:129], lhsT=Pb[pi][:, sub * 128:(sub + 1) * 128], rhs=Vaug(kb, h, 129),
                            start=(kb == 0), stop=last),
                            reads=["P%d" % pi, "V%d" % kb], writes=[ob[sub][1]], sig=(sub == 1 or kb == nkb - 2))
            for sub in range(2):
                S.op("dve", lambda e, sub=sub: e.reciprocal(out=rden[:, 2 * slot + sub:2 * slot + sub + 1],
                                                            in_=ob[sub][0][:, 128:129]),
                     reads=[ob[sub][1]], writes=["rden%d" % slot])
                S.op("dve", lambda e, sub=sub: e.scalar_tensor_tensor(
                    out=ybuf[sub][:, h * 128:(h + 1) * 128], in0=ob[sub][0][:, 0:128],
                    scalar=rden[:, 2 * slot + sub:2 * slot + sub + 1],
                    in1=gsil[sub][:, h * 128:(h + 1) * 128], op0=ALU.mult, op1=ALU.mult),
                    reads=[ob[sub][1], "rden%d" % slot, "gsil%d" % sub], writes=["y%d_%d" % (sub, h)])

        def fox_tile(Q):
            nkb = 2 * Q + 2
            for h in range(4):
                S.op("dve", lambda e, h=h: e.tensor_scalar(out=biasT[:, 0:nkb, h], in0=cpall[:, 0:nkb, h],
                                                           scalar1=cpall[:, 2 * Q + 1, 4 + h:5 + h], scalar2=None,
                                                           op0=ALU.subtract),
                     reads=["cpall"], writes=["biasT"])
            run_streams([(lambda sl, h=h: fox_stream(sl, Q, h)) for h in range(4)], FOXW)

        def sb_stream(slot, qrows, qT_ap, qT_names, chunks, o_ap, o_name, pv, total_pv, carry_in=None, aT_hook=None,
                      fin=None, out_state=None):
            prev = carry_in
            th, Rx, a_ = thb[slot], Rext[slot], ab[slot]
            tn = ["th0", "th1", "th2", "gtmp"][slot]
            rn = ["Rx0", "Rx1", "Rx2", "kf321"][slot]
            an = ["a0", "a1", "a2", "k_tok"][slot]
            aTnames = [["aT0"], ["aT1"], ["P0", "P1"], ["q_tok"]][slot]
            scb, scn = [(psA, "psA"), (psB, "psB"), (psE, "psE"), (psG, "psG")][slot]
            pst, pn, pso = scb[:, :].bitcast(BF16), scn, 0
            for ci, ch in enumerate(chunks):
                W = ch["W"]
                yield ("pe", 0.45)
                S.op("pe", lambda e, ch=ch, W=W, scb=scb: e.matmul(scb[0:qrows, 0:W], lhsT=qT_ap, rhs=ch["kT"], start=True, stop=True),
                     reads=qT_names + ch["names"], writes=[scn])
                yield ("act", 0.65)
                S.op("act", lambda e, W=W, scb=scb: e.activation(out=th[0:qrows, 0:W], in_=scb[0:qrows, 0:W], func=AF.Sigmoid,
                                                        scale=-SCALE),
                     reads=[scn], writes=[tn])
                yield ("dve", 1.5)
                if ch.get("mask") is not None:
                    M, Mc, dw = ch["mask"]
                    S.op("dve", lambda e, W=W, dw=dw, M=M: e.tensor_tensor(out=th[0:qrows, W - dw:W], in0=th[0:qrows, W - dw:W],
                                                                           in1=M, op=ALU.mult),
                         reads=[tn, "cstf"], writes=[tn])
                    S.op("dve", lambda e, W=W, dw=dw, Mc=Mc: e.tensor_tensor(out=th[0:qrows, W - dw:W], in0=th[0:qrows, W - dw:W],
                                                                             in1=Mc, op=ALU.add),
                         reads=[tn, "cstf"], writes=[tn])
                if prev is None:
                    S.op("dve", lambda e, W=W: e.memset(Rx[0:qrows, W:W + 1], 1.0), reads=[], writes=[rn])
                else:
                    pR, pname = prev
                    S.op("dve", lambda e, W=W, pR=pR: e.tensor_copy(out=Rx[0:qrows, W:W + 1], in_=pR),
                         reads=[pname, an], writes=[rn])
                S.op("dve", lambda e, W=W: e.tensor_tensor_scan(
                    out=Rx[0:qrows, 0:W][:, ::-1], data0=th[0:qrows, 0:W][:, ::-1], data1=zeros[0:qrows, 0:W],
                    initial=Rx[0:qrows, W:W + 1], op0=ALU.mult, op1=ALU.add),
                    reads=[tn, rn, "zeros"], writes=[rn])
                prev = (Rx[0:qrows, 0:1], rn)
                if True:
                    yield ("pool", 1.3)
                S.op("pool", lambda e, W=W: e.tensor_tensor(out=a_[0:qrows, 0:W], in0=Rx[0:qrows, 1:W + 1],
                                                                                   in1=Rx[0:qrows, 0:W], op=ALU.subtract),
                     reads=[rn], writes=[an])
                yield ("pe", 0.7)
                vbl = ch["vblocks"]
                off = 0
                for j, (rhs, wk, vn) in enumerate(vbl):
                    S.op("pe", lambda e, off=off, wk=wk, j=j: e.transpose(out=pst[0:wk, pso + j * 128:pso + j * 128 + qrows],
                                                                          in_=a_[0:qrows, off:off + wk],
                                                                          identity=ident[0:qrows, 0:qrows]),
                         reads=[an, "cstb"], writes=[pn], sig=(j == len(vbl) - 1))
                    off += wk
                yield ("act", 0.65)
                if aT_hook is None:
                    aT = aTb[slot]
                    nb_ = len(vbl)
                    pvw = pst[:, pso:pso + nb_ * 128].rearrange("p (j t) -> p j t", t=128)
                    S.op("act", lambda e, aT=aT, pvw=pvw, nb_=nb_: e.activation(out=aT[:, 0:nb_, 0:qrows], in_=pvw[:, :, 0:qrows],
                                                                                func=AF.Identity),
                         reads=[pn], writes=aTnames)
                    lhs_list = [(aT[0:wk, j, 0:qrows], aTnames) for j, (_, wk, _) in enumerate(vbl)]
                else:
                    lhs_list = aT_hook(ci, vbl, pso, pn, pst)
                if aT_hook is None:
                    yield ("pe", 0.7)
                for j, (rhs, wk, vn) in enumerate(vbl):
                    lhsT, ln = lhs_list[j]
                    n = pv["n"]
                    S.op("pe", lambda e, lhsT=lhsT, rhs=rhs, n=n: e.matmul(o_ap, lhsT=lhsT, rhs=rhs, start=(n == 0),
                                                                           stop=(n == total_pv - 1)),
                         reads=(ln if isinstance(ln, list) else [ln]) + [vn], writes=[o_name], sig=True)
                    pv["n"] += 1
            if out_state is not None:
                out_state["carry"] = prev
            if fin is not None:
                yield ("dve", 0.3)
                fin()

        sb_obank = {0: [(psC, "psC")], 1: [(psD, "psD")], 2: [(psF, "psF")], 3: [(pst_f32, "pstbank")]}
        sb_ocnt = {0: 0, 1: 0}

        def sb_tile(Q):
            makers = []
            for sub in range(2):
                qb = 2 * Q + sub
                e_ = 128 * (qb + 1)
                for h in range(4):
                    def mk(slot, sub=sub, h=h, e_=e_):
                        chunks = []
                        hi = e_
                        while hi > 0:
                            lo = max(0, hi - 512)
                            vbl = [(Vaug(b, h, 128), 128, "V%d" % b) for b in range(lo // 128, hi // 128)]
                            chunks.append(dict(kT=kT(h, lo, hi), W=hi - lo,
                                               names=["kT%d_%d" % (h, b) for b in range(lo // 128, hi // 128)],
                                               vblocks=vbl,
                                               mask=(cf(C_MLT), cf(C_MLTC), 128) if hi == e_ else None))
                            hi = lo
                        ob, on = sb_obank[slot][0]

                        def fin():
                            S.op("dve", lambda e: e.tensor_tensor(out=ybuf[sub][:, h * 128:(h + 1) * 128], in0=ob[:, 0:128],
                                                                  in1=gsil[sub][:, h * 128:(h + 1) * 128], op=ALU.mult),
                                 reads=[on, "gsil%d" % sub], writes=["y%d_%d" % (sub, h)])
                        return sb_stream(slot, 128, qT[:, h, sub * 128:(sub + 1) * 128], ["qT"], chunks, ob[:, 0:128], on,
                                         {"n": 0}, sum(len(c["vblocks"]) for c in chunks), fin=fin)
                    makers.append(mk)
            run_streams(makers, SBW)

        SEQSZ = 8 * 520 + 4096
        assert 4 * SEQSZ <= 4 * T + 32 * 4 * 130
        kstage_f = actT[1][:, :, :].rearrange("p c t -> p (c t)").bitcast(F32).rearrange("p (j c) -> p j c", c=512)

        def sVc_all(s_):
            b0 = s_ * SEQSZ
            return regB[:, b0:b0 + 8 * 520].rearrange("p (j h e) -> p j h e", h=4, e=130)

        def sVc(s_, j, h, n):
            o = s_ * SEQSZ + (j * 4 + h) * 130
            return regB[:, o:o + n]

        def skT(s_, h, lo, hi):
            o = s_ * SEQSZ + 8 * 520 + h * 1024
            return regB[:, o + lo:o + hi]

        def Pw8(s_):
            v = thb[s_ // 2][:, :].bitcast(BF16)
            return v[:, (s_ % 2) * 512:(s_ % 2 + 1) * 512].rearrange("p (j c) -> p j c", c=64)

        def Vn_v():
            return Vn_t[:, :].rearrange("p (h e) -> p h e", e=130)

        def sample_stage(l):
            kcache, vcache = (cfk, cfv) if l == 0 else (csk, csv)
            for s_ in range(4):
                for j in range(8):
                    S.op("pool", lambda e, s_=s_, j=j: e.dma_start(
                        out=sVc_all(s_)[:, j, :, 0:128],
                        in_=vcache[s_][j * 128:(j + 1) * 128, :].rearrange("p (h d) -> p h d", d=128)),
                         writes=["sVc%d" % s_] + (REGB_ALL + ["w_out"] if j == 0 else []), chan="sVc")
                S.op("pool", lambda e, s_=s_: e.memset(sVc_all(s_)[:, :, :, 128:129], 1.0), writes=["sVc%d" % s_])
                for hf in range(2):
                    S.op("sp", lambda e, s_=s_, hf=hf: e.dma_start(
                        out=kstage_f, in_=kcache[s_][hf * 512:(hf + 1) * 512, :].rearrange("(j p) c -> p j c", p=128)),
                         writes=["actT1"], chan="sKc")
                    for jj in range(4):
                        j = hf * 4 + jj
                        for h in range(4):
                            S.op("pe", lambda e, jj=jj, h=h: e.transpose(out=pst_f32[:, h * 128:(h + 1) * 128],
                                                                         in_=kstage_f[:, jj, h * 128:(h + 1) * 128],
                                                                         identity=cstf[:, C_ID:C_ID + 128]),
                                 reads=["actT1", "cstf"], writes=PST_ALL, sig=(h == 3))
                        kb_ = s_ * SEQSZ + 8 * 520
                        S.op("act", lambda e, kb_=kb_, j=j: e.activation(
                            out=regB[:, kb_:kb_ + 4096].rearrange("p (h t) -> p h t", t=1024)[:, :, j * 128:(j + 1) * 128],
                            in_=pst_f32[:, 0:512].rearrange("p (h t) -> p h t", t=128), func=AF.Identity),
                             reads=PST_ALL, writes=["skT%d" % s_] + (REGB_ALL + ["w_out"] if j == 0 else []))
            if l == 0:
                for s_ in range(4):
                    S.op("sp", lambda e, s_=s_: e.dma_start(out=clf[:, :, s_ * 4:(s_ + 1) * 4],
                                                            in_=cfl[s_].rearrange("(j p) h -> p j h", p=128)),
                         writes=["clf"], chan="clf")
                for j in range(8):
                    S.op("pe", lambda e, j=j: e.matmul(psE[:, 16 + 16 * j:32 + 16 * j], lhsT=cf(C_LS), rhs=clf[:, j, :], start=True,
                                                       stop=(j == 7)), reads=["cstf", "clf"], writes=["psE"], sig=(j == 7))
                    for j2 in range(j + 1, 8):
                        S.op("pe", lambda e, j=j, j2=j2: e.matmul(psE[:, 16 + 16 * j:32 + 16 * j], lhsT=cf(C_ONES), rhs=clf[:, j2, :],
                                                                  start=False, stop=(j2 == 7)), reads=["cstf", "clf"],
                             writes=["psE"], sig=(j2 == 7))
                S.op("dve", lambda e: e.tensor_copy(out=sufb[:].rearrange("p j c -> p (j c)"), in_=psE[:, 16:144]), reads=["psE"],
                     writes=["sufb"])

        def sample_block(l, slot):
            tb = 32
            rows = TS
            Vn = Vn_v()
            project_block(l, slot, 0, tb, rows, lambda h: qT[:, :, 128:128 + rows], ["qT"],
                          Vn_v, ["Vn"], 0)
            if l == 0:
                S.op("pe", lambda e: e.matmul(psE[0:64, 8:12], lhsT=cf(C_UB16, 64, 64), rhs=lps[0:64, 0:4], start=True, stop=True),
                     reads=["cstf", "lps"], writes=["psE"])
                S.op("dve", lambda e: e.tensor_copy(out=cpn[:, :], in_=psE[0:64, 8:12]), reads=["psE"], writes=["cpn"])
            for h in range(4):
                ob, on = o_banks[h % 2][0]
                if l == 0:
                    o_ap = ob[0:64, 0:129]
                    first = True
                    for s_ in range(4):
                        scb, scn = sc_banks[pcount["sc"] % 2]
                        pcount["sc"] += 1
                        pw = Pw8(s_)
                        pwn = "th%d" % (s_ // 2)
                        for j in range(8):
                            S.op("pe", lambda e, s_=s_, j=j, h=h, scb=scb: e.matmul(
                                scb[:, j * 16:(j + 1) * 16], lhsT=skT(s_, h, j * 128, (j + 1) * 128),
                                rhs=qT[:, h, s_ * 16:(s_ + 1) * 16], start=True, stop=True),
                                reads=["skT%d" % s_, "qT"], writes=[scn], sig=(j == 7))
                        for j in range(8):
                            S.op("act", lambda e, s_=s_, j=j, h=h, scb=scb, pw=pw: e.activation(
                                out=pw[:, j, s_ * 16:(s_ + 1) * 16], in_=scb[:, j * 16:(j + 1) * 16], func=AF.Exp, scale=SCALE,
                                bias=sufb[:, j, s_ * 4 + h:s_ * 4 + h + 1]), reads=[scn, "sufb"], writes=[pwn])
                        for j in range(8):
                            S.op("pe", lambda e, s_=s_, j=j, h=h, first=first, o_ap=o_ap, pw=pw: e.matmul(
                                o_ap, lhsT=pw[:, j, 0:64], rhs=sVc(s_, j, h, 129), start=first, stop=False),
                                reads=[pwn, "sVc%d" % s_], writes=[on], sig=(j == 7))
                            first = False
                    scb, scn = sc_banks[pcount["sc"] % 2]
                    pcount["sc"] += 1
                    S.op("pe", lambda e, h=h, scb=scb: e.matmul(scb[0:64, 0:64], lhsT=qT[:, h, 128:192], rhs=qT[:, h, 0:64],
                                                                start=True, stop=True), reads=["qT"], writes=[scn])
                    S.op("act", lambda e, h=h, scb=scb: e.activation(out=Pn[:, :], in_=scb[0:64, 0:64], func=AF.Exp, scale=SCALE,
                                                                     bias=cpn[:, h:h + 1]), reads=[scn, "cpn"], writes=["Pn"])
                    S.op("pool", lambda e: e.tensor_tensor(out=Pn[:, :], in0=Pn[:, :], in1=mask64, op=ALU.mult),
                         reads=["Pn", "cstb"], writes=["Pn"])
                    S.op("pe", lambda e, h=h, o_ap=o_ap: e.matmul(o_ap, lhsT=Pn[:, :], rhs=Vn[0:64, h, 0:129], start=False, stop=True),
                         reads=["Pn", "Vn"], writes=[on])
                    S.op("dve", lambda e, ob=ob: e.reciprocal(out=rden[0:64, 0:1], in_=ob[0:64, 128:129]), reads=[on],
                         writes=["rden0"])
                    S.op("dve", lambda e, h=h, ob=ob: e.scalar_tensor_tensor(
                        out=ybuf[0][0:64, h * 128:(h + 1) * 128], in0=ob[0:64, 0:128], scalar=rden[0:64, 0:1],
                        in1=gsil[0][0:64, h * 128:(h + 1) * 128], op0=ALU.mult, op1=ALU.mult),
                        reads=[on, "rden0", "gsil0"], writes=["y0_%d" % h])
                else:
                    pass
            if l == 1:
                def head_stream(slot, h):
                    ob, on = sb_obank[slot][0]
                    o_ap = ob[0:64, 0:128]
                    pv = {"n": 0}
                    total = 1 + 4 * 8

                    def hook0(ci, vbl, pso, pn, pst_=None):
                        S.op("act", lambda e: e.activation(out=aTn[:, :], in_=pst_[0:64, pso:pso + 64], func=AF.Identity),
                             reads=[pn], writes=["aTn"])
                        return [(aTn[:, :], "aTn")]
                    ch0 = dict(kT=qT[:, h, 128:192], W=64, names=["qT"], vblocks=[(Vn[0:64, h, 0:128], 64, "Vn")],
                               mask=(cf(C_MS, 64, 64), cf(C_MSC, 64, 64), 64))
                    st = {}
                    yield from sb_stream(slot, 64, qT[:, h, 0:64], ["qT"], [ch0], o_ap, on, pv, total, None, hook0, None, st)
                    car = st["carry"]
                    cr = carry0[:, slot:slot + 1]
                    yield ("dve", 0.1)
                    S.op("dve", lambda e: e.tensor_copy(out=cr, in_=car[0]), reads=[car[1]], writes=["carry0_%d" % slot])
                    for s_ in range(4):
                        def hook(ci, vbl, pso, pn, pst_=None, s_=s_):
                            base = 4 if ci == 0 else 0
                            pvw = pst_[:, pso:pso + 512].rearrange("p (j t) -> p j t", t=128)
                            S.op("act", lambda e: e.activation(out=aTw[s_][:, base:base + 4, s_ * 16:(s_ + 1) * 16],
                                                               in_=pvw[:, :, s_ * 16:(s_ + 1) * 16], func=AF.Identity),
                                 reads=[pn], writes=["aTw%d" % s_])
                            return [(aTw[s_][:, base + j, 0:64], "aTw%d" % s_) for j in range(4)]
                        chunks = []
                        for (lo, hi) in ((512, 1024), (0, 512)):
                            chunks.append(dict(kT=skT(s_, h, lo, hi), W=512, names=["skT%d" % s_],
                                               vblocks=[(sVc(s_, b, h, 128), 128, "sVc%d" % s_) for b in range(lo // 128, hi // 128)],
                                               mask=None))
                        yield from sb_stream(slot, 64, qT[:, h, 0:64], ["qT"], chunks, o_ap, on, pv, total,
                                             (cr, "carry0_%d" % slot), hook)
                    yield ("dve", 0.3)
                    S.op("dve", lambda e: e.tensor_tensor(out=ybuf[0][0:64, h * 128:(h + 1) * 128], in0=ob[0:64, 0:128],
                                                          in1=gsil[0][0:64, h * 128:(h + 1) * 128], op=ALU.mult),
                         reads=[on, "gsil0"], writes=["y0_%d" % h])
                run_streams([(lambda sl, h=h: head_stream(sl, h)) for h in range(4)], SBW)
            load_w_out(l)
            y_out([tb], [rows])

        def p_phase(l):
            load_act(0, xT_all, "xT", 0, 256)
            nt = 16
            for Q in range(nt):
                slot = Q % 2
                if Q < 15:
                    load_act(1 - slot, xT_all, "xT", (Q + 1) * 256, 256)
                else:
                    load_act(1 - slot, xT_all, "xT", T, TS)
                for sub in range(2):
                    tb = 2 * Q + sub
                    project_block(l, slot, sub, tb, 128,
                                  lambda h, tb=tb: regB[:, 0:4 * T].rearrange("p (h t) -> p h t", t=T)[:, :, tb * 128:(tb + 1) * 128],
                                  ["kT%d_%d" % (h, tb) for h in range(4)],
                                  lambda tb=tb: Vaug_blk(tb), ["V%d" % tb], sub * 128)
                    if l == 0:
                        cumsum_block(tb)
                if l == 0:
                    fox_tile(Q)
                else:
                    sb_tile(Q)
                y_out([2 * Q, 2 * Q + 1], [128, 128])
            if nt == 16:
                sample_stage(l)
                sample_block(l, 0)

        def o_phase(l):
            wq = load_w_in(1) if l == 0 else iter(())
            wv = w_out()
            load_act(0, yT_all, "yT", 0, 256)
            xsrc = xc0 if l == 0 else xres[1].ap()
            xdst = xres[l + 1].ap()

            def o_load(tb):
                rows = blk_rows(tb)
                i = tb % 2
                S.op("sp", lambda e: e.dma_start(out=xcb[i][0:rows, :], in_=xsrc[tb * 128:tb * 128 + rows, :]),
                     reads=["xres%d" % l], writes=["kf32%d" % i], chan="kf32%d" % i)

            for Q in range(17):
                slot = Q % 2
                if Q < 15:
                    load_act(1 - slot, yT_all, "yT", (Q + 1) * 256, 256)
                elif Q == 15:
                    load_act(1 - slot, yT_all, "yT", T, TS)
                next(wq, None)
                for sub in range(2 if Q < 16 else 1):
                    tb = 2 * Q + sub
                    rows = blk_rows(tb)
                    i = tb % 2
                    if tb == 0:
                        o_load(0)
                    if tb + 1 < NB:
                        o_load(tb + 1)
                    for c in range(16):
                        S.op("pe", lambda e, c=c, slot=slot, sub=sub, rows=rows: e.matmul(
                            psA[0:rows, :], lhsT=actT[slot][:, c, sub * 128:sub * 128 + rows], rhs=wv[:, c, :], start=(c == 0),
                            stop=(c == 15)), reads=["actT%d" % slot, "w_out"], writes=["psA"], sig=(c == 15))
                    S.op("dve", lambda e, i=i, rows=rows: e.tensor_tensor(out=xnb[i][0:rows, :], in0=psA[0:rows, :],
                                                                          in1=xcb[i][0:rows, :], op=ALU.add),
                         reads=["psA", "kf32%d" % i], writes=["vf32%d" % i])
                    S.op("sp", lambda e, i=i, tb=tb, rows=rows: e.dma_start(out=xdst[tb * 128:tb * 128 + rows, :],
                                                                              in_=xnb[i][0:rows, :]),
                         reads=["vf32%d" % i], writes=["xres%d" % (l + 1)], chan="vf32%d" % i)
                    norm_prep(xnb[i], "vf32%d" % i, tb, rows, l + 1, l == 0)
            for _ in wq:
                pass

        stage = 99
        if stage >= 2:
            p_phase(0)
        if stage >= 4:
            load_nw(1)
            o_phase(0)
            rstd_from_ssq("1")
        if stage >= 5:
            p_phase(1)
        if stage >= 7:
            load_nw(2)
            o_phase(1)
            rstd_from_ssq("2")
            x2 = xres[2].ap()
            fin_in = [(kf32[0], "kf320"), (kf32[1], "kf321"), (thb[0], "th0"), (thb[1], "th1")]
            fin_out = [(vf32[0], "vf320"), (vf32[1], "vf321"), (thb[2], "th2"), (gtmp, "gtmp")]

            def f_load(tb):
                rows = blk_rows(tb)
                buf, nm = fin_in[tb % 4]
                S.op("sp", lambda e: e.dma_start(out=buf[0:rows, 0:512], in_=x2[tb * 128:tb * 128 + rows, :]),
                     reads=["xres2"], writes=[nm], chan=nm)

            for t_ in range(3):
                f_load(t_)
            for tb in range(NB):
                rows = blk_rows(tb)
                if tb + 3 < NB:
                    f_load(tb + 3)
                ib, inm = fin_in[tb % 4]
                ob_, onm = fin_out[tb % 4]
                S.op("dve", lambda e, tb=tb, rows=rows, ib=ib, ob_=ob_: e.scalar_tensor_tensor(
                    out=ob_[0:rows, 0:512], in0=ib[0:rows, 0:512], scalar=rstd[0:rows, tb:tb + 1], in1=nw[2][0:rows, :],
                    op0=ALU.mult, op1=ALU.mult), reads=[inm, "rstd", "nw"], writes=[onm])
                S.op("sp", lambda e, tb=tb, rows=rows, ob_=ob_: e.dma_start(out=yout[tb * 128:tb * 128 + rows, :],
                                                                             in_=ob_[0:rows, 0:512]),
                     reads=[onm], writes=[], chan=onm)
        S.op("sp", lambda e: e.dma_start(out=lfout[0:T, :].rearrange("(b p) h -> p b h", p=128), in_=lfo[:, 0:32, :]),
             reads=["lfo"], writes=[], chan="lfo")
        S.op("sp", lambda e: e.dma_start(out=lfout[T:TT, :], in_=lfo[0:TS, 32, :]), reads=["lfo"], writes=[], chan="lfo")
        for sn, v in list(S.cnt.items()):
            if sn.startswith("d_"):
                S.prog["sp"].append(("wait", sn, v))

        sem_names = sorted(S.cnt.keys())
        sems = {}
        for sn in sem_names:
            sems[sn] = es.enter_context(nc.semaphore(sn))
        block = es.enter_context(nc.Block())

        def emit(eng_obj, name):
            for item in S.prog[name]:
                if item[0] == "wait":
                    eng_obj.wait_ge(sems[item[1]], item[2])
                else:
                    _, fn, sn, inc = item
                    ins = fn(eng_obj)
                    if sn is not None:
                        if inc is None:
                            ins.then_inc(sems[sn])
                        else:
                            ins.then_inc(sems[sn], inc)

        @block.sync
        def _(e):
            emit(e, "sp")

        @block.scalar
        def _(e):
            emit(e, "act")

        @block.vector
        def _(e):
            emit(e, "dve")

        @block.gpsimd
        def _(e):
            emit(e, "pool")

        @block.tensor
        def _(e):
            emit(e, "pe")
    return nc


def _consts():
    c = np.zeros((128, NCST), np.float32)
    k = np.arange(128)[:, None]
    m = np.arange(128)[None, :]
    c[:, C_U:C_U + 128] = (k <= m)
    c[:, C_E127:C_E127 + 128] = (k == 127)
    c[:, C_UB16:C_UB16 + 128] = (k <= m) & (k // 16 == m // 16)
    c[:, C_LS:C_LS + 128] = (k > m)
    c[:, C_ONES:C_ONES + 128] = 1.0
    c[:, C_MS:C_MS + 128] = (m < k) & (k // 16 == m // 16)
    c[:, C_MSC:C_MSC + 128] = 1.0 - ((m < k) & (k // 16 == m // 16))
    c[:, C_MLT:C_MLT + 128] = (m < k)
    c[:, C_MLTC:C_MLTC + 128] = 1.0 - (m < k)
    c[:, C_ID:C_ID + 128] = (k == m)
    c[:, C_MLE:C_MLE + 128] = (k <= m)
    c[:, C_M64:C_M64 + 128] = (k <= m) & (k // 16 == m // 16)
    return c


_NC_CACHE = {}


def kernel(x_prompt, x_sample, cache_fox_k, cache_fox_v, cache_fox_logf, cache_sb_k, cache_sb_v,
           norm_0, w_in_0, b_f_0, w_out_0, norm_1, w_in_1, w_out_1, norm_f):
    f = np.float32
    A = lambda a: np.ascontiguousarray(np.asarray(a, dtype=f))
    x_prompt, x_sample = A(x_prompt), A(x_sample)
    w_in_0, w_in_1, w_out_0, w_out_1 = A(w_in_0), A(w_in_1), A(w_out_0), A(w_out_1)
    caches = [A(cache_fox_k), A(cache_fox_v), A(cache_sb_k), A(cache_sb_v)]
    cache_fox_logf = A(cache_fox_logf)
    norms = [A(norm_0), A(norm_1), A(norm_f)]
    b_f_0 = A(b_f_0)
    if "nc" not in _NC_CACHE:
        _NC_CACHE["nc"] = build_program()
    nc = _NC_CACHE["nc"]
    cst = _consts()
    in_maps = []
    for core in range(8):
        b, g = core // 4, core % 4
        cs = slice(g * 512, (g + 1) * 512)
        xc0 = np.concatenate([x_prompt[b][:, cs], x_sample[4 * b:4 * b + 4].reshape(64, D)[:, cs]], axis=0)
        m = {"xc0": A(xc0), "cst": cst}
        for i, nm in enumerate(["nw0", "nw1", "nwf"]):
            m[nm] = A(np.broadcast_to(norms[i][cs][None, :], (128, 512)))
        m["win0"] = A(np.concatenate([w_in_0[:, j * D + g * 512: j * D + (g + 1) * 512] for j in range(4)]
                                     + [w_in_0[:, 4 * D + 4 * g: 4 * D + 4 * g + 4]], axis=1))
        m["win1"] = A(np.concatenate([w_in_1[:, j * D + g * 512: j * D + (g + 1) * 512] for j in range(4)], axis=1))
        m["wout0"] = A(w_out_0[:, cs])
        m["wout1"] = A(w_out_1[:, cs])
        m["bfb"] = A(np.broadcast_to(b_f_0[4 * g:4 * g + 4][None, :], (128, 4)))
        for nm, cch in zip(["cfk", "cfv", "csk", "csv"], caches):
            m[nm] = A(cch[4 * b:4 * b + 4, :, 4 * g:4 * g + 4, :].reshape(4, PAST, 512))
        m["cfl"] = A(cache_fox_logf[4 * b:4 * b + 4, :, 4 * g:4 * g + 4])
        in_maps.append(m)
    res = run_bass_kernel_spmd(nc, in_maps, core_ids=list(range(8)))
    R = res.results
    y_prompt = np.zeros((2, T, D), f)
    y_sample = np.zeros((8, 16, D), f)
    pk = [np.zeros((2, T, 16, 128), f) for _ in range(4)]
    sk = [np.zeros((8, 16, 16, 128), f) for _ in range(4)]
    plf = np.zeros((2, T, 16), f)
    slf = np.zeros((8, 16, 16), f)
    for core in range(8):
        b, g = core // 4, core % 4
        cs = slice(g * 512, (g + 1) * 512)
        r = R[core]
        y_prompt[b][:, cs] = r["yout"][:T]
        y_sample[4 * b:4 * b + 4][:, :, cs] = r["yout"][T:].reshape(4, 16, 512)
        for i, nm in enumerate(["kf", "vf", "ks", "vs"]):
            pk[i][b][:, 4 * g:4 * g + 4, :] = r[nm][:T].reshape(T, 4, 128)
            sk[i][4 * b:4 * b + 4][:, :, 4 * g:4 * g + 4, :] = r[nm][T:].reshape(4, 16, 4, 128)
        plf[b][:, 4 * g:4 * g + 4] = r["lf"][:T]
        slf[4 * b:4 * b + 4][:, :, 4 * g:4 * g + 4] = r["lf"][T:].reshape(4, 16, 4)
    return (y_prompt, y_sample, pk[0], pk[1], plf, pk[2], pk[3], sk[0], sk[1], slf, sk[2], sk[3])
```

```python
import numpy as np
import concourse.bass as bass
import concourse.mybir as mybir
from concourse.bass_utils import run_bass_kernel_spmd

F32 = mybir.dt.float32
BF16 = mybir.dt.bfloat16
AF = mybir.ActivationFunctionType
ALU = mybir.AluOpType

D = 2048
T = 4096
TS = 64
TT = T + TS
NB = 33
PAST = 1024
SCALE = 128 ** -0.5
EPS = 1e-6
GROUPS = [[0, 1, 2, 3], [4, 5, 6, 7]]
ENG = ("sp", "act", "dve", "pool", "pe")
FOXW = 1
SBW = 4

C_U, C_E127, C_UB16, C_LS, C_ONES, C_MLT, C_MLTC, C_MS, C_MSC, C_ID, C_MLE, C_M64 = [128 * i for i in range(12)]
NCST = 128 * 12


class Sched:
    def __init__(self):
        self.prog = {e: [] for e in ENG}
        self.cnt = {}
        self.waited = {}
        self.lastw = {}
        self.readers = {}

    def _need(self, eng, events):
        for sn, val in events:
            if sn.startswith("d_"):
                val = self.cnt[sn]
            if sn == "pe" and eng == "pe":
                continue
            key = (eng, sn)
            if self.waited.get(key, 0) >= val:
                continue
            self.waited[key] = val
            self.prog[eng].append(("wait", sn, val))

    def op(self, eng, fn, reads=(), writes=(), sig=True, chan=None, cc=None):
        ps_reads = [r for r in reads if r.startswith("ps") and r not in writes]
        if ps_reads:
            writes = list(writes) + ps_reads
        ev = []
        for r in reads:
            if r in self.lastw:
                ev.append(self.lastw[r])
        for w in writes:
            if w in self.lastw:
                ev.append(self.lastw[w])
            ev.extend(self.readers.get(w, {}).items())
        self._need(eng, ev)
        if cc is not None:
            sn = "c_" + cc
            self.cnt[sn] = 1
            myev = (sn, 1)
            self.prog[eng].append(("op", fn, sn, None))
        elif chan is not None:
            sn = "d_" + chan
            self.cnt[sn] = self.cnt.get(sn, 0) + 16
            myev = (sn, self.cnt[sn])
            self.prog[eng].append(("op", fn, sn, 16))
        elif sig:
            self.cnt[eng] = self.cnt.get(eng, 0) + 1
            myev = (eng, self.cnt[eng])
            self.prog[eng].append(("op", fn, eng, 1))
        else:
            myev = (eng, self.cnt.get(eng, 0) + 1)
            self.prog[eng].append(("op", fn, None, 0))
        for r in reads:
            d = self.readers.setdefault(r, {})
            d[myev[0]] = max(d.get(myev[0], 0), myev[1])
        for w in writes:
            self.lastw[w] = myev
            self.readers[w] = {}


def build_program():
    nc = bass.Bass("TRN2", target_bir_lowering=False)
    S = Sched()

    def din(name, shape, dt=F32):
        return nc.dram_tensor(name, shape, dt, kind="ExternalInput").ap()

    def dout(name, shape, dt=F32):
        return nc.dram_tensor(name, shape, dt, kind="ExternalOutput").ap()

    xc0 = din("xc0", [TT, 512])
    nwd = [din("nw0", [128, 512]), din("nw1", [128, 512]), din("nwf", [128, 512])]
    wind = [din("win0", [D, 2052]), din("win1", [D, 2048])]
    woutd = [din("wout0", [D, 512]), din("wout1", [D, 512])]
    bfd = din("bfb", [128, 4])
    cstd = din("cst", [128, NCST])
    cfk = din("cfk", [4, PAST, 512])
    cfv = din("cfv", [4, PAST, 512])
    csk = din("csk", [4, PAST, 512])
    csv = din("csv", [4, PAST, 512])
    cfl = din("cfl", [4, PAST, 4])

    yout = dout("yout", [TT, 512])
    kvout = [[dout("kf", [TT, 512]), dout("vf", [TT, 512])], [dout("ks", [TT, 512]), dout("vs", [TT, 512])]]
    lfout = dout("lf", [TT, 4])

    PW = [1024, 1024, 1024, 1024, TS]
    xT_loc = [nc.dram_tensor("xT_loc%d" % p, [512, PW[p]], BF16) for p in range(5)]
    xT_all = [nc.dram_tensor("xT_all%d" % p, [D, PW[p]], BF16) for p in range(5)]
    yT_loc = [nc.dram_tensor("yT_loc%d" % p, [512, PW[p]], BF16) for p in range(5)]
    yT_all = [nc.dram_tensor("yT_all%d" % p, [D, PW[p]], BF16) for p in range(5)]

    def piece_of(tb):
        return (tb // 8, (tb % 8) * 128) if tb < 32 else (4, 0)
    ssq_loc = nc.dram_tensor("ssq_loc", [128, NB], F32)
    ssq_all = nc.dram_tensor("ssq_all", [512, NB], F32)
    xres = [None, nc.dram_tensor("x1s", [TT, 512], F32), nc.dram_tensor("x2s", [TT, 512], F32)]

    from contextlib import ExitStack
    es = ExitStack()

    def sb(name, shape, dt):
        return es.enter_context(nc.sbuf_tensor(name, shape, dt))

    def ps(name, shape, dt):
        return es.enter_context(nc.psum_tensor(name, shape, dt))

    with es:
        cstf = sb("cstf", [128, C_ID + 128], F32)
        cstb = sb("cstb", [128, 3 * 128], BF16)
        zeros = sb("zeros", [128, 512], BF16)
        nwt = sb("nwt", [128, 512], F32)
        nw = [nwt, nwt, nwt]
        bfb = sb("bfbs", [128, 4], F32)
        w_in = sb("w_in", [128, 16, 2052], BF16)
        regB = sb("regB", [128, 4 * T + 32 * 4 * 130], BF16)
        actT = [sb("actT0", [128, 16, 256], BF16), sb("actT1", [128, 16, 256], BF16)]
        ssq_part = sb("ssq_part", [128, NB], F32)
        ssq4 = sb("ssq4", [128, 4, NB], F32)
        ssum = sb("ssum", [128, NB], F32)
        rstd = sb("rstd", [128, NB], F32)
        nrstd = sb("nrstd", [128, NB], F32)
        hrstd = sb("hrstd", [128, NB], F32)
        cpall = sb("cpall", [128, NB, 8], F32)
        lfo = sb("lfo", [128, NB, 4], F32)
        lps = sb("lps", [128, 4], F32)
        xl = sb("xl", [128, 4], F32)
        el = sb("el", [128, 4], F32)
        q_tok = sb("q_tok", [128, 512], BF16)
        k_tok = sb("k_tok", [128, 512], BF16)
        kf32b_big = sb("kf32b", [128, 516], F32)
        kf32 = [sb("kf32a", [128, 512], F32)[:, :], kf32b_big[:, 0:512]]
        vf32 = [sb("vf32a", [128, 512], F32), sb("vf32b", [128, 512], F32)]
        gsil = [sb("gsil0", [128, 512], F32), sb("gsil1", [128, 512], F32)]
        gtmp = sb("gtmp", [128, 512], F32)
        qT = sb("qT", [128, 4, 256], BF16)
        ybuf = [sb("y0", [128, 512], BF16), sb("y1", [128, 512], BF16)]
        yTs = [sb("yTs0", [128, 4, 128], BF16), sb("yTs1", [128, 4, 128], BF16)]
        Pball = sb("Pball", [128, 6, 256], BF16)
        Pb = [Pball[:, i, :] for i in range(6)]
        biasT = sb("biasT", [128, 32, 4], F32)
        rden = sb("rden", [128, 4], F32)
        thb = [sb("th%d" % i, [128, 512], F32) for i in range(3)] + [gtmp]
        Rext = [sb("Rx%d" % i, [128, 516], F32) for i in range(3)] + [kf32b_big]
        ab = [sb("a%d" % i, [128, 512], BF16) for i in range(3)] + [k_tok]
        aTb = [sb("aT0", [128, 4, 128], BF16), sb("aT1", [128, 4, 128], BF16),
               Pball[:, 0:2, :].rearrange("p a (b t) -> p (a b) t", t=128),
               q_tok[:, :].rearrange("p (j t) -> p j t", t=128)]
        xcb = kf32
        xnb = vf32
        xtb = q_tok
        sqj = gtmp
        xTs = yTs
        Pw = [sb("Pw%d" % i, [128, 64], BF16) for i in range(4)]
        Pn = sb("Pn", [64, 64], BF16)
        clf = sb("clf", [128, 8, 16], F32)
        sufb = sb("sufb", [128, 8, 16], F32)
        cpn = sb("cpn", [64, 4], F32)
        aTw = [sb("aTw%d" % i, [128, 8, 64], BF16) for i in range(4)]
        aTn = sb("aTn", [64, 64], BF16)
        Vn_t = sb("Vn_t", [128, 520], BF16)
        carry0 = sb("carry0", [64, 4], F32)
        cbias = sb("cbias", [128, 2], F32)

        psA = ps("psA", [128, 512], F32)
        psB = ps("psB", [128, 512], F32)
        psC = ps("psC", [128, 512], F32)
        psD = ps("psD", [128, 512], F32)
        psE = ps("psE", [128, 512], F32)
        psF = ps("psF", [128, 512], F32)
        psG = ps("psG", [128, 512], F32)
        pst = ps("pst", [128, 1024], BF16)

        KT_OFF = 0
        V_OFF = 4 * T

        def kT(h, lo, hi):
            return regB[:, KT_OFF + h * T + lo: KT_OFF + h * T + hi]

        def Vaug(blk, h, n):
            o = V_OFF + (blk * 4 + h) * 130
            return regB[:, o:o + n]

        def Vaug_blk(blk):
            o = V_OFF + blk * 4 * 130
            return regB[:, o:o + 520].rearrange("p (h e) -> p h e", e=130)

        def w_out():
            o = 4096 + 8 * 520 + 4096
            return regB[:, o:o + 16 * 512].rearrange("p (c n) -> p c n", n=512)

        PST_ALL = ["pstbank"]
        pst_full = pst
        psE_bf = psE[:, :].bitcast(BF16)
        pst_f32 = pst[:, :].bitcast(F32)
        psG_bf = psG[:, :].bitcast(BF16)
        REGB_ALL = ["kT%d_%d" % (h, b) for h in range(4) for b in range(32)] + ["V%d" % b for b in range(32)]

        ident = cstb[:, 0:128]
        maskLE = cstb[:, 128:256]
        mask64 = cstb[0:64, 256:320]

        def cf(c0, n=128, rows=128):
            return cstf[0:rows, c0:c0 + n]

        S.op("sp", lambda e: e.dma_start(out=cstf[:], in_=cstd[:, 0:C_ID + 128]), writes=["cstf"], chan="cst")
        S.op("pool", lambda e: e.dma_start(out=cstb[:], in_=cstd[:, C_ID:C_ID + 384]), writes=["cstb"], chan="cstb")
        def load_nw(i):
            S.op("sp", lambda e: e.dma_start(out=nwt[:], in_=nwd[i][:, :]), writes=["nw"], chan="cst")
        load_nw(0)
        S.op("sp", lambda e: e.dma_start(out=bfb[:], in_=bfd[:, :]), writes=["bfb"], chan="cst")
        S.op("dve", lambda e: e.memset(zeros[:], 0.0), writes=["zeros"])
        S.op("dve", lambda e: e.memset(cbias[:, 0:1], EPS), writes=["cbias"])
        S.op("dve", lambda e: e.memset(cbias[:, 1:2], 1.0), writes=["cbias"])
        S.op("dve", lambda e: e.memset(ssq_part[:], 1.0), writes=["ssq_part"])
        S.op("dve", lambda e: e.memset(cpall[:], 0.0), writes=["cpall"])
        S.op("dve", lambda e: e.memset(lfo[:], 0.0), writes=["lfo"])
        for i in range(2):
            S.op("pool", lambda e, i=i: e.memset(thb[i][:], 0.0), writes=["th%d" % i])
        for i in range(4):
            S.op("pool", lambda e, i=i: e.memset(aTw[i][:], 0.0), writes=["aTw%d" % i])

        if False:
            for nm_, ap_ in [("win0", wind[0][0:128, 0:512]), ("win1", wind[1][0:128, 0:512]), ("wout0", woutd[0][0:128, :]),
                             ("wout1", woutd[1][0:128, :]), ("cfk", cfk[0][0:128, :]), ("cfv", cfv[0][0:128, :]),
                             ("csk", csk[0][0:128, :]), ("csv", csv[0][0:128, :])]:
                S.op("sp", lambda e, ap_=ap_: e.dma_start(out=gtmp[:], in_=ap_), writes=["gtmp"], chan="dbg")
            S.op("sp", lambda e: e.dma_start(out=gtmp[:, 0:4], in_=cfl[0][0:128, :]), writes=["gtmp"], chan="dbg")

        def load_w_in(l):
            ncol = 2052 if l == 0 else 2048
            src = wind[l].rearrange("(c p) n -> p c n", p=128)
            for c in range(16):
                yield
                si = c % 3
                o = 20544 + si * 4104
                stg = regB[:, o:o + 4104].bitcast(F32)
                S.op("sp", lambda e, c=c, stg=stg: e.dma_start(out=stg[:, 0:ncol], in_=src[:, c, :]),
                     writes=["wst%d" % si] + (REGB_ALL + ["sVc%d" % q for q in range(4)] + ["skT%d" % q for q in range(4)]
                                              if c < 3 else []), chan="wst%d" % si)
                if c % 2 == 0:
                    S.op("act", lambda e, c=c, stg=stg: e.activation(out=w_in[:, c, 0:ncol], in_=stg[:, 0:ncol], func=AF.Identity),
                         reads=["wst%d" % si] + (REGB_ALL if c >= 13 else []), writes=["w_in"])
                else:
                    S.op("dve", lambda e, c=c, stg=stg: e.tensor_copy(out=w_in[:, c, 0:ncol], in_=stg[:, 0:ncol]),
                         reads=["wst%d" % si] + (REGB_ALL if c >= 13 else []), writes=["w_in"])

        def load_w_out(l):
            src = woutd[l].rearrange("(c p) n -> p c n", p=128)
            wv = w_out()
            for c in range(0, 16, 4):
                S.op("pool", lambda e, c=c: e.dma_start(out=wv[:, c:c + 4, :], in_=src[:, c:c + 4, :]),
                     writes=["w_out"] + REGB_ALL + ["sVc%d" % q for q in range(4)] + ["skT%d" % q for q in range(4)], chan="wout")

        def blk_rows(tb):
            return 128 if tb < 32 else TS

        cnt = {"xts": 0, "tr": 0}
        deferred_g = []

        def norm_prep(xblk, xres_name, tb, rows, nwi, want_xT, want_gather=True):
            S.op("act", lambda e: e.activation(out=sqj[0:rows, :], in_=xblk[0:rows, :], func=AF.Square,
                                               accum_out=ssq_part[0:rows, tb:tb + 1]),
                 reads=[xres_name], writes=["gtmp", "ssq_part"])
            if not want_xT:
                return
            S.op("dve", lambda e: e.tensor_tensor(out=xtb[0:rows, :], in0=xblk[0:rows, :], in1=nw[nwi][0:rows, :],
                                                  op=ALU.mult),
                 reads=[xres_name, "nw"], writes=["q_tok"])
            for c in range(4):
                S.op("pe", lambda e, c=c: e.transpose(out=pst[:, c * 128:c * 128 + rows],
                                                      in_=xtb[0:rows, c * 128:(c + 1) * 128],
                                                      identity=ident[0:rows, 0:rows]),
                     reads=["q_tok", "cstb"], writes=PST_ALL, sig=(c == 3))
            i = cnt["tr"] % 2
            cnt["tr"] += 1
            pv = pst[:, 0:512].rearrange("p (c t) -> p c t", t=128)
            S.op("act", lambda e: e.activation(out=xTs[i][:, :, 0:rows], in_=pv[:, :, 0:rows], func=AF.Identity),
                 reads=PST_ALL, writes=["yTs%d" % i])
            pc, c0 = piece_of(tb)
            dst = xT_loc[pc].ap().rearrange("(c p) t -> p c t", p=128)
            S.op("sp", lambda e: e.dma_start(out=dst[:, :, c0:c0 + rows], in_=xTs[i][:, :, 0:rows]),
                 reads=["yTs%d" % i], writes=["xT_loc%d" % pc], chan="yTs%d" % i)
            if want_gather and (tb % 8 == 7 or tb == 32):
                if pc >= 3:
                    deferred_g.append(pc)
                else:
                    gather(xT_loc[pc], xT_all[pc], "xT", pc)

        def rstd_from_ssq(tag):
            S.op("sp", lambda e: e.dma_start(out=ssq_loc.ap(), in_=ssq_part[:]), reads=["ssq_part"],
                 writes=["ssq_loc"], chan="ssq")
            S.op("pool", lambda e: e.collective_compute("AllGather", ALU.bypass, replica_groups=GROUPS,
                                                        ins=[ssq_loc.ap().opt()], outs=[ssq_all.ap().opt()]),
                 reads=["ssq_loc"], writes=["ssq_all"], cc="ssq" + tag)
            S.op("sp", lambda e: e.dma_start(out=ssq4[:], in_=ssq_all.ap().rearrange("(r p) b -> p r b", p=128)),
                 reads=["ssq_all"], writes=["ssq4"], chan="ssq")
            S.op("dve", lambda e: e.tensor_tensor(out=ssum[:], in0=ssq4[:, 0, :], in1=ssq4[:, 1, :], op=ALU.add),
                 reads=["ssq4"], writes=["ssum"])
            S.op("dve", lambda e: e.tensor_tensor(out=ssum[:], in0=ssum[:], in1=ssq4[:, 2, :], op=ALU.add),
                 reads=["ssq4", "ssum"], writes=["ssum"])
            S.op("dve", lambda e: e.tensor_tensor(out=ssum[:], in0=ssum[:], in1=ssq4[:, 3, :], op=ALU.add),
                 reads=["ssq4", "ssum"], writes=["ssum"])
            S.op("act", lambda e: e.activation(out=ssum[:], in_=ssum[:], func=AF.Ln, scale=1.0 / D, bias=cbias[:, 0:1]),
                 reads=["ssum", "cbias"], writes=["ssum"])
            S.op("act", lambda e: e.activation(out=rstd[:], in_=ssum[:], func=AF.Exp, scale=-0.5),
                 reads=["ssum"], writes=["rstd"])
            S.op("dve", lambda e: e.tensor_scalar(out=nrstd[:], in0=rstd[:], scalar1=-1.0, scalar2=None, op0=ALU.mult),
                 reads=["rstd"], writes=["nrstd"])
            S.op("dve", lambda e: e.tensor_scalar(out=hrstd[:], in0=rstd[:], scalar1=0.5, scalar2=None, op0=ALU.mult),
                 reads=["rstd"], writes=["hrstd"])
            S.op("dve", lambda e: e.memset(ssq_part[:], 1.0), reads=[], writes=["ssq_part"])
            for pc in deferred_g:
                gather(xT_loc[pc], xT_all[pc], "xT", pc)
            del deferred_g[:]

        gcount = {"n": 0}

        def gather(loc, allt, rname, pc):
            gcount["n"] += 1
            S.op("pool", lambda e: e.collective_compute("AllGather", ALU.bypass, replica_groups=GROUPS,
                                                        ins=[loc.ap().opt()], outs=[allt.ap().opt()]),
                 reads=["%s_loc%d" % (rname, pc)], writes=["%s_all%d" % (rname, pc)], cc="g%d" % gcount["n"])

        wq = load_w_in(0)

        def n0_load(tb):
            rows = blk_rows(tb)
            i = tb % 2
            S.op("sp", lambda e: e.dma_start(out=xcb[i][0:rows, :], in_=xc0[tb * 128:tb * 128 + rows, :]),
                 writes=["kf32%d" % i], chan="kf32%d" % i)

        for tb in range(NB):
            rows = blk_rows(tb)
            i = tb % 2
            if tb % 2 == 0:
                next(wq, None)
            if tb == 0:
                n0_load(0)
            if tb + 1 < NB:
                n0_load(tb + 1)
            norm_prep(xcb[i], "kf32%d" % i, tb, rows, 0, True)
        for _ in wq:
            pass
        rstd_from_ssq("0")

        def load_act(slot, src_all, rname, c0, ncols):
            pc, lc = (c0 // 1024, c0 % 1024) if c0 < T else (4, 0)
            src = src_all[pc].ap().rearrange("(c p) t -> p c t", p=128)
            S.op("sp", lambda e: e.dma_start(out=actT[slot][:, :, 0:ncols], in_=src[:, :, lc:lc + ncols]),
                 reads=["%s_all%d" % (rname, pc)], writes=["actT%d" % slot], chan="actT%d" % slot)

        def transposes_to(src_tok, src_name, rows, dst_fn, dst_names, evac_eng):
            for h in range(4):
                S.op("pe", lambda e, h=h: e.transpose(out=pst[:, h * 128:h * 128 + rows],
                                                      in_=src_tok[0:rows, h * 128:(h + 1) * 128],
                                                      identity=ident[0:rows, 0:rows]),
                     reads=[src_name, "cstb"], writes=PST_ALL, sig=(h == 3))
            pv4 = pst[:, 0:512].rearrange("p (h t) -> p h t", t=128)
            if evac_eng == "act":
                S.op("act", lambda e: e.activation(out=dst_fn(None), in_=pv4[:, :, 0:rows], func=AF.Identity),
                     reads=PST_ALL, writes=dst_names)
            else:
                S.op("dve", lambda e: e.tensor_copy(out=dst_fn(None), in_=pv4[:, :, 0:rows]), reads=PST_ALL, writes=dst_names)

        def project_block(l, slot, sub, tb, rows, kT_dst, kT_names, v_dst_fn, v_names, qcol0):
            banks = [psA, psB, psC, psD]
            bn = ["psA", "psB", "psC", "psD"]
            for c in range(16):
                lhsT = actT[slot][:, c, sub * 128: sub * 128 + rows]
                for j in range(4):
                    S.op("pe", lambda e, c=c, j=j, lhsT=lhsT: e.matmul(banks[j][0:rows, :], lhsT=lhsT,
                                                                        rhs=w_in[:, c, j * 512:(j + 1) * 512],
                                                                        start=(c == 0), stop=(c == 15)),
                         reads=["actT%d" % slot, "w_in"], writes=[bn[j]], sig=(c == 15))
                if l == 0:
                    S.op("pe", lambda e, c=c, lhsT=lhsT: e.matmul(psE[0:rows, 0:4], lhsT=lhsT,
                                                                   rhs=w_in[:, c, 2048:2052],
                                                                   start=(c == 0), stop=(c == 15)),
                         reads=["actT%d" % slot, "w_in"], writes=["psE"], sig=(c == 15))
            rs = rstd[0:rows, tb:tb + 1]
            nrs = nrstd[0:rows, tb:tb + 1]
            i = tb % 2
            S.op("dve", lambda e: e.tensor_scalar(out=q_tok[0:rows, :], in0=psA[0:rows, :], scalar1=rs, scalar2=None,
                                                  op0=ALU.mult),
                 reads=["psA", "rstd"], writes=["q_tok"])
            S.op("act", lambda e: e.activation(out=kf32[i][0:rows, :], in_=psB[0:rows, :], func=AF.Identity, scale=rs),
                 reads=["psB", "rstd"], writes=["kf32%d" % i])
            S.op("dve", lambda e: e.tensor_scalar(vf32[i][0:rows, :], psC[0:rows, :], rs, None, ALU.mult),
                 reads=["psC", "rstd"], writes=["vf32%d" % i])
            S.op("sp", lambda e: e.dma_start(out=kvout[l][0][tb * 128:tb * 128 + rows, :], in_=kf32[i][0:rows, :]),
                 reads=["kf32%d" % i], writes=[], chan="kf32%d" % i)
            S.op("sp", lambda e: e.dma_start(out=kvout[l][1][tb * 128:tb * 128 + rows, :], in_=vf32[i][0:rows, :]),
                 reads=["vf32%d" % i], writes=[], chan="vf32%d" % i)
            S.op("act", lambda e: e.activation(out=k_tok[0:rows, :], in_=psB[0:rows, :], func=AF.Identity, scale=rs),
                 reads=["psB", "rstd"], writes=["k_tok"])
            vd = v_dst_fn()
            S.op("dve", lambda e: e.tensor_copy(out=vd[0:rows, :, 0:128],
                                                in_=vf32[i][0:rows, :].rearrange("p (h d) -> p h d", d=128)),
                 reads=["vf32%d" % i], writes=v_names)
            S.op("pool", lambda e: e.memset(vd[0:rows, :, 128:129], 1.0), reads=[], writes=v_names)
            if l == 0:
                S.op("act", lambda e: e.activation(out=gtmp[0:rows, :], in_=psD[0:rows, :], func=AF.Exp, scale=nrs),
                     reads=["psD", "nrstd"], writes=["gtmp"])
                S.op("act", lambda e: e.activation(out=gtmp[0:rows, :], in_=gtmp[0:rows, :], func=AF.Ln, bias=cbias[0:rows, 1:2]),
                     reads=["gtmp", "cbias"], writes=["gtmp"])
                S.op("act", lambda e: e.activation(out=gtmp[0:rows, :], in_=gtmp[0:rows, :], func=AF.Exp, scale=-1.0),
                     reads=["gtmp"], writes=["gtmp"])
            else:
                S.op("act", lambda e: e.activation(out=gtmp[0:rows, :], in_=psD[0:rows, :], func=AF.Sigmoid, scale=rs),
                     reads=["psD", "rstd"], writes=["gtmp"])
            S.op("dve", lambda e: e.scalar_tensor_tensor(out=gsil[sub][0:rows, :], in0=psD[0:rows, :], scalar=rs,
                                                         in1=gtmp[0:rows, :], op0=ALU.mult, op1=ALU.mult),
                 reads=["psD", "rstd", "gtmp"], writes=["gsil%d" % sub])
            if l == 0:
                S.op("dve", lambda e: e.scalar_tensor_tensor(out=xl[0:rows, :], in0=psE[0:rows, 0:4], scalar=rs,
                                                             in1=bfb[0:rows, :], op0=ALU.mult, op1=ALU.add),
                     reads=["psE", "rstd", "bfb"], writes=["xl"])
                S.op("act", lambda e: e.activation(out=el[0:rows, :], in_=xl[0:rows, :], func=AF.Exp, scale=-1.0),
                     reads=["xl"], writes=["el"])
                S.op("act", lambda e: e.activation(out=lps[0:rows, :], in_=el[0:rows, :], func=AF.Ln, bias=cbias[0:rows, 1:2]),
                     reads=["el", "cbias"], writes=["lps"])
                S.op("pool", lambda e: e.tensor_scalar(out=lfo[0:rows, tb, :], in0=lps[0:rows, :], scalar1=-1.0,
                                                       scalar2=0.0, op0=ALU.mult, op1=ALU.add),
                     reads=["lps"], writes=["lfo"])
            transposes_to(q_tok, "q_tok", rows, lambda h: qT[:, :, qcol0:qcol0 + rows], ["qT"], "dve")
            transposes_to(k_tok, "k_tok", rows, kT_dst, kT_names, "act")

        def cumsum_block(tb):
            first = (tb == 0)
            S.op("pe", lambda e: e.matmul(psE[:, 8:12], lhsT=cf(C_U), rhs=lps[:, 0:4], start=True, stop=first),
                 reads=["cstf", "lps"], writes=["psE"], sig=first)
            if not first:
                S.op("pe", lambda e: e.matmul(psE[:, 8:12], lhsT=cf(C_E127), rhs=cpall[:, tb - 1, 0:4], start=False,
                                              stop=True),
                     reads=["cstf", "cpall"], writes=["psE"], sig=False)
                S.op("pe", lambda e: e.matmul(psE[:, 12:16], lhsT=cf(C_E127), rhs=cpall[:, tb - 1, 0:4], start=True,
                                              stop=True),
                     reads=["cstf", "cpall"], writes=["psE"])
                S.op("dve", lambda e: e.tensor_copy(out=cpall[:, tb, 0:8], in_=psE[:, 8:16]), reads=["psE"],
                     writes=["cpall"])
            else:
                S.op("dve", lambda e: e.tensor_copy(out=cpall[:, tb, 0:4], in_=psE[:, 8:12]), reads=["psE"],
                     writes=["cpall"])

        sc_banks = [(psA, "psA"), (psB, "psB")]
        o_banks = [((psC, "psC"), (psD, "psD")), ((psF, "psF"), (psG, "psG"))]

        def y_out(tbs, rows_l):
            for sub, tb in enumerate(tbs):
                rows = rows_l[sub]
                for c in range(4):
                    S.op("pe", lambda e, c=c, sub=sub, rows=rows: e.transpose(out=pst[:, c * 128:c * 128 + rows],
                                                                               in_=ybuf[sub][0:rows, c * 128:(c + 1) * 128],
                                                                               identity=ident[0:rows, 0:rows]),
                         reads=["y%d_%d" % (sub, hh) for hh in range(4)] + ["cstb"], writes=PST_ALL, sig=(c == 3))
                i = cnt["tr"] % 2
                cnt["tr"] += 1
                pv = pst[:, 0:512].rearrange("p (c t) -> p c t", t=128)
                S.op("act", lambda e, i=i, rows=rows, pv=pv: e.activation(out=yTs[i][:, :, 0:rows], in_=pv[:, :, 0:rows],
                                                                           func=AF.Identity),
                     reads=PST_ALL, writes=["yTs%d" % i])
                pc, c0 = piece_of(tb)
                dst = yT_loc[pc].ap().rearrange("(c p) t -> p c t", p=128)
                S.op("sp", lambda e, i=i, rows=rows, c0=c0, dst=dst: e.dma_start(out=dst[:, :, c0:c0 + rows],
                                                                                  in_=yTs[i][:, :, 0:rows]),
                     reads=["yTs%d" % i], writes=["yT_loc%d" % pc], chan="yTs%d" % i)
                if tb % 8 == 7 or tb == 32:
                    gather(yT_loc[pc], yT_all[pc], "yT", pc)

        pcount = {"p": 0, "sc": 0, "ch": 0}

        def run_streams(makers, width):
            pending = list(makers)
            active = []
            free = list(range(width))
            eng_free = {}
            now = 0.0
            while pending or active:
                while pending and free:
                    sl = free.pop(0)
                    g = pending.pop(0)(sl)
                    try:
                        nxt = next(g)
                    except StopIteration:
                        free.append(sl)
                        continue
                    active.append({"g": g, "sl": sl, "ready": now, "nxt": nxt})
                if not active:
                    continue
                best = min(active, key=lambda a: max(eng_free.get(a["nxt"][0], 0.0), a["ready"]))
                eng, dur = best["nxt"]
                start = max(eng_free.get(eng, 0.0), best["ready"])
                end = start + dur
                eng_free[eng] = end
                best["ready"] = end + 0.25
                try:
                    best["nxt"] = next(best["g"])
                except StopIteration:
                    active.remove(best)
                    free.append(best["sl"])
                    now = end

        def fox_stream(slot, Q, h):
            nkb = 2 * Q + 2
            ob = o_banks[slot]
            fsets = [[(psA, "psA"), (psB, "psB"), (psE, "psE")], [(pst_f32, "pstbank"), (psF, "psF"), (psG, "psG")]]
            batches = [list(range(k0, min(k0 + 3, nkb))) for k0 in range(0, nkb, 3)]

            def emit_qk(b):
                for i, kb in enumerate(batches[b]):
                    qlo = 128 if kb == nkb - 1 else 0
                    scb, scn = fsets[b % 2][i]
                    S.op("pe", lambda e, kb=kb, qlo=qlo, scb=scb: e.matmul(
                        scb[:, qlo:256], lhsT=kT(h, kb * 128, (kb + 1) * 128), rhs=qT[:, h, qlo:256], start=True, stop=True),
                        reads=["kT%d_%d" % (h, kb), "qT"], writes=[scn])

            yield ("pe", 0.4)
            emit_qk(0)
            for b, kbs in enumerate(batches):
                if b + 1 < len(batches):
                    emit_qk(b + 1)
                for i, kb in enumerate(kbs):
                    qlo = 128 if kb == nkb - 1 else 0
                    scb, scn = fsets[b % 2][i]
                    pi = 3 * (b % 2) + i
                    S.op("act", lambda e, kb=kb, qlo=qlo, pi=pi, scb=scb: e.activation(
                        out=Pb[pi][:, qlo:256], in_=scb[:, qlo:256], func=AF.Exp, scale=SCALE, bias=biasT[:, kb, h:h + 1]),
                        reads=[scn, "biasT"], writes=["P%d" % pi])
                    if kb >= nkb - 2:
                        dq = 0 if kb == nkb - 2 else 128
                        S.op("pool", lambda e, pi=pi, dq=dq: e.tensor_tensor(out=Pb[pi][:, dq:dq + 128],
                                                                             in0=Pb[pi][:, dq:dq + 128], in1=maskLE,
                                                                             op=ALU.mult),
                             reads=["P%d" % pi, "cstb"], writes=["P%d" % pi])
                for i, kb in enumerate(kbs):
                    pi = 3 * (b % 2) + i
                    for sub in range(2):
                        if kb == nkb - 1 and sub == 0:
                            continue
                        last = (kb == nkb - 2) if sub == 0 else (kb == nkb - 1)
                        S.op("pe", lambda e, kb=kb, sub=sub, pi=pi, last=last: e.matmul(
                            ob[sub][0][:, 0:129], lhsT=Pb[pi][:, sub * 128:(sub + 1) * 128], rhs=Vaug(kb, h, 129),
                            start=(kb == 0), stop=last),
                            reads=["P%d" % pi, "V%d" % kb], writes=[ob[sub][1]], sig=(sub == 1 or kb == nkb - 2))
            for sub in range(2):
                S.op("dve", lambda e, sub=sub: e.reciprocal(out=rden[:, 2 * slot + sub:2 * slot + sub + 1],
                                                            in_=ob[sub][0][:, 128:129]),
                     reads=[ob[sub][1]], writes=["rden%d" % slot])
                S.op("dve", lambda e, sub=sub: e.scalar_tensor_tensor(
                    out=ybuf[sub][:, h * 128:(h + 1) * 128], in0=ob[sub][0][:, 0:128],
                    scalar=rden[:, 2 * slot + sub:2 * slot + sub + 1],
                    in1=gsil[sub][:, h * 128:(h + 1) * 128], op0=ALU.mult, op1=ALU.mult),
                    reads=[ob[sub][1], "rden%d" % slot, "gsil%d" % sub], writes=["y%d_%d" % (sub, h)])

        def fox_tile(Q):
            nkb = 2 * Q + 2
            for h in range(4):
                S.op("dve", lambda e, h=h: e.tensor_scalar(out=biasT[:, 0:nkb, h], in0=cpall[:, 0:nkb, h],
                                                           scalar1=cpall[:, 2 * Q + 1, 4 + h:5 + h], scalar2=None,
                                                           op0=ALU.subtract),
                     reads=["cpall"], writes=["biasT"])
            run_streams([(lambda sl, h=h: fox_stream(sl, Q, h)) for h in range(4)], FOXW)

        def sb_stream(slot, qrows, qT_ap, qT_names, chunks, o_ap, o_name, pv, total_pv, carry_in=None, aT_hook=None,
                      fin=None, out_state=None):
            prev = carry_in
            th, Rx, a_ = thb[slot], Rext[slot], ab[slot]
            tn = ["th0", "th1", "th2", "gtmp"][slot]
            rn = ["Rx0", "Rx1", "Rx2", "kf321"][slot]
            an = ["a0", "a1", "a2", "k_tok"][slot]
            aTnames = [["aT0"], ["aT1"], ["P0", "P1"], ["q_tok"]][slot]
            scb, scn = [(psA, "psA"), (psB, "psB"), (psE, "psE"), (psG, "psG")][slot]
            pst, pn, pso = scb[:, :].bitcast(BF16), scn, 0
            for ci, ch in enumerate(chunks):
                W = ch["W"]
                yield ("pe", 0.45)
                S.op("pe", lambda e, ch=ch, W=W, scb=scb: e.matmul(scb[0:qrows, 0:W], lhsT=qT_ap, rhs=ch["kT"], start=True, stop=True),
                     reads=qT_names + ch["names"], writes=[scn])
                yield ("act", 0.65)
                S.op("act", lambda e, W=W, scb=scb: e.activation(out=th[0:qrows, 0:W], in_=scb[0:qrows, 0:W], func=AF.Sigmoid,
                                                        scale=-SCALE),
                     reads=[scn], writes=[tn])
                yield ("dve", 1.5)
                if ch.get("mask") is not None:
                    M, Mc, dw = ch["mask"]
                    S.op("dve", lambda e, W=W, dw=dw, M=M: e.tensor_tensor(out=th[0:qrows, W - dw:W], in0=th[0:qrows, W - dw:W],
                                                                           in1=M, op=ALU.mult),
                         reads=[tn, "cstf"], writes=[tn])
                    S.op("dve", lambda e, W=W, dw=dw, Mc=Mc: e.tensor_tensor(out=th[0:qrows, W - dw:W], in0=th[0:qrows, W - dw:W],
                                                                             in1=Mc, op=ALU.add),
                         reads=[tn, "cstf"], writes=[tn])
                if prev is None:
                    S.op("dve", lambda e, W=W: e.memset(Rx[0:qrows, W:W + 1], 1.0), reads=[], writes=[rn])
                else:
                    pR, pname = prev
                    S.op("dve", lambda e, W=W, pR=pR: e.tensor_copy(out=Rx[0:qrows, W:W + 1], in_=pR),
                         reads=[pname, an], writes=[rn])
                S.op("dve", lambda e, W=W: e.tensor_tensor_scan(
                    out=Rx[0:qrows, 0:W][:, ::-1], data0=th[0:qrows, 0:W][:, ::-1], data1=zeros[0:qrows, 0:W],
                    initial=Rx[0:qrows, W:W + 1], op0=ALU.mult, op1=ALU.add),
                    reads=[tn, rn, "zeros"], writes=[rn])
                prev = (Rx[0:qrows, 0:1], rn)
                if True:
                    yield ("pool", 1.3)
                S.op("pool", lambda e, W=W: e.tensor_tensor(out=a_[0:qrows, 0:W], in0=Rx[0:qrows, 1:W + 1],
                                                                                   in1=Rx[0:qrows, 0:W], op=ALU.subtract),
                     reads=[rn], writes=[an])
                yield ("pe", 0.7)
                vbl = ch["vblocks"]
                off = 0
                for j, (rhs, wk, vn) in enumerate(vbl):
                    S.op("pe", lambda e, off=off, wk=wk, j=j: e.transpose(out=pst[0:wk, pso + j * 128:pso + j * 128 + qrows],
                                                                          in_=a_[0:qrows, off:off + wk],
                                                                          identity=ident[0:qrows, 0:qrows]),
                         reads=[an, "cstb"], writes=[pn], sig=(j == len(vbl) - 1))
                    off += wk
                yield ("act", 0.65)
                if aT_hook is None:
                    aT = aTb[slot]
                    nb_ = len(vbl)
                    pvw = pst[:, pso:pso + nb_ * 128].rearrange("p (j t) -> p j t", t=128)
                    S.op("act", lambda e, aT=aT, pvw=pvw, nb_=nb_: e.activation(out=aT[:, 0:nb_, 0:qrows], in_=pvw[:, :, 0:qrows],
                                                                                func=AF.Identity),
                         reads=[pn], writes=aTnames)
                    lhs_list = [(aT[0:wk, j, 0:qrows], aTnames) for j, (_, wk, _) in enumerate(vbl)]
                else:
                    lhs_list = aT_hook(ci, vbl, pso, pn, pst)
                if aT_hook is None:
                    yield ("pe", 0.7)
                for j, (rhs, wk, vn) in enumerate(vbl):
                    lhsT, ln = lhs_list[j]
                    n = pv["n"]
                    S.op("pe", lambda e, lhsT=lhsT, rhs=rhs, n=n: e.matmul(o_ap, lhsT=lhsT, rhs=rhs, start=(n == 0),
                                                                           stop=(n == total_pv - 1)),
                         reads=(ln if isinstance(ln, list) else [ln]) + [vn], writes=[o_name], sig=True)
                    pv["n"] += 1
            if out_state is not None:
                out_state["carry"] = prev
            if fin is not None:
                yield ("dve", 0.3)
                fin()

        sb_obank = {0: [(psC, "psC")], 1: [(psD, "psD")], 2: [(psF, "psF")], 3: [(pst_f32, "pstbank")]}
        sb_ocnt = {0: 0, 1: 0}

        def sb_tile(Q):
            makers = []
            for sub in range(2):
                qb = 2 * Q + sub
                e_ = 128 * (qb + 1)
                for h in range(4):
                    def mk(slot, sub=sub, h=h, e_=e_):
                        chunks = []
                        hi = e_
                        while hi > 0:
                            lo = max(0, hi - 512)
                            vbl = [(Vaug(b, h, 128), 128, "V%d" % b) for b in range(lo // 128, hi // 128)]
                            chunks.append(dict(kT=kT(h, lo, hi), W=hi - lo,
                                               names=["kT%d_%d" % (h, b) for b in range(lo // 128, hi // 128)],
                                               vblocks=vbl,
                                               mask=(cf(C_MLT), cf(C_MLTC), 128) if hi == e_ else None))
                            hi = lo
                        ob, on = sb_obank[slot][0]

                        def fin():
                            S.op("dve", lambda e: e.tensor_tensor(out=ybuf[sub][:, h * 128:(h + 1) * 128], in0=ob[:, 0:128],
                                                                  in1=gsil[sub][:, h * 128:(h + 1) * 128], op=ALU.mult),
                                 reads=[on, "gsil%d" % sub], writes=["y%d_%d" % (sub, h)])
                        return sb_stream(slot, 128, qT[:, h, sub * 128:(sub + 1) * 128], ["qT"], chunks, ob[:, 0:128], on,
                                         {"n": 0}, sum(len(c["vblocks"]) for c in chunks), fin=fin)
                    makers.append(mk)
            run_streams(makers, SBW)

        SEQSZ = 8 * 520 + 4096
        assert 4 * SEQSZ <= 4 * T + 32 * 4 * 130
        kstage_f = actT[1][:, :, :].rearrange("p c t -> p (c t)").bitcast(F32).rearrange("p (j c) -> p j c", c=512)

        def sVc_all(s_):
            b0 = s_ * SEQSZ
            return regB[:, b0:b0 + 8 * 520].rearrange("p (j h e) -> p j h e", h=4, e=130)

        def sVc(s_, j, h, n):
            o = s_ * SEQSZ + (j * 4 + h) * 130
            return regB[:, o:o + n]

        def skT(s_, h, lo, hi):
            o = s_ * SEQSZ + 8 * 520 + h * 1024
            return regB[:, o + lo:o + hi]

        def Pw8(s_):
            v = thb[s_ // 2][:, :].bitcast(BF16)
            return v[:, (s_ % 2) * 512:(s_ % 2 + 1) * 512].rearrange("p (j c) -> p j c", c=64)

        def Vn_v():
            return Vn_t[:, :].rearrange("p (h e) -> p h e", e=130)

        def sample_stage(l):
            kcache, vcache = (cfk, cfv) if l == 0 else (csk, csv)
            for s_ in range(4):
                for j in range(8):
                    S.op("pool", lambda e, s_=s_, j=j: e.dma_start(
                        out=sVc_all(s_)[:, j, :, 0:128],
                        in_=vcache[s_][j * 128:(j + 1) * 128, :].rearrange("p (h d) -> p h d", d=128)),
                         writes=["sVc%d" % s_] + (REGB_ALL + ["w_out"] if j == 0 else []), chan="sVc")
                S.op("pool", lambda e, s_=s_: e.memset(sVc_all(s_)[:, :, :, 128:129], 1.0), writes=["sVc%d" % s_])
                for hf in range(2):
                    S.op("sp", lambda e, s_=s_, hf=hf: e.dma_start(
                        out=kstage_f, in_=kcache[s_][hf * 512:(hf + 1) * 512, :].rearrange("(j p) c -> p j c", p=128)),
                         writes=["actT1"], chan="sKc")
                    for jj in range(4):
                        j = hf * 4 + jj
                        for h in range(4):
                            S.op("pe", lambda e, jj=jj, h=h: e.transpose(out=pst_f32[:, h * 128:(h + 1) * 128],
                                                                         in_=kstage_f[:, jj, h * 128:(h + 1) * 128],
                                                                         identity=cstf[:, C_ID:C_ID + 128]),
                                 reads=["actT1", "cstf"], writes=PST_ALL, sig=(h == 3))
                        kb_ = s_ * SEQSZ + 8 * 520
                        S.op("act", lambda e, kb_=kb_, j=j: e.activation(
                            out=regB[:, kb_:kb_ + 4096].rearrange("p (h t) -> p h t", t=1024)[:, :, j * 128:(j + 1) * 128],
                            in_=pst_f32[:, 0:512].rearrange("p (h t) -> p h t", t=128), func=AF.Identity),
                             reads=PST_ALL, writes=["skT%d" % s_] + (REGB_ALL + ["w_out"] if j == 0 else []))
            if l == 0:
                for s_ in range(4):
                    S.op("sp", lambda e, s_=s_: e.dma_start(out=clf[:, :, s_ * 4:(s_ + 1) * 4],
                                                            in_=cfl[s_].rearrange("(j p) h -> p j h", p=128)),
                         writes=["clf"], chan="clf")
                for j in range(8):
                    S.op("pe", lambda e, j=j: e.matmul(psE[:, 16 + 16 * j:32 + 16 * j], lhsT=cf(C_LS), rhs=clf[:, j, :], start=True,
                                                       stop=(j == 7)), reads=["cstf", "clf"], writes=["psE"], sig=(j == 7))
                    for j2 in range(j + 1, 8):
                        S.op("pe", lambda e, j=j, j2=j2: e.matmul(psE[:, 16 + 16 * j:32 + 16 * j], lhsT=cf(C_ONES), rhs=clf[:, j2, :],
                                                                  start=False, stop=(j2 == 7)), reads=["cstf", "clf"],
                             writes=["psE"], sig=(j2 == 7))
                S.op("dve", lambda e: e.tensor_copy(out=sufb[:].rearrange("p j c -> p (j c)"), in_=psE[:, 16:144]), reads=["psE"],
                     writes=["sufb"])

        def sample_block(l, slot):
            tb = 32
            rows = TS
            Vn = Vn_v()
            project_block(l, slot, 0, tb, rows, lambda h: qT[:, :, 128:128 + rows], ["qT"],
                          Vn_v, ["Vn"], 0)
            if l == 0:
                S.op("pe", lambda e: e.matmul(psE[0:64, 8:12], lhsT=cf(C_UB16, 64, 64), rhs=lps[0:64, 0:4], start=True, stop=True),
                     reads=["cstf", "lps"], writes=["psE"])
                S.op("dve", lambda e: e.tensor_copy(out=cpn[:, :], in_=psE[0:64, 8:12]), reads=["psE"], writes=["cpn"])
            for h in range(4):
                ob, on = o_banks[h % 2][0]
                if l == 0:
                    o_ap = ob[0:64, 0:129]
                    first = True
                    for s_ in range(4):
                        scb, scn = sc_banks[pcount["sc"] % 2]
                        pcount["sc"] += 1
                        pw = Pw8(s_)
                        pwn = "th%d" % (s_ // 2)
                        for j in range(8):
                            S.op("pe", lambda e, s_=s_, j=j, h=h, scb=scb: e.matmul(
                                scb[:, j * 16:(j + 1) * 16], lhsT=skT(s_, h, j * 128, (j + 1) * 128),
                                rhs=qT[:, h, s_ * 16:(s_ + 1) * 16], start=True, stop=True),
                                reads=["skT%d" % s_, "qT"], writes=[scn], sig=(j == 7))
                        for j in range(8):
                            S.op("act", lambda e, s_=s_, j=j, h=h, scb=scb, pw=pw: e.activation(
                                out=pw[:, j, s_ * 16:(s_ + 1) * 16], in_=scb[:, j * 16:(j + 1) * 16], func=AF.Exp, scale=SCALE,
                                bias=sufb[:, j, s_ * 4 + h:s_ * 4 + h + 1]), reads=[scn, "sufb"], writes=[pwn])
                        for j in range(8):
                            S.op("pe", lambda e, s_=s_, j=j, h=h, first=first, o_ap=o_ap, pw=pw: e.matmul(
                                o_ap, lhsT=pw[:, j, 0:64], rhs=sVc(s_, j, h, 129), start=first, stop=False),
                                reads=[pwn, "sVc%d" % s_], writes=[on], sig=(j == 7))
                            first = False
                    scb, scn = sc_banks[pcount["sc"] % 2]
                    pcount["sc"] += 1
                    S.op("pe", lambda e, h=h, scb=scb: e.matmul(scb[0:64, 0:64], lhsT=qT[:, h, 128:192], rhs=qT[:, h, 0:64],
                                                                start=True, stop=True), reads=["qT"], writes=[scn])
                    S.op("act", lambda e, h=h, scb=scb: e.activation(out=Pn[:, :], in_=scb[0:64, 0:64], func=AF.Exp, scale=SCALE,
                                                                     bias=cpn[:, h:h + 1]), reads=[scn, "cpn"], writes=["Pn"])
                    S.op("pool", lambda e: e.tensor_tensor(out=Pn[:, :], in0=Pn[:, :], in1=mask64, op=ALU.mult),
                         reads=["Pn", "cstb"], writes=["Pn"])
                    S.op("pe", lambda e, h=h, o_ap=o_ap: e.matmul(o_ap, lhsT=Pn[:, :], rhs=Vn[0:64, h, 0:129], start=False, stop=True),
                         reads=["Pn", "Vn"], writes=[on])
                    S.op("dve", lambda e, ob=ob: e.reciprocal(out=rden[0:64, 0:1], in_=ob[0:64, 128:129]), reads=[on],
                         writes=["rden0"])
                    S.op("dve", lambda e, h=h, ob=ob: e.scalar_tensor_tensor(
                        out=ybuf[0][0:64, h * 128:(h + 1) * 128], in0=ob[0:64, 0:128], scalar=rden[0:64, 0:1],
                        in1=gsil[0][0:64, h * 128:(h + 1) * 128], op0=ALU.mult, op1=ALU.mult),
                        reads=[on, "rden0", "gsil0"], writes=["y0_%d" % h])
                else:
                    pass
            if l == 1:
                def head_stream(slot, h):
                    ob, on = sb_obank[slot][0]
                    o_ap = ob[0:64, 0:128]
                    pv = {"n": 0}
                    total = 1 + 4 * 8

                    def hook0(ci, vbl, pso, pn, pst_=None):
                        S.op("act", lambda e: e.activation(out=aTn[:, :], in_=pst_[0:64, pso:pso + 64], func=AF.Identity),
                             reads=[pn], writes=["aTn"])
                        return [(aTn[:, :], "aTn")]
                    ch0 = dict(kT=qT[:, h, 128:192], W=64, names=["qT"], vblocks=[(Vn[0:64, h, 0:128], 64, "Vn")],
                               mask=(cf(C_MS, 64, 64), cf(C_MSC, 64, 64), 64))
                    st = {}
                    yield from sb_stream(slot, 64, qT[:, h, 0:64], ["qT"], [ch0], o_ap, on, pv, total, None, hook0, None, st)
                    car = st["carry"]
                    cr = carry0[:, slot:slot + 1]
                    yield ("dve", 0.1)
                    S.op("dve", lambda e: e.tensor_copy(out=cr, in_=car[0]), reads=[car[1]], writes=["carry0_%d" % slot])
                    for s_ in range(4):
                        def hook(ci, vbl, pso, pn, pst_=None, s_=s_):
                            base = 4 if ci == 0 else 0
                            pvw = pst_[:, pso:pso + 512].rearrange("p (j t) -> p j t", t=128)
                            S.op("act", lambda e: e.activation(out=aTw[s_][:, base:base + 4, s_ * 16:(s_ + 1) * 16],
                                                               in_=pvw[:, :, s_ * 16:(s_ + 1) * 16], func=AF.Identity),
                                 reads=[pn], writes=["aTw%d" % s_])
                            return [(aTw[s_][:, base + j, 0:64], "aTw%d" % s_) for j in range(4)]
                        chunks = []
                        for (lo, hi) in ((512, 1024), (0, 512)):
                            chunks.append(dict(kT=skT(s_, h, lo, hi), W=512, names=["skT%d" % s_],
                                               vblocks=[(sVc(s_, b, h, 128), 128, "sVc%d" % s_) for b in range(lo // 128, hi // 128)],
                                               mask=None))
                        yield from sb_stream(slot, 64, qT[:, h, 0:64], ["qT"], chunks, o_ap, on, pv, total,
                                             (cr, "carry0_%d" % slot), hook)
                    yield ("dve", 0.3)
                    S.op("dve", lambda e: e.tensor_tensor(out=ybuf[0][0:64, h * 128:(h + 1) * 128], in0=ob[0:64, 0:128],
                                                          in1=gsil[0][0:64, h * 128:(h + 1) * 128], op=ALU.mult),
                         reads=[on, "gsil0"], writes=["y0_%d" % h])
                run_streams([(lambda sl, h=h: head_stream(sl, h)) for h in range(4)], SBW)
            load_w_out(l)
            y_out([tb], [rows])

        def p_phase(l):
            load_act(0, xT_all, "xT", 0, 256)
            nt = 16
            for Q in range(nt):
                slot = Q % 2
                if Q < 15:
                    load_act(1 - slot, xT_all, "xT", (Q + 1) * 256, 256)
                else:
                    load_act(1 - slot, xT_all, "xT", T, TS)
                for sub in range(2):
                    tb = 2 * Q + sub
                    project_block(l, slot, sub, tb, 128,
                                  lambda h, tb=tb: regB[:, 0:4 * T].rearrange("p (h t) -> p h t", t=T)[:, :, tb * 128:(tb + 1) * 128],
                                  ["kT%d_%d" % (h, tb) for h in range(4)],
                                  lambda tb=tb: Vaug_blk(tb), ["V%d" % tb], sub * 128)
                    if l == 0:
                        cumsum_block(tb)
                if l == 0:
                    fox_tile(Q)
                else:
                    sb_tile(Q)
                y_out([2 * Q, 2 * Q + 1], [128, 128])
            if nt == 16:
                sample_stage(l)
                sample_block(l, 0)

        def o_phase(l):
            wq = load_w_in(1) if l == 0 else iter(())
            wv = w_out()
            load_act(0, yT_all, "yT", 0, 256)
            xsrc = xc0 if l == 0 else xres[1].ap()
            xdst = xres[l + 1].ap()

            def o_load(tb):
                rows = blk_rows(tb)
                i = tb % 2
                S.op("sp", lambda e: e.dma_start(out=xcb[i][0:rows, :], in_=xsrc[tb * 128:tb * 128 + rows, :]),
                     reads=["xres%d" % l], writes=["kf32%d" % i], chan="kf32%d" % i)

            for Q in range(17):
                slot = Q % 2
                if Q < 15:
                    load_act(1 - slot, yT_all, "yT", (Q + 1) * 256, 256)
                elif Q == 15:
                    load_act(1 - slot, yT_all, "yT", T, TS)
                next(wq, None)
                for sub in range(2 if Q < 16 else 1):
                    tb = 2 * Q + sub
                    rows = blk_rows(tb)
                    i = tb % 2
                    if tb == 0:
                        o_load(0)
                    if tb + 1 < NB:
                        o_load(tb + 1)
                    for c in range(16):
                        S.op("pe", lambda e, c=c, slot=slot, sub=sub, rows=rows: e.matmul(
                            psA[0:rows, :], lhsT=actT[slot][:, c, sub * 128:sub * 128 + rows], rhs=wv[:, c, :], start=(c == 0),
                            stop=(c == 15)), reads=["actT%d" % slot, "w_out"], writes=["psA"], sig=(c == 15))
                    S.op("dve", lambda e, i=i, rows=rows: e.tensor_tensor(out=xnb[i][0:rows, :], in0=psA[0:rows, :],
                                                                          in1=xcb[i][0:rows, :], op=ALU.add),
                         reads=["psA", "kf32%d" % i], writes=["vf32%d" % i])
                    S.op("sp", lambda e, i=i, tb=tb, rows=rows: e.dma_start(out=xdst[tb * 128:tb * 128 + rows, :],
                                                                              in_=xnb[i][0:rows, :]),
                         reads=["vf32%d" % i], writes=["xres%d" % (l + 1)], chan="vf32%d" % i)
                    norm_prep(xnb[i], "vf32%d" % i, tb, rows, l + 1, l == 0)
            for _ in wq:
                pass

        stage = 99
        if stage >= 2:
            p_phase(0)
        if stage >= 4:
            load_nw(1)
            o_phase(0)
            rstd_from_ssq("1")
        if stage >= 5:
            p_phase(1)
        if stage >= 7:
            load_nw(2)
            o_phase(1)
            rstd_from_ssq("2")
            x2 = xres[2].ap()
            fin_in = [(kf32[0], "kf320"), (kf32[1], "kf321"), (thb[0], "th0"), (thb[1], "th1")]
            fin_out = [(vf32[0], "vf320"), (vf32[1], "vf321"), (thb[2], "th2"), (gtmp, "gtmp")]

            def f_load(tb):
                rows = blk_rows(tb)
                buf, nm = fin_in[tb % 4]
                S.op("sp", lambda e: e.dma_start(out=buf[0:rows, 0:512], in_=x2[tb * 128:tb * 128 + rows, :]),
                     reads=["xres2"], writes=[nm], chan=nm)

            for t_ in range(3):
                f_load(t_)
            for tb in range(NB):
                rows = blk_rows(tb)
                if tb + 3 < NB:
                    f_load(tb + 3)
                ib, inm = fin_in[tb % 4]
                ob_, onm = fin_out[tb % 4]
                S.op("dve", lambda e, tb=tb, rows=rows, ib=ib, ob_=ob_: e.scalar_tensor_tensor(
                    out=ob_[0:rows, 0:512], in0=ib[0:rows, 0:512], scalar=rstd[0:rows, tb:tb + 1], in1=nw[2][0:rows, :],
                    op0=ALU.mult, op1=ALU.mult), reads=[inm, "rstd", "nw"], writes=[onm])
                S.op("sp", lambda e, tb=tb, rows=rows, ob_=ob_: e.dma_start(out=yout[tb * 128:tb * 128 + rows, :],
                                                                             in_=ob_[0:rows, 0:512]),
                     reads=[onm], writes=[], chan=onm)
        S.op("sp", lambda e: e.dma_start(out=lfout[0:T, :].rearrange("(b p) h -> p b h", p=128), in_=lfo[:, 0:32, :]),
             reads=["lfo"], writes=[], chan="lfo")
        S.op("sp", lambda e: e.dma_start(out=lfout[T:TT, :], in_=lfo[0:TS, 32, :]), reads=["lfo"], writes=[], chan="lfo")
        for sn, v in list(S.cnt.items()):
            if sn.startswith("d_"):
                S.prog["sp"].append(("wait", sn, v))

        sem_names = sorted(S.cnt.keys())
        sems = {}
        for sn in sem_names:
            sems[sn] = es.enter_context(nc.semaphore(sn))
        block = es.enter_context(nc.Block())

        def emit(eng_obj, name):
            for item in S.prog[name]:
                if item[0] == "wait":
                    eng_obj.wait_ge(sems[item[1]], item[2])
                else:
                    _, fn, sn, inc = item
                    ins = fn(eng_obj)
                    if sn is not None:
                        if inc is None:
                            ins.then_inc(sems[sn])
                        else:
                            ins.then_inc(sems[sn], inc)

        @block.sync
        def _(e):
            emit(e, "sp")

        @block.scalar
        def _(e):
            emit(e, "act")

        @block.vector
        def _(e):
            emit(e, "dve")

        @block.gpsimd
        def _(e):
            emit(e, "pool")

        @block.tensor
        def _(e):
            emit(e, "pe")
    return nc


def _consts():
    c = np.zeros((128, NCST), np.float32)
    k = np.arange(128)[:, None]
    m = np.arange(128)[None, :]
    c[:, C_U:C_U + 128] = (k <= m)
    c[:, C_E127:C_E127 + 128] = (k == 127)
    c[:, C_UB16:C_UB16 + 128] = (k <= m) & (k // 16 == m // 16)
    c[:, C_LS:C_LS + 128] = (k > m)
    c[:, C_ONES:C_ONES + 128] = 1.0
    c[:, C_MS:C_MS + 128] = (m < k) & (k // 16 == m // 16)
    c[:, C_MSC:C_MSC + 128] = 1.0 - ((m < k) & (k // 16 == m // 16))
    c[:, C_MLT:C_MLT + 128] = (m < k)
    c[:, C_MLTC:C_MLTC + 128] = 1.0 - (m < k)
    c[:, C_ID:C_ID + 128] = (k == m)
    c[:, C_MLE:C_MLE + 128] = (k <= m)
    c[:, C_M64:C_M64 + 128] = (k <= m) & (k // 16 == m // 16)
    return c


_NC_CACHE = {}


def kernel(x_prompt, x_sample, cache_fox_k, cache_fox_v, cache_fox_logf, cache_sb_k, cache_sb_v,
           norm_0, w_in_0, b_f_0, w_out_0, norm_1, w_in_1, w_out_1, norm_f):
    f = np.float32
    A = lambda a: np.ascontiguousarray(np.asarray(a, dtype=f))
    x_prompt, x_sample = A(x_prompt), A(x_sample)
    w_in_0, w_in_1, w_out_0, w_out_1 = A(w_in_0), A(w_in_1), A(w_out_0), A(w_out_1)
    caches = [A(cache_fox_k), A(cache_fox_v), A(cache_sb_k), A(cache_sb_v)]
    cache_fox_logf = A(cache_fox_logf)
    norms = [A(norm_0), A(norm_1), A(norm_f)]
    b_f_0 = A(b_f_0)
    if "nc" not in _NC_CACHE:
        _NC_CACHE["nc"] = build_program()
    nc = _NC_CACHE["nc"]
    cst = _consts()
    in_maps = []
    for core in range(8):
        b, g = core // 4, core % 4
        cs = slice(g * 512, (g + 1) * 512)
        xc0 = np.concatenate([x_prompt[b][:, cs], x_sample[4 * b:4 * b + 4].reshape(64, D)[:, cs]], axis=0)
        m = {"xc0": A(xc0), "cst": cst}
        for i, nm in enumerate(["nw0", "nw1", "nwf"]):
            m[nm] = A(np.broadcast_to(norms[i][cs][None, :], (128, 512)))
        m["win0"] = A(np.concatenate([w_in_0[:, j * D + g * 512: j * D + (g + 1) * 512] for j in range(4)]
                                     + [w_in_0[:, 4 * D + 4 * g: 4 * D + 4 * g + 4]], axis=1))
        m["win1"] = A(np.concatenate([w_in_1[:, j * D + g * 512: j * D + (g + 1) * 512] for j in range(4)], axis=1))
        m["wout0"] = A(w_out_0[:, cs])
        m["wout1"] = A(w_out_1[:, cs])
        m["bfb"] = A(np.broadcast_to(b_f_0[4 * g:4 * g + 4][None, :], (128, 4)))
        for nm, cch in zip(["cfk", "cfv", "csk", "csv"], caches):
            m[nm] = A(cch[4 * b:4 * b + 4, :, 4 * g:4 * g + 4, :].reshape(4, PAST, 512))
        m["cfl"] = A(cache_fox_logf[4 * b:4 * b + 4, :, 4 * g:4 * g + 4])
        in_maps.append(m)
    res = run_bass_kernel_spmd(nc, in_maps, core_ids=list(range(8)))
    R = res.results
    y_prompt = np.zeros((2, T, D), f)
    y_sample = np.zeros((8, 16, D), f)
    pk = [np.zeros((2, T, 16, 128), f) for _ in range(4)]
    sk = [np.zeros((8, 16, 16, 128), f) for _ in range(4)]
    plf = np.zeros((2, T, 16), f)
    slf = np.zeros((8, 16, 16), f)
    for core in range(8):
        b, g = core // 4, core % 4
        cs = slice(g * 512, (g + 1) * 512)
        r = R[core]
        y_prompt[b][:, cs] = r["yout"][:T]
        y_sample[4 * b:4 * b + 4][:, :, cs] = r["yout"][T:].reshape(4, 16, 512)
        for i, nm in enumerate(["kf", "vf", "ks", "vs"]):
            pk[i][b][:, 4 * g:4 * g + 4, :] = r[nm][:T].reshape(T, 4, 128)
            sk[i][4 * b:4 * b + 4][:, :, 4 * g:4 * g + 4, :] = r[nm][T:].reshape(4, 16, 4, 128)
        plf[b][:, 4 * g:4 * g + 4] = r["lf"][:T]
        slf[4 * b:4 * b + 4][:, :, 4 * g:4 * g + 4] = r["lf"][T:].reshape(4, 16, 4)
    return (y_prompt, y_sample, pk[0], pk[1], plf, pk[2], pk[3], sk[0], sk[1], slf, sk[2], sk[3])
```

```python
import numpy as np
import concourse.bass as bass
import concourse.mybir as mybir
from concourse.bass_utils import run_bass_kernel_spmd

F32 = mybir.dt.float32
BF16 = mybir.dt.bfloat16
AF = mybir.ActivationFunctionType
ALU = mybir.AluOpType

D = 2048
T = 4096
TS = 64
TT = T + TS
NB = 33
PAST = 1024
SCALE = 128 ** -0.5
EPS = 1e-6
GROUPS = [[0, 1, 2, 3], [4, 5, 6, 7]]
ENG = ("sp", "act", "dve", "pool", "pe")
FOXW = 1
SBW = 4

C_U, C_E127, C_UB16, C_LS, C_ONES, C_MLT, C_MLTC, C_MS, C_MSC, C_ID, C_MLE, C_M64 = [128 * i for i in range(12)]
NCST = 128 * 12


class Sched:
    def __init__(self):
        self.prog = {e: [] for e in ENG}
        self.cnt = {}
        self.waited = {}
        self.lastw = {}
        self.readers = {}

    def _need(self, eng, events):
        for sn, val in events:
            if sn.startswith("d_"):
                val = self.cnt[sn]
            if sn == "pe" and eng == "pe":
                continue
            key = (eng, sn)
            if self.waited.get(key, 0) >= val:
                continue
            self.waited[key] = val
            self.prog[eng].append(("wait", sn, val))

    def op(self, eng, fn, reads=(), writes=(), sig=True, chan=None, cc=None):
        ps_reads = [r for r in reads if r.startswith("ps") and r not in writes]
        if ps_reads:
            writes = list(writes) + ps_reads
        ev = []
        for r in reads:
            if r in self.lastw:
                ev.append(self.lastw[r])
        for w in writes:
            if w in self.lastw:
                ev.append(self.lastw[w])
            ev.extend(self.readers.get(w, {}).items())
        self._need(eng, ev)
        if cc is not None:
            sn = "c_" + cc
            self.cnt[sn] = 1
            myev = (sn, 1)
            self.prog[eng].append(("op", fn, sn, None))
        elif chan is not None:
            sn = "d_" + chan
            self.cnt[sn] = self.cnt.get(sn, 0) + 16
            myev = (sn, self.cnt[sn])
            self.prog[eng].append(("op", fn, sn, 16))
        elif sig:
            self.cnt[eng] = self.cnt.get(eng, 0) + 1
            myev = (eng, self.cnt[eng])
            self.prog[eng].append(("op", fn, eng, 1))
        else:
            myev = (eng, self.cnt.get(eng, 0) + 1)
            self.prog[eng].append(("op", fn, None, 0))
        for r in reads:
            d = self.readers.setdefault(r, {})
            d[myev[0]] = max(d.get(myev[0], 0), myev[1])
        for w in writes:
            self.lastw[w] = myev
            self.readers[w] = {}


def build_program():
    nc = bass.Bass("TRN2", target_bir_lowering=False)
    S = Sched()

    def din(name, shape, dt=F32):
        return nc.dram_tensor(name, shape, dt, kind="ExternalInput").ap()

    def dout(name, shape, dt=F32):
        return nc.dram_tensor(name, shape, dt, kind="ExternalOutput").ap()

    xc0 = din("xc0", [TT, 512])
    nwd = [din("nw0", [128, 512]), din("nw1", [128, 512]), din("nwf", [128, 512])]
    wind = [din("win0", [D, 2052]), din("win1", [D, 2048])]
    woutd = [din("wout0", [D, 512]), din("wout1", [D, 512])]
    bfd = din("bfb", [128, 4])
    cstd = din("cst", [128, NCST])
    cfk = din("cfk", [4, PAST, 512])
    cfv = din("cfv", [4, PAST, 512])
    csk = din("csk", [4, PAST, 512])
    csv = din("csv", [4, PAST, 512])
    cfl = din("cfl", [4, PAST, 4])

    yout = dout("yout", [TT, 512])
    kvout = [[dout("kf", [TT, 512]), dout("vf", [TT, 512])], [dout("ks", [TT, 512]), dout("vs", [TT, 512])]]
    lfout = dout("lf", [TT, 4])

    PW = [1024, 1024, 1024, 1024, TS]
    xT_loc = [nc.dram_tensor("xT_loc%d" % p, [512, PW[p]], BF16) for p in range(5)]
    xT_all = [nc.dram_tensor("xT_all%d" % p, [D, PW[p]], BF16) for p in range(5)]
    yT_loc = [nc.dram_tensor("yT_loc%d" % p, [512, PW[p]], BF16) for p in range(5)]
    yT_all = [nc.dram_tensor("yT_all%d" % p, [D, PW[p]], BF16) for p in range(5)]

    def piece_of(tb):
        return (tb // 8, (tb % 8) * 128) if tb < 32 else (4, 0)
    ssq_loc = nc.dram_tensor("ssq_loc", [128, NB], F32)
    ssq_all = nc.dram_tensor("ssq_all", [512, NB], F32)
    xres = [None, nc.dram_tensor("x1s", [TT, 512], F32), nc.dram_tensor("x2s", [TT, 512], F32)]

    from contextlib import ExitStack
    es = ExitStack()

    def sb(name, shape, dt):
        return es.enter_context(nc.sbuf_tensor(name, shape, dt))

    def ps(name, shape, dt):
        return es.enter_context(nc.psum_tensor(name, shape, dt))

    with es:
        cstf = sb("cstf", [128, C_ID + 128], F32)
        cstb = sb("cstb", [128, 3 * 128], BF16)
        zeros = sb("zeros", [128, 512], BF16)
        nwt = sb("nwt", [128, 512], F32)
        nw = [nwt, nwt, nwt]
        bfb = sb("bfbs", [128, 4], F32)
        w_in = sb("w_in", [128, 16, 2052], BF16)
        regB = sb("regB", [128, 4 * T + 32 * 4 * 130], BF16)
        actT = [sb("actT0", [128, 16, 256], BF16), sb("actT1", [128, 16, 256], BF16)]
        ssq_part = sb("ssq_part", [128, NB], F32)
        ssq4 = sb("ssq4", [128, 4, NB], F32)
        ssum = sb("ssum", [128, NB], F32)
        rstd = sb("rstd", [128, NB], F32)
        nrstd = sb("nrstd", [128, NB], F32)
        hrstd = sb("hrstd", [128, NB], F32)
        cpall = sb("cpall", [128, NB, 8], F32)
        lfo = sb("lfo", [128, NB, 4], F32)
        lps = sb("lps", [128, 4], F32)
        xl = sb("xl", [128, 4], F32)
        el = sb("el", [128, 4], F32)
        q_tok = sb("q_tok", [128, 512], BF16)
        k_tok = sb("k_tok", [128, 512], BF16)
        kf32b_big = sb("kf32b", [128, 516], F32)
        kf32 = [sb("kf32a", [128, 512], F32)[:, :], kf32b_big[:, 0:512]]
        vf32 = [sb("vf32a", [128, 512], F32), sb("vf32b", [128, 512], F32)]
        gsil = [sb("gsil0", [128, 512], F32), sb("gsil1", [128, 512], F32)]
        gtmp = sb("gtmp", [128, 512], F32)
        qT = sb("qT", [128, 4, 256], BF16)
        ybuf = [sb("y0", [128, 512], BF16), sb("y1", [128, 512], BF16)]
        yTs = [sb("yTs0", [128, 4, 128], BF16), sb("yTs1", [128, 4, 128], BF16)]
        Pball = sb("Pball", [128, 6, 256], BF16)
        Pb = [Pball[:, i, :] for i in range(6)]
        biasT = sb("biasT", [128, 32, 4], F32)
        rden = sb("rden", [128, 4], F32)
        thb = [sb("th%d" % i, [128, 512], F32) for i in range(3)] + [gtmp]
        Rext = [sb("Rx%d" % i, [128, 516], F32) for i in range(3)] + [kf32b_big]
        ab = [sb("a%d" % i, [128, 512], BF16) for i in range(3)] + [k_tok]
        aTb = [sb("aT0", [128, 4, 128], BF16), sb("aT1", [128, 4, 128], BF16),
               Pball[:, 0:2, :].rearrange("p a (b t) -> p (a b) t", t=128),
               q_tok[:, :].rearrange("p (j t) -> p j t", t=128)]
        xcb = kf32
        xnb = vf32
        xtb = q_tok
        sqj = gtmp
        xTs = yTs
        Pw = [sb("Pw%d" % i, [128, 64], BF16) for i in range(4)]
        Pn = sb("Pn", [64, 64], BF16)
        clf = sb("clf", [128, 8, 16], F32)
        sufb = sb("sufb", [128, 8, 16], F32)
        cpn = sb("cpn", [64, 4], F32)
        aTw = [sb("aTw%d" % i, [128, 8, 64], BF16) for i in range(4)]
        aTn = sb("aTn", [64, 64], BF16)
        Vn_t = sb("Vn_t", [128, 520], BF16)
        carry0 = sb("carry0", [64, 4], F32)
        cbias = sb("cbias", [128, 2], F32)

        psA = ps("psA", [128, 512], F32)
        psB = ps("psB", [128, 512], F32)
        psC = ps("psC", [128, 512], F32)
        psD = ps("psD", [128, 512], F32)
        psE = ps("psE", [128, 512], F32)
        psF = ps("psF", [128, 512], F32)
        psG = ps("psG", [128, 512], F32)
        pst = ps("pst", [128, 1024], BF16)

        KT_OFF = 0
        V_OFF = 4 * T

        def kT(h, lo, hi):
            return regB[:, KT_OFF + h * T + lo: KT_OFF + h * T + hi]

        def Vaug(blk, h, n):
            o = V_OFF + (blk * 4 + h) * 130
            return regB[:, o:o + n]

        def Vaug_blk(blk):
            o = V_OFF + blk * 4 * 130
            return regB[:, o:o + 520].rearrange("p (h e) -> p h e", e=130)

        def w_out():
            o = 4096 + 8 * 520 + 4096
            return regB[:, o:o + 16 * 512].rearrange("p (c n) -> p c n", n=512)

        PST_ALL = ["pstbank"]
        pst_full = pst
        psE_bf = psE[:, :].bitcast(BF16)
        pst_f32 = pst[:, :].bitcast(F32)
        psG_bf = psG[:, :].bitcast(BF16)
        REGB_ALL = ["kT%d_%d" % (h, b) for h in range(4) for b in range(32)] + ["V%d" % b for b in range(32)]

        ident = cstb[:, 0:128]
        maskLE = cstb[:, 128:256]
        mask64 = cstb[0:64, 256:320]

        def cf(c0, n=128, rows=128):
            return cstf[0:rows, c0:c0 + n]

        S.op("sp", lambda e: e.dma_start(out=cstf[:], in_=cstd[:, 0:C_ID + 128]), writes=["cstf"], chan="cst")
        S.op("pool", lambda e: e.dma_start(out=cstb[:], in_=cstd[:, C_ID:C_ID + 384]), writes=["cstb"], chan="cstb")
        def load_nw(i):
            S.op("sp", lambda e: e.dma_start(out=nwt[:], in_=nwd[i][:, :]), writes=["nw"], chan="cst")
        load_nw(0)
        S.op("sp", lambda e: e.dma_start(out=bfb[:], in_=bfd[:, :]), writes=["bfb"], chan="cst")
        S.op("dve", lambda e: e.memset(zeros[:], 0.0), writes=["zeros"])
        S.op("dve", lambda e: e.memset(cbias[:, 0:1], EPS), writes=["cbias"])
        S.op("dve", lambda e: e.memset(cbias[:, 1:2], 1.0), writes=["cbias"])
        S.op("dve", lambda e: e.memset(ssq_part[:], 1.0), writes=["ssq_part"])
        S.op("dve", lambda e: e.memset(cpall[:], 0.0), writes=["cpall"])
        S.op("dve", lambda e: e.memset(lfo[:], 0.0), writes=["lfo"])
        for i in range(2):
            S.op("pool", lambda e, i=i: e.memset(thb[i][:], 0.0), writes=["th%d" % i])
        for i in range(4):
            S.op("pool", lambda e, i=i: e.memset(aTw[i][:], 0.0), writes=["aTw%d" % i])

        if False:
            for nm_, ap_ in [("win0", wind[0][0:128, 0:512]), ("win1", wind[1][0:128, 0:512]), ("wout0", woutd[0][0:128, :]),
                             ("wout1", woutd[1][0:128, :]), ("cfk", cfk[0][0:128, :]), ("cfv", cfv[0][0:128, :]),
                             ("csk", csk[0][0:128, :]), ("csv", csv[0][0:128, :])]:
                S.op("sp", lambda e, ap_=ap_: e.dma_start(out=gtmp[:], in_=ap_), writes=["gtmp"], chan="dbg")
            S.op("sp", lambda e: e.dma_start(out=gtmp[:, 0:4], in_=cfl[0][0:128, :]), writes=["gtmp"], chan="dbg")

        def load_w_in(l):
            ncol = 2052 if l == 0 else 2048
            src = wind[l].rearrange("(c p) n -> p c n", p=128)
            for c in range(16):
                yield
                si = c % 3
                o = 20544 + si * 4104
                stg = regB[:, o:o + 4104].bitcast(F32)
                S.op("sp", lambda e, c=c, stg=stg: e.dma_start(out=stg[:, 0:ncol], in_=src[:, c, :]),
                     writes=["wst%d" % si] + (REGB_ALL + ["sVc%d" % q for q in range(4)] + ["skT%d" % q for q in range(4)]
                                              if c < 3 else []), chan="wst%d" % si)
                if c % 2 == 0:
                    S.op("act", lambda e, c=c, stg=stg: e.activation(out=w_in[:, c, 0:ncol], in_=stg[:, 0:ncol], func=AF.Identity),
                         reads=["wst%d" % si] + (REGB_ALL if c >= 13 else []), writes=["w_in"])
                else:
                    S.op("dve", lambda e, c=c, stg=stg: e.tensor_copy(out=w_in[:, c, 0:ncol], in_=stg[:, 0:ncol]),
                         reads=["wst%d" % si] + (REGB_ALL if c >= 13 else []), writes=["w_in"])

        def load_w_out(l):
            src = woutd[l].rearrange("(c p) n -> p c n", p=128)
            wv = w_out()
            for c in range(0, 16, 4):
                S.op("pool", lambda e, c=c: e.dma_start(out=wv[:, c:c + 4, :], in_=src[:, c:c + 4, :]),
                     writes=["w_out"] + REGB_ALL + ["sVc%d" % q for q in range(4)] + ["skT%d" % q for q in range(4)], chan="wout")

        def blk_rows(tb):
            return 128 if tb < 32 else TS

        cnt = {"xts": 0, "tr": 0}
        deferred_g = []
        deferred_y = []

        def norm_prep(xblk, xres_name, tb, rows, nwi, want_xT, want_gather=True):
            S.op("act", lambda e: e.activation(out=sqj[0:rows, :], in_=xblk[0:rows, :], func=AF.Square,
                                               accum_out=ssq_part[0:rows, tb:tb + 1]),
                 reads=[xres_name], writes=["gtmp", "ssq_part"])
            if not want_xT:
                return
            S.op("dve", lambda e: e.tensor_tensor(out=xtb[0:rows, :], in0=xblk[0:rows, :], in1=nw[nwi][0:rows, :],
                                                  op=ALU.mult),
                 reads=[xres_name, "nw"], writes=["q_tok"])
            for c in range(4):
                S.op("pe", lambda e, c=c: e.transpose(out=pst[:, c * 128:c * 128 + rows],
                                                      in_=xtb[0:rows, c * 128:(c + 1) * 128],
                                                      identity=ident[0:rows, 0:rows]),
                     reads=["q_tok", "cstb"], writes=PST_ALL, sig=(c == 3))
            i = cnt["tr"] % 2
            cnt["tr"] += 1
            pv = pst[:, 0:512].rearrange("p (c t) -> p c t", t=128)
            S.op("act", lambda e: e.activation(out=xTs[i][:, :, 0:rows], in_=pv[:, :, 0:rows], func=AF.Identity),
                 reads=PST_ALL, writes=["yTs%d" % i])
            pc, c0 = piece_of(tb)
            dst = xT_loc[pc].ap().rearrange("(c p) t -> p c t", p=128)
            S.op("sp", lambda e: e.dma_start(out=dst[:, :, c0:c0 + rows], in_=xTs[i][:, :, 0:rows]),
                 reads=["yTs%d" % i], writes=["xT_loc%d" % pc], chan="yTs%d" % i)
            if want_gather and (tb % 8 == 7 or tb == 32):
                if pc >= 3:
                    deferred_g.append(pc)
                else:
                    gather(xT_loc[pc], xT_all[pc], "xT", pc)

        def rstd_from_ssq(tag):
            S.op("sp", lambda e: e.dma_start(out=ssq_loc.ap(), in_=ssq_part[:]), reads=["ssq_part"],
                 writes=["ssq_loc"], chan="ssq")
            S.op("pool", lambda e: e.collective_compute("AllGather", ALU.bypass, replica_groups=GROUPS,
                                                        ins=[ssq_loc.ap().opt()], outs=[ssq_all.ap().opt()]),
                 reads=["ssq_loc"], writes=["ssq_all"], cc="ssq" + tag)
            S.op("sp", lambda e: e.dma_start(out=ssq4[:], in_=ssq_all.ap().rearrange("(r p) b -> p r b", p=128)),
                 reads=["ssq_all"], writes=["ssq4"], chan="ssq")
            S.op("dve", lambda e: e.tensor_tensor(out=ssum[:], in0=ssq4[:, 0, :], in1=ssq4[:, 1, :], op=ALU.add),
                 reads=["ssq4"], writes=["ssum"])
            S.op("dve", lambda e: e.tensor_tensor(out=ssum[:], in0=ssum[:], in1=ssq4[:, 2, :], op=ALU.add),
                 reads=["ssq4", "ssum"], writes=["ssum"])
            S.op("dve", lambda e: e.tensor_tensor(out=ssum[:], in0=ssum[:], in1=ssq4[:, 3, :], op=ALU.add),
                 reads=["ssq4", "ssum"], writes=["ssum"])
            S.op("act", lambda e: e.activation(out=ssum[:], in_=ssum[:], func=AF.Ln, scale=1.0 / D, bias=cbias[:, 0:1]),
                 reads=["ssum", "cbias"], writes=["ssum"])
            S.op("act", lambda e: e.activation(out=rstd[:], in_=ssum[:], func=AF.Exp, scale=-0.5),
                 reads=["ssum"], writes=["rstd"])
            S.op("dve", lambda e: e.tensor_scalar(out=nrstd[:], in0=rstd[:], scalar1=-1.0, scalar2=None, op0=ALU.mult),
                 reads=["rstd"], writes=["nrstd"])
            S.op("dve", lambda e: e.tensor_scalar(out=hrstd[:], in0=rstd[:], scalar1=0.5, scalar2=None, op0=ALU.mult),
                 reads=["rstd"], writes=["hrstd"])
            S.op("dve", lambda e: e.memset(ssq_part[:], 1.0), reads=[], writes=["ssq_part"])
            for pc in deferred_g:
                gather(xT_loc[pc], xT_all[pc], "xT", pc)
            del deferred_g[:]

        gcount = {"n": 0}

        def gather(loc, allt, rname, pc):
            gcount["n"] += 1
            S.op("pool", lambda e: e.collective_compute("AllGather", ALU.bypass, replica_groups=GROUPS,
                                                        ins=[loc.ap().opt()], outs=[allt.ap().opt()]),
                 reads=["%s_loc%d" % (rname, pc)], writes=["%s_all%d" % (rname, pc)], cc="g%d" % gcount["n"])

        wq = load_w_in(0)

        def n0_load(tb):
            rows = blk_rows(tb)
            i = tb % 2
            S.op("sp", lambda e: e.dma_start(out=xcb[i][0:rows, :], in_=xc0[tb * 128:tb * 128 + rows, :]),
                 writes=["kf32%d" % i], chan="kf32%d" % i)

        for tb in range(NB):
            rows = blk_rows(tb)
            i = tb % 2
            if tb % 2 == 0:
                next(wq, None)
            if tb == 0:
                n0_load(0)
            if tb + 1 < NB:
                n0_load(tb + 1)
            norm_prep(xcb[i], "kf32%d" % i, tb, rows, 0, True)
        for _ in wq:
            pass
        rstd_from_ssq("0")

        def load_act(slot, src_all, rname, c0, ncols):
            pc, lc = (c0 // 1024, c0 % 1024) if c0 < T else (4, 0)
            src = src_all[pc].ap().rearrange("(c p) t -> p c t", p=128)
            S.op("sp", lambda e: e.dma_start(out=actT[slot][:, :, 0:ncols], in_=src[:, :, lc:lc + ncols]),
                 reads=["%s_all%d" % (rname, pc)], writes=["actT%d" % slot], chan="actT%d" % slot)

        def transposes_to(src_tok, src_name, rows, dst_fn, dst_names, evac_eng):
            for h in range(4):
                S.op("pe", lambda e, h=h: e.transpose(out=pst[:, h * 128:h * 128 + rows],
                                                      in_=src_tok[0:rows, h * 128:(h + 1) * 128],
                                                      identity=ident[0:rows, 0:rows]),
                     reads=[src_name, "cstb"], writes=PST_ALL, sig=(h == 3))
            pv4 = pst[:, 0:512].rearrange("p (h t) -> p h t", t=128)
            if evac_eng == "act":
                S.op("act", lambda e: e.activation(out=dst_fn(None), in_=pv4[:, :, 0:rows], func=AF.Identity),
                     reads=PST_ALL, writes=dst_names)
            else:
                S.op("dve", lambda e: e.tensor_copy(out=dst_fn(None), in_=pv4[:, :, 0:rows]), reads=PST_ALL, writes=dst_names)

        def project_block(l, slot, sub, tb, rows, kT_dst, kT_names, v_dst_fn, v_names, qcol0):
            banks = [psA, psB, psC, psD]
            bn = ["psA", "psB", "psC", "psD"]
            for c in range(16):
                lhsT = actT[slot][:, c, sub * 128: sub * 128 + rows]
                for j in range(4):
                    S.op("pe", lambda e, c=c, j=j, lhsT=lhsT: e.matmul(banks[j][0:rows, :], lhsT=lhsT,
                                                                        rhs=w_in[:, c, j * 512:(j + 1) * 512],
                                                                        start=(c == 0), stop=(c == 15)),
                         reads=["actT%d" % slot, "w_in"], writes=[bn[j]], sig=(c == 15))
                if l == 0:
                    S.op("pe", lambda e, c=c, lhsT=lhsT: e.matmul(psE[0:rows, 0:4], lhsT=lhsT,
                                                                   rhs=w_in[:, c, 2048:2052],
                                                                   start=(c == 0), stop=(c == 15)),
                         reads=["actT%d" % slot, "w_in"], writes=["psE"], sig=(c == 15))
            rs = rstd[0:rows, tb:tb + 1]
            nrs = nrstd[0:rows, tb:tb + 1]
            i = tb % 2
            S.op("dve", lambda e: e.tensor_scalar(out=q_tok[0:rows, :], in0=psA[0:rows, :], scalar1=rs, scalar2=None,
                                                  op0=ALU.mult),
                 reads=["psA", "rstd"], writes=["q_tok"])
            S.op("act", lambda e: e.activation(out=kf32[i][0:rows, :], in_=psB[0:rows, :], func=AF.Identity, scale=rs),
                 reads=["psB", "rstd"], writes=["kf32%d" % i])
            S.op("dve", lambda e: e.tensor_scalar(vf32[i][0:rows, :], psC[0:rows, :], rs, None, ALU.mult),
                 reads=["psC", "rstd"], writes=["vf32%d" % i])
            S.op("sp", lambda e: e.dma_start(out=kvout[l][0][tb * 128:tb * 128 + rows, :], in_=kf32[i][0:rows, :]),
                 reads=["kf32%d" % i], writes=[], chan="kf32%d" % i)
            S.op("sp", lambda e: e.dma_start(out=kvout[l][1][tb * 128:tb * 128 + rows, :], in_=vf32[i][0:rows, :]),
                 reads=["vf32%d" % i], writes=[], chan="vf32%d" % i)
            S.op("act", lambda e: e.activation(out=k_tok[0:rows, :], in_=psB[0:rows, :], func=AF.Identity, scale=rs),
                 reads=["psB", "rstd"], writes=["k_tok"])
            vd = v_dst_fn()
            S.op("dve", lambda e: e.tensor_copy(out=vd[0:rows, :, 0:128],
                                                in_=vf32[i][0:rows, :].rearrange("p (h d) -> p h d", d=128)),
                 reads=["vf32%d" % i], writes=v_names)
            S.op("pool", lambda e: e.memset(vd[0:rows, :, 128:129], 1.0), reads=[], writes=v_names)
            if l == 0:
                S.op("act", lambda e: e.activation(out=gtmp[0:rows, :], in_=psD[0:rows, :], func=AF.Exp, scale=nrs),
                     reads=["psD", "nrstd"], writes=["gtmp"])
                S.op("act", lambda e: e.activation(out=gtmp[0:rows, :], in_=gtmp[0:rows, :], func=AF.Ln, bias=cbias[0:rows, 1:2]),
                     reads=["gtmp", "cbias"], writes=["gtmp"])
                S.op("act", lambda e: e.activation(out=gtmp[0:rows, :], in_=gtmp[0:rows, :], func=AF.Exp, scale=-1.0),
                     reads=["gtmp"], writes=["gtmp"])
            else:
                S.op("act", lambda e: e.activation(out=gtmp[0:rows, :], in_=psD[0:rows, :], func=AF.Sigmoid, scale=rs),
                     reads=["psD", "rstd"], writes=["gtmp"])
            S.op("dve", lambda e: e.scalar_tensor_tensor(out=gsil[sub][0:rows, :], in0=psD[0:rows, :], scalar=rs,
                                                         in1=gtmp[0:rows, :], op0=ALU.mult, op1=ALU.mult),
                 reads=["psD", "rstd", "gtmp"], writes=["gsil%d" % sub])
            if l == 0:
                S.op("dve", lambda e: e.scalar_tensor_tensor(out=xl[0:rows, :], in0=psE[0:rows, 0:4], scalar=rs,
                                                             in1=bfb[0:rows, :], op0=ALU.mult, op1=ALU.add),
                     reads=["psE", "rstd", "bfb"], writes=["xl"])
                S.op("act", lambda e: e.activation(out=el[0:rows, :], in_=xl[0:rows, :], func=AF.Exp, scale=-1.0),
                     reads=["xl"], writes=["el"])
                S.op("act", lambda e: e.activation(out=lps[0:rows, :], in_=el[0:rows, :], func=AF.Ln, bias=cbias[0:rows, 1:2]),
                     reads=["el", "cbias"], writes=["lps"])
                S.op("pool", lambda e: e.tensor_scalar(out=lfo[0:rows, tb, :], in0=lps[0:rows, :], scalar1=-1.0,
                                                       scalar2=0.0, op0=ALU.mult, op1=ALU.add),
                     reads=["lps"], writes=["lfo"])
            transposes_to(q_tok, "q_tok", rows, lambda h: qT[:, :, qcol0:qcol0 + rows], ["qT"], "dve")
            transposes_to(k_tok, "k_tok", rows, kT_dst, kT_names, "act")

        def cumsum_block(tb):
            first = (tb == 0)
            S.op("pe", lambda e: e.matmul(psE[:, 8:12], lhsT=cf(C_U), rhs=lps[:, 0:4], start=True, stop=first),
                 reads=["cstf", "lps"], writes=["psE"], sig=first)
            if not first:
                S.op("pe", lambda e: e.matmul(psE[:, 8:12], lhsT=cf(C_E127), rhs=cpall[:, tb - 1, 0:4], start=False,
                                              stop=True),
                     reads=["cstf", "cpall"], writes=["psE"], sig=False)
                S.op("pe", lambda e: e.matmul(psE[:, 12:16], lhsT=cf(C_E127), rhs=cpall[:, tb - 1, 0:4], start=True,
                                              stop=True),
                     reads=["cstf", "cpall"], writes=["psE"])
                S.op("dve", lambda e: e.tensor_copy(out=cpall[:, tb, 0:8], in_=psE[:, 8:16]), reads=["psE"],
                     writes=["cpall"])
            else:
                S.op("dve", lambda e: e.tensor_copy(out=cpall[:, tb, 0:4], in_=psE[:, 8:12]), reads=["psE"],
                     writes=["cpall"])

        sc_banks = [(psA, "psA"), (psB, "psB")]
        o_banks = [((psC, "psC"), (psD, "psD")), ((psF, "psF"), (psG, "psG"))]

        def y_out(tbs, rows_l):
            for sub, tb in enumerate(tbs):
                rows = rows_l[sub]
                for c in range(4):
                    S.op("pe", lambda e, c=c, sub=sub, rows=rows: e.transpose(out=pst[:, c * 128:c * 128 + rows],
                                                                               in_=ybuf[sub][0:rows, c * 128:(c + 1) * 128],
                                                                               identity=ident[0:rows, 0:rows]),
                         reads=["y%d_%d" % (sub, hh) for hh in range(4)] + ["cstb"], writes=PST_ALL, sig=(c == 3))
                i = cnt["tr"] % 2
                cnt["tr"] += 1
                pv = pst[:, 0:512].rearrange("p (c t) -> p c t", t=128)
                S.op("act", lambda e, i=i, rows=rows, pv=pv: e.activation(out=yTs[i][:, :, 0:rows], in_=pv[:, :, 0:rows],
                                                                           func=AF.Identity),
                     reads=PST_ALL, writes=["yTs%d" % i])
                pc, c0 = piece_of(tb)
                dst = yT_loc[pc].ap().rearrange("(c p) t -> p c t", p=128)
                S.op("sp", lambda e, i=i, rows=rows, c0=c0, dst=dst: e.dma_start(out=dst[:, :, c0:c0 + rows],
                                                                                  in_=yTs[i][:, :, 0:rows]),
                     reads=["yTs%d" % i], writes=["yT_loc%d" % pc], chan="yTs%d" % i)
                if tb % 8 == 7 or tb == 32:
                    if pc == 3:
                        deferred_y.append(pc)
                    else:
                        for dpc in deferred_y:
                            gather(yT_loc[dpc], yT_all[dpc], "yT", dpc)
                        del deferred_y[:]
                        gather(yT_loc[pc], yT_all[pc], "yT", pc)

        pcount = {"p": 0, "sc": 0, "ch": 0}

        def run_streams(makers, width):
            pending = list(makers)
            active = []
            free = list(range(width))
            eng_free = {}
            now = 0.0
            while pending or active:
                while pending and free:
                    sl = free.pop(0)
                    g = pending.pop(0)(sl)
                    try:
                        nxt = next(g)
                    except StopIteration:
                        free.append(sl)
                        continue
                    active.append({"g": g, "sl": sl, "ready": now, "nxt": nxt})
                if not active:
                    continue
                best = min(active, key=lambda a: max(eng_free.get(a["nxt"][0], 0.0), a["ready"]))
                eng, dur = best["nxt"]
                start = max(eng_free.get(eng, 0.0), best["ready"])
                end = start + dur
                eng_free[eng] = end
                best["ready"] = end + 0.25
                try:
                    best["nxt"] = next(best["g"])
                except StopIteration:
                    active.remove(best)
                    free.append(best["sl"])
                    now = end

        def fox_stream(slot, Q, h):
            nkb = 2 * Q + 2
            ob = o_banks[slot]
            fsets = [[(psA, "psA"), (psB, "psB"), (psE, "psE")], [(pst_f32, "pstbank"), (psF, "psF"), (psG, "psG")]]
            batches = [list(range(k0, min(k0 + 3, nkb))) for k0 in range(0, nkb, 3)]

            def emit_qk(b):
                for i, kb in enumerate(batches[b]):
                    qlo = 128 if kb == nkb - 1 else 0
                    scb, scn = fsets[b % 2][i]
                    S.op("pe", lambda e, kb=kb, qlo=qlo, scb=scb: e.matmul(
                        scb[:, qlo:256], lhsT=kT(h, kb * 128, (kb + 1) * 128), rhs=qT[:, h, qlo:256], start=True, stop=True),
                        reads=["kT%d_%d" % (h, kb), "qT"], writes=[scn])

            yield ("pe", 0.4)
            emit_qk(0)
            for b, kbs in enumerate(batches):
                if b + 1 < len(batches):
                    emit_qk(b + 1)
                for i, kb in enumerate(kbs):
                    qlo = 128 if kb == nkb - 1 else 0
                    scb, scn = fsets[b % 2][i]
                    pi = 3 * (b % 2) + i
                    S.op("act", lambda e, kb=kb, qlo=qlo, pi=pi, scb=scb: e.activation(
                        out=Pb[pi][:, qlo:256], in_=scb[:, qlo:256], func=AF.Exp, scale=SCALE, bias=biasT[:, kb, h:h + 1]),
                        reads=[scn, "biasT"], writes=["P%d" % pi])
                    if kb >= nkb - 2:
                        dq = 0 if kb == nkb - 2 else 128
                        S.op("pool", lambda e, pi=pi, dq=dq: e.tensor_tensor(out=Pb[pi][:, dq:dq + 128],
                                                                             in0=Pb[pi][:, dq:dq + 128], in1=maskLE,
                                                                             op=ALU.mult),
                             reads=["P%d" % pi, "cstb"], writes=["P%d" % pi])
                for i, kb in enumerate(kbs):
                    pi = 3 * (b % 2) + i
                    for sub in range(2):
                        if kb == nkb - 1 and sub == 0:
                            continue
                        last = (kb == nkb - 2) if sub == 0 else (kb == nkb - 1)
                        S.op("pe", lambda e, kb=kb, sub=sub, pi=pi, last=last: e.matmul(
                            ob[sub][0][:, 0:129], lhsT=Pb[pi][:, sub * 128:(sub + 1) * 128], rhs=Vaug(kb, h, 129),
                            start=(kb == 0), stop=last),
                            reads=["P%d" % pi, "V%d" % kb], writes=[ob[sub][1]], sig=(sub == 1 or kb == nkb - 2))
            for sub in range(2):
                S.op("dve", lambda e, sub=sub: e.reciprocal(out=rden[:, 2 * slot + sub:2 * slot + sub + 1],
                                                            in_=ob[sub][0][:, 128:129]),
                     reads=[ob[sub][1]], writes=["rden%d" % slot])
                S.op("dve", lambda e, sub=sub: e.scalar_tensor_tensor(
                    out=ybuf[sub][:, h * 128:(h + 1) * 128], in0=ob[sub][0][:, 0:128],
                    scalar=rden[:, 2 * slot + sub:2 * slot + sub + 1],
                    in1=gsil[sub][:, h * 128:(h + 1) * 128], op0=ALU.mult, op1=ALU.mult),
                    reads=[ob[sub][1], "rden%d" % slot, "gsil%d" % sub], writes=["y%d_%d" % (sub, h)])

        def fox_tile(Q):
            nkb = 2 * Q + 2
            for h in range(4):
                S.op("dve", lambda e, h=h: e.tensor_scalar(out=biasT[:, 0:nkb, h], in0=cpall[:, 0:nkb, h],
                                                           scalar1=cpall[:, 2 * Q + 1, 4 + h:5 + h], scalar2=None,
                                                           op0=ALU.subtract),
                     reads=["cpall"], writes=["biasT"])
            run_streams([(lambda sl, h=h: fox_stream(sl, Q, h)) for h in range(4)], FOXW)

        def sb_stream(slot, qrows, qT_ap, qT_names, chunks, o_ap, o_name, pv, total_pv, carry_in=None, aT_hook=None,
                      fin=None, out_state=None):
            prev = carry_in
            th, Rx, a_ = thb[slot], Rext[slot], ab[slot]
            tn = ["th0", "th1", "th2", "gtmp"][slot]
            rn = ["Rx0", "Rx1", "Rx2", "kf321"][slot]
            an = ["a0", "a1", "a2", "k_tok"][slot]
            aTnames = [["aT0"], ["aT1"], ["P0", "P1"], ["q_tok"]][slot]
            scb, scn = [(psA, "psA"), (psB, "psB"), (psE, "psE"), (psG, "psG")][slot]
            pst, pn, pso = scb[:, :].bitcast(BF16), scn, 0
            for ci, ch in enumerate(chunks):
                W = ch["W"]
                yield ("pe", 0.45)
                S.op("pe", lambda e, ch=ch, W=W, scb=scb: e.matmul(scb[0:qrows, 0:W], lhsT=qT_ap, rhs=ch["kT"], start=True, stop=True),
                     reads=qT_names + ch["names"], writes=[scn])
                yield ("act", 0.65)
                S.op("act", lambda e, W=W, scb=scb: e.activation(out=th[0:qrows, 0:W], in_=scb[0:qrows, 0:W], func=AF.Sigmoid,
                                                        scale=-SCALE),
                     reads=[scn], writes=[tn])
                yield ("dve", 1.5)
                if ch.get("mask") is not None:
                    M, Mc, dw = ch["mask"]
                    S.op("dve", lambda e, W=W, dw=dw, M=M: e.tensor_tensor(out=th[0:qrows, W - dw:W], in0=th[0:qrows, W - dw:W],
                                                                           in1=M, op=ALU.mult),
                         reads=[tn, "cstf"], writes=[tn])
                    S.op("dve", lambda e, W=W, dw=dw, Mc=Mc: e.tensor_tensor(out=th[0:qrows, W - dw:W], in0=th[0:qrows, W - dw:W],
                                                                             in1=Mc, op=ALU.add),
                         reads=[tn, "cstf"], writes=[tn])
                if prev is None:
                    S.op("dve", lambda e, W=W: e.memset(Rx[0:qrows, W:W + 1], 1.0), reads=[], writes=[rn])
                else:
                    pR, pname = prev
                    S.op("dve", lambda e, W=W, pR=pR: e.tensor_copy(out=Rx[0:qrows, W:W + 1], in_=pR),
                         reads=[pname, an], writes=[rn])
                S.op("dve", lambda e, W=W: e.tensor_tensor_scan(
                    out=Rx[0:qrows, 0:W][:, ::-1], data0=th[0:qrows, 0:W][:, ::-1], data1=zeros[0:qrows, 0:W],
                    initial=Rx[0:qrows, W:W + 1], op0=ALU.mult, op1=ALU.add),
                    reads=[tn, rn, "zeros"], writes=[rn])
                prev = (Rx[0:qrows, 0:1], rn)
                if True:
                    yield ("pool", 1.3)
                S.op("pool", lambda e, W=W: e.tensor_tensor(out=a_[0:qrows, 0:W], in0=Rx[0:qrows, 1:W + 1],
                                                                                   in1=Rx[0:qrows, 0:W], op=ALU.subtract),
                     reads=[rn], writes=[an])
                yield ("pe", 0.7)
                vbl = ch["vblocks"]
                off = 0
                for j, (rhs, wk, vn) in enumerate(vbl):
                    S.op("pe", lambda e, off=off, wk=wk, j=j: e.transpose(out=pst[0:wk, pso + j * 128:pso + j * 128 + qrows],
                                                                          in_=a_[0:qrows, off:off + wk],
                                                                          identity=ident[0:qrows, 0:qrows]),
                         reads=[an, "cstb"], writes=[pn], sig=(j == len(vbl) - 1))
                    off += wk
                yield ("act", 0.65)
                if aT_hook is None:
                    aT = aTb[slot]
                    nb_ = len(vbl)
                    pvw = pst[:, pso:pso + nb_ * 128].rearrange("p (j t) -> p j t", t=128)
                    S.op("act", lambda e, aT=aT, pvw=pvw, nb_=nb_: e.activation(out=aT[:, 0:nb_, 0:qrows], in_=pvw[:, :, 0:qrows],
                                                                                func=AF.Identity),
                         reads=[pn], writes=aTnames)
                    lhs_list = [(aT[0:wk, j, 0:qrows], aTnames) for j, (_, wk, _) in enumerate(vbl)]
                else:
                    lhs_list = aT_hook(ci, vbl, pso, pn, pst)
                if aT_hook is None:
                    yield ("pe", 0.7)
                for j, (rhs, wk, vn) in enumerate(vbl):
                    lhsT, ln = lhs_list[j]
                    n = pv["n"]
                    S.op("pe", lambda e, lhsT=lhsT, rhs=rhs, n=n: e.matmul(o_ap, lhsT=lhsT, rhs=rhs, start=(n == 0),
                                                                           stop=(n == total_pv - 1)),
                         reads=(ln if isinstance(ln, list) else [ln]) + [vn], writes=[o_name], sig=True)
                    pv["n"] += 1
            if out_state is not None:
                out_state["carry"] = prev
            if fin is not None:
                yield ("dve", 0.3)
                fin()

        sb_obank = {0: [(psC, "psC")], 1: [(psD, "psD")], 2: [(psF, "psF")], 3: [(pst_f32, "pstbank")]}
        sb_ocnt = {0: 0, 1: 0}

        def sb_tile(Q):
            makers = []
            for sub in range(2):
                qb = 2 * Q + sub
                e_ = 128 * (qb + 1)
                for h in range(4):
                    def mk(slot, sub=sub, h=h, e_=e_):
                        chunks = []
                        hi = e_
                        while hi > 0:
                            lo = max(0, hi - 512)
                            vbl = [(Vaug(b, h, 128), 128, "V%d" % b) for b in range(lo // 128, hi // 128)]
                            chunks.append(dict(kT=kT(h, lo, hi), W=hi - lo,
                                               names=["kT%d_%d" % (h, b) for b in range(lo // 128, hi // 128)],
                                               vblocks=vbl,
                                               mask=(cf(C_MLT), cf(C_MLTC), 128) if hi == e_ else None))
                            hi = lo
                        ob, on = sb_obank[slot][0]

                        def fin():
                            S.op("dve", lambda e: e.tensor_tensor(out=ybuf[sub][:, h * 128:(h + 1) * 128], in0=ob[:, 0:128],
                                                                  in1=gsil[sub][:, h * 128:(h + 1) * 128], op=ALU.mult),
                                 reads=[on, "gsil%d" % sub], writes=["y%d_%d" % (sub, h)])
                        return sb_stream(slot, 128, qT[:, h, sub * 128:(sub + 1) * 128], ["qT"], chunks, ob[:, 0:128], on,
                                         {"n": 0}, sum(len(c["vblocks"]) for c in chunks), fin=fin)
                    makers.append(mk)
            run_streams(makers, SBW)

        SEQSZ = 8 * 520 + 4096
        assert 4 * SEQSZ <= 4 * T + 32 * 4 * 130
        kstage_f = actT[1][:, :, :].rearrange("p c t -> p (c t)").bitcast(F32).rearrange("p (j c) -> p j c", c=512)

        def sVc_all(s_):
            b0 = s_ * SEQSZ
            return regB[:, b0:b0 + 8 * 520].rearrange("p (j h e) -> p j h e", h=4, e=130)

        def sVc(s_, j, h, n):
            o = s_ * SEQSZ + (j * 4 + h) * 130
            return regB[:, o:o + n]

        def skT(s_, h, lo, hi):
            o = s_ * SEQSZ + 8 * 520 + h * 1024
            return regB[:, o + lo:o + hi]

        def Pw8(s_):
            v = thb[s_ // 2][:, :].bitcast(BF16)
            return v[:, (s_ % 2) * 512:(s_ % 2 + 1) * 512].rearrange("p (j c) -> p j c", c=64)

        def Vn_v():
            return Vn_t[:, :].rearrange("p (h e) -> p h e", e=130)

        def sample_stage(l):
            kcache, vcache = (cfk, cfv) if l == 0 else (csk, csv)
            for s_ in range(4):
                for j in range(8):
                    S.op("pool", lambda e, s_=s_, j=j: e.dma_start(
                        out=sVc_all(s_)[:, j, :, 0:128],
                        in_=vcache[s_][j * 128:(j + 1) * 128, :].rearrange("p (h d) -> p h d", d=128)),
                         writes=["sVc%d" % s_] + (REGB_ALL + ["w_out"] if j == 0 else []), chan="sVc")
                S.op("pool", lambda e, s_=s_: e.memset(sVc_all(s_)[:, :, :, 128:129], 1.0), writes=["sVc%d" % s_])
                for hf in range(2):
                    S.op("sp", lambda e, s_=s_, hf=hf: e.dma_start(
                        out=kstage_f, in_=kcache[s_][hf * 512:(hf + 1) * 512, :].rearrange("(j p) c -> p j c", p=128)),
                         writes=["actT1"], chan="sKc")
                    for jj in range(4):
                        j = hf * 4 + jj
                        for h in range(4):
                            S.op("pe", lambda e, jj=jj, h=h: e.transpose(out=pst_f32[:, h * 128:(h + 1) * 128],
                                                                         in_=kstage_f[:, jj, h * 128:(h + 1) * 128],
                                                                         identity=cstf[:, C_ID:C_ID + 128]),
                                 reads=["actT1", "cstf"], writes=PST_ALL, sig=(h == 3))
                        kb_ = s_ * SEQSZ + 8 * 520
                        S.op("act", lambda e, kb_=kb_, j=j: e.activation(
                            out=regB[:, kb_:kb_ + 4096].rearrange("p (h t) -> p h t", t=1024)[:, :, j * 128:(j + 1) * 128],
                            in_=pst_f32[:, 0:512].rearrange("p (h t) -> p h t", t=128), func=AF.Identity),
                             reads=PST_ALL, writes=["skT%d" % s_] + (REGB_ALL + ["w_out"] if j == 0 else []))
            if l == 0:
                for s_ in range(4):
                    S.op("sp", lambda e, s_=s_: e.dma_start(out=clf[:, :, s_ * 4:(s_ + 1) * 4],
                                                            in_=cfl[s_].rearrange("(j p) h -> p j h", p=128)),
                         writes=["clf"], chan="clf")
                for j in range(8):
                    S.op("pe", lambda e, j=j: e.matmul(psE[:, 16 + 16 * j:32 + 16 * j], lhsT=cf(C_LS), rhs=clf[:, j, :], start=True,
                                                       stop=(j == 7)), reads=["cstf", "clf"], writes=["psE"], sig=(j == 7))
                    for j2 in range(j + 1, 8):
                        S.op("pe", lambda e, j=j, j2=j2: e.matmul(psE[:, 16 + 16 * j:32 + 16 * j], lhsT=cf(C_ONES), rhs=clf[:, j2, :],
                                                                  start=False, stop=(j2 == 7)), reads=["cstf", "clf"],
                             writes=["psE"], sig=(j2 == 7))
                S.op("dve", lambda e: e.tensor_copy(out=sufb[:].rearrange("p j c -> p (j c)"), in_=psE[:, 16:144]), reads=["psE"],
                     writes=["sufb"])

        def sample_block(l, slot):
            tb = 32
            rows = TS
            Vn = Vn_v()
            project_block(l, slot, 0, tb, rows, lambda h: qT[:, :, 128:128 + rows], ["qT"],
                          Vn_v, ["Vn"], 0)
            if l == 0:
                S.op("pe", lambda e: e.matmul(psE[0:64, 8:12], lhsT=cf(C_UB16, 64, 64), rhs=lps[0:64, 0:4], start=True, stop=True),
                     reads=["cstf", "lps"], writes=["psE"])
                S.op("dve", lambda e: e.tensor_copy(out=cpn[:, :], in_=psE[0:64, 8:12]), reads=["psE"], writes=["cpn"])
            for h in range(4):
                ob, on = o_banks[h % 2][0]
                if l == 0:
                    o_ap = ob[0:64, 0:129]
                    first = True
                    for s_ in range(4):
                        scb, scn = sc_banks[pcount["sc"] % 2]
                        pcount["sc"] += 1
                        pw = Pw8(s_)
                        pwn = "th%d" % (s_ // 2)
                        for j in range(8):
                            S.op("pe", lambda e, s_=s_, j=j, h=h, scb=scb: e.matmul(
                                scb[:, j * 16:(j + 1) * 16], lhsT=skT(s_, h, j * 128, (j + 1) * 128),
                                rhs=qT[:, h, s_ * 16:(s_ + 1) * 16], start=True, stop=True),
                                reads=["skT%d" % s_, "qT"], writes=[scn], sig=(j == 7))
                        for j in range(8):
                            S.op("act", lambda e, s_=s_, j=j, h=h, scb=scb, pw=pw: e.activation(
                                out=pw[:, j, s_ * 16:(s_ + 1) * 16], in_=scb[:, j * 16:(j + 1) * 16], func=AF.Exp, scale=SCALE,
                                bias=sufb[:, j, s_ * 4 + h:s_ * 4 + h + 1]), reads=[scn, "sufb"], writes=[pwn])
                        for j in range(8):
                            S.op("pe", lambda e, s_=s_, j=j, h=h, first=first, o_ap=o_ap, pw=pw: e.matmul(
                                o_ap, lhsT=pw[:, j, 0:64], rhs=sVc(s_, j, h, 129), start=first, stop=False),
                                reads=[pwn, "sVc%d" % s_], writes=[on], sig=(j == 7))
                            first = False
                    scb, scn = sc_banks[pcount["sc"] % 2]
                    pcount["sc"] += 1
                    S.op("pe", lambda e, h=h, scb=scb: e.matmul(scb[0:64, 0:64], lhsT=qT[:, h, 128:192], rhs=qT[:, h, 0:64],
                                                                start=True, stop=True), reads=["qT"], writes=[scn])
                    S.op("act", lambda e, h=h, scb=scb: e.activation(out=Pn[:, :], in_=scb[0:64, 0:64], func=AF.Exp, scale=SCALE,
                                                                     bias=cpn[:, h:h + 1]), reads=[scn, "cpn"], writes=["Pn"])
                    S.op("pool", lambda e: e.tensor_tensor(out=Pn[:, :], in0=Pn[:, :], in1=mask64, op=ALU.mult),
                         reads=["Pn", "cstb"], writes=["Pn"])
                    S.op("pe", lambda e, h=h, o_ap=o_ap: e.matmul(o_ap, lhsT=Pn[:, :], rhs=Vn[0:64, h, 0:129], start=False, stop=True),
                         reads=["Pn", "Vn"], writes=[on])
                    S.op("dve", lambda e, ob=ob: e.reciprocal(out=rden[0:64, 0:1], in_=ob[0:64, 128:129]), reads=[on],
                         writes=["rden0"])
                    S.op("dve", lambda e, h=h, ob=ob: e.scalar_tensor_tensor(
                        out=ybuf[0][0:64, h * 128:(h + 1) * 128], in0=ob[0:64, 0:128], scalar=rden[0:64, 0:1],
                        in1=gsil[0][0:64, h * 128:(h + 1) * 128], op0=ALU.mult, op1=ALU.mult),
                        reads=[on, "rden0", "gsil0"], writes=["y0_%d" % h])
                else:
                    pass
            if l == 1:
                def head_stream(slot, h):
                    ob, on = sb_obank[slot][0]
                    o_ap = ob[0:64, 0:128]
                    pv = {"n": 0}
                    total = 1 + 4 * 8

                    def hook0(ci, vbl, pso, pn, pst_=None):
                        S.op("act", lambda e: e.activation(out=aTn[:, :], in_=pst_[0:64, pso:pso + 64], func=AF.Identity),
                             reads=[pn], writes=["aTn"])
                        return [(aTn[:, :], "aTn")]
                    ch0 = dict(kT=qT[:, h, 128:192], W=64, names=["qT"], vblocks=[(Vn[0:64, h, 0:128], 64, "Vn")],
                               mask=(cf(C_MS, 64, 64), cf(C_MSC, 64, 64), 64))
                    st = {}
                    yield from sb_stream(slot, 64, qT[:, h, 0:64], ["qT"], [ch0], o_ap, on, pv, total, None, hook0, None, st)
                    car = st["carry"]
                    cr = carry0[:, slot:slot + 1]
                    yield ("dve", 0.1)
                    S.op("dve", lambda e: e.tensor_copy(out=cr, in_=car[0]), reads=[car[1]], writes=["carry0_%d" % slot])
                    for s_ in range(4):
                        def hook(ci, vbl, pso, pn, pst_=None, s_=s_):
                            base = 4 if ci == 0 else 0
                            pvw = pst_[:, pso:pso + 512].rearrange("p (j t) -> p j t", t=128)
                            S.op("act", lambda e: e.activation(out=aTw[s_][:, base:base + 4, s_ * 16:(s_ + 1) * 16],
                                                               in_=pvw[:, :, s_ * 16:(s_ + 1) * 16], func=AF.Identity),
                                 reads=[pn], writes=["aTw%d" % s_])
                            return [(aTw[s_][:, base + j, 0:64], "aTw%d" % s_) for j in range(4)]
                        chunks = []
                        for (lo, hi) in ((512, 1024), (0, 512)):
                            chunks.append(dict(kT=skT(s_, h, lo, hi), W=512, names=["skT%d" % s_],
                                               vblocks=[(sVc(s_, b, h, 128), 128, "sVc%d" % s_) for b in range(lo // 128, hi // 128)],
                                               mask=None))
                        yield from sb_stream(slot, 64, qT[:, h, 0:64], ["qT"], chunks, o_ap, on, pv, total,
                                             (cr, "carry0_%d" % slot), hook)
                    yield ("dve", 0.3)
                    S.op("dve", lambda e: e.tensor_tensor(out=ybuf[0][0:64, h * 128:(h + 1) * 128], in0=ob[0:64, 0:128],
                                                          in1=gsil[0][0:64, h * 128:(h + 1) * 128], op=ALU.mult),
                         reads=[on, "gsil0"], writes=["y0_%d" % h])
                run_streams([(lambda sl, h=h: head_stream(sl, h)) for h in range(4)], SBW)
            load_w_out(l)
            y_out([tb], [rows])

        def p_phase(l):
            load_act(0, xT_all, "xT", 0, 256)
            nt = 16
            for Q in range(nt):
                slot = Q % 2
                if Q < 15:
                    load_act(1 - slot, xT_all, "xT", (Q + 1) * 256, 256)
                else:
                    load_act(1 - slot, xT_all, "xT", T, TS)
                for sub in range(2):
                    tb = 2 * Q + sub
                    project_block(l, slot, sub, tb, 128,
                                  lambda h, tb=tb: regB[:, 0:4 * T].rearrange("p (h t) -> p h t", t=T)[:, :, tb * 128:(tb + 1) * 128],
                                  ["kT%d_%d" % (h, tb) for h in range(4)],
                                  lambda tb=tb: Vaug_blk(tb), ["V%d" % tb], sub * 128)
                    if l == 0:
                        cumsum_block(tb)
                if l == 0:
                    fox_tile(Q)
                else:
                    sb_tile(Q)
                y_out([2 * Q, 2 * Q + 1], [128, 128])
            if nt == 16:
                sample_stage(l)
                sample_block(l, 0)

        def o_phase(l):
            wq = load_w_in(1) if l == 0 else iter(())
            wv = w_out()
            load_act(0, yT_all, "yT", 0, 256)
            xsrc = xc0 if l == 0 else xres[1].ap()
            xdst = xres[l + 1].ap()

            def o_load(tb):
                rows = blk_rows(tb)
                i = tb % 2
                S.op("sp", lambda e: e.dma_start(out=xcb[i][0:rows, :], in_=xsrc[tb * 128:tb * 128 + rows, :]),
                     reads=["xres%d" % l], writes=["kf32%d" % i], chan="kf32%d" % i)

            for Q in range(17):
                slot = Q % 2
                if Q < 15:
                    load_act(1 - slot, yT_all, "yT", (Q + 1) * 256, 256)
                elif Q == 15:
                    load_act(1 - slot, yT_all, "yT", T, TS)
                next(wq, None)
                for sub in range(2 if Q < 16 else 1):
                    tb = 2 * Q + sub
                    rows = blk_rows(tb)
                    i = tb % 2
                    if tb == 0:
                        o_load(0)
                    if tb + 1 < NB:
                        o_load(tb + 1)
                    for c in range(16):
                        S.op("pe", lambda e, c=c, slot=slot, sub=sub, rows=rows: e.matmul(
                            psA[0:rows, :], lhsT=actT[slot][:, c, sub * 128:sub * 128 + rows], rhs=wv[:, c, :], start=(c == 0),
                            stop=(c == 15)), reads=["actT%d" % slot, "w_out"], writes=["psA"], sig=(c == 15))
                    S.op("dve", lambda e, i=i, rows=rows: e.tensor_tensor(out=xnb[i][0:rows, :], in0=psA[0:rows, :],
                                                                          in1=xcb[i][0:rows, :], op=ALU.add),
                         reads=["psA", "kf32%d" % i], writes=["vf32%d" % i])
                    S.op("sp", lambda e, i=i, tb=tb, rows=rows: e.dma_start(out=xdst[tb * 128:tb * 128 + rows, :],
                                                                              in_=xnb[i][0:rows, :]),
                         reads=["vf32%d" % i], writes=["xres%d" % (l + 1)], chan="vf32%d" % i)
                    norm_prep(xnb[i], "vf32%d" % i, tb, rows, l + 1, l == 0)
            for _ in wq:
                pass

        stage = 99
        if stage >= 2:
            p_phase(0)
        if stage >= 4:
            load_nw(1)
            o_phase(0)
            rstd_from_ssq("1")
        if stage >= 5:
            p_phase(1)
        if stage >= 7:
            load_nw(2)
            o_phase(1)
            rstd_from_ssq("2")
            x2 = xres[2].ap()
            fin_in = [(kf32[0], "kf320"), (kf32[1], "kf321"), (thb[0], "th0"), (thb[1], "th1")]
            fin_out = [(vf32[0], "vf320"), (vf32[1], "vf321"), (thb[2], "th2"), (gtmp, "gtmp")]

            def f_load(tb):
                rows = blk_rows(tb)
                buf, nm = fin_in[tb % 4]
                S.op("sp", lambda e: e.dma_start(out=buf[0:rows, 0:512], in_=x2[tb * 128:tb * 128 + rows, :]),
                     reads=["xres2"], writes=[nm], chan=nm)

            for t_ in range(3):
                f_load(t_)
            for tb in range(NB):
                rows = blk_rows(tb)
                if tb + 3 < NB:
                    f_load(tb + 3)
                ib, inm = fin_in[tb % 4]
                ob_, onm = fin_out[tb % 4]
                S.op("dve", lambda e, tb=tb, rows=rows, ib=ib, ob_=ob_: e.scalar_tensor_tensor(
                    out=ob_[0:rows, 0:512], in0=ib[0:rows, 0:512], scalar=rstd[0:rows, tb:tb + 1], in1=nw[2][0:rows, :],
                    op0=ALU.mult, op1=ALU.mult), reads=[inm, "rstd", "nw"], writes=[onm])
                S.op("sp", lambda e, tb=tb, rows=rows, ob_=ob_: e.dma_start(out=yout[tb * 128:tb * 128 + rows, :],
                                                                             in_=ob_[0:rows, 0:512]),
                     reads=[onm], writes=[], chan=onm)
        S.op("sp", lambda e: e.dma_start(out=lfout[0:T, :].rearrange("(b p) h -> p b h", p=128), in_=lfo[:, 0:32, :]),
             reads=["lfo"], writes=[], chan="lfo")
        S.op("sp", lambda e: e.dma_start(out=lfout[T:TT, :], in_=lfo[0:TS, 32, :]), reads=["lfo"], writes=[], chan="lfo")
        for sn, v in list(S.cnt.items()):
            if sn.startswith("d_"):
                S.prog["sp"].append(("wait", sn, v))

        sem_names = sorted(S.cnt.keys())
        sems = {}
        for sn in sem_names:
            sems[sn] = es.enter_context(nc.semaphore(sn))
        block = es.enter_context(nc.Block())

        def emit(eng_obj, name):
            for item in S.prog[name]:
                if item[0] == "wait":
                    eng_obj.wait_ge(sems[item[1]], item[2])
                else:
                    _, fn, sn, inc = item
                    ins = fn(eng_obj)
                    if sn is not None:
                        if inc is None:
                            ins.then_inc(sems[sn])
                        else:
                            ins.then_inc(sems[sn], inc)

        @block.sync
        def _(e):
            emit(e, "sp")

        @block.scalar
        def _(e):
            emit(e, "act")

        @block.vector
        def _(e):
            emit(e, "dve")

        @block.gpsimd
        def _(e):
            emit(e, "pool")

        @block.tensor
        def _(e):
            emit(e, "pe")
    return nc


def _consts():
    c = np.zeros((128, NCST), np.float32)
    k = np.arange(128)[:, None]
    m = np.arange(128)[None, :]
    c[:, C_U:C_U + 128] = (k <= m)
    c[:, C_E127:C_E127 + 128] = (k == 127)
    c[:, C_UB16:C_UB16 + 128] = (k <= m) & (k // 16 == m // 16)
    c[:, C_LS:C_LS + 128] = (k > m)
    c[:, C_ONES:C_ONES + 128] = 1.0
    c[:, C_MS:C_MS + 128] = (m < k) & (k // 16 == m // 16)
    c[:, C_MSC:C_MSC + 128] = 1.0 - ((m < k) & (k // 16 == m // 16))
    c[:, C_MLT:C_MLT + 128] = (m < k)
    c[:, C_MLTC:C_MLTC + 128] = 1.0 - (m < k)
    c[:, C_ID:C_ID + 128] = (k == m)
    c[:, C_MLE:C_MLE + 128] = (k <= m)
    c[:, C_M64:C_M64 + 128] = (k <= m) & (k // 16 == m // 16)
    return c


_NC_CACHE = {}


def kernel(x_prompt, x_sample, cache_fox_k, cache_fox_v, cache_fox_logf, cache_sb_k, cache_sb_v,
           norm_0, w_in_0, b_f_0, w_out_0, norm_1, w_in_1, w_out_1, norm_f):
    f = np.float32
    A = lambda a: np.ascontiguousarray(np.asarray(a, dtype=f))
    x_prompt, x_sample = A(x_prompt), A(x_sample)
    w_in_0, w_in_1, w_out_0, w_out_1 = A(w_in_0), A(w_in_1), A(w_out_0), A(w_out_1)
    caches = [A(cache_fox_k), A(cache_fox_v), A(cache_sb_k), A(cache_sb_v)]
    cache_fox_logf = A(cache_fox_logf)
    norms = [A(norm_0), A(norm_1), A(norm_f)]
    b_f_0 = A(b_f_0)
    if "nc" not in _NC_CACHE:
        _NC_CACHE["nc"] = build_program()
    nc = _NC_CACHE["nc"]
    cst = _consts()
    in_maps = []
    for core in range(8):
        b, g = core // 4, core % 4
        cs = slice(g * 512, (g + 1) * 512)
        xc0 = np.concatenate([x_prompt[b][:, cs], x_sample[4 * b:4 * b + 4].reshape(64, D)[:, cs]], axis=0)
        m = {"xc0": A(xc0), "cst": cst}
        for i, nm in enumerate(["nw0", "nw1", "nwf"]):
            m[nm] = A(np.broadcast_to(norms[i][cs][None, :], (128, 512)))
        m["win0"] = A(np.concatenate([w_in_0[:, j * D + g * 512: j * D + (g + 1) * 512] for j in range(4)]
                                     + [w_in_0[:, 4 * D + 4 * g: 4 * D + 4 * g + 4]], axis=1))
        m["win1"] = A(np.concatenate([w_in_1[:, j * D + g * 512: j * D + (g + 1) * 512] for j in range(4)], axis=1))
        m["wout0"] = A(w_out_0[:, cs])
        m["wout1"] = A(w_out_1[:, cs])
        m["bfb"] = A(np.broadcast_to(b_f_0[4 * g:4 * g + 4][None, :], (128, 4)))
        for nm, cch in zip(["cfk", "cfv", "csk", "csv"], caches):
            m[nm] = A(cch[4 * b:4 * b + 4, :, 4 * g:4 * g + 4, :].reshape(4, PAST, 512))
        m["cfl"] = A(cache_fox_logf[4 * b:4 * b + 4, :, 4 * g:4 * g + 4])
        in_maps.append(m)
    res = run_bass_kernel_spmd(nc, in_maps, core_ids=list(range(8)))
    R = res.results
    y_prompt = np.zeros((2, T, D), f)
    y_sample = np.zeros((8, 16, D), f)
    pk = [np.zeros((2, T, 16, 128), f) for _ in range(4)]
    sk = [np.zeros((8, 16, 16, 128), f) for _ in range(4)]
    plf = np.zeros((2, T, 16), f)
    slf = np.zeros((8, 16, 16), f)
    for core in range(8):
        b, g = core // 4, core % 4
        cs = slice(g * 512, (g + 1) * 512)
        r = R[core]
        y_prompt[b][:, cs] = r["yout"][:T]
        y_sample[4 * b:4 * b + 4][:, :, cs] = r["yout"][T:].reshape(4, 16, 512)
        for i, nm in enumerate(["kf", "vf", "ks", "vs"]):
            pk[i][b][:, 4 * g:4 * g + 4, :] = r[nm][:T].reshape(T, 4, 128)
            sk[i][4 * b:4 * b + 4][:, :, 4 * g:4 * g + 4, :] = r[nm][T:].reshape(4, 16, 4, 128)
        plf[b][:, 4 * g:4 * g + 4] = r["lf"][:T]
        slf[4 * b:4 * b + 4][:, :, 4 * g:4 * g + 4] = r["lf"][T:].reshape(4, 16, 4)
    return (y_prompt, y_sample, pk[0], pk[1], plf, pk[2], pk[3], sk[0], sk[1], slf, sk[2], sk[3])
```

```python
import numpy as np
import concourse.bass as bass
import concourse.mybir as mybir
from concourse.bass_utils import run_bass_kernel_spmd

F32 = mybir.dt.float32
BF16 = mybir.dt.bfloat16
AF = mybir.ActivationFunctionType
ALU = mybir.AluOpType

D = 2048
T = 4096
TS = 64
TT = T + TS
NB = 33
PAST = 1024
SCALE = 128 ** -0.5
EPS = 1e-6
GROUPS = [[0, 1, 2, 3], [4, 5, 6, 7]]
ENG = ("sp", "act", "dve", "pool", "pe")
FOXW = 1
SBW = 4

C_U, C_E127, C_UB16, C_LS, C_ONES, C_MLT, C_MLTC, C_MS, C_MSC, C_ID, C_MLE, C_M64 = [128 * i for i in range(12)]
NCST = 128 * 12


class Sched:
    def __init__(self):
        self.prog = {e: [] for e in ENG}
        self.cnt = {}
        self.waited = {}
        self.lastw = {}
        self.readers = {}

    def _need(self, eng, events):
        for sn, val in events:
            if sn.startswith("d_"):
                val = self.cnt[sn]
            if sn == "pe" and eng == "pe":
                continue
            key = (eng, sn)
            if self.waited.get(key, 0) >= val:
                continue
            self.waited[key] = val
            self.prog[eng].append(("wait", sn, val))

    def op(self, eng, fn, reads=(), writes=(), sig=True, chan=None, cc=None):
        ps_reads = [r for r in reads if r.startswith("ps") and r not in writes]
        if ps_reads:
            writes = list(writes) + ps_reads
        ev = []
        for r in reads:
            if r in self.lastw:
                ev.append(self.lastw[r])
        for w in writes:
            if w in self.lastw:
                ev.append(self.lastw[w])
            ev.extend(self.readers.get(w, {}).items())
        self._need(eng, ev)
        if cc is not None:
            sn = "c_" + cc
            self.cnt[sn] = 1
            myev = (sn, 1)
            self.prog[eng].append(("op", fn, sn, None))
        elif chan is not None:
            sn = "d_" + chan
            self.cnt[sn] = self.cnt.get(sn, 0) + 16
            myev = (sn, self.cnt[sn])
            self.prog[eng].append(("op", fn, sn, 16))
        elif sig:
            self.cnt[eng] = self.cnt.get(eng, 0) + 1
            myev = (eng, self.cnt[eng])
            self.prog[eng].append(("op", fn, eng, 1))
        else:
            myev = (eng, self.cnt.get(eng, 0) + 1)
            self.prog[eng].append(("op", fn, None, 0))
        for r in reads:
            d = self.readers.setdefault(r, {})
            d[myev[0]] = max(d.get(myev[0], 0), myev[1])
        for w in writes:
            self.lastw[w] = myev
            self.readers[w] = {}


def build_program():
    nc = bass.Bass("TRN2", target_bir_lowering=False)
    S = Sched()

    def din(name, shape, dt=F32):
        return nc.dram_tensor(name, shape, dt, kind="ExternalInput").ap()

    def dout(name, shape, dt=F32):
        return nc.dram_tensor(name, shape, dt, kind="ExternalOutput").ap()

    xc0 = din("xc0", [TT, 512])
    nwd = [din("nw0", [128, 512]), din("nw1", [128, 512]), din("nwf", [128, 512])]
    wind = [din("win0", [D, 2052]), din("win1", [D, 2048])]
    woutd = [din("wout0", [D, 512]), din("wout1", [D, 512])]
    bfd = din("bfb", [128, 4])
    cstd = din("cst", [128, NCST])
    cfk = din("cfk", [4, PAST, 512])
    cfv = din("cfv", [4, PAST, 512])
    csk = din("csk", [4, PAST, 512])
    csv = din("csv", [4, PAST, 512])
    cfl = din("cfl", [4, PAST, 4])

    yout = dout("yout", [TT, 512])
    kvout = [[dout("kf", [TT, 512]), dout("vf", [TT, 512])], [dout("ks", [TT, 512]), dout("vs", [TT, 512])]]
    lfout = dout("lf", [TT, 4])

    PW = [1024, 1024, 1024, 1024, TS]
    xT_loc = [nc.dram_tensor("xT_loc%d" % p, [512, PW[p]], BF16) for p in range(5)]
    xT_all = [nc.dram_tensor("xT_all%d" % p, [D, PW[p]], BF16) for p in range(5)]
    yT_loc = [nc.dram_tensor("yT_loc%d" % p, [512, PW[p]], BF16) for p in range(5)]
    yT_all = [nc.dram_tensor("yT_all%d" % p, [D, PW[p]], BF16) for p in range(5)]

    def piece_of(tb):
        return (tb // 8, (tb % 8) * 128) if tb < 32 else (4, 0)
    ssq_loc = nc.dram_tensor("ssq_loc", [128, NB], F32)
    ssq_all = nc.dram_tensor("ssq_all", [512, NB], F32)
    xres = [None, nc.dram_tensor("x1s", [TT, 512], F32), nc.dram_tensor("x2s", [TT, 512], F32)]

    from contextlib import ExitStack
    es = ExitStack()

    def sb(name, shape, dt):
        return es.enter_context(nc.sbuf_tensor(name, shape, dt))

    def ps(name, shape, dt):
        return es.enter_context(nc.psum_tensor(name, shape, dt))

    with es:
        cstf = sb("cstf", [128, C_ID + 128], F32)
        cstb = sb("cstb", [128, 3 * 128], BF16)
        zeros = sb("zeros", [128, 512], BF16)
        nwt = sb("nwt", [128, 512], F32)
        nw = [nwt, nwt, nwt]
        bfb = sb("bfbs", [128, 4], F32)
        w_in = sb("w_in", [128, 16, 2052], BF16)
        regB = sb("regB", [128, 4 * T + 32 * 4 * 130], BF16)
        actT = [sb("actT0", [128, 16, 256], BF16), sb("actT1", [128, 16, 256], BF16)]
        ssq_part = sb("ssq_part", [128, NB], F32)
        ssq4 = sb("ssq4", [128, 4, NB], F32)
        ssum = sb("ssum", [128, NB], F32)
        rstd = sb("rstd", [128, NB], F32)
        nrstd = sb("nrstd", [128, NB], F32)
        hrstd = sb("hrstd", [128, NB], F32)
        cpall = sb("cpall", [128, NB, 8], F32)
        lfo = sb("lfo", [128, NB, 4], F32)
        lps = sb("lps", [128, 4], F32)
        xl = sb("xl", [128, 4], F32)
        el = sb("el", [128, 4], F32)
        q_tok = sb("q_tok", [128, 512], BF16)
        k_tok = sb("k_tok", [128, 512], BF16)
        kf32b_big = sb("kf32b", [128, 516], F32)
        kf32 = [sb("kf32a", [128, 512], F32)[:, :], kf32b_big[:, 0:512]]
        vf32 = [sb("vf32a", [128, 512], F32), sb("vf32b", [128, 512], F32)]
        gsil = [sb("gsil0", [128, 512], F32), sb("gsil1", [128, 512], F32)]
        gtmp = sb("gtmp", [128, 512], F32)
        qT = sb("qT", [128, 4, 256], BF16)
        ybuf = [sb("y0", [128, 512], BF16), sb("y1", [128, 512], BF16)]
        yTs = [sb("yTs0", [128, 4, 128], BF16), sb("yTs1", [128, 4, 128], BF16)]
        Pball = sb("Pball", [128, 6, 256], BF16)
        Pb = [Pball[:, i, :] for i in range(6)]
        biasT = sb("biasT", [128, 32, 4], F32)
        rden = sb("rden", [128, 4], F32)
        thb = [sb("th%d" % i, [128, 512], F32) for i in range(3)] + [gtmp]
        Rext = [sb("Rx%d" % i, [128, 516], F32) for i in range(3)] + [kf32b_big]
        ab = [sb("a%d" % i, [128, 512], BF16) for i in range(3)] + [k_tok]
        aTb = [sb("aT0", [128, 4, 128], BF16), sb("aT1", [128, 4, 128], BF16),
               Pball[:, 0:2, :].rearrange("p a (b t) -> p (a b) t", t=128),
               q_tok[:, :].rearrange("p (j t) -> p j t", t=128)]
        xcb = kf32
        xnb = vf32
        xtb = q_tok
        sqj = gtmp
        xTs = yTs
        Pw = [sb("Pw%d" % i, [128, 64], BF16) for i in range(4)]
        Pn = sb("Pn", [64, 64], BF16)
        clf = sb("clf", [128, 8, 16], F32)
        sufb = sb("sufb", [128, 8, 16], F32)
        cpn = sb("cpn", [64, 4], F32)
        aTw = [sb("aTw%d" % i, [128, 8, 64], BF16) for i in range(4)]
        aTn = sb("aTn", [64, 64], BF16)
        Vn_t = sb("Vn_t", [128, 520], BF16)
        carry0 = sb("carry0", [64, 4], F32)
        cbias = sb("cbias", [128, 2], F32)

        psA = ps("psA", [128, 512], F32)
        psB = ps("psB", [128, 512], F32)
        psC = ps("psC", [128, 512], F32)
        psD = ps("psD", [128, 512], F32)
        psE = ps("psE", [128, 512], F32)
        psF = ps("psF", [128, 512], F32)
        psG = ps("psG", [128, 512], F32)
        pst = ps("pst", [128, 1024], BF16)

        KT_OFF = 0
        V_OFF = 4 * T

        def kT(h, lo, hi):
            return regB[:, KT_OFF + h * T + lo: KT_OFF + h * T + hi]

        def Vaug(blk, h, n):
            o = V_OFF + (blk * 4 + h) * 130
            return regB[:, o:o + n]

        def Vaug_blk(blk):
            o = V_OFF + blk * 4 * 130
            return regB[:, o:o + 520].rearrange("p (h e) -> p h e", e=130)

        def w_out():
            o = 4096 + 8 * 520 + 4096
            return regB[:, o:o + 16 * 512].rearrange("p (c n) -> p c n", n=512)

        PST_ALL = ["pstbank"]
        pst_full = pst
        psE_bf = psE[:, :].bitcast(BF16)
        pst_f32 = pst[:, :].bitcast(F32)
        psG_bf = psG[:, :].bitcast(BF16)
        REGB_ALL = ["kT%d_%d" % (h, b) for h in range(4) for b in range(32)] + ["V%d" % b for b in range(32)]

        ident = cstb[:, 0:128]
        maskLE = cstb[:, 128:256]
        mask64 = cstb[0:64, 256:320]

        def cf(c0, n=128, rows=128):
            return cstf[0:rows, c0:c0 + n]

        S.op("sp", lambda e: e.dma_start(out=cstf[:], in_=cstd[:, 0:C_ID + 128]), writes=["cstf"], chan="cst")
        S.op("pool", lambda e: e.dma_start(out=cstb[:], in_=cstd[:, C_ID:C_ID + 384]), writes=["cstb"], chan="cstb")
        def load_nw(i):
            S.op("sp", lambda e: e.dma_start(out=nwt[:], in_=nwd[i][:, :]), writes=["nw"], chan="cst")
        load_nw(0)
        S.op("sp", lambda e: e.dma_start(out=bfb[:], in_=bfd[:, :]), writes=["bfb"], chan="cst")
        S.op("dve", lambda e: e.memset(zeros[:], 0.0), writes=["zeros"])
        S.op("dve", lambda e: e.memset(cbias[:, 0:1], EPS), writes=["cbias"])
        S.op("dve", lambda e: e.memset(cbias[:, 1:2], 1.0), writes=["cbias"])
        S.op("dve", lambda e: e.memset(ssq_part[:], 1.0), writes=["ssq_part"])
        S.op("dve", lambda e: e.memset(cpall[:], 0.0), writes=["cpall"])
        S.op("dve", lambda e: e.memset(lfo[:], 0.0), writes=["lfo"])
        for i in range(2):
            S.op("pool", lambda e, i=i: e.memset(thb[i][:], 0.0), writes=["th%d" % i])
        for i in range(4):
            S.op("pool", lambda e, i=i: e.memset(aTw[i][:], 0.0), writes=["aTw%d" % i])

        if False:
            for nm_, ap_ in [("win0", wind[0][0:128, 0:512]), ("win1", wind[1][0:128, 0:512]), ("wout0", woutd[0][0:128, :]),
                             ("wout1", woutd[1][0:128, :]), ("cfk", cfk[0][0:128, :]), ("cfv", cfv[0][0:128, :]),
                             ("csk", csk[0][0:128, :]), ("csv", csv[0][0:128, :])]:
                S.op("sp", lambda e, ap_=ap_: e.dma_start(out=gtmp[:], in_=ap_), writes=["gtmp"], chan="dbg")
            S.op("sp", lambda e: e.dma_start(out=gtmp[:, 0:4], in_=cfl[0][0:128, :]), writes=["gtmp"], chan="dbg")

        def load_w_in(l):
            ncol = 2052 if l == 0 else 2048
            src = wind[l].rearrange("(c p) n -> p c n", p=128)
            for c in range(16):
                yield
                si = c % 3
                o = 20544 + si * 4104
                stg = regB[:, o:o + 4104].bitcast(F32)
                S.op("sp", lambda e, c=c, stg=stg: e.dma_start(out=stg[:, 0:ncol], in_=src[:, c, :]),
                     writes=["wst%d" % si] + (REGB_ALL + ["sVc%d" % q for q in range(4)] + ["skT%d" % q for q in range(4)]
                                              if c < 3 else []), chan="wst%d" % si)
                if c % 2 == 0:
                    S.op("act", lambda e, c=c, stg=stg: e.activation(out=w_in[:, c, 0:ncol], in_=stg[:, 0:ncol], func=AF.Identity),
                         reads=["wst%d" % si] + (REGB_ALL if c >= 13 else []), writes=["w_in"])
                else:
                    S.op("dve", lambda e, c=c, stg=stg: e.tensor_copy(out=w_in[:, c, 0:ncol], in_=stg[:, 0:ncol]),
                         reads=["wst%d" % si] + (REGB_ALL if c >= 13 else []), writes=["w_in"])

        def load_w_out(l):
            src = woutd[l].rearrange("(c p) n -> p c n", p=128)
            wv = w_out()
            alln = REGB_ALL + ["sVc%d" % q for q in range(4)] + ["skT%d" % q for q in range(4)]
            for g4 in range(4):
                si = g4 % 3
                o = 20544 + si * 4104
                stg = regB[:, o:o + 4096].bitcast(F32).rearrange("p (c n) -> p c n", n=512)
                S.op("sp", lambda e, g4=g4, stg=stg: e.dma_start(out=stg, in_=src[:, 4 * g4:4 * g4 + 4, :]),
                     writes=["wst%d" % si] + (alln if g4 == 0 else []), chan="wst%d" % si)
                if g4 % 2 == 0:
                    S.op("act", lambda e, g4=g4, stg=stg: e.activation(out=wv[:, 4 * g4:4 * g4 + 4, :], in_=stg, func=AF.Identity),
                         reads=["wst%d" % si], writes=["w_out"] + (alln if g4 == 0 else []))
                else:
                    S.op("dve", lambda e, g4=g4, stg=stg: e.tensor_copy(out=wv[:, 4 * g4:4 * g4 + 4, :], in_=stg),
                         reads=["wst%d" % si], writes=["w_out"])

        def blk_rows(tb):
            return 128 if tb < 32 else TS

        cnt = {"xts": 0, "tr": 0}
        deferred_g = []
        deferred_y = []

        def norm_prep(xblk, xres_name, tb, rows, nwi, want_xT, want_gather=True):
            S.op("act", lambda e: e.activation(out=sqj[0:rows, :], in_=xblk[0:rows, :], func=AF.Square,
                                               accum_out=ssq_part[0:rows, tb:tb + 1]),
                 reads=[xres_name], writes=["gtmp", "ssq_part"])
            if not want_xT:
                return
            S.op("dve", lambda e: e.tensor_tensor(out=xtb[0:rows, :], in0=xblk[0:rows, :], in1=nw[nwi][0:rows, :],
                                                  op=ALU.mult),
                 reads=[xres_name, "nw"], writes=["q_tok"])
            for c in range(4):
                S.op("pe", lambda e, c=c: e.transpose(out=pst[:, c * 128:c * 128 + rows],
                                                      in_=xtb[0:rows, c * 128:(c + 1) * 128],
                                                      identity=ident[0:rows, 0:rows]),
                     reads=["q_tok", "cstb"], writes=PST_ALL, sig=(c == 3))
            i = cnt["tr"] % 2
            cnt["tr"] += 1
            pv = pst[:, 0:512].rearrange("p (c t) -> p c t", t=128)
            S.op("act", lambda e: e.activation(out=xTs[i][:, :, 0:rows], in_=pv[:, :, 0:rows], func=AF.Identity),
                 reads=PST_ALL, writes=["yTs%d" % i])
            pc, c0 = piece_of(tb)
            dst = xT_loc[pc].ap().rearrange("(c p) t -> p c t", p=128)
            S.op("sp", lambda e: e.dma_start(out=dst[:, :, c0:c0 + rows], in_=xTs[i][:, :, 0:rows]),
                 reads=["yTs%d" % i], writes=["xT_loc%d" % pc], chan="yTs%d" % i)
            if want_gather and (tb % 8 == 7 or tb == 32):
                if pc >= 3:
                    deferred_g.append(pc)
                else:
                    gather(xT_loc[pc], xT_all[pc], "xT", pc)

        def rstd_from_ssq(tag):
            S.op("sp", lambda e: e.dma_start(out=ssq_loc.ap(), in_=ssq_part[:]), reads=["ssq_part"],
                 writes=["ssq_loc"], chan="ssq")
            S.op("pool", lambda e: e.collective_compute("AllGather", ALU.bypass, replica_groups=GROUPS,
                                                        ins=[ssq_loc.ap().opt()], outs=[ssq_all.ap().opt()]),
                 reads=["ssq_loc"], writes=["ssq_all"], cc="ssq" + tag)
            S.op("sp", lambda e: e.dma_start(out=ssq4[:], in_=ssq_all.ap().rearrange("(r p) b -> p r b", p=128)),
                 reads=["ssq_all"], writes=["ssq4"], chan="ssq")
            S.op("dve", lambda e: e.tensor_tensor(out=ssum[:], in0=ssq4[:, 0, :], in1=ssq4[:, 1, :], op=ALU.add),
                 reads=["ssq4"], writes=["ssum"])
            S.op("dve", lambda e: e.tensor_tensor(out=ssum[:], in0=ssum[:], in1=ssq4[:, 2, :], op=ALU.add),
                 reads=["ssq4", "ssum"], writes=["ssum"])
            S.op("dve", lambda e: e.tensor_tensor(out=ssum[:], in0=ssum[:], in1=ssq4[:, 3, :], op=ALU.add),
                 reads=["ssq4", "ssum"], writes=["ssum"])
            S.op("act", lambda e: e.activation(out=ssum[:], in_=ssum[:], func=AF.Ln, scale=1.0 / D, bias=cbias[:, 0:1]),
                 reads=["ssum", "cbias"], writes=["ssum"])
            S.op("act", lambda e: e.activation(out=rstd[:], in_=ssum[:], func=AF.Exp, scale=-0.5),
                 reads=["ssum"], writes=["rstd"])
            S.op("dve", lambda e: e.tensor_scalar(out=nrstd[:], in0=rstd[:], scalar1=-1.0, scalar2=None, op0=ALU.mult),
                 reads=["rstd"], writes=["nrstd"])
            S.op("dve", lambda e: e.tensor_scalar(out=hrstd[:], in0=rstd[:], scalar1=0.5, scalar2=None, op0=ALU.mult),
                 reads=["rstd"], writes=["hrstd"])
            S.op("dve", lambda e: e.memset(ssq_part[:], 1.0), reads=[], writes=["ssq_part"])
            for pc in deferred_g:
                gather(xT_loc[pc], xT_all[pc], "xT", pc)
            del deferred_g[:]

        gcount = {"n": 0}

        def gather(loc, allt, rname, pc):
            gcount["n"] += 1
            S.op("pool", lambda e: e.collective_compute("AllGather", ALU.bypass, replica_groups=GROUPS,
                                                        ins=[loc.ap().opt()], outs=[allt.ap().opt()]),
                 reads=["%s_loc%d" % (rname, pc)], writes=["%s_all%d" % (rname, pc)], cc="g%d" % gcount["n"])

        wq = load_w_in(0)

        def n0_load(tb):
            rows = blk_rows(tb)
            i = tb % 2
            S.op("sp", lambda e: e.dma_start(out=xcb[i][0:rows, :], in_=xc0[tb * 128:tb * 128 + rows, :]),
                 writes=["kf32%d" % i], chan="kf32%d" % i)

        for tb in range(NB):
            rows = blk_rows(tb)
            i = tb % 2
            if tb % 2 == 0:
                next(wq, None)
            if tb == 0:
                n0_load(0)
            if tb + 1 < NB:
                n0_load(tb + 1)
            norm_prep(xcb[i], "kf32%d" % i, tb, rows, 0, True)
        for _ in wq:
            pass
        rstd_from_ssq("0")

        def load_act(slot, src_all, rname, c0, ncols):
            pc, lc = (c0 // 1024, c0 % 1024) if c0 < T else (4, 0)
            src = src_all[pc].ap().rearrange("(c p) t -> p c t", p=128)
            S.op("sp", lambda e: e.dma_start(out=actT[slot][:, :, 0:ncols], in_=src[:, :, lc:lc + ncols]),
                 reads=["%s_all%d" % (rname, pc)], writes=["actT%d" % slot], chan="actT%d" % slot)

        def transposes_to(src_tok, src_name, rows, dst_fn, dst_names, evac_eng):
            for h in range(4):
                S.op("pe", lambda e, h=h: e.transpose(out=pst[:, h * 128:h * 128 + rows],
                                                      in_=src_tok[0:rows, h * 128:(h + 1) * 128],
                                                      identity=ident[0:rows, 0:rows]),
                     reads=[src_name, "cstb"], writes=PST_ALL, sig=(h == 3))
            pv4 = pst[:, 0:512].rearrange("p (h t) -> p h t", t=128)
            if evac_eng == "act":
                S.op("act", lambda e: e.activation(out=dst_fn(None), in_=pv4[:, :, 0:rows], func=AF.Identity),
                     reads=PST_ALL, writes=dst_names)
            else:
                S.op("dve", lambda e: e.tensor_copy(out=dst_fn(None), in_=pv4[:, :, 0:rows]), reads=PST_ALL, writes=dst_names)

        def project_block(l, slot, sub, tb, rows, kT_dst, kT_names, v_dst_fn, v_names, qcol0):
            banks = [psA, psB, psC, psD]
            bn = ["psA", "psB", "psC", "psD"]
            for c in range(16):
                lhsT = actT[slot][:, c, sub * 128: sub * 128 + rows]
                for j in range(4):
                    S.op("pe", lambda e, c=c, j=j, lhsT=lhsT: e.matmul(banks[j][0:rows, :], lhsT=lhsT,
                                                                        rhs=w_in[:, c, j * 512:(j + 1) * 512],
                                                                        start=(c == 0), stop=(c == 15)),
                         reads=["actT%d" % slot, "w_in"], writes=[bn[j]], sig=(c == 15))
                if l == 0:
                    S.op("pe", lambda e, c=c, lhsT=lhsT: e.matmul(psE[0:rows, 0:4], lhsT=lhsT,
                                                                   rhs=w_in[:, c, 2048:2052],
                                                                   start=(c == 0), stop=(c == 15)),
                         reads=["actT%d" % slot, "w_in"], writes=["psE"], sig=(c == 15))
            rs = rstd[0:rows, tb:tb + 1]
            nrs = nrstd[0:rows, tb:tb + 1]
            i = tb % 2
            S.op("dve", lambda e: e.tensor_scalar(out=q_tok[0:rows, :], in0=psA[0:rows, :], scalar1=rs, scalar2=None,
                                                  op0=ALU.mult),
                 reads=["psA", "rstd"], writes=["q_tok"])
            S.op("act", lambda e: e.activation(out=kf32[i][0:rows, :], in_=psB[0:rows, :], func=AF.Identity, scale=rs),
                 reads=["psB", "rstd"], writes=["kf32%d" % i])
            S.op("dve", lambda e: e.tensor_scalar(vf32[i][0:rows, :], psC[0:rows, :], rs, None, ALU.mult),
                 reads=["psC", "rstd"], writes=["vf32%d" % i])
            S.op("sp", lambda e: e.dma_start(out=kvout[l][0][tb * 128:tb * 128 + rows, :], in_=kf32[i][0:rows, :]),
                 reads=["kf32%d" % i], writes=[], chan="kf32%d" % i)
            S.op("sp", lambda e: e.dma_start(out=kvout[l][1][tb * 128:tb * 128 + rows, :], in_=vf32[i][0:rows, :]),
                 reads=["vf32%d" % i], writes=[], chan="vf32%d" % i)
            S.op("act", lambda e: e.activation(out=k_tok[0:rows, :], in_=psB[0:rows, :], func=AF.Identity, scale=rs),
                 reads=["psB", "rstd"], writes=["k_tok"])
            vd = v_dst_fn()
            S.op("dve", lambda e: e.tensor_copy(out=vd[0:rows, :, 0:128],
                                                in_=vf32[i][0:rows, :].rearrange("p (h d) -> p h d", d=128)),
                 reads=["vf32%d" % i], writes=v_names)
            S.op("pool", lambda e: e.memset(vd[0:rows, :, 128:129], 1.0), reads=[], writes=v_names)
            if l == 0:
                S.op("act", lambda e: e.activation(out=gtmp[0:rows, :], in_=psD[0:rows, :], func=AF.Exp, scale=nrs),
                     reads=["psD", "nrstd"], writes=["gtmp"])
                S.op("act", lambda e: e.activation(out=gtmp[0:rows, :], in_=gtmp[0:rows, :], func=AF.Ln, bias=cbias[0:rows, 1:2]),
                     reads=["gtmp", "cbias"], writes=["gtmp"])
                S.op("act", lambda e: e.activation(out=gtmp[0:rows, :], in_=gtmp[0:rows, :], func=AF.Exp, scale=-1.0),
                     reads=["gtmp"], writes=["gtmp"])
            else:
                S.op("act", lambda e: e.activation(out=gtmp[0:rows, :], in_=psD[0:rows, :], func=AF.Sigmoid, scale=rs),
                     reads=["psD", "rstd"], writes=["gtmp"])
            S.op("dve", lambda e: e.scalar_tensor_tensor(out=gsil[sub][0:rows, :], in0=psD[0:rows, :], scalar=rs,
                                                         in1=gtmp[0:rows, :], op0=ALU.mult, op1=ALU.mult),
                 reads=["psD", "rstd", "gtmp"], writes=["gsil%d" % sub])
            if l == 0:
                S.op("dve", lambda e: e.scalar_tensor_tensor(out=xl[0:rows, :], in0=psE[0:rows, 0:4], scalar=rs,
                                                             in1=bfb[0:rows, :], op0=ALU.mult, op1=ALU.add),
                     reads=["psE", "rstd", "bfb"], writes=["xl"])
                S.op("act", lambda e: e.activation(out=el[0:rows, :], in_=xl[0:rows, :], func=AF.Exp, scale=-1.0),
                     reads=["xl"], writes=["el"])
                S.op("act", lambda e: e.activation(out=lps[0:rows, :], in_=el[0:rows, :], func=AF.Ln, bias=cbias[0:rows, 1:2]),
                     reads=["el", "cbias"], writes=["lps"])
                S.op("pool", lambda e: e.tensor_scalar(out=lfo[0:rows, tb, :], in0=lps[0:rows, :], scalar1=-1.0,
                                                       scalar2=0.0, op0=ALU.mult, op1=ALU.add),
                     reads=["lps"], writes=["lfo"])
            transposes_to(q_tok, "q_tok", rows, lambda h: qT[:, :, qcol0:qcol0 + rows], ["qT"], "dve")
            transposes_to(k_tok, "k_tok", rows, kT_dst, kT_names, "act")

        def cumsum_block(tb):
            first = (tb == 0)
            S.op("pe", lambda e: e.matmul(psE[:, 8:12], lhsT=cf(C_U), rhs=lps[:, 0:4], start=True, stop=first),
                 reads=["cstf", "lps"], writes=["psE"], sig=first)
            if not first:
                S.op("pe", lambda e: e.matmul(psE[:, 8:12], lhsT=cf(C_E127), rhs=cpall[:, tb - 1, 0:4], start=False,
                                              stop=True),
                     reads=["cstf", "cpall"], writes=["psE"], sig=False)
                S.op("pe", lambda e: e.matmul(psE[:, 12:16], lhsT=cf(C_E127), rhs=cpall[:, tb - 1, 0:4], start=True,
                                              stop=True),
                     reads=["cstf", "cpall"], writes=["psE"])
                S.op("dve", lambda e: e.tensor_copy(out=cpall[:, tb, 0:8], in_=psE[:, 8:16]), reads=["psE"],
                     writes=["cpall"])
            else:
                S.op("dve", lambda e: e.tensor_copy(out=cpall[:, tb, 0:4], in_=psE[:, 8:12]), reads=["psE"],
                     writes=["cpall"])

        sc_banks = [(psA, "psA"), (psB, "psB")]
        o_banks = [((psC, "psC"), (psD, "psD")), ((psF, "psF"), (psG, "psG"))]

        def y_out(tbs, rows_l):
            for sub, tb in enumerate(tbs):
                rows = rows_l[sub]
                for c in range(4):
                    S.op("pe", lambda e, c=c, sub=sub, rows=rows: e.transpose(out=pst[:, c * 128:c * 128 + rows],
                                                                               in_=ybuf[sub][0:rows, c * 128:(c + 1) * 128],
                                                                               identity=ident[0:rows, 0:rows]),
                         reads=["y%d_%d" % (sub, hh) for hh in range(4)] + ["cstb"], writes=PST_ALL, sig=(c == 3))
                i = cnt["tr"] % 2
                cnt["tr"] += 1
                pv = pst[:, 0:512].rearrange("p (c t) -> p c t", t=128)
                S.op("act", lambda e, i=i, rows=rows, pv=pv: e.activation(out=yTs[i][:, :, 0:rows], in_=pv[:, :, 0:rows],
                                                                           func=AF.Identity),
                     reads=PST_ALL, writes=["yTs%d" % i])
                pc, c0 = piece_of(tb)
                dst = yT_loc[pc].ap().rearrange("(c p) t -> p c t", p=128)
                S.op("sp", lambda e, i=i, rows=rows, c0=c0, dst=dst: e.dma_start(out=dst[:, :, c0:c0 + rows],
                                                                                  in_=yTs[i][:, :, 0:rows]),
                     reads=["yTs%d" % i], writes=["yT_loc%d" % pc], chan="yTs%d" % i)
                if tb % 8 == 7 or tb == 32:
                    if pc == 3:
                        deferred_y.append(pc)
                    else:
                        for dpc in deferred_y:
                            gather(yT_loc[dpc], yT_all[dpc], "yT", dpc)
                        del deferred_y[:]
                        gather(yT_loc[pc], yT_all[pc], "yT", pc)

        pcount = {"p": 0, "sc": 0, "ch": 0}

        def run_streams(makers, width):
            pending = list(makers)
            active = []
            free = list(range(width))
            eng_free = {}
            now = 0.0
            while pending or active:
                while pending and free:
                    sl = free.pop(0)
                    g = pending.pop(0)(sl)
                    try:
                        nxt = next(g)
                    except StopIteration:
                        free.append(sl)
                        continue
                    active.append({"g": g, "sl": sl, "ready": now, "nxt": nxt})
                if not active:
                    continue
                best = min(active, key=lambda a: max(eng_free.get(a["nxt"][0], 0.0), a["ready"]))
                eng, dur = best["nxt"]
                start = max(eng_free.get(eng, 0.0), best["ready"])
                end = start + dur
                eng_free[eng] = end
                best["ready"] = end + 0.25
                try:
                    best["nxt"] = next(best["g"])
                except StopIteration:
                    active.remove(best)
                    free.append(best["sl"])
                    now = end

        def fox_stream(slot, Q, h):
            nkb = 2 * Q + 2
            ob = o_banks[slot]
            fsets = [[(psA, "psA"), (psB, "psB"), (psE, "psE")], [(pst_f32, "pstbank"), (psF, "psF"), (psG, "psG")]]
            batches = [list(range(k0, min(k0 + 3, nkb))) for k0 in range(0, nkb, 3)]

            def emit_qk(b):
                for i, kb in enumerate(batches[b]):
                    qlo = 128 if kb == nkb - 1 else 0
                    scb, scn = fsets[b % 2][i]
                    S.op("pe", lambda e, kb=kb, qlo=qlo, scb=scb: e.matmul(
                        scb[:, qlo:256], lhsT=kT(h, kb * 128, (kb + 1) * 128), rhs=qT[:, h, qlo:256], start=True, stop=True),
                        reads=["kT%d_%d" % (h, kb), "qT"], writes=[scn])

            yield ("pe", 0.4)
            emit_qk(0)
            for b, kbs in enumerate(batches):
                if b + 1 < len(batches):
                    emit_qk(b + 1)
                for i, kb in enumerate(kbs):
                    qlo = 128 if kb == nkb - 1 else 0
                    scb, scn = fsets[b % 2][i]
                    pi = 3 * (b % 2) + i
                    S.op("act", lambda e, kb=kb, qlo=qlo, pi=pi, scb=scb: e.activation(
                        out=Pb[pi][:, qlo:256], in_=scb[:, qlo:256], func=AF.Exp, scale=SCALE, bias=biasT[:, kb, h:h + 1]),
                        reads=[scn, "biasT"], writes=["P%d" % pi])
                    if kb >= nkb - 2:
                        dq = 0 if kb == nkb - 2 else 128
                        S.op("pool", lambda e, pi=pi, dq=dq: e.tensor_tensor(out=Pb[pi][:, dq:dq + 128],
                                                                             in0=Pb[pi][:, dq:dq + 128], in1=maskLE,
                                                                             op=ALU.mult),
                             reads=["P%d" % pi, "cstb"], writes=["P%d" % pi])
                for i, kb in enumerate(kbs):
                    pi = 3 * (b % 2) + i
                    for sub in range(2):
                        if kb == nkb - 1 and sub == 0:
                            continue
                        last = (kb == nkb - 2) if sub == 0 else (kb == nkb - 1)
                        S.op("pe", lambda e, kb=kb, sub=sub, pi=pi, last=last: e.matmul(
                            ob[sub][0][:, 0:129], lhsT=Pb[pi][:, sub * 128:(sub + 1) * 128], rhs=Vaug(kb, h, 129),
                            start=(kb == 0), stop=last),
                            reads=["P%d" % pi, "V%d" % kb], writes=[ob[sub][1]], sig=(sub == 1 or kb == nkb - 2))
            for sub in range(2):
                S.op("dve", lambda e, sub=sub: e.reciprocal(out=rden[:, 2 * slot + sub:2 * slot + sub + 1],
                                                            in_=ob[sub][0][:, 128:129]),
                     reads=[ob[sub][1]], writes=["rden%d" % slot])
                S.op("dve", lambda e, sub=sub: e.scalar_tensor_tensor(
                    out=ybuf[sub][:, h * 128:(h + 1) * 128], in0=ob[sub][0][:, 0:128],
                    scalar=rden[:, 2 * slot + sub:2 * slot + sub + 1],
                    in1=gsil[sub][:, h * 128:(h + 1) * 128], op0=ALU.mult, op1=ALU.mult),
                    reads=[ob[sub][1], "rden%d" % slot, "gsil%d" % sub], writes=["y%d_%d" % (sub, h)])

        def fox_tile(Q):
            nkb = 2 * Q + 2
            for h in range(4):
                S.op("dve", lambda e, h=h: e.tensor_scalar(out=biasT[:, 0:nkb, h], in0=cpall[:, 0:nkb, h],
                                                           scalar1=cpall[:, 2 * Q + 1, 4 + h:5 + h], scalar2=None,
                                                           op0=ALU.subtract),
                     reads=["cpall"], writes=["biasT"])
            run_streams([(lambda sl, h=h: fox_stream(sl, Q, h)) for h in range(4)], FOXW)

        def sb_stream(slot, qrows, qT_ap, qT_names, chunks, o_ap, o_name, pv, total_pv, carry_in=None, aT_hook=None,
                      fin=None, out_state=None):
            prev = carry_in
            th, Rx, a_ = thb[slot], Rext[slot], ab[slot]
            tn = ["th0", "th1", "th2", "gtmp"][slot]
            rn = ["Rx0", "Rx1", "Rx2", "kf321"][slot]
            an = ["a0", "a1", "a2", "k_tok"][slot]
            aTnames = [["aT0"], ["aT1"], ["P0", "P1"], ["q_tok"]][slot]
            scb, scn = [(psA, "psA"), (psB, "psB"), (psE, "psE"), (psG, "psG")][slot]
            pst, pn, pso = scb[:, :].bitcast(BF16), scn, 0
            for ci, ch in enumerate(chunks):
                W = ch["W"]
                yield ("pe", 0.45)
                S.op("pe", lambda e, ch=ch, W=W, scb=scb: e.matmul(scb[0:qrows, 0:W], lhsT=qT_ap, rhs=ch["kT"], start=True, stop=True),
                     reads=qT_names + ch["names"], writes=[scn])
                yield ("act", 0.65)
                S.op("act", lambda e, W=W, scb=scb: e.activation(out=th[0:qrows, 0:W], in_=scb[0:qrows, 0:W], func=AF.Sigmoid,
                                                        scale=-SCALE),
                     reads=[scn], writes=[tn])
                yield ("dve", 1.5)
                if ch.get("mask") is not None:
                    M, Mc, dw = ch["mask"]
                    S.op("dve", lambda e, W=W, dw=dw, M=M: e.tensor_tensor(out=th[0:qrows, W - dw:W], in0=th[0:qrows, W - dw:W],
                                                                           in1=M, op=ALU.mult),
                         reads=[tn, "cstf"], writes=[tn])
                    S.op("dve", lambda e, W=W, dw=dw, Mc=Mc: e.tensor_tensor(out=th[0:qrows, W - dw:W], in0=th[0:qrows, W - dw:W],
                                                                             in1=Mc, op=ALU.add),
                         reads=[tn, "cstf"], writes=[tn])
                if prev is None:
                    S.op("dve", lambda e, W=W: e.memset(Rx[0:qrows, W:W + 1], 1.0), reads=[], writes=[rn])
                else:
                    pR, pname = prev
                    S.op("dve", lambda e, W=W, pR=pR: e.tensor_copy(out=Rx[0:qrows, W:W + 1], in_=pR),
                         reads=[pname, an], writes=[rn])
                S.op("dve", lambda e, W=W: e.tensor_tensor_scan(
                    out=Rx[0:qrows, 0:W][:, ::-1], data0=th[0:qrows, 0:W][:, ::-1], data1=zeros[0:qrows, 0:W],
                    initial=Rx[0:qrows, W:W + 1], op0=ALU.mult, op1=ALU.add),
                    reads=[tn, rn, "zeros"], writes=[rn])
                prev = (Rx[0:qrows, 0:1], rn)
                if True:
                    yield ("pool", 1.3)
                S.op("pool", lambda e, W=W: e.tensor_tensor(out=a_[0:qrows, 0:W], in0=Rx[0:qrows, 1:W + 1],
                                                                                   in1=Rx[0:qrows, 0:W], op=ALU.subtract),
                     reads=[rn], writes=[an])
                yield ("pe", 0.7)
                vbl = ch["vblocks"]
                off = 0
                for j, (rhs, wk, vn) in enumerate(vbl):
                    S.op("pe", lambda e, off=off, wk=wk, j=j: e.transpose(out=pst[0:wk, pso + j * 128:pso + j * 128 + qrows],
                                                                          in_=a_[0:qrows, off:off + wk],
                                                                          identity=ident[0:qrows, 0:qrows]),
                         reads=[an, "cstb"], writes=[pn], sig=(j == len(vbl) - 1))
                    off += wk
                yield ("act", 0.65)
                if aT_hook is None:
                    aT = aTb[slot]
                    nb_ = len(vbl)
                    pvw = pst[:, pso:pso + nb_ * 128].rearrange("p (j t) -> p j t", t=128)
                    S.op("act", lambda e, aT=aT, pvw=pvw, nb_=nb_: e.activation(out=aT[:, 0:nb_, 0:qrows], in_=pvw[:, :, 0:qrows],
                                                                                func=AF.Identity),
                         reads=[pn], writes=aTnames)
                    lhs_list = [(aT[0:wk, j, 0:qrows], aTnames) for j, (_, wk, _) in enumerate(vbl)]
                else:
                    lhs_list = aT_hook(ci, vbl, pso, pn, pst)
                if aT_hook is None:
                    yield ("pe", 0.7)
                for j, (rhs, wk, vn) in enumerate(vbl):
                    lhsT, ln = lhs_list[j]
                    n = pv["n"]
                    S.op("pe", lambda e, lhsT=lhsT, rhs=rhs, n=n: e.matmul(o_ap, lhsT=lhsT, rhs=rhs, start=(n == 0),
                                                                           stop=(n == total_pv - 1)),
                         reads=(ln if isinstance(ln, list) else [ln]) + [vn], writes=[o_name], sig=True)
                    pv["n"] += 1
            if out_state is not None:
                out_state["carry"] = prev
            if fin is not None:
                yield ("dve", 0.3)
                fin()

        sb_obank = {0: [(psC, "psC")], 1: [(psD, "psD")], 2: [(psF, "psF")], 3: [(pst_f32, "pstbank")]}
        sb_ocnt = {0: 0, 1: 0}

        def sb_tile(Q):
            makers = []
            for sub in range(2):
                qb = 2 * Q + sub
                e_ = 128 * (qb + 1)
                for h in range(4):
                    def mk(slot, sub=sub, h=h, e_=e_):
                        chunks = []
                        hi = e_
                        while hi > 0:
                            lo = max(0, hi - 512)
                            vbl = [(Vaug(b, h, 128), 128, "V%d" % b) for b in range(lo // 128, hi // 128)]
                            chunks.append(dict(kT=kT(h, lo, hi), W=hi - lo,
                                               names=["kT%d_%d" % (h, b) for b in range(lo // 128, hi // 128)],
                                               vblocks=vbl,
                                               mask=(cf(C_MLT), cf(C_MLTC), 128) if hi == e_ else None))
                            hi = lo
                        ob, on = sb_obank[slot][0]

                        def fin():
                            S.op("dve", lambda e: e.tensor_tensor(out=ybuf[sub][:, h * 128:(h + 1) * 128], in0=ob[:, 0:128],
                                                                  in1=gsil[sub][:, h * 128:(h + 1) * 128], op=ALU.mult),
                                 reads=[on, "gsil%d" % sub], writes=["y%d_%d" % (sub, h)])
                        return sb_stream(slot, 128, qT[:, h, sub * 128:(sub + 1) * 128], ["qT"], chunks, ob[:, 0:128], on,
                                         {"n": 0}, sum(len(c["vblocks"]) for c in chunks), fin=fin)
                    makers.append(mk)
            run_streams(makers, SBW)

        SEQSZ = 8 * 520 + 4096
        assert 4 * SEQSZ <= 4 * T + 32 * 4 * 130
        kstage_f = actT[1][:, :, :].rearrange("p c t -> p (c t)").bitcast(F32).rearrange("p (j c) -> p j c", c=512)

        def sVc_all(s_):
            b0 = s_ * SEQSZ
            return regB[:, b0:b0 + 8 * 520].rearrange("p (j h e) -> p j h e", h=4, e=130)

        def sVc(s_, j, h, n):
            o = s_ * SEQSZ + (j * 4 + h) * 130
            return regB[:, o:o + n]

        def skT(s_, h, lo, hi):
            o = s_ * SEQSZ + 8 * 520 + h * 1024
            return regB[:, o + lo:o + hi]

        def Pw8(s_):
            v = thb[s_ // 2][:, :].bitcast(BF16)
            return v[:, (s_ % 2) * 512:(s_ % 2 + 1) * 512].rearrange("p (j c) -> p j c", c=64)

        def Vn_v():
            return Vn_t[:, :].rearrange("p (h e) -> p h e", e=130)

        def sample_stage(l):
            kcache, vcache = (cfk, cfv) if l == 0 else (csk, csv)
            for s_ in range(4):
                for j in range(8):
                    S.op("pool", lambda e, s_=s_, j=j: e.dma_start(
                        out=sVc_all(s_)[:, j, :, 0:128],
                        in_=vcache[s_][j * 128:(j + 1) * 128, :].rearrange("p (h d) -> p h d", d=128)),
                         writes=["sVc%d" % s_] + (REGB_ALL + ["w_out"] if j == 0 else []), chan="sVc")
                S.op("pool", lambda e, s_=s_: e.memset(sVc_all(s_)[:, :, :, 128:129], 1.0), writes=["sVc%d" % s_])
                for hf in range(2):
                    S.op("sp", lambda e, s_=s_, hf=hf: e.dma_start(
                        out=kstage_f, in_=kcache[s_][hf * 512:(hf + 1) * 512, :].rearrange("(j p) c -> p j c", p=128)),
                         writes=["actT1"], chan="sKc")
                    for jj in range(4):
                        j = hf * 4 + jj
                        for h in range(4):
                            S.op("pe", lambda e, jj=jj, h=h: e.transpose(out=pst_f32[:, h * 128:(h + 1) * 128],
                                                                         in_=kstage_f[:, jj, h * 128:(h + 1) * 128],
                                                                         identity=cstf[:, C_ID:C_ID + 128]),
                                 reads=["actT1", "cstf"], writes=PST_ALL, sig=(h == 3))
                        kb_ = s_ * SEQSZ + 8 * 520
                        S.op("act", lambda e, kb_=kb_, j=j: e.activation(
                            out=regB[:, kb_:kb_ + 4096].rearrange("p (h t) -> p h t", t=1024)[:, :, j * 128:(j + 1) * 128],
                            in_=pst_f32[:, 0:512].rearrange("p (h t) -> p h t", t=128), func=AF.Identity),
                             reads=PST_ALL, writes=["skT%d" % s_] + (REGB_ALL + ["w_out"] if j == 0 else []))
            if l == 0:
                for s_ in range(4):
                    S.op("sp", lambda e, s_=s_: e.dma_start(out=clf[:, :, s_ * 4:(s_ + 1) * 4],
                                                            in_=cfl[s_].rearrange("(j p) h -> p j h", p=128)),
                         writes=["clf"], chan="clf")
                for j in range(8):
                    S.op("pe", lambda e, j=j: e.matmul(psE[:, 16 + 16 * j:32 + 16 * j], lhsT=cf(C_LS), rhs=clf[:, j, :], start=True,
                                                       stop=(j == 7)), reads=["cstf", "clf"], writes=["psE"], sig=(j == 7))
                    for j2 in range(j + 1, 8):
                        S.op("pe", lambda e, j=j, j2=j2: e.matmul(psE[:, 16 + 16 * j:32 + 16 * j], lhsT=cf(C_ONES), rhs=clf[:, j2, :],
                                                                  start=False, stop=(j2 == 7)), reads=["cstf", "clf"],
                             writes=["psE"], sig=(j2 == 7))
                S.op("dve", lambda e: e.tensor_copy(out=sufb[:].rearrange("p j c -> p (j c)"), in_=psE[:, 16:144]), reads=["psE"],
                     writes=["sufb"])

        def sample_block(l, slot):
            tb = 32
            rows = TS
            Vn = Vn_v()
            project_block(l, slot, 0, tb, rows, lambda h: qT[:, :, 128:128 + rows], ["qT"],
                          Vn_v, ["Vn"], 0)
            if l == 0:
                S.op("pe", lambda e: e.matmul(psE[0:64, 8:12], lhsT=cf(C_UB16, 64, 64), rhs=lps[0:64, 0:4], start=True, stop=True),
                     reads=["cstf", "lps"], writes=["psE"])
                S.op("dve", lambda e: e.tensor_copy(out=cpn[:, :], in_=psE[0:64, 8:12]), reads=["psE"], writes=["cpn"])
            for h in range(4):
                ob, on = o_banks[h % 2][0]
                if l == 0:
                    o_ap = ob[0:64, 0:129]
                    first = True
                    for s_ in range(4):
                        scb, scn = sc_banks[pcount["sc"] % 2]
                        pcount["sc"] += 1
                        pw = Pw8(s_)
                        pwn = "th%d" % (s_ // 2)
                        for j in range(8):
                            S.op("pe", lambda e, s_=s_, j=j, h=h, scb=scb: e.matmul(
                                scb[:, j * 16:(j + 1) * 16], lhsT=skT(s_, h, j * 128, (j + 1) * 128),
                                rhs=qT[:, h, s_ * 16:(s_ + 1) * 16], start=True, stop=True),
                                reads=["skT%d" % s_, "qT"], writes=[scn], sig=(j == 7))
                        for j in range(8):
                            S.op("act", lambda e, s_=s_, j=j, h=h, scb=scb, pw=pw: e.activation(
                                out=pw[:, j, s_ * 16:(s_ + 1) * 16], in_=scb[:, j * 16:(j + 1) * 16], func=AF.Exp, scale=SCALE,
                                bias=sufb[:, j, s_ * 4 + h:s_ * 4 + h + 1]), reads=[scn, "sufb"], writes=[pwn])
                        for j in range(8):
                            S.op("pe", lambda e, s_=s_, j=j, h=h, first=first, o_ap=o_ap, pw=pw: e.matmul(
                                o_ap, lhsT=pw[:, j, 0:64], rhs=sVc(s_, j, h, 129), start=first, stop=False),
                                reads=[pwn, "sVc%d" % s_], writes=[on], sig=(j == 7))
                            first = False
                    scb, scn = sc_banks[pcount["sc"] % 2]
                    pcount["sc"] += 1
                    S.op("pe", lambda e, h=h, scb=scb: e.matmul(scb[0:64, 0:64], lhsT=qT[:, h, 128:192], rhs=qT[:, h, 0:64],
                                                                start=True, stop=True), reads=["qT"], writes=[scn])
                    S.op("act", lambda e, h=h, scb=scb: e.activation(out=Pn[:, :], in_=scb[0:64, 0:64], func=AF.Exp, scale=SCALE,
                                                                     bias=cpn[:, h:h + 1]), reads=[scn, "cpn"], writes=["Pn"])
                    S.op("pool", lambda e: e.tensor_tensor(out=Pn[:, :], in0=Pn[:, :], in1=mask64, op=ALU.mult),
                         reads=["Pn", "cstb"], writes=["Pn"])
                    S.op("pe", lambda e, h=h, o_ap=o_ap: e.matmul(o_ap, lhsT=Pn[:, :], rhs=Vn[0:64, h, 0:129], start=False, stop=True),
                         reads=["Pn", "Vn"], writes=[on])
                    S.op("dve", lambda e, ob=ob: e.reciprocal(out=rden[0:64, 0:1], in_=ob[0:64, 128:129]), reads=[on],
                         writes=["rden0"])
                    S.op("dve", lambda e, h=h, ob=ob: e.scalar_tensor_tensor(
                        out=ybuf[0][0:64, h * 128:(h + 1) * 128], in0=ob[0:64, 0:128], scalar=rden[0:64, 0:1],
                        in1=gsil[0][0:64, h * 128:(h + 1) * 128], op0=ALU.mult, op1=ALU.mult),
                        reads=[on, "rden0", "gsil0"], writes=["y0_%d" % h])
                else:
                    pass
            if l == 1:
                def head_stream(slot, h):
                    ob, on = sb_obank[slot][0]
                    o_ap = ob[0:64, 0:128]
                    pv = {"n": 0}
                    total = 1 + 4 * 8

                    def hook0(ci, vbl, pso, pn, pst_=None):
                        S.op("act", lambda e: e.activation(out=aTn[:, :], in_=pst_[0:64, pso:pso + 64], func=AF.Identity),
                             reads=[pn], writes=["aTn"])
                        return [(aTn[:, :], "aTn")]
                    ch0 = dict(kT=qT[:, h, 128:192], W=64, names=["qT"], vblocks=[(Vn[0:64, h, 0:128], 64, "Vn")],
                               mask=(cf(C_MS, 64, 64), cf(C_MSC, 64, 64), 64))
                    st = {}
                    yield from sb_stream(slot, 64, qT[:, h, 0:64], ["qT"], [ch0], o_ap, on, pv, total, None, hook0, None, st)
                    car = st["carry"]
                    cr = carry0[:, slot:slot + 1]
                    yield ("dve", 0.1)
                    S.op("dve", lambda e: e.tensor_copy(out=cr, in_=car[0]), reads=[car[1]], writes=["carry0_%d" % slot])
                    for s_ in range(4):
                        def hook(ci, vbl, pso, pn, pst_=None, s_=s_):
                            base = 4 if ci == 0 else 0
                            pvw = pst_[:, pso:pso + 512].rearrange("p (j t) -> p j t", t=128)
                            S.op("act", lambda e: e.activation(out=aTw[s_][:, base:base + 4, s_ * 16:(s_ + 1) * 16],
                                                               in_=pvw[:, :, s_ * 16:(s_ + 1) * 16], func=AF.Identity),
                                 reads=[pn], writes=["aTw%d" % s_])
                            return [(aTw[s_][:, base + j, 0:64], "aTw%d" % s_) for j in range(4)]
                        chunks = []
                        for (lo, hi) in ((512, 1024), (0, 512)):
                            chunks.append(dict(kT=skT(s_, h, lo, hi), W=512, names=["skT%d" % s_],
                                               vblocks=[(sVc(s_, b, h, 128), 128, "sVc%d" % s_) for b in range(lo // 128, hi // 128)],
                                               mask=None))
                        yield from sb_stream(slot, 64, qT[:, h, 0:64], ["qT"], chunks, o_ap, on, pv, total,
                                             (cr, "carry0_%d" % slot), hook)
                    yield ("dve", 0.3)
                    S.op("dve", lambda e: e.tensor_tensor(out=ybuf[0][0:64, h * 128:(h + 1) * 128], in0=ob[0:64, 0:128],
                                                          in1=gsil[0][0:64, h * 128:(h + 1) * 128], op=ALU.mult),
                         reads=[on, "gsil0"], writes=["y0_%d" % h])
                run_streams([(lambda sl, h=h: head_stream(sl, h)) for h in range(4)], SBW)
            load_w_out(l)
            y_out([tb], [rows])

        def p_phase(l):
            load_act(0, xT_all, "xT", 0, 256)
            nt = 16
            for Q in range(nt):
                slot = Q % 2
                if Q < 15:
                    load_act(1 - slot, xT_all, "xT", (Q + 1) * 256, 256)
                else:
                    load_act(1 - slot, xT_all, "xT", T, TS)
                for sub in range(2):
                    tb = 2 * Q + sub
                    project_block(l, slot, sub, tb, 128,
                                  lambda h, tb=tb: regB[:, 0:4 * T].rearrange("p (h t) -> p h t", t=T)[:, :, tb * 128:(tb + 1) * 128],
                                  ["kT%d_%d" % (h, tb) for h in range(4)],
                                  lambda tb=tb: Vaug_blk(tb), ["V%d" % tb], sub * 128)
                    if l == 0:
                        cumsum_block(tb)
                if l == 0:
                    fox_tile(Q)
                else:
                    sb_tile(Q)
                y_out([2 * Q, 2 * Q + 1], [128, 128])
            if nt == 16:
                sample_stage(l)
                sample_block(l, 0)

        def o_phase(l):
            wq = load_w_in(1) if l == 0 else iter(())
            wv = w_out()
            load_act(0, yT_all, "yT", 0, 256)
            xsrc = xc0 if l == 0 else xres[1].ap()
            xdst = xres[l + 1].ap()

            def o_load(tb):
                rows = blk_rows(tb)
                i = tb % 2
                S.op("sp", lambda e: e.dma_start(out=xcb[i][0:rows, :], in_=xsrc[tb * 128:tb * 128 + rows, :]),
                     reads=["xres%d" % l], writes=["kf32%d" % i], chan="kf32%d" % i)

            for Q in range(17):
                slot = Q % 2
                if Q < 15:
                    load_act(1 - slot, yT_all, "yT", (Q + 1) * 256, 256)
                elif Q == 15:
                    load_act(1 - slot, yT_all, "yT", T, TS)
                next(wq, None)
                for sub in range(2 if Q < 16 else 1):
                    tb = 2 * Q + sub
                    rows = blk_rows(tb)
                    i = tb % 2
                    if tb == 0:
                        o_load(0)
                    if tb + 1 < NB:
                        o_load(tb + 1)
                    for c in range(16):
                        S.op("pe", lambda e, c=c, slot=slot, sub=sub, rows=rows: e.matmul(
                            psA[0:rows, :], lhsT=actT[slot][:, c, sub * 128:sub * 128 + rows], rhs=wv[:, c, :], start=(c == 0),
                            stop=(c == 15)), reads=["actT%d" % slot, "w_out"], writes=["psA"], sig=(c == 15))
                    S.op("dve", lambda e, i=i, rows=rows: e.tensor_tensor(out=xnb[i][0:rows, :], in0=psA[0:rows, :],
                                                                          in1=xcb[i][0:rows, :], op=ALU.add),
                         reads=["psA", "kf32%d" % i], writes=["vf32%d" % i])
                    S.op("sp", lambda e, i=i, tb=tb, rows=rows: e.dma_start(out=xdst[tb * 128:tb * 128 + rows, :],
                                                                              in_=xnb[i][0:rows, :]),
                         reads=["vf32%d" % i], writes=["xres%d" % (l + 1)], chan="vf32%d" % i)
                    norm_prep(xnb[i], "vf32%d" % i, tb, rows, l + 1, l == 0)
            for _ in wq:
                pass

        stage = 99
        if stage >= 2:
            p_phase(0)
        if stage >= 4:
            load_nw(1)
            o_phase(0)
            rstd_from_ssq("1")
        if stage >= 5:
            p_phase(1)
        if stage >= 7:
            load_nw(2)
            o_phase(1)
            rstd_from_ssq("2")
            x2 = xres[2].ap()
            fin_in = [(kf32[0], "kf320"), (kf32[1], "kf321"), (thb[0], "th0"), (thb[1], "th1")]
            fin_out = [(vf32[0], "vf320"), (vf32[1], "vf321"), (thb[2], "th2"), (gtmp, "gtmp")]

            def f_load(tb):
                rows = blk_rows(tb)
                buf, nm = fin_in[tb % 4]
                S.op("sp", lambda e: e.dma_start(out=buf[0:rows, 0:512], in_=x2[tb * 128:tb * 128 + rows, :]),
                     reads=["xres2"], writes=[nm], chan=nm)

            for t_ in range(3):
                f_load(t_)
            for tb in range(NB):
                rows = blk_rows(tb)
                if tb + 3 < NB:
                    f_load(tb + 3)
                ib, inm = fin_in[tb % 4]
                ob_, onm = fin_out[tb % 4]
                S.op("dve", lambda e, tb=tb, rows=rows, ib=ib, ob_=ob_: e.scalar_tensor_tensor(
                    out=ob_[0:rows, 0:512], in0=ib[0:rows, 0:512], scalar=rstd[0:rows, tb:tb + 1], in1=nw[2][0:rows, :],
                    op0=ALU.mult, op1=ALU.mult), reads=[inm, "rstd", "nw"], writes=[onm])
                S.op("sp", lambda e, tb=tb, rows=rows, ob_=ob_: e.dma_start(out=yout[tb * 128:tb * 128 + rows, :],
                                                                             in_=ob_[0:rows, 0:512]),
                     reads=[onm], writes=[], chan=onm)
        S.op("sp", lambda e: e.dma_start(out=lfout[0:T, :].rearrange("(b p) h -> p b h", p=128), in_=lfo[:, 0:32, :]),
             reads=["lfo"], writes=[], chan="lfo")
        S.op("sp", lambda e: e.dma_start(out=lfout[T:TT, :], in_=lfo[0:TS, 32, :]), reads=["lfo"], writes=[], chan="lfo")
        for sn, v in list(S.cnt.items()):
            if sn.startswith("d_"):
                S.prog["sp"].append(("wait", sn, v))

        sem_names = sorted(S.cnt.keys())
        sems = {}
        for sn in sem_names:
            sems[sn] = es.enter_context(nc.semaphore(sn))
        block = es.enter_context(nc.Block())

        def emit(eng_obj, name):
            for item in S.prog[name]:
                if item[0] == "wait":
                    eng_obj.wait_ge(sems[item[1]], item[2])
                else:
                    _, fn, sn, inc = item
                    ins = fn(eng_obj)
                    if sn is not None:
                        if inc is None:
                            ins.then_inc(sems[sn])
                        else:
                            ins.then_inc(sems[sn], inc)

        @block.sync
        def _(e):
            emit(e, "sp")

        @block.scalar
        def _(e):
            emit(e, "act")

        @block.vector
        def _(e):
            emit(e, "dve")

        @block.gpsimd
        def _(e):
            emit(e, "pool")

        @block.tensor
        def _(e):
            emit(e, "pe")
    return nc


def _consts():
    c = np.zeros((128, NCST), np.float32)
    k = np.arange(128)[:, None]
    m = np.arange(128)[None, :]
    c[:, C_U:C_U + 128] = (k <= m)
    c[:, C_E127:C_E127 + 128] = (k == 127)
    c[:, C_UB16:C_UB16 + 128] = (k <= m) & (k // 16 == m // 16)
    c[:, C_LS:C_LS + 128] = (k > m)
    c[:, C_ONES:C_ONES + 128] = 1.0
    c[:, C_MS:C_MS + 128] = (m < k) & (k // 16 == m // 16)
    c[:, C_MSC:C_MSC + 128] = 1.0 - ((m < k) & (k // 16 == m // 16))
    c[:, C_MLT:C_MLT + 128] = (m < k)
    c[:, C_MLTC:C_MLTC + 128] = 1.0 - (m < k)
    c[:, C_ID:C_ID + 128] = (k == m)
    c[:, C_MLE:C_MLE + 128] = (k <= m)
    c[:, C_M64:C_M64 + 128] = (k <= m) & (k // 16 == m // 16)
    return c


_NC_CACHE = {}


def kernel(x_prompt, x_sample, cache_fox_k, cache_fox_v, cache_fox_logf, cache_sb_k, cache_sb_v,
           norm_0, w_in_0, b_f_0, w_out_0, norm_1, w_in_1, w_out_1, norm_f):
    f = np.float32
    A = lambda a: np.ascontiguousarray(np.asarray(a, dtype=f))
    x_prompt, x_sample = A(x_prompt), A(x_sample)
    w_in_0, w_in_1, w_out_0, w_out_1 = A(w_in_0), A(w_in_1), A(w_out_0), A(w_out_1)
    caches = [A(cache_fox_k), A(cache_fox_v), A(cache_sb_k), A(cache_sb_v)]
    cache_fox_logf = A(cache_fox_logf)
    norms = [A(norm_0), A(norm_1), A(norm_f)]
    b_f_0 = A(b_f_0)
    if "nc" not in _NC_CACHE:
        _NC_CACHE["nc"] = build_program()
    nc = _NC_CACHE["nc"]
    cst = _consts()
    in_maps = []
    for core in range(8):
        b, g = core // 4, core % 4
        cs = slice(g * 512, (g + 1) * 512)
        xc0 = np.concatenate([x_prompt[b][:, cs], x_sample[4 * b:4 * b + 4].reshape(64, D)[:, cs]], axis=0)
        m = {"xc0": A(xc0), "cst": cst}
        for i, nm in enumerate(["nw0", "nw1", "nwf"]):
            m[nm] = A(np.broadcast_to(norms[i][cs][None, :], (128, 512)))
        m["win0"] = A(np.concatenate([w_in_0[:, j * D + g * 512: j * D + (g + 1) * 512] for j in range(4)]
                                     + [w_in_0[:, 4 * D + 4 * g: 4 * D + 4 * g + 4]], axis=1))
        m["win1"] = A(np.concatenate([w_in_1[:, j * D + g * 512: j * D + (g + 1) * 512] for j in range(4)], axis=1))
        m["wout0"] = A(w_out_0[:, cs])
        m["wout1"] = A(w_out_1[:, cs])
        m["bfb"] = A(np.broadcast_to(b_f_0[4 * g:4 * g + 4][None, :], (128, 4)))
        for nm, cch in zip(["cfk", "cfv", "csk", "csv"], caches):
            m[nm] = A(cch[4 * b:4 * b + 4, :, 4 * g:4 * g + 4, :].reshape(4, PAST, 512))
        m["cfl"] = A(cache_fox_logf[4 * b:4 * b + 4, :, 4 * g:4 * g + 4])
        in_maps.append(m)
    res = run_bass_kernel_spmd(nc, in_maps, core_ids=list(range(8)))
    R = res.results
    y_prompt = np.zeros((2, T, D), f)
    y_sample = np.zeros((8, 16, D), f)
    pk = [np.zeros((2, T, 16, 128), f) for _ in range(4)]
    sk = [np.zeros((8, 16, 16, 128), f) for _ in range(4)]
    plf = np.zeros((2, T, 16), f)
    slf = np.zeros((8, 16, 16), f)
    for core in range(8):
        b, g = core // 4, core % 4
        cs = slice(g * 512, (g + 1) * 512)
        r = R[core]
        y_prompt[b][:, cs] = r["yout"][:T]
        y_sample[4 * b:4 * b + 4][:, :, cs] = r["yout"][T:].reshape(4, 16, 512)
        for i, nm in enumerate(["kf", "vf", "ks", "vs"]):
            pk[i][b][:, 4 * g:4 * g + 4, :] = r[nm][:T].reshape(T, 4, 128)
            sk[i][4 * b:4 * b + 4][:, :, 4 * g:4 * g + 4, :] = r[nm][T:].reshape(4, 16, 4, 128)
        plf[b][:, 4 * g:4 * g + 4] = r["lf"][:T]
        slf[4 * b:4 * b + 4][:, :, 4 * g:4 * g + 4] = r["lf"][T:].reshape(4, 16, 4)
    return (y_prompt, y_sample, pk[0], pk[1], plf, pk[2], pk[3], sk[0], sk[1], slf, sk[2], sk[3])
```

```python
import numpy as np
import concourse.bass as bass
import concourse.mybir as mybir
from concourse.bass_utils import run_bass_kernel_spmd

F32 = mybir.dt.float32
BF16 = mybir.dt.bfloat16
AF = mybir.ActivationFunctionType
ALU = mybir.AluOpType

D = 2048
T = 4096
TS = 64
TT = T + TS
NB = 33
PAST = 1024
SCALE = 128 ** -0.5
EPS = 1e-6
GROUPS = [[0, 1, 2, 3], [4, 5, 6, 7]]
ENG = ("sp", "act", "dve", "pool", "pe")
FOXW = 1
SBW = 4

C_U, C_E127, C_UB16, C_LS, C_ONES, C_MLT, C_MLTC, C_MS, C_MSC, C_ID, C_MLE, C_M64 = [128 * i for i in range(12)]
NCST = 128 * 12


class Sched:
    def __init__(self):
        self.prog = {e: [] for e in ENG}
        self.cnt = {}
        self.waited = {}
        self.lastw = {}
        self.readers = {}

    def _need(self, eng, events):
        for sn, val in events:
            if sn.startswith("d_"):
                val = self.cnt[sn]
            if sn == "pe" and eng == "pe":
                continue
            key = (eng, sn)
            if self.waited.get(key, 0) >= val:
                continue
            self.waited[key] = val
            self.prog[eng].append(("wait", sn, val))

    def op(self, eng, fn, reads=(), writes=(), sig=True, chan=None, cc=None):
        ps_reads = [r for r in reads if r.startswith("ps") and r not in writes]
        if ps_reads:
            writes = list(writes) + ps_reads
        ev = []
        for r in reads:
            if r in self.lastw:
                ev.append(self.lastw[r])
        for w in writes:
            if w in self.lastw:
                ev.append(self.lastw[w])
            ev.extend(self.readers.get(w, {}).items())
        self._need(eng, ev)
        if cc is not None:
            sn = "c_" + cc
            self.cnt[sn] = 1
            myev = (sn, 1)
            self.prog[eng].append(("op", fn, sn, None))
        elif chan is not None:
            sn = "d_" + chan
            self.cnt[sn] = self.cnt.get(sn, 0) + 16
            myev = (sn, self.cnt[sn])
            self.prog[eng].append(("op", fn, sn, 16))
        elif sig:
            self.cnt[eng] = self.cnt.get(eng, 0) + 1
            myev = (eng, self.cnt[eng])
            self.prog[eng].append(("op", fn, eng, 1))
        else:
            myev = (eng, self.cnt.get(eng, 0) + 1)
            self.prog[eng].append(("op", fn, None, 0))
        for r in reads:
            d = self.readers.setdefault(r, {})
            d[myev[0]] = max(d.get(myev[0], 0), myev[1])
        for w in writes:
            self.lastw[w] = myev
            self.readers[w] = {}


def build_program():
    nc = bass.Bass("TRN2", target_bir_lowering=False)
    S = Sched()

    def din(name, shape, dt=F32):
        return nc.dram_tensor(name, shape, dt, kind="ExternalInput").ap()

    def dout(name, shape, dt=F32):
        return nc.dram_tensor(name, shape, dt, kind="ExternalOutput").ap()

    xc0 = din("xc0", [TT, 512])
    nwd = [din("nw0", [128, 512]), din("nw1", [128, 512]), din("nwf", [128, 512])]
    wind = [din("win0", [D, 2052]), din("win1", [D, 2048])]
    woutd = [din("wout0", [D, 512]), din("wout1", [D, 512])]
    bfd = din("bfb", [128, 4])
    cstd = din("cst", [128, NCST])
    cfk = din("cfk", [4, PAST, 512])
    cfv = din("cfv", [4, PAST, 512])
    csk = din("csk", [4, PAST, 512])
    csv = din("csv", [4, PAST, 512])
    cfl = din("cfl", [4, PAST, 4])

    yout = dout("yout", [TT, 512])
    kvout = [[dout("kf", [TT, 512]), dout("vf", [TT, 512])], [dout("ks", [TT, 512]), dout("vs", [TT, 512])]]
    lfout = dout("lf", [TT, 4])

    PW = [1024, 1024, 1024, 1024, TS]
    xT_loc = [nc.dram_tensor("xT_loc%d" % p, [512, PW[p]], BF16) for p in range(5)]
    xT_all = [nc.dram_tensor("xT_all%d" % p, [D, PW[p]], BF16) for p in range(5)]
    yT_loc = [nc.dram_tensor("yT_loc%d" % p, [512, PW[p]], BF16) for p in range(5)]
    yT_all = [nc.dram_tensor("yT_all%d" % p, [D, PW[p]], BF16) for p in range(5)]

    def piece_of(tb):
        return (tb // 8, (tb % 8) * 128) if tb < 32 else (4, 0)
    ssq_loc = nc.dram_tensor("ssq_loc", [128, NB], F32)
    ssq_all = nc.dram_tensor("ssq_all", [512, NB], F32)
    xres = [None, nc.dram_tensor("x1s", [TT, 512], F32), nc.dram_tensor("x2s", [TT, 512], F32)]

    from contextlib import ExitStack
    es = ExitStack()

    def sb(name, shape, dt):
        return es.enter_context(nc.sbuf_tensor(name, shape, dt))

    def ps(name, shape, dt):
        return es.enter_context(nc.psum_tensor(name, shape, dt))

    with es:
        cstf = sb("cstf", [128, C_ID + 128], F32)
        cstb = sb("cstb", [128, 3 * 128], BF16)
        zeros = sb("zeros", [128, 512], BF16)
        nwt = sb("nwt", [128, 512], F32)
        nw = [nwt, nwt, nwt]
        bfb = sb("bfbs", [128, 4], F32)
        w_in = sb("w_in", [128, 16, 2052], BF16)
        regB = sb("regB", [128, 4 * T + 32 * 4 * 130], BF16)
        actT = [sb("actT0", [128, 16, 256], BF16), sb("actT1", [128, 16, 256], BF16)]
        ssq_part = sb("ssq_part", [128, NB], F32)
        ssq4 = sb("ssq4", [128, 4, NB], F32)
        ssum = sb("ssum", [128, NB], F32)
        rstd = sb("rstd", [128, NB], F32)
        nrstd = sb("nrstd", [128, NB], F32)
        hrstd = sb("hrstd", [128, NB], F32)
        cpall = sb("cpall", [128, NB, 8], F32)
        lfo = sb("lfo", [128, NB, 4], F32)
        lps = sb("lps", [128, 4], F32)
        xl = sb("xl", [128, 4], F32)
        el = sb("el", [128, 4], F32)
        q_tok = sb("q_tok", [128, 512], BF16)
        k_tok = sb("k_tok", [128, 512], BF16)
        kf32b_big = sb("kf32b", [128, 516], F32)
        kf32 = [sb("kf32a", [128, 512], F32)[:, :], kf32b_big[:, 0:512]]
        vf32 = [sb("vf32a", [128, 512], F32), sb("vf32b", [128, 512], F32)]
        gsil = [sb("gsil0", [128, 512], F32), sb("gsil1", [128, 512], F32)]
        gtmp = sb("gtmp", [128, 512], F32)
        qT = sb("qT", [128, 4, 256], BF16)
        ybuf = [sb("y0", [128, 512], BF16), sb("y1", [128, 512], BF16)]
        yTs = [sb("yTs0", [128, 4, 128], BF16), sb("yTs1", [128, 4, 128], BF16)]
        Pball = sb("Pball", [128, 6, 256], BF16)
        Pb = [Pball[:, i, :] for i in range(6)]
        biasT = sb("biasT", [128, 32, 4], F32)
        rden = sb("rden", [128, 4], F32)
        thb = [sb("th%d" % i, [128, 512], F32) for i in range(3)] + [gtmp]
        Rext = [sb("Rx%d" % i, [128, 516], F32) for i in range(3)] + [kf32b_big]
        ab = [sb("a%d" % i, [128, 512], BF16) for i in range(3)] + [k_tok]
        aTb = [sb("aT0", [128, 4, 128], BF16), sb("aT1", [128, 4, 128], BF16),
               Pball[:, 0:2, :].rearrange("p a (b t) -> p (a b) t", t=128),
               q_tok[:, :].rearrange("p (j t) -> p j t", t=128)]
        xcb = kf32
        xnb = vf32
        xtb = q_tok
        sqj = gtmp
        xTs = yTs
        Pw = [sb("Pw%d" % i, [128, 64], BF16) for i in range(4)]
        Pn = sb("Pn", [64, 64], BF16)
        clf = sb("clf", [128, 8, 16], F32)
        sufb = sb("sufb", [128, 8, 16], F32)
        cpn = sb("cpn", [64, 4], F32)
        aTw = [sb("aTw%d" % i, [128, 8, 64], BF16) for i in range(4)]
        aTn = sb("aTn", [64, 64], BF16)
        Vn_t = sb("Vn_t", [128, 520], BF16)
        carry0 = sb("carry0", [64, 4], F32)
        cbias = sb("cbias", [128, 2], F32)

        psA = ps("psA", [128, 512], F32)
        psB = ps("psB", [128, 512], F32)
        psC = ps("psC", [128, 512], F32)
        psD = ps("psD", [128, 512], F32)
        psE = ps("psE", [128, 512], F32)
        psF = ps("psF", [128, 512], F32)
        psG = ps("psG", [128, 512], F32)
        pst = ps("pst", [128, 1024], BF16)

        KT_OFF = 0
        V_OFF = 4 * T

        def kT(h, lo, hi):
            return regB[:, KT_OFF + h * T + lo: KT_OFF + h * T + hi]

        def Vaug(blk, h, n):
            o = V_OFF + (blk * 4 + h) * 130
            return regB[:, o:o + n]

        def Vaug_blk(blk):
            o = V_OFF + blk * 4 * 130
            return regB[:, o:o + 520].rearrange("p (h e) -> p h e", e=130)

        def w_out():
            o = 4096 + 8 * 520 + 4096
            return regB[:, o:o + 16 * 512].rearrange("p (c n) -> p c n", n=512)

        PST_ALL = ["pstbank"]
        pst_full = pst
        psE_bf = psE[:, :].bitcast(BF16)
        pst_f32 = pst[:, :].bitcast(F32)
        psG_bf = psG[:, :].bitcast(BF16)
        REGB_ALL = ["kT%d_%d" % (h, b) for h in range(4) for b in range(32)] + ["V%d" % b for b in range(32)]

        ident = cstb[:, 0:128]
        maskLE = cstb[:, 128:256]
        mask64 = cstb[0:64, 256:320]

        def cf(c0, n=128, rows=128):
            return cstf[0:rows, c0:c0 + n]

        S.op("sp", lambda e: e.dma_start(out=cstf[:], in_=cstd[:, 0:C_ID + 128]), writes=["cstf"], chan="cst")
        S.op("pool", lambda e: e.dma_start(out=cstb[:], in_=cstd[:, C_ID:C_ID + 384]), writes=["cstb"], chan="cstb")
        def load_nw(i):
            S.op("sp", lambda e: e.dma_start(out=nwt[:], in_=nwd[i][:, :]), writes=["nw"], chan="cst")
        load_nw(0)
        S.op("sp", lambda e: e.dma_start(out=bfb[:], in_=bfd[:, :]), writes=["bfb"], chan="cst")
        S.op("dve", lambda e: e.memset(zeros[:], 0.0), writes=["zeros"])
        S.op("dve", lambda e: e.memset(cbias[:, 0:1], EPS), writes=["cbias"])
        S.op("dve", lambda e: e.memset(cbias[:, 1:2], 1.0), writes=["cbias"])
        S.op("dve", lambda e: e.memset(ssq_part[:], 1.0), writes=["ssq_part"])
        S.op("dve", lambda e: e.memset(cpall[:], 0.0), writes=["cpall"])
        S.op("dve", lambda e: e.memset(lfo[:], 0.0), writes=["lfo"])
        for i in range(2):
            S.op("pool", lambda e, i=i: e.memset(thb[i][:], 0.0), writes=["th%d" % i])
        for i in range(4):
            S.op("pool", lambda e, i=i: e.memset(aTw[i][:], 0.0), writes=["aTw%d" % i])

        if False:
            for nm_, ap_ in [("win0", wind[0][0:128, 0:512]), ("win1", wind[1][0:128, 0:512]), ("wout0", woutd[0][0:128, :]),
                             ("wout1", woutd[1][0:128, :]), ("cfk", cfk[0][0:128, :]), ("cfv", cfv[0][0:128, :]),
                             ("csk", csk[0][0:128, :]), ("csv", csv[0][0:128, :])]:
                S.op("sp", lambda e, ap_=ap_: e.dma_start(out=gtmp[:], in_=ap_), writes=["gtmp"], chan="dbg")
            S.op("sp", lambda e: e.dma_start(out=gtmp[:, 0:4], in_=cfl[0][0:128, :]), writes=["gtmp"], chan="dbg")

        def load_w_in(l):
            ncol = 2052 if l == 0 else 2048
            src = wind[l].rearrange("(c p) n -> p c n", p=128)
            for c in range(16):
                yield
                si = c % 3
                o = 20544 + si * 4104
                stg = regB[:, o:o + 4104].bitcast(F32)
                S.op("sp", lambda e, c=c, stg=stg: e.dma_start(out=stg[:, 0:ncol], in_=src[:, c, :]),
                     writes=["wst%d" % si] + (REGB_ALL + ["sVc%d" % q for q in range(4)] + ["skT%d" % q for q in range(4)]
                                              if c < 3 else []), chan="wst%d" % si)
                if c % 2 == 0:
                    S.op("act", lambda e, c=c, stg=stg: e.activation(out=w_in[:, c, 0:ncol], in_=stg[:, 0:ncol], func=AF.Identity),
                         reads=["wst%d" % si] + (REGB_ALL if c >= 13 else []), writes=["w_in"])
                else:
                    S.op("dve", lambda e, c=c, stg=stg: e.tensor_copy(out=w_in[:, c, 0:ncol], in_=stg[:, 0:ncol]),
                         reads=["wst%d" % si] + (REGB_ALL if c >= 13 else []), writes=["w_in"])

        def load_w_out(l):
            src = woutd[l].rearrange("(c p) n -> p c n", p=128)
            wv = w_out()
            for c in range(0, 16, 4):
                S.op("pool", lambda e, c=c: e.dma_start(out=wv[:, c:c + 4, :], in_=src[:, c:c + 4, :]),
                     writes=["w_out"] + REGB_ALL + ["sVc%d" % q for q in range(4)] + ["skT%d" % q for q in range(4)], chan="wout")

        def blk_rows(tb):
            return 128 if tb < 32 else TS

        cnt = {"xts": 0, "tr": 0}
        deferred_g = []
        deferred_y = []

        def norm_prep(xblk, xres_name, tb, rows, nwi, want_xT, want_gather=True):
            S.op("act", lambda e: e.activation(out=sqj[0:rows, :], in_=xblk[0:rows, :], func=AF.Square,
                                               accum_out=ssq_part[0:rows, tb:tb + 1]),
                 reads=[xres_name], writes=["gtmp", "ssq_part"])
            if not want_xT:
                return
            S.op("dve", lambda e: e.tensor_tensor(out=xtb[0:rows, :], in0=xblk[0:rows, :], in1=nw[nwi][0:rows, :],
                                                  op=ALU.mult),
                 reads=[xres_name, "nw"], writes=["q_tok"])
            for c in range(4):
                S.op("pe", lambda e, c=c: e.transpose(out=pst[:, c * 128:c * 128 + rows],
                                                      in_=xtb[0:rows, c * 128:(c + 1) * 128],
                                                      identity=ident[0:rows, 0:rows]),
                     reads=["q_tok", "cstb"], writes=PST_ALL, sig=(c == 3))
            i = cnt["tr"] % 2
            cnt["tr"] += 1
            pv = pst[:, 0:512].rearrange("p (c t) -> p c t", t=128)
            S.op("act", lambda e: e.activation(out=xTs[i][:, :, 0:rows], in_=pv[:, :, 0:rows], func=AF.Identity),
                 reads=PST_ALL, writes=["yTs%d" % i])
            pc, c0 = piece_of(tb)
            dst = xT_loc[pc].ap().rearrange("(c p) t -> p c t", p=128)
            S.op("sp", lambda e: e.dma_start(out=dst[:, :, c0:c0 + rows], in_=xTs[i][:, :, 0:rows]),
                 reads=["yTs%d" % i], writes=["xT_loc%d" % pc], chan="yTs%d" % i)
            if want_gather and (tb % 8 == 7 or tb == 32):
                if pc >= 2:
                    deferred_g.append(pc)
                else:
                    gather(xT_loc[pc], xT_all[pc], "xT", pc)

        def rstd_from_ssq(tag):
            S.op("sp", lambda e: e.dma_start(out=ssq_loc.ap(), in_=ssq_part[:]), reads=["ssq_part"],
                 writes=["ssq_loc"], chan="ssq")
            S.op("pool", lambda e: e.collective_compute("AllGather", ALU.bypass, replica_groups=GROUPS,
                                                        ins=[ssq_loc.ap().opt()], outs=[ssq_all.ap().opt()]),
                 reads=["ssq_loc"], writes=["ssq_all"], cc="ssq" + tag)
            S.op("sp", lambda e: e.dma_start(out=ssq4[:], in_=ssq_all.ap().rearrange("(r p) b -> p r b", p=128)),
                 reads=["ssq_all"], writes=["ssq4"], chan="ssq")
            S.op("dve", lambda e: e.tensor_tensor(out=ssum[:], in0=ssq4[:, 0, :], in1=ssq4[:, 1, :], op=ALU.add),
                 reads=["ssq4"], writes=["ssum"])
            S.op("dve", lambda e: e.tensor_tensor(out=ssum[:], in0=ssum[:], in1=ssq4[:, 2, :], op=ALU.add),
                 reads=["ssq4", "ssum"], writes=["ssum"])
            S.op("dve", lambda e: e.tensor_tensor(out=ssum[:], in0=ssum[:], in1=ssq4[:, 3, :], op=ALU.add),
                 reads=["ssq4", "ssum"], writes=["ssum"])
            S.op("act", lambda e: e.activation(out=ssum[:], in_=ssum[:], func=AF.Ln, scale=1.0 / D, bias=cbias[:, 0:1]),
                 reads=["ssum", "cbias"], writes=["ssum"])
            S.op("act", lambda e: e.activation(out=rstd[:], in_=ssum[:], func=AF.Exp, scale=-0.5),
                 reads=["ssum"], writes=["rstd"])
            S.op("dve", lambda e: e.tensor_scalar(out=nrstd[:], in0=rstd[:], scalar1=-1.0, scalar2=None, op0=ALU.mult),
                 reads=["rstd"], writes=["nrstd"])
            S.op("dve", lambda e: e.tensor_scalar(out=hrstd[:], in0=rstd[:], scalar1=0.5, scalar2=None, op0=ALU.mult),
                 reads=["rstd"], writes=["hrstd"])
            S.op("dve", lambda e: e.memset(ssq_part[:], 1.0), reads=[], writes=["ssq_part"])
            for pc in deferred_g:
                gather(xT_loc[pc], xT_all[pc], "xT", pc)
            del deferred_g[:]

        gcount = {"n": 0}

        def gather(loc, allt, rname, pc):
            gcount["n"] += 1
            S.op("pool", lambda e: e.collective_compute("AllGather", ALU.bypass, replica_groups=GROUPS,
                                                        ins=[loc.ap().opt()], outs=[allt.ap().opt()]),
                 reads=["%s_loc%d" % (rname, pc)], writes=["%s_all%d" % (rname, pc)], cc="g%d" % gcount["n"])

        wq = load_w_in(0)

        def n0_load(tb):
            rows = blk_rows(tb)
            i = tb % 2
            S.op("sp", lambda e: e.dma_start(out=xcb[i][0:rows, :], in_=xc0[tb * 128:tb * 128 + rows, :]),
                 writes=["kf32%d" % i], chan="kf32%d" % i)

        for tb in range(NB):
            rows = blk_rows(tb)
            i = tb % 2
            if tb % 2 == 0:
                next(wq, None)
            if tb == 0:
                n0_load(0)
            if tb + 1 < NB:
                n0_load(tb + 1)
            norm_prep(xcb[i], "kf32%d" % i, tb, rows, 0, True)
        for _ in wq:
            pass
        rstd_from_ssq("0")

        def load_act(slot, src_all, rname, c0, ncols):
            pc, lc = (c0 // 1024, c0 % 1024) if c0 < T else (4, 0)
            src = src_all[pc].ap().rearrange("(c p) t -> p c t", p=128)
            S.op("sp", lambda e: e.dma_start(out=actT[slot][:, :, 0:ncols], in_=src[:, :, lc:lc + ncols]),
                 reads=["%s_all%d" % (rname, pc)], writes=["actT%d" % slot], chan="actT%d" % slot)

        def transposes_to(src_tok, src_name, rows, dst_fn, dst_names, evac_eng):
            for h in range(4):
                S.op("pe", lambda e, h=h: e.transpose(out=pst[:, h * 128:h * 128 + rows],
                                                      in_=src_tok[0:rows, h * 128:(h + 1) * 128],
                                                      identity=ident[0:rows, 0:rows]),
                     reads=[src_name, "cstb"], writes=PST_ALL, sig=(h == 3))
            pv4 = pst[:, 0:512].rearrange("p (h t) -> p h t", t=128)
            if evac_eng == "act":
                S.op("act", lambda e: e.activation(out=dst_fn(None), in_=pv4[:, :, 0:rows], func=AF.Identity),
                     reads=PST_ALL, writes=dst_names)
            else:
                S.op("dve", lambda e: e.tensor_copy(out=dst_fn(None), in_=pv4[:, :, 0:rows]), reads=PST_ALL, writes=dst_names)

        def project_block(l, slot, sub, tb, rows, kT_dst, kT_names, v_dst_fn, v_names, qcol0):
            banks = [psA, psB, psC, psD]
            bn = ["psA", "psB", "psC", "psD"]
            for c in range(16):
                lhsT = actT[slot][:, c, sub * 128: sub * 128 + rows]
                for j in range(4):
                    S.op("pe", lambda e, c=c, j=j, lhsT=lhsT: e.matmul(banks[j][0:rows, :], lhsT=lhsT,
                                                                        rhs=w_in[:, c, j * 512:(j + 1) * 512],
                                                                        start=(c == 0), stop=(c == 15)),
                         reads=["actT%d" % slot, "w_in"], writes=[bn[j]], sig=(c == 15))
                if l == 0:
                    S.op("pe", lambda e, c=c, lhsT=lhsT: e.matmul(psE[0:rows, 0:4], lhsT=lhsT,
                                                                   rhs=w_in[:, c, 2048:2052],
                                                                   start=(c == 0), stop=(c == 15)),
                         reads=["actT%d" % slot, "w_in"], writes=["psE"], sig=(c == 15))
            rs = rstd[0:rows, tb:tb + 1]
            nrs = nrstd[0:rows, tb:tb + 1]
            i = tb % 2
            S.op("dve", lambda e: e.tensor_scalar(out=q_tok[0:rows, :], in0=psA[0:rows, :], scalar1=rs, scalar2=None,
                                                  op0=ALU.mult),
                 reads=["psA", "rstd"], writes=["q_tok"])
            S.op("act", lambda e: e.activation(out=kf32[i][0:rows, :], in_=psB[0:rows, :], func=AF.Identity, scale=rs),
                 reads=["psB", "rstd"], writes=["kf32%d" % i])
            S.op("dve", lambda e: e.tensor_scalar(vf32[i][0:rows, :], psC[0:rows, :], rs, None, ALU.mult),
                 reads=["psC", "rstd"], writes=["vf32%d" % i])
            S.op("sp", lambda e: e.dma_start(out=kvout[l][0][tb * 128:tb * 128 + rows, :], in_=kf32[i][0:rows, :]),
                 reads=["kf32%d" % i], writes=[], chan="kf32%d" % i)
            S.op("sp", lambda e: e.dma_start(out=kvout[l][1][tb * 128:tb * 128 + rows, :], in_=vf32[i][0:rows, :]),
                 reads=["vf32%d" % i], writes=[], chan="vf32%d" % i)
            S.op("act", lambda e: e.activation(out=k_tok[0:rows, :], in_=psB[0:rows, :], func=AF.Identity, scale=rs),
                 reads=["psB", "rstd"], writes=["k_tok"])
            vd = v_dst_fn()
            S.op("dve", lambda e: e.tensor_copy(out=vd[0:rows, :, 0:128],
                                                in_=vf32[i][0:rows, :].rearrange("p (h d) -> p h d", d=128)),
                 reads=["vf32%d" % i], writes=v_names)
            S.op("pool", lambda e: e.memset(vd[0:rows, :, 128:129], 1.0), reads=[], writes=v_names)
            if l == 0:
                S.op("act", lambda e: e.activation(out=gtmp[0:rows, :], in_=psD[0:rows, :], func=AF.Exp, scale=nrs),
                     reads=["psD", "nrstd"], writes=["gtmp"])
                S.op("act", lambda e: e.activation(out=gtmp[0:rows, :], in_=gtmp[0:rows, :], func=AF.Ln, bias=cbias[0:rows, 1:2]),
                     reads=["gtmp", "cbias"], writes=["gtmp"])
                S.op("act", lambda e: e.activation(out=gtmp[0:rows, :], in_=gtmp[0:rows, :], func=AF.Exp, scale=-1.0),
                     reads=["gtmp"], writes=["gtmp"])
            else:
                S.op("act", lambda e: e.activation(out=gtmp[0:rows, :], in_=psD[0:rows, :], func=AF.Sigmoid, scale=rs),
                     reads=["psD", "rstd"], writes=["gtmp"])
            S.op("dve", lambda e: e.scalar_tensor_tensor(out=gsil[sub][0:rows, :], in0=psD[0:rows, :], scalar=rs,
                                                         in1=gtmp[0:rows, :], op0=ALU.mult, op1=ALU.mult),
                 reads=["psD", "rstd", "gtmp"], writes=["gsil%d" % sub])
            if l == 0:
                S.op("dve", lambda e: e.scalar_tensor_tensor(out=xl[0:rows, :], in0=psE[0:rows, 0:4], scalar=rs,
                                                             in1=bfb[0:rows, :], op0=ALU.mult, op1=ALU.add),
                     reads=["psE", "rstd", "bfb"], writes=["xl"])
                S.op("act", lambda e: e.activation(out=el[0:rows, :], in_=xl[0:rows, :], func=AF.Exp, scale=-1.0),
                     reads=["xl"], writes=["el"])
                S.op("act", lambda e: e.activation(out=lps[0:rows, :], in_=el[0:rows, :], func=AF.Ln, bias=cbias[0:rows, 1:2]),
                     reads=["el", "cbias"], writes=["lps"])
                S.op("pool", lambda e: e.tensor_scalar(out=lfo[0:rows, tb, :], in0=lps[0:rows, :], scalar1=-1.0,
                                                       scalar2=0.0, op0=ALU.mult, op1=ALU.add),
                     reads=["lps"], writes=["lfo"])
            transposes_to(q_tok, "q_tok", rows, lambda h: qT[:, :, qcol0:qcol0 + rows], ["qT"], "dve")
            transposes_to(k_tok, "k_tok", rows, kT_dst, kT_names, "act")

        def cumsum_block(tb):
            first = (tb == 0)
            S.op("pe", lambda e: e.matmul(psE[:, 8:12], lhsT=cf(C_U), rhs=lps[:, 0:4], start=True, stop=first),
                 reads=["cstf", "lps"], writes=["psE"], sig=first)
            if not first:
                S.op("pe", lambda e: e.matmul(psE[:, 8:12], lhsT=cf(C_E127), rhs=cpall[:, tb - 1, 0:4], start=False,
                                              stop=True),
                     reads=["cstf", "cpall"], writes=["psE"], sig=False)
                S.op("pe", lambda e: e.matmul(psE[:, 12:16], lhsT=cf(C_E127), rhs=cpall[:, tb - 1, 0:4], start=True,
                                              stop=True),
                     reads=["cstf", "cpall"], writes=["psE"])
                S.op("dve", lambda e: e.tensor_copy(out=cpall[:, tb, 0:8], in_=psE[:, 8:16]), reads=["psE"],
                     writes=["cpall"])
            else:
                S.op("dve", lambda e: e.tensor_copy(out=cpall[:, tb, 0:4], in_=psE[:, 8:12]), reads=["psE"],
                     writes=["cpall"])

        sc_banks = [(psA, "psA"), (psB, "psB")]
        o_banks = [((psC, "psC"), (psD, "psD")), ((psF, "psF"), (psG, "psG"))]

        def y_out(tbs, rows_l):
            for sub, tb in enumerate(tbs):
                rows = rows_l[sub]
                for c in range(4):
                    S.op("pe", lambda e, c=c, sub=sub, rows=rows: e.transpose(out=pst[:, c * 128:c * 128 + rows],
                                                                               in_=ybuf[sub][0:rows, c * 128:(c + 1) * 128],
                                                                               identity=ident[0:rows, 0:rows]),
                         reads=["y%d_%d" % (sub, hh) for hh in range(4)] + ["cstb"], writes=PST_ALL, sig=(c == 3))
                i = cnt["tr"] % 2
                cnt["tr"] += 1
                pv = pst[:, 0:512].rearrange("p (c t) -> p c t", t=128)
                S.op("act", lambda e, i=i, rows=rows, pv=pv: e.activation(out=yTs[i][:, :, 0:rows], in_=pv[:, :, 0:rows],
                                                                           func=AF.Identity),
                     reads=PST_ALL, writes=["yTs%d" % i])
                pc, c0 = piece_of(tb)
                dst = yT_loc[pc].ap().rearrange("(c p) t -> p c t", p=128)
                S.op("sp", lambda e, i=i, rows=rows, c0=c0, dst=dst: e.dma_start(out=dst[:, :, c0:c0 + rows],
                                                                                  in_=yTs[i][:, :, 0:rows]),
                     reads=["yTs%d" % i], writes=["yT_loc%d" % pc], chan="yTs%d" % i)
                if tb % 8 == 7 or tb == 32:
                    if pc == 3:
                        deferred_y.append(pc)
                    else:
                        for dpc in deferred_y:
                            gather(yT_loc[dpc], yT_all[dpc], "yT", dpc)
                        del deferred_y[:]
                        gather(yT_loc[pc], yT_all[pc], "yT", pc)

        pcount = {"p": 0, "sc": 0, "ch": 0}

        def run_streams(makers, width):
            pending = list(makers)
            active = []
            free = list(range(width))
            eng_free = {}
            now = 0.0
            while pending or active:
                while pending and free:
                    sl = free.pop(0)
                    g = pending.pop(0)(sl)
                    try:
                        nxt = next(g)
                    except StopIteration:
                        free.append(sl)
                        continue
                    active.append({"g": g, "sl": sl, "ready": now, "nxt": nxt})
                if not active:
                    continue
                best = min(active, key=lambda a: max(eng_free.get(a["nxt"][0], 0.0), a["ready"]))
                eng, dur = best["nxt"]
                start = max(eng_free.get(eng, 0.0), best["ready"])
                end = start + dur
                eng_free[eng] = end
                best["ready"] = end + 0.25
                try:
                    best["nxt"] = next(best["g"])
                except StopIteration:
                    active.remove(best)
                    free.append(best["sl"])
                    now = end

        def fox_stream(slot, Q, h):
            nkb = 2 * Q + 2
            ob = o_banks[slot]
            fsets = [[(psA, "psA"), (psB, "psB"), (psE, "psE")], [(pst_f32, "pstbank"), (psF, "psF"), (psG, "psG")]]
            batches = [list(range(k0, min(k0 + 3, nkb))) for k0 in range(0, nkb, 3)]

            def emit_qk(b):
                for i, kb in enumerate(batches[b]):
                    qlo = 128 if kb == nkb - 1 else 0
                    scb, scn = fsets[b % 2][i]
                    S.op("pe", lambda e, kb=kb, qlo=qlo, scb=scb: e.matmul(
                        scb[:, qlo:256], lhsT=kT(h, kb * 128, (kb + 1) * 128), rhs=qT[:, h, qlo:256], start=True, stop=True),
                        reads=["kT%d_%d" % (h, kb), "qT"], writes=[scn])

            yield ("pe", 0.4)
            emit_qk(0)
            for b, kbs in enumerate(batches):
                if b + 1 < len(batches):
                    emit_qk(b + 1)
                for i, kb in enumerate(kbs):
                    qlo = 128 if kb == nkb - 1 else 0
                    scb, scn = fsets[b % 2][i]
                    pi = 3 * (b % 2) + i
                    S.op("act", lambda e, kb=kb, qlo=qlo, pi=pi, scb=scb: e.activation(
                        out=Pb[pi][:, qlo:256], in_=scb[:, qlo:256], func=AF.Exp, scale=SCALE, bias=biasT[:, kb, h:h + 1]),
                        reads=[scn, "biasT"], writes=["P%d" % pi])
                    if kb >= nkb - 2:
                        dq = 0 if kb == nkb - 2 else 128
                        S.op("pool", lambda e, pi=pi, dq=dq: e.tensor_tensor(out=Pb[pi][:, dq:dq + 128],
                                                                             in0=Pb[pi][:, dq:dq + 128], in1=maskLE,
                                                                             op=ALU.mult),
                             reads=["P%d" % pi, "cstb"], writes=["P%d" % pi])
                for i, kb in enumerate(kbs):
                    pi = 3 * (b % 2) + i
                    for sub in range(2):
                        if kb == nkb - 1 and sub == 0:
                            continue
                        last = (kb == nkb - 2) if sub == 0 else (kb == nkb - 1)
                        S.op("pe", lambda e, kb=kb, sub=sub, pi=pi, last=last: e.matmul(
                            ob[sub][0][:, 0:129], lhsT=Pb[pi][:, sub * 128:(sub + 1) * 128], rhs=Vaug(kb, h, 129),
                            start=(kb == 0), stop=last),
                            reads=["P%d" % pi, "V%d" % kb], writes=[ob[sub][1]], sig=(sub == 1 or kb == nkb - 2))
            for sub in range(2):
                S.op("dve", lambda e, sub=sub: e.reciprocal(out=rden[:, 2 * slot + sub:2 * slot + sub + 1],
                                                            in_=ob[sub][0][:, 128:129]),
                     reads=[ob[sub][1]], writes=["rden%d" % slot])
                S.op("dve", lambda e, sub=sub: e.scalar_tensor_tensor(
                    out=ybuf[sub][:, h * 128:(h + 1) * 128], in0=ob[sub][0][:, 0:128],
                    scalar=rden[:, 2 * slot + sub:2 * slot + sub + 1],
                    in1=gsil[sub][:, h * 128:(h + 1) * 128], op0=ALU.mult, op1=ALU.mult),
                    reads=[ob[sub][1], "rden%d" % slot, "gsil%d" % sub], writes=["y%d_%d" % (sub, h)])

        def fox_tile(Q):
            nkb = 2 * Q + 2
            for h in range(4):
                S.op("dve", lambda e, h=h: e.tensor_scalar(out=biasT[:, 0:nkb, h], in0=cpall[:, 0:nkb, h],
                                                           scalar1=cpall[:, 2 * Q + 1, 4 + h:5 + h], scalar2=None,
                                                           op0=ALU.subtract),
                     reads=["cpall"], writes=["biasT"])
            run_streams([(lambda sl, h=h: fox_stream(sl, Q, h)) for h in range(4)], FOXW)

        def sb_stream(slot, qrows, qT_ap, qT_names, chunks, o_ap, o_name, pv, total_pv, carry_in=None, aT_hook=None,
                      fin=None, out_state=None):
            prev = carry_in
            th, Rx, a_ = thb[slot], Rext[slot], ab[slot]
            tn = ["th0", "th1", "th2", "gtmp"][slot]
            rn = ["Rx0", "Rx1", "Rx2", "kf321"][slot]
            an = ["a0", "a1", "a2", "k_tok"][slot]
            aTnames = [["aT0"], ["aT1"], ["P0", "P1"], ["q_tok"]][slot]
            scb, scn = [(psA, "psA"), (psB, "psB"), (psE, "psE"), (psG, "psG")][slot]
            pst, pn, pso = scb[:, :].bitcast(BF16), scn, 0
            for ci, ch in enumerate(chunks):
                W = ch["W"]
                yield ("pe", 0.45)
                S.op("pe", lambda e, ch=ch, W=W, scb=scb: e.matmul(scb[0:qrows, 0:W], lhsT=qT_ap, rhs=ch["kT"], start=True, stop=True),
                     reads=qT_names + ch["names"], writes=[scn])
                yield ("act", 0.65)
                S.op("act", lambda e, W=W, scb=scb: e.activation(out=th[0:qrows, 0:W], in_=scb[0:qrows, 0:W], func=AF.Sigmoid,
                                                        scale=-SCALE),
                     reads=[scn], writes=[tn])
                yield ("dve", 1.5)
                if ch.get("mask") is not None:
                    M, Mc, dw = ch["mask"]
                    S.op("dve", lambda e, W=W, dw=dw, M=M: e.tensor_tensor(out=th[0:qrows, W - dw:W], in0=th[0:qrows, W - dw:W],
                                                                           in1=M, op=ALU.mult),
                         reads=[tn, "cstf"], writes=[tn])
                    S.op("dve", lambda e, W=W, dw=dw, Mc=Mc: e.tensor_tensor(out=th[0:qrows, W - dw:W], in0=th[0:qrows, W - dw:W],
                                                                             in1=Mc, op=ALU.add),
                         reads=[tn, "cstf"], writes=[tn])
                if prev is None:
                    S.op("dve", lambda e, W=W: e.memset(Rx[0:qrows, W:W + 1], 1.0), reads=[], writes=[rn])
                else:
                    pR, pname = prev
                    S.op("dve", lambda e, W=W, pR=pR: e.tensor_copy(out=Rx[0:qrows, W:W + 1], in_=pR),
                         reads=[pname, an], writes=[rn])
                S.op("dve", lambda e, W=W: e.tensor_tensor_scan(
                    out=Rx[0:qrows, 0:W][:, ::-1], data0=th[0:qrows, 0:W][:, ::-1], data1=zeros[0:qrows, 0:W],
                    initial=Rx[0:qrows, W:W + 1], op0=ALU.mult, op1=ALU.add),
                    reads=[tn, rn, "zeros"], writes=[rn])
                prev = (Rx[0:qrows, 0:1], rn)
                if True:
                    yield ("pool", 1.3)
                S.op("pool", lambda e, W=W: e.tensor_tensor(out=a_[0:qrows, 0:W], in0=Rx[0:qrows, 1:W + 1],
                                                                                   in1=Rx[0:qrows, 0:W], op=ALU.subtract),
                     reads=[rn], writes=[an])
                yield ("pe", 0.7)
                vbl = ch["vblocks"]
                off = 0
                for j, (rhs, wk, vn) in enumerate(vbl):
                    S.op("pe", lambda e, off=off, wk=wk, j=j: e.transpose(out=pst[0:wk, pso + j * 128:pso + j * 128 + qrows],
                                                                          in_=a_[0:qrows, off:off + wk],
                                                                          identity=ident[0:qrows, 0:qrows]),
                         reads=[an, "cstb"], writes=[pn], sig=(j == len(vbl) - 1))
                    off += wk
                yield ("act", 0.65)
                if aT_hook is None:
                    aT = aTb[slot]
                    nb_ = len(vbl)
                    pvw = pst[:, pso:pso + nb_ * 128].rearrange("p (j t) -> p j t", t=128)
                    S.op("act", lambda e, aT=aT, pvw=pvw, nb_=nb_: e.activation(out=aT[:, 0:nb_, 0:qrows], in_=pvw[:, :, 0:qrows],
                                                                                func=AF.Identity),
                         reads=[pn], writes=aTnames)
                    lhs_list = [(aT[0:wk, j, 0:qrows], aTnames) for j, (_, wk, _) in enumerate(vbl)]
                else:
                    lhs_list = aT_hook(ci, vbl, pso, pn, pst)
                if aT_hook is None:
                    yield ("pe", 0.7)
                for j, (rhs, wk, vn) in enumerate(vbl):
                    lhsT, ln = lhs_list[j]
                    n = pv["n"]
                    S.op("pe", lambda e, lhsT=lhsT, rhs=rhs, n=n: e.matmul(o_ap, lhsT=lhsT, rhs=rhs, start=(n == 0),
                                                                           stop=(n == total_pv - 1)),
                         reads=(ln if isinstance(ln, list) else [ln]) + [vn], writes=[o_name], sig=True)
                    pv["n"] += 1
            if out_state is not None:
                out_state["carry"] = prev
            if fin is not None:
                yield ("dve", 0.3)
                fin()

        sb_obank = {0: [(psC, "psC")], 1: [(psD, "psD")], 2: [(psF, "psF")], 3: [(pst_f32, "pstbank")]}
        sb_ocnt = {0: 0, 1: 0}

        def sb_tile(Q):
            makers = []
            for sub in range(2):
                qb = 2 * Q + sub
                e_ = 128 * (qb + 1)
                for h in range(4):
                    def mk(slot, sub=sub, h=h, e_=e_):
                        chunks = []
                        hi = e_
                        while hi > 0:
                            lo = max(0, hi - 512)
                            vbl = [(Vaug(b, h, 128), 128, "V%d" % b) for b in range(lo // 128, hi // 128)]
                            chunks.append(dict(kT=kT(h, lo, hi), W=hi - lo,
                                               names=["kT%d_%d" % (h, b) for b in range(lo // 128, hi // 128)],
                                               vblocks=vbl,
                                               mask=(cf(C_MLT), cf(C_MLTC), 128) if hi == e_ else None))
                            hi = lo
                        ob, on = sb_obank[slot][0]

                        def fin():
                            S.op("dve", lambda e: e.tensor_tensor(out=ybuf[sub][:, h * 128:(h + 1) * 128], in0=ob[:, 0:128],
                                                                  in1=gsil[sub][:, h * 128:(h + 1) * 128], op=ALU.mult),
                                 reads=[on, "gsil%d" % sub], writes=["y%d_%d" % (sub, h)])
                        return sb_stream(slot, 128, qT[:, h, sub * 128:(sub + 1) * 128], ["qT"], chunks, ob[:, 0:128], on,
                                         {"n": 0}, sum(len(c["vblocks"]) for c in chunks), fin=fin)
                    makers.append(mk)
            run_streams(makers, SBW)

        SEQSZ = 8 * 520 + 4096
        assert 4 * SEQSZ <= 4 * T + 32 * 4 * 130
        kstage_f = actT[1][:, :, :].rearrange("p c t -> p (c t)").bitcast(F32).rearrange("p (j c) -> p j c", c=512)

        def sVc_all(s_):
            b0 = s_ * SEQSZ
            return regB[:, b0:b0 + 8 * 520].rearrange("p (j h e) -> p j h e", h=4, e=130)

        def sVc(s_, j, h, n):
            o = s_ * SEQSZ + (j * 4 + h) * 130
            return regB[:, o:o + n]

        def skT(s_, h, lo, hi):
            o = s_ * SEQSZ + 8 * 520 + h * 1024
            return regB[:, o + lo:o + hi]

        def Pw8(s_):
            v = thb[s_ // 2][:, :].bitcast(BF16)
            return v[:, (s_ % 2) * 512:(s_ % 2 + 1) * 512].rearrange("p (j c) -> p j c", c=64)

        def Vn_v():
            return Vn_t[:, :].rearrange("p (h e) -> p h e", e=130)

        def sample_stage(l):
            kcache, vcache = (cfk, cfv) if l == 0 else (csk, csv)
            for s_ in range(4):
                for j in range(8):
                    S.op("pool", lambda e, s_=s_, j=j: e.dma_start(
                        out=sVc_all(s_)[:, j, :, 0:128],
                        in_=vcache[s_][j * 128:(j + 1) * 128, :].rearrange("p (h d) -> p h d", d=128)),
                         writes=["sVc%d" % s_] + (REGB_ALL + ["w_out"] if j == 0 else []), chan="sVc")
                S.op("pool", lambda e, s_=s_: e.memset(sVc_all(s_)[:, :, :, 128:129], 1.0), writes=["sVc%d" % s_])
                for hf in range(2):
                    S.op("sp", lambda e, s_=s_, hf=hf: e.dma_start(
                        out=kstage_f, in_=kcache[s_][hf * 512:(hf + 1) * 512, :].rearrange("(j p) c -> p j c", p=128)),
                         writes=["actT1"], chan="sKc")
                    for jj in range(4):
                        j = hf * 4 + jj
                        for h in range(4):
                            S.op("pe", lambda e, jj=jj, h=h: e.transpose(out=pst_f32[:, h * 128:(h + 1) * 128],
                                                                         in_=kstage_f[:, jj, h * 128:(h + 1) * 128],
                                                                         identity=cstf[:, C_ID:C_ID + 128]),
                                 reads=["actT1", "cstf"], writes=PST_ALL, sig=(h == 3))
                        kb_ = s_ * SEQSZ + 8 * 520
                        S.op("act", lambda e, kb_=kb_, j=j: e.activation(
                            out=regB[:, kb_:kb_ + 4096].rearrange("p (h t) -> p h t", t=1024)[:, :, j * 128:(j + 1) * 128],
                            in_=pst_f32[:, 0:512].rearrange("p (h t) -> p h t", t=128), func=AF.Identity),
                             reads=PST_ALL, writes=["skT%d" % s_] + (REGB_ALL + ["w_out"] if j == 0 else []))
            if l == 0:
                for s_ in range(4):
                    S.op("sp", lambda e, s_=s_: e.dma_start(out=clf[:, :, s_ * 4:(s_ + 1) * 4],
                                                            in_=cfl[s_].rearrange("(j p) h -> p j h", p=128)),
                         writes=["clf"], chan="clf")
                for j in range(8):
                    S.op("pe", lambda e, j=j: e.matmul(psE[:, 16 + 16 * j:32 + 16 * j], lhsT=cf(C_LS), rhs=clf[:, j, :], start=True,
                                                       stop=(j == 7)), reads=["cstf", "clf"], writes=["psE"], sig=(j == 7))
                    for j2 in range(j + 1, 8):
                        S.op("pe", lambda e, j=j, j2=j2: e.matmul(psE[:, 16 + 16 * j:32 + 16 * j], lhsT=cf(C_ONES), rhs=clf[:, j2, :],
                                                                  start=False, stop=(j2 == 7)), reads=["cstf", "clf"],
                             writes=["psE"], sig=(j2 == 7))
                S.op("dve", lambda e: e.tensor_copy(out=sufb[:].rearrange("p j c -> p (j c)"), in_=psE[:, 16:144]), reads=["psE"],
                     writes=["sufb"])

        def sample_block(l, slot):
            tb = 32
            rows = TS
            Vn = Vn_v()
            project_block(l, slot, 0, tb, rows, lambda h: qT[:, :, 128:128 + rows], ["qT"],
                          Vn_v, ["Vn"], 0)
            if l == 0:
                S.op("pe", lambda e: e.matmul(psE[0:64, 8:12], lhsT=cf(C_UB16, 64, 64), rhs=lps[0:64, 0:4], start=True, stop=True),
                     reads=["cstf", "lps"], writes=["psE"])
                S.op("dve", lambda e: e.tensor_copy(out=cpn[:, :], in_=psE[0:64, 8:12]), reads=["psE"], writes=["cpn"])
            for h in range(4):
                ob, on = o_banks[h % 2][0]
                if l == 0:
                    o_ap = ob[0:64, 0:129]
                    first = True
                    for s_ in range(4):
                        scb, scn = sc_banks[pcount["sc"] % 2]
                        pcount["sc"] += 1
                        pw = Pw8(s_)
                        pwn = "th%d" % (s_ // 2)
                        for j in range(8):
                            S.op("pe", lambda e, s_=s_, j=j, h=h, scb=scb: e.matmul(
                                scb[:, j * 16:(j + 1) * 16], lhsT=skT(s_, h, j * 128, (j + 1) * 128),
                                rhs=qT[:, h, s_ * 16:(s_ + 1) * 16], start=True, stop=True),
                                reads=["skT%d" % s_, "qT"], writes=[scn], sig=(j == 7))
                        for j in range(8):
                            S.op("act", lambda e, s_=s_, j=j, h=h, scb=scb, pw=pw: e.activation(
                                out=pw[:, j, s_ * 16:(s_ + 1) * 16], in_=scb[:, j * 16:(j + 1) * 16], func=AF.Exp, scale=SCALE,
                                bias=sufb[:, j, s_ * 4 + h:s_ * 4 + h + 1]), reads=[scn, "sufb"], writes=[pwn])
                        for j in range(8):
                            S.op("pe", lambda e, s_=s_, j=j, h=h, first=first, o_ap=o_ap, pw=pw: e.matmul(
                                o_ap, lhsT=pw[:, j, 0:64], rhs=sVc(s_, j, h, 129), start=first, stop=False),
                                reads=[pwn, "sVc%d" % s_], writes=[on], sig=(j == 7))
                            first = False
                    scb, scn = sc_banks[pcount["sc"] % 2]
                    pcount["sc"] += 1
                    S.op("pe", lambda e, h=h, scb=scb: e.matmul(scb[0:64, 0:64], lhsT=qT[:, h, 128:192], rhs=qT[:, h, 0:64],
                                                                start=True, stop=True), reads=["qT"], writes=[scn])
                    S.op("act", lambda e, h=h, scb=scb: e.activation(out=Pn[:, :], in_=scb[0:64, 0:64], func=AF.Exp, scale=SCALE,
                                                                     bias=cpn[:, h:h + 1]), reads=[scn, "cpn"], writes=["Pn"])
                    S.op("pool", lambda e: e.tensor_tensor(out=Pn[:, :], in0=Pn[:, :], in1=mask64, op=ALU.mult),
                         reads=["Pn", "cstb"], writes=["Pn"])
                    S.op("pe", lambda e, h=h, o_ap=o_ap: e.matmul(o_ap, lhsT=Pn[:, :], rhs=Vn[0:64, h, 0:129], start=False, stop=True),
                         reads=["Pn", "Vn"], writes=[on])
                    S.op("dve", lambda e, ob=ob: e.reciprocal(out=rden[0:64, 0:1], in_=ob[0:64, 128:129]), reads=[on],
                         writes=["rden0"])
                    S.op("dve", lambda e, h=h, ob=ob: e.scalar_tensor_tensor(
                        out=ybuf[0][0:64, h * 128:(h + 1) * 128], in0=ob[0:64, 0:128], scalar=rden[0:64, 0:1],
                        in1=gsil[0][0:64, h * 128:(h + 1) * 128], op0=ALU.mult, op1=ALU.mult),
                        reads=[on, "rden0", "gsil0"], writes=["y0_%d" % h])
                else:
                    pass
            if l == 1:
                def head_stream(slot, h):
                    ob, on = sb_obank[slot][0]
                    o_ap = ob[0:64, 0:128]
                    pv = {"n": 0}
                    total = 1 + 4 * 8

                    def hook0(ci, vbl, pso, pn, pst_=None):
                        S.op("act", lambda e: e.activation(out=aTn[:, :], in_=pst_[0:64, pso:pso + 64], func=AF.Identity),
                             reads=[pn], writes=["aTn"])
                        return [(aTn[:, :], "aTn")]
                    ch0 = dict(kT=qT[:, h, 128:192], W=64, names=["qT"], vblocks=[(Vn[0:64, h, 0:128], 64, "Vn")],
                               mask=(cf(C_MS, 64, 64), cf(C_MSC, 64, 64), 64))
                    st = {}
                    yield from sb_stream(slot, 64, qT[:, h, 0:64], ["qT"], [ch0], o_ap, on, pv, total, None, hook0, None, st)
                    car = st["carry"]
                    cr = carry0[:, slot:slot + 1]
                    yield ("dve", 0.1)
                    S.op("dve", lambda e: e.tensor_copy(out=cr, in_=car[0]), reads=[car[1]], writes=["carry0_%d" % slot])
                    for s_ in range(4):
                        def hook(ci, vbl, pso, pn, pst_=None, s_=s_):
                            base = 4 if ci == 0 else 0
                            pvw = pst_[:, pso:pso + 512].rearrange("p (j t) -> p j t", t=128)
                            S.op("act", lambda e: e.activation(out=aTw[s_][:, base:base + 4, s_ * 16:(s_ + 1) * 16],
                                                               in_=pvw[:, :, s_ * 16:(s_ + 1) * 16], func=AF.Identity),
                                 reads=[pn], writes=["aTw%d" % s_])
                            return [(aTw[s_][:, base + j, 0:64], "aTw%d" % s_) for j in range(4)]
                        chunks = []
                        for (lo, hi) in ((512, 1024), (0, 512)):
                            chunks.append(dict(kT=skT(s_, h, lo, hi), W=512, names=["skT%d" % s_],
                                               vblocks=[(sVc(s_, b, h, 128), 128, "sVc%d" % s_) for b in range(lo // 128, hi // 128)],
                                               mask=None))
                        yield from sb_stream(slot, 64, qT[:, h, 0:64], ["qT"], chunks, o_ap, on, pv, total,
                                             (cr, "carry0_%d" % slot), hook)
                    yield ("dve", 0.3)
                    S.op("dve", lambda e: e.tensor_tensor(out=ybuf[0][0:64, h * 128:(h + 1) * 128], in0=ob[0:64, 0:128],
                                                          in1=gsil[0][0:64, h * 128:(h + 1) * 128], op=ALU.mult),
                         reads=[on, "gsil0"], writes=["y0_%d" % h])
                run_streams([(lambda sl, h=h: head_stream(sl, h)) for h in range(4)], SBW)
            load_w_out(l)
            y_out([tb], [rows])

        def p_phase(l):
            load_act(0, xT_all, "xT", 0, 256)
            nt = 16
            for Q in range(nt):
                slot = Q % 2
                if Q < 15:
                    load_act(1 - slot, xT_all, "xT", (Q + 1) * 256, 256)
                else:
                    load_act(1 - slot, xT_all, "xT", T, TS)
                for sub in range(2):
                    tb = 2 * Q + sub
                    project_block(l, slot, sub, tb, 128,
                                  lambda h, tb=tb: regB[:, 0:4 * T].rearrange("p (h t) -> p h t", t=T)[:, :, tb * 128:(tb + 1) * 128],
                                  ["kT%d_%d" % (h, tb) for h in range(4)],
                                  lambda tb=tb: Vaug_blk(tb), ["V%d" % tb], sub * 128)
                    if l == 0:
                        cumsum_block(tb)
                if l == 0:
                    fox_tile(Q)
                else:
                    sb_tile(Q)
                y_out([2 * Q, 2 * Q + 1], [128, 128])
            if nt == 16:
                sample_stage(l)
                sample_block(l, 0)

        def o_phase(l):
            wq = load_w_in(1) if l == 0 else iter(())
            wv = w_out()
            load_act(0, yT_all, "yT", 0, 256)
            xsrc = xc0 if l == 0 else xres[1].ap()
            xdst = xres[l + 1].ap()

            def o_load(tb):
                rows = blk_rows(tb)
                i = tb % 2
                S.op("sp", lambda e: e.dma_start(out=xcb[i][0:rows, :], in_=xsrc[tb * 128:tb * 128 + rows, :]),
                     reads=["xres%d" % l], writes=["kf32%d" % i], chan="kf32%d" % i)

            for Q in range(17):
                slot = Q % 2
                if Q < 15:
                    load_act(1 - slot, yT_all, "yT", (Q + 1) * 256, 256)
                elif Q == 15:
                    load_act(1 - slot, yT_all, "yT", T, TS)
                next(wq, None)
                for sub in range(2 if Q < 16 else 1):
                    tb = 2 * Q + sub
                    rows = blk_rows(tb)
                    i = tb % 2
                    if tb == 0:
                        o_load(0)
                    if tb + 1 < NB:
                        o_load(tb + 1)
                    for c in range(16):
                        S.op("pe", lambda e, c=c, slot=slot, sub=sub, rows=rows: e.matmul(
                            psA[0:rows, :], lhsT=actT[slot][:, c, sub * 128:sub * 128 + rows], rhs=wv[:, c, :], start=(c == 0),
                            stop=(c == 15)), reads=["actT%d" % slot, "w_out"], writes=["psA"], sig=(c == 15))
                    S.op("dve", lambda e, i=i, rows=rows: e.tensor_tensor(out=xnb[i][0:rows, :], in0=psA[0:rows, :],
                                                                          in1=xcb[i][0:rows, :], op=ALU.add),
                         reads=["psA", "kf32%d" % i], writes=["vf32%d" % i])
                    S.op("sp", lambda e, i=i, tb=tb, rows=rows: e.dma_start(out=xdst[tb * 128:tb * 128 + rows, :],
                                                                              in_=xnb[i][0:rows, :]),
                         reads=["vf32%d" % i], writes=["xres%d" % (l + 1)], chan="vf32%d" % i)
                    norm_prep(xnb[i], "vf32%d" % i, tb, rows, l + 1, l == 0)
            for _ in wq:
                pass

        stage = 99
        if stage >= 2:
            p_phase(0)
        if stage >= 4:
            load_nw(1)
            o_phase(0)
            rstd_from_ssq("1")
        if stage >= 5:
            p_phase(1)
        if stage >= 7:
            load_nw(2)
            o_phase(1)
            rstd_from_ssq("2")
            x2 = xres[2].ap()
            fin_in = [(kf32[0], "kf320"), (kf32[1], "kf321"), (thb[0], "th0"), (thb[1], "th1")]
            fin_out = [(vf32[0], "vf320"), (vf32[1], "vf321"), (thb[2], "th2"), (gtmp, "gtmp")]

            def f_load(tb):
                rows = blk_rows(tb)
                buf, nm = fin_in[tb % 4]
                S.op("sp", lambda e: e.dma_start(out=buf[0:rows, 0:512], in_=x2[tb * 128:tb * 128 + rows, :]),
                     reads=["xres2"], writes=[nm], chan=nm)

            for t_ in range(3):
                f_load(t_)
            for tb in range(NB):
                rows = blk_rows(tb)
                if tb + 3 < NB:
                    f_load(tb + 3)
                ib, inm = fin_in[tb % 4]
                ob_, onm = fin_out[tb % 4]
                S.op("dve", lambda e, tb=tb, rows=rows, ib=ib, ob_=ob_: e.scalar_tensor_tensor(
                    out=ob_[0:rows, 0:512], in0=ib[0:rows, 0:512], scalar=rstd[0:rows, tb:tb + 1], in1=nw[2][0:rows, :],
                    op0=ALU.mult, op1=ALU.mult), reads=[inm, "rstd", "nw"], writes=[onm])
                S.op("sp", lambda e, tb=tb, rows=rows, ob_=ob_: e.dma_start(out=yout[tb * 128:tb * 128 + rows, :],
                                                                             in_=ob_[0:rows, 0:512]),
                     reads=[onm], writes=[], chan=onm)
        S.op("sp", lambda e: e.dma_start(out=lfout[0:T, :].rearrange("(b p) h -> p b h", p=128), in_=lfo[:, 0:32, :]),
             reads=["lfo"], writes=[], chan="lfo")
        S.op("sp", lambda e: e.dma_start(out=lfout[T:TT, :], in_=lfo[0:TS, 32, :]), reads=["lfo"], writes=[], chan="lfo")
        for sn, v in list(S.cnt.items()):
            if sn.startswith("d_"):
                S.prog["sp"].append(("wait", sn, v))

        sem_names = sorted(S.cnt.keys())
        sems = {}
        for sn in sem_names:
            sems[sn] = es.enter_context(nc.semaphore(sn))
        block = es.enter_context(nc.Block())

        def emit(eng_obj, name):
            for item in S.prog[name]:
                if item[0] == "wait":
                    eng_obj.wait_ge(sems[item[1]], item[2])
                else:
                    _, fn, sn, inc = item
                    ins = fn(eng_obj)
                    if sn is not None:
                        if inc is None:
                            ins.then_inc(sems[sn])
                        else:
                            ins.then_inc(sems[sn], inc)

        @block.sync
        def _(e):
            emit(e, "sp")

        @block.scalar
        def _(e):
            emit(e, "act")

        @block.vector
        def _(e):
            emit(e, "dve")

        @block.gpsimd
        def _(e):
            emit(e, "pool")

        @block.tensor
        def _(e):
            emit(e, "pe")
    return nc


def _consts():
    c = np.zeros((128, NCST), np.float32)
    k = np.arange(128)[:, None]
    m = np.arange(128)[None, :]
    c[:, C_U:C_U + 128] = (k <= m)
    c[:, C_E127:C_E127 + 128] = (k == 127)
    c[:, C_UB16:C_UB16 + 128] = (k <= m) & (k // 16 == m // 16)
    c[:, C_LS:C_LS + 128] = (k > m)
    c[:, C_ONES:C_ONES + 128] = 1.0
    c[:, C_MS:C_MS + 128] = (m < k) & (k // 16 == m // 16)
    c[:, C_MSC:C_MSC + 128] = 1.0 - ((m < k) & (k // 16 == m // 16))
    c[:, C_MLT:C_MLT + 128] = (m < k)
    c[:, C_MLTC:C_MLTC + 128] = 1.0 - (m < k)
    c[:, C_ID:C_ID + 128] = (k == m)
    c[:, C_MLE:C_MLE + 128] = (k <= m)
    c[:, C_M64:C_M64 + 128] = (k <= m) & (k // 16 == m // 16)
    return c


_NC_CACHE = {}


def kernel(x_prompt, x_sample, cache_fox_k, cache_fox_v, cache_fox_logf, cache_sb_k, cache_sb_v,
           norm_0, w_in_0, b_f_0, w_out_0, norm_1, w_in_1, w_out_1, norm_f):
    f = np.float32
    A = lambda a: np.ascontiguousarray(np.asarray(a, dtype=f))
    x_prompt, x_sample = A(x_prompt), A(x_sample)
    w_in_0, w_in_1, w_out_0, w_out_1 = A(w_in_0), A(w_in_1), A(w_out_0), A(w_out_1)
    caches = [A(cache_fox_k), A(cache_fox_v), A(cache_sb_k), A(cache_sb_v)]
    cache_fox_logf = A(cache_fox_logf)
    norms = [A(norm_0), A(norm_1), A(norm_f)]
    b_f_0 = A(b_f_0)
    if "nc" not in _NC_CACHE:
        _NC_CACHE["nc"] = build_program()
    nc = _NC_CACHE["nc"]
    cst = _consts()
    in_maps = []
    for core in range(8):
        b, g = core // 4, core % 4
        cs = slice(g * 512, (g + 1) * 512)
        xc0 = np.concatenate([x_prompt[b][:, cs], x_sample[4 * b:4 * b + 4].reshape(64, D)[:, cs]], axis=0)
        m = {"xc0": A(xc0), "cst": cst}
        for i, nm in enumerate(["nw0", "nw1", "nwf"]):
            m[nm] = A(np.broadcast_to(norms[i][cs][None, :], (128, 512)))
        m["win0"] = A(np.concatenate([w_in_0[:, j * D + g * 512: j * D + (g + 1) * 512] for j in range(4)]
                                     + [w_in_0[:, 4 * D + 4 * g: 4 * D + 4 * g + 4]], axis=1))
        m["win1"] = A(np.concatenate([w_in_1[:, j * D + g * 512: j * D + (g + 1) * 512] for j in range(4)], axis=1))
        m["wout0"] = A(w_out_0[:, cs])
        m["wout1"] = A(w_out_1[:, cs])
        m["bfb"] = A(np.broadcast_to(b_f_0[4 * g:4 * g + 4][None, :], (128, 4)))
        for nm, cch in zip(["cfk", "cfv", "csk", "csv"], caches):
            m[nm] = A(cch[4 * b:4 * b + 4, :, 4 * g:4 * g + 4, :].reshape(4, PAST, 512))
        m["cfl"] = A(cache_fox_logf[4 * b:4 * b + 4, :, 4 * g:4 * g + 4])
        in_maps.append(m)
    res = run_bass_kernel_spmd(nc, in_maps, core_ids=list(range(8)))
    R = res.results
    y_prompt = np.zeros((2, T, D), f)
    y_sample = np.zeros((8, 16, D), f)
    pk = [np.zeros((2, T, 16, 128), f) for _ in range(4)]
    sk = [np.zeros((8, 16, 16, 128), f) for _ in range(4)]
    plf = np.zeros((2, T, 16), f)
    slf = np.zeros((8, 16, 16), f)
    for core in range(8):
        b, g = core // 4, core % 4
        cs = slice(g * 512, (g + 1) * 512)
        r = R[core]
        y_prompt[b][:, cs] = r["yout"][:T]
        y_sample[4 * b:4 * b + 4][:, :, cs] = r["yout"][T:].reshape(4, 16, 512)
        for i, nm in enumerate(["kf", "vf", "ks", "vs"]):
            pk[i][b][:, 4 * g:4 * g + 4, :] = r[nm][:T].reshape(T, 4, 128)
            sk[i][4 * b:4 * b + 4][:, :, 4 * g:4 * g + 4, :] = r[nm][T:].reshape(4, 16, 4, 128)
        plf[b][:, 4 * g:4 * g + 4] = r["lf"][:T]
        slf[4 * b:4 * b + 4][:, :, 4 * g:4 * g + 4] = r["lf"][T:].reshape(4, 16, 4)
    return (y_prompt, y_sample, pk[0], pk[1], plf, pk[2], pk[3], sk[0], sk[1], slf, sk[2], sk[3])
```
